# Optimizing a Trainium2 kernel written in Bass

```python
import jax, jax.numpy as jnp
from jax import lax
import numpy as np

D_MODEL = 4096
BATCH = 2
SEQ = 4096
DEPTH = 2

GRID_W = 64
CTX_LEN = 256
HEAD_DIM = 128
W_MIX = D_MODEL
W_GROUP = W_MIX // 4
N_HEADS = W_GROUP // HEAD_DIM
KV_HEADS = N_HEADS // 4
Q_PER_KV = N_HEADS // KV_HEADS
CHUNK = 128
Q_BLOCK = 128
CONV_WIDTH = 31
CONV_PAD = CONV_WIDTH // 2
ROPE_THETA = 10000.0
ROPE_PAIRS = HEAD_DIM // 4
N_DIR = 2
EPS = 1e-6
PROJ_SIZES = (
    W_GROUP, W_GROUP, W_GROUP,
    W_GROUP, W_GROUP, W_GROUP,
    W_GROUP, KV_HEADS * HEAD_DIM, KV_HEADS * HEAD_DIM, W_GROUP,
    W_GROUP, W_GROUP, W_GROUP, W_GROUP, W_GROUP,
    N_DIR * 2 * N_HEADS,
)
P_IN = sum(PROJ_SIZES)
SPLIT_IDX = tuple(int(s) for s in np.cumsum(PROJ_SIZES)[:-1])

kernel_name = 'hymba_style_diffusion_hybrid'


def rmsnorm(x, g):
    xf = x.astype(jnp.float32)
    y = xf * lax.rsqrt(jnp.mean(xf * xf, axis=-1, keepdims=True) + EPS)
    return (y * g.astype(jnp.float32)).astype(x.dtype)


def layernorm(x, g, b):
    xf = x.astype(jnp.float32)
    mu = jnp.mean(xf, axis=-1, keepdims=True)
    xc = xf - mu
    y = xc * lax.rsqrt(jnp.mean(xc * xc, axis=-1, keepdims=True) + EPS)
    return (y * g.astype(jnp.float32) + b.astype(jnp.float32)).astype(x.dtype)


def heads(t, h):
    return t.reshape(t.shape[0], t.shape[1], h, HEAD_DIM)


def axial_rope(n):
    rows = n // GRID_W
    row = jnp.repeat(jnp.arange(rows), GRID_W)
    col = jnp.tile(jnp.arange(GRID_W), rows)
    freq = ROPE_THETA ** (-jnp.arange(ROPE_PAIRS, dtype=jnp.float32) / ROPE_PAIRS)
    ang = jnp.stack([row, col], axis=-1).astype(jnp.float32)[..., None] * freq
    return jnp.cos(ang), jnp.sin(ang)


def apply_rope(x, cos, sin):
    B_, n, H, d = x.shape
    xr = x.astype(jnp.float32).reshape(B_, n, H, 2, 2, ROPE_PAIRS)
    x1, x2 = xr[..., 0, :], xr[..., 1, :]
    c, s = cos[:, None], sin[:, None]
    out = jnp.stack([x1 * c - x2 * s, x2 * c + x1 * s], axis=-2)
    return out.reshape(B_, n, H, d).astype(x.dtype)


def chunk_gmlp(u, v, w_s, b_s, ln_g, ln_b):
    B_, T, W = v.shape
    vh = layernorm(v, ln_g, ln_b).reshape(B_, T // CHUNK, CHUNK, N_HEADS, HEAD_DIM)
    mixed = jnp.einsum('hpq,bnqhd->bnphd', w_s, vh) + b_s.T[None, None, :, :, None]
    return u * mixed.reshape(B_, T, W)


def conformer_conv(a, g, conv_w, conv_b, ln_g, ln_b):
    y = a * jax.nn.sigmoid(g)
    y = lax.conv_general_dilated(y, conv_w[:, None, :], (1,), [(CONV_PAD, CONV_PAD)],
                                 dimension_numbers=('NWC', 'WIO', 'NWC'),
                                 feature_group_count=y.shape[-1]) + conv_b
    return jax.nn.silu(layernorm(y, ln_g, ln_b))


def gqa_attend(q, k, v):
    B_, Lq = q.shape[:2]
    qg = q.reshape(B_, Lq, KV_HEADS, Q_PER_KV, HEAD_DIM)
    s = jnp.einsum('bqgrd,bkgd->bgrqk', qg, k).astype(jnp.float32) * HEAD_DIM ** -0.5
    p = jax.nn.softmax(s, axis=-1).astype(v.dtype)
    return jnp.einsum('bgrqk,bkgd->bqgrd', p, v).reshape(B_, Lq, W_GROUP)


def attend_latent(q, k_all, v_all):
    B_, N = q.shape[:2]
    nb = N // Q_BLOCK
    qb = jnp.moveaxis(q.reshape(B_, nb, Q_BLOCK, N_HEADS, HEAD_DIM), 1, 0)
    out = lax.map(lambda qq: gqa_attend(qq, k_all, v_all), qb)
    return jnp.moveaxis(out, 0, 1).reshape(B_, N, W_GROUP)


def mlstm_scan(q, k, v, ig, lf, state):
    B_, H, T, d = q.shape
    nc = T // CHUNK

    def to_chunks(t):
        return jnp.moveaxis(t.reshape(t.shape[:2] + (nc, CHUNK) + t.shape[3:]), 2, 0)

    causal = jnp.tril(jnp.ones((CHUNK, CHUNK), dtype=bool))

    def step(carry, inp):
        C, n, m = carry
        qc, kc, vc, ic, fc = inp
        b = jnp.cumsum(fc, axis=-1)
        dmat = jnp.where(causal, b[..., :, None] - b[..., None, :] + ic[..., None, :], -jnp.inf)
        inter = b + m[..., None]
        m_row = jnp.maximum(inter, jnp.max(dmat, axis=-1))
        w_intra = jnp.exp(dmat - m_row[..., None])
        w_inter = jnp.exp(inter - m_row)
        s = jnp.einsum('bhld,bhsd->bhls', qc, kc) * w_intra
        num = jnp.einsum('bhls,bhsd->bhld', s, vc) + w_inter[..., None] * jnp.einsum('bhvk,bhlk->bhlv', C, qc)
        den = jnp.sum(s, axis=-1) + w_inter * jnp.einsum('bhk,bhlk->bhl', n, qc)
        h = num / jnp.maximum(jnp.abs(den), jnp.exp(-m_row))[..., None]
        w_end, decay = w_intra[..., -1, :], w_inter[..., -1]
        C_new = decay[..., None, None] * C + jnp.einsum('bhs,bhsv,bhsk->bhvk', w_end, vc, kc)
        n_new = decay[..., None] * n + jnp.einsum('bhs,bhsk->bhk', w_end, kc)
        return (C_new, n_new, m_row[..., -1]), h

    state, h = lax.scan(step, state, (to_chunks(q), to_chunks(k), to_chunks(v), to_chunks(ig), to_chunks(lf)))
    return jnp.moveaxis(h, 0, 2).reshape(B_, H, T, d), state


def mlstm_inputs(qd, kd, vd, gates, i_bias, f_bias):
    B_, T, _ = qd.shape
    hd = lambda t: t.reshape(B_, T, N_HEADS, HEAD_DIM).transpose(0, 2, 1, 3).astype(jnp.float32)
    g = gates.reshape(B_, T, N_DIR, 2, N_HEADS).astype(jnp.float32)
    ig = jnp.transpose(g[..., 0, :] + i_bias.astype(jnp.float32), (2, 0, 3, 1))
    lf = jax.nn.log_sigmoid(jnp.transpose(g[..., 1, :] + f_bias.astype(jnp.float32), (2, 0, 3, 1)))
    return hd(qd), hd(kd) * HEAD_DIM ** -0.5, hd(vd), ig, lf


def bidir_mlstm(ctx_in, lat_in):
    qc, kc, vc, igc, lfc = ctx_in
    ql, kl, vl, igl, lfl = lat_in
    B_, H, _, d = qc.shape
    zero = (jnp.zeros((B_, H, d, d), jnp.float32), jnp.zeros((B_, H, d), jnp.float32),
            jnp.zeros((B_, H), jnp.float32))
    fl = lambda t: jnp.flip(t, axis=2)
    h_cf, st_f = mlstm_scan(qc, kc, vc, igc[0], lfc[0], zero)
    h_lf, _ = mlstm_scan(ql, kl, vl, igl[0], lfl[0], st_f)
    h_cb, st_b = mlstm_scan(fl(qc), fl(kc), fl(vc), fl(igc[1]), fl(lfc[1]), zero)
    h_lb, _ = mlstm_scan(fl(ql), fl(kl), fl(vl), fl(igl[1]), fl(lfl[1]), st_b)
    return h_cf + fl(h_cb), h_lf + fl(h_lb)


def mlstm_out(h, o, z, mh_norm_g):
    B_, H, T, d = h.shape
    hn = rmsnorm(h.transpose(0, 2, 1, 3), mh_norm_g.reshape(N_HEADS, HEAD_DIM)).reshape(B_, T, W_GROUP)
    return hn.astype(o.dtype) * jax.nn.sigmoid(o) * jax.nn.silu(z)


def hybrid_layer(x, ctx, c_act, cctx_act, w_ada, b_ada, norm_g, w_in, sgu_w, sgu_b, sgu_ln_g, sgu_ln_b,
                 conv_w, conv_b, conv_ln_g, conv_ln_b, q_norm_g, k_norm_g, i_bias, f_bias, mh_norm_g,
                 w_out, last):
    shift, scale, gate = jnp.split(c_act @ w_ada + b_ada, 3, axis=-1)
    shift_c, scale_c, gate_c = jnp.split(cctx_act @ w_ada + b_ada, 3, axis=-1)
    h_lat = rmsnorm(x, norm_g) * (1 + scale[:, None]) + shift[:, None]
    h_ctx = rmsnorm(ctx, norm_g) * (1 + scale_c) + shift_c
    pl = jnp.split(h_lat @ w_in, SPLIT_IDX, axis=-1)
    pc = jnp.split(h_ctx @ w_in, SPLIT_IDX, axis=-1)

    def local_branches(p):
        a = chunk_gmlp(p[0], p[1], sgu_w, sgu_b, sgu_ln_g, sgu_ln_b) * jax.nn.silu(p[2])
        b = conformer_conv(p[3], p[4], conv_w, conv_b, conv_ln_g, conv_ln_b) * jax.nn.silu(p[5])
        return a, b

    cos, sin = axial_rope(x.shape[1])
    q_l = apply_rope(rmsnorm(heads(pl[6], N_HEADS), q_norm_g), cos, sin)
    k_l = apply_rope(rmsnorm(heads(pl[7], KV_HEADS), k_norm_g), cos, sin)
    k_c = rmsnorm(heads(pc[7], KV_HEADS), k_norm_g)
    v_c = heads(pc[8], KV_HEADS)
    k_all = jnp.concatenate([k_c, k_l], axis=1)
    v_all = jnp.concatenate([v_c, heads(pl[8], KV_HEADS)], axis=1)
    att_l = attend_latent(q_l, k_all, v_all) * jax.nn.silu(pl[9])

    hd_c, hd_l = bidir_mlstm(mlstm_inputs(pc[10], pc[11], pc[12], pc[15], i_bias, f_bias),
                             mlstm_inputs(pl[10], pl[11], pl[12], pl[15], i_bias, f_bias))
    mem_l = mlstm_out(hd_l, pl[13], pl[14], mh_norm_g)

    a_l, b_l = local_branches(pl)
    x = x + gate[:, None] * (jnp.concatenate([a_l, b_l, att_l, mem_l], axis=-1) @ w_out)
    if not last:
        a_c, b_c = local_branches(pc)
        q_c = rmsnorm(heads(pc[6], N_HEADS), q_norm_g)
        att_c = gqa_attend(q_c, k_c, v_c) * jax.nn.silu(pc[9])
        mem_c = mlstm_out(hd_c, pc[13], pc[14], mh_norm_g)
        ctx = ctx + gate_c * (jnp.concatenate([a_c, b_c, att_c, mem_c], axis=-1) @ w_out)
    return x, ctx


def setup_inputs(seed: int = 0) -> dict:
    key = jax.random.key(seed)
    ks = jax.random.split(key, 24)
    nrm = lambda k, shape: jax.random.normal(k, shape, jnp.float32)
    f_base = jnp.linspace(3.0, 6.0, N_HEADS, dtype=jnp.float32)
    return {
        'x': nrm(ks[0], (BATCH, SEQ, D_MODEL)),
        'c': nrm(ks[1], (BATCH, D_MODEL)),
        'ctx': nrm(ks[2], (BATCH, CTX_LEN, D_MODEL)),
        'c_ctx': nrm(ks[3], (D_MODEL,)),
        'w_ada': nrm(ks[4], (DEPTH, D_MODEL, 3 * D_MODEL)) * (0.5 * D_MODEL ** -0.5),
        'b_ada': 0.02 * nrm(ks[5], (DEPTH, 3 * D_MODEL)),
        'norm_g': 1.0 + 0.02 * nrm(ks[6], (DEPTH, D_MODEL)),
        'w_in': nrm(ks[7], (DEPTH, D_MODEL, P_IN)) * D_MODEL ** -0.5,
        'sgu_w': nrm(ks[8], (DEPTH, N_HEADS, CHUNK, CHUNK)) * CHUNK ** -0.5,
        'sgu_b': 1.0 + 0.02 * nrm(ks[9], (DEPTH, N_HEADS, CHUNK)),
        'sgu_ln_g': 1.0 + 0.02 * nrm(ks[10], (DEPTH, W_GROUP)),
        'sgu_ln_b': 0.02 * nrm(ks[11], (DEPTH, W_GROUP)),
        'conv_w': nrm(ks[12], (DEPTH, CONV_WIDTH, W_GROUP)) * CONV_WIDTH ** -0.5,
        'conv_b': 0.02 * nrm(ks[13], (DEPTH, W_GROUP)),
        'conv_ln_g': 1.0 + 0.02 * nrm(ks[14], (DEPTH, W_GROUP)),
        'conv_ln_b': 0.02 * nrm(ks[15], (DEPTH, W_GROUP)),
        'q_norm_g': 1.0 + 0.02 * nrm(ks[16], (DEPTH, HEAD_DIM)),
        'k_norm_g': 1.0 + 0.02 * nrm(ks[17], (DEPTH, HEAD_DIM)),
        'mlstm_i_bias': 0.1 * nrm(ks[18], (DEPTH, N_DIR, N_HEADS)),
        'mlstm_f_bias': f_base + 0.1 * nrm(ks[19], (DEPTH, N_DIR, N_HEADS)),
        'mh_norm_g': 1.0 + 0.02 * nrm(ks[20], (DEPTH, W_GROUP)),
        'w_out': nrm(ks[21], (DEPTH, W_MIX, D_MODEL)) * W_MIX ** -0.5,
    }


def reference(x, c, ctx, c_ctx, w_ada, b_ada, norm_g, w_in, sgu_w, sgu_b, sgu_ln_g, sgu_ln_b, conv_w,
              conv_b, conv_ln_g, conv_ln_b, q_norm_g, k_norm_g, mlstm_i_bias, mlstm_f_bias, mh_norm_g, w_out):
    c_act = jax.nn.silu(c)
    cctx_act = jax.nn.silu(c_ctx)
    for l in range(DEPTH):
        x, ctx = hybrid_layer(x, ctx, c_act, cctx_act, w_ada[l], b_ada[l], norm_g[l], w_in[l], sgu_w[l],
                              sgu_b[l], sgu_ln_g[l], sgu_ln_b[l], conv_w[l], conv_b[l], conv_ln_g[l],
                              conv_ln_b[l], q_norm_g[l], k_norm_g[l], mlstm_i_bias[l], mlstm_f_bias[l],
                              mh_norm_g[l], w_out[l], l == DEPTH - 1)
    return x
```

```python
import math
import numpy as np
import ml_dtypes
from contextlib import ExitStack
import concourse.bass as bass
import concourse.mybir as mybir
from concourse.bass_utils import run_bass_kernel_spmd

F32 = mybir.dt.float32
BF16 = mybir.dt.bfloat16
AF = mybir.ActivationFunctionType
ALU = mybir.AluOpType
AX = mybir.AxisListType

D = 4096
PIN = 13856
DEPTH = 2
NT = 1280
NL = 1024
EPS = 1e-6
LNSC = math.log(128.0 ** -0.5)

TOKB = [("Av", 1024, 1024, 0), ("Cq", 6144, 1024, 1024), ("Ckv", 7168, 512, 2048),
        ("Dk", 9728, 1024, 2560), ("Dv", 10752, 1024, 3584), ("G", 13824, 32, 4608)]
NTOKC = 4640
TC = {n: d for n, _, _, d in TOKB}
CHB = [("Au", 0, 0), ("Az", 2048, 1024), ("Ba", 3072, 2048), ("Bg", 4096, 3072), ("Bz", 5120, 4096),
       ("Cz", 7680, 5120), ("Dq", 8704, 6144), ("DkT", 9728, 7168), ("Do", 11776, 8192), ("Dz", 12800, 9216)]
NCHR = 10240
CR = {n: r for n, _, r in CHB}

EXC_ROWS = [512, 512, 512, 512, 64]
EX_HALO_OFF = 0
EX_DT_OFF = 128 * 240


class Res:
    __slots__ = ("name", "w", "r")

    def __init__(self, name):
        self.name = name
        self.w = None
        self.r = {}


class KB:
    CE = ["pe", "act", "dve", "pool"]

    def __init__(self, nc, es):
        self.nc = nc
        self.engs = ["pe", "act", "dve", "pool", "sp"]
        self.q = {e: [] for e in self.engs}
        self.sem = {e: es.enter_context(nc.semaphore("s_" + e)) for e in self.CE}
        self.cnt = {e: 0 for e in self.CE}
        self.known = {e: {} for e in self.engs}
        self.dsem = {"sp": [es.enter_context(nc.semaphore("d_sp%d" % i)) for i in range(16)],
                     "pool": [es.enter_context(nc.semaphore("d_pl%d" % i)) for i in range(8)]}
        self.dcnt = {"sp": 0, "pool": 0}
        self.duse = {k: [0] * len(v) for k, v in self.dsem.items()}
        self.dlast = {k: [None] * len(v) for k, v in self.dsem.items()}
        self.ccsem = es.enter_context(nc.semaphore("s_cc"))
        self.cccnt = 0
        self.cclast = None

    def _wait(self, eng, ev):
        if ev is None:
            return
        key, sem, val = ev
        if self.known[eng].get(key, 0) >= val:
            return
        self.known[eng][key] = val
        self.q[eng].append(("w", sem, val))

    def _deps(self, eng, reads, writes):
        deps = {}

        def add(ev):
            if ev is None:
                return
            k = ev[0]
            if k not in deps or deps[k][2] < ev[2]:
                deps[k] = ev
        for r in reads:
            add(r.w)
        for w in writes:
            add(w.w)
            for ev in w.r.values():
                add(ev)
        for k, ev in deps.items():
            if eng == "pe" and k == "pe":
                continue
            self._wait(eng, ev)

    def _mark(self, ev, reads, writes):
        for r in reads:
            r.r[ev[0]] = ev
        for w in writes:
            w.w = ev
            w.r = {}

    def op(self, eng, fn, reads=(), writes=()):
        self._deps(eng, reads, writes)
        self.cnt[eng] += 1
        ev = (eng, self.sem[eng], self.cnt[eng])
        self.q[eng].append(("o", fn, self.sem[eng]))
        self._mark(ev, reads, writes)
        return ev

    def dma(self, qn, out, in_, reads=(), writes=()):
        self._deps(qn, reads, writes)
        n = len(self.dsem[qn])
        slot = self.dcnt[qn] % n
        self.dcnt[qn] += 1
        self._wait(qn, self.dlast[qn][slot])
        self.duse[qn][slot] += 1
        sem = self.dsem[qn][slot]
        ev = ("%s%d" % (qn, slot), sem, 16 * self.duse[qn][slot])
        self.dlast[qn][slot] = ev
        self.q[qn].append(("d", out, in_, sem))
        self._mark(ev, reads, writes)
        return ev

    def all_events(self):
        evs = [(e, self.sem[e], self.cnt[e]) for e in self.CE if self.cnt[e] > 0]
        for qn in self.dlast:
            evs += [ev for ev in self.dlast[qn] if ev is not None]
        if self.cclast is not None:
            evs.append(self.cclast)
        return evs

    def barrier(self):
        evs = self.all_events()
        for eng in self.engs:
            for ev in evs:
                if eng == "pe" and ev[0] == "pe":
                    continue
                self._wait(eng, ev)

    def collective(self, kind, alu, groups, in_ap, out_ap):
        self.barrier()
        self.cccnt += 1
        self.q["pool"].append(("c", kind, alu, groups, in_ap, out_ap))
        self.cclast = ("cc", self.ccsem, self.cccnt)
        self.barrier()

    def replay(self, block):
        def mk(eng):
            def f(e):
                for it in self.q[eng]:
                    k = it[0]
                    if k == "w":
                        e.wait_ge(it[1], it[2])
                    elif k == "o":
                        it[1](e).then_inc(it[2], 1)
                    elif k == "d":
                        e.dma_start(out=it[1], in_=it[2]).then_inc(it[3], 16)
                    elif k == "c":
                        e.collective_compute(it[1], it[2], replica_groups=it[3], ins=[it[4]],
                                             outs=[it[5]]).then_inc(self.ccsem)
            return f
        block.sync(mk("sp"))
        block.gpsimd(mk("pool"))
        block.scalar(mk("act"))
        block.vector(mk("dve"))
        block.tensor(mk("pe"))


class Arena:
    def __init__(self, ap, nfloats):
        self.ap = ap
        self.n = nfloats
        self.top = 0
        self.kb = None

    def mark(self):
        return self.top

    def release(self, m):
        if self.kb is not None:
            self.kb.barrier()
        self.top = m

    def f32(self, n):
        n4 = (n + 7) // 8 * 8
        assert self.top + n4 <= self.n, ("SBUF arena overflow", self.top, n4, self.n)
        v = self.ap[:, self.top:self.top + n]
        self.top += n4
        return v

    def bf16(self, n):
        nf = (n + 1) // 2
        return self.f32(nf).bitcast(BF16)[:, 0:n]


def build_program(debug=None, stop_after=None, nlayers=DEPTH):
    debug = debug or set()
    nc = bass.Bass("TRN2", target_bir_lowering=False)

    def din(name, shape, dt=F32):
        return nc.dram_tensor(name, list(shape), dt, kind="ExternalInput").ap()

    def dscr(name, shape, dt=F32):
        kind = "ExternalOutput" if name in debug else "Internal"
        return nc.dram_tensor(name, list(shape), dt, kind=kind)

    x_in = din("x", [NL, D])
    ctx_in = din("ctx", [256, D])
    ccT_in = din("ccT", [1024, 2])
    wada_in = din("w_ada_s", [DEPTH, 1024, 3 * D])
    bada_in = din("b_ada", [DEPTH, 3 * D])
    normg_in = din("norm_g", [DEPTH, D])
    win_in = din("w_in", [DEPTH, D, PIN])
    sguw_in = din("sgu_w", [DEPTH, 8, 128, 128])
    sgub_in = din("sgu_b", [DEPTH, 1024])
    sglg_in = din("sgu_ln_g", [DEPTH, 1024])
    sglb_in = din("sgu_ln_b", [DEPTH, 1024])
    convw_in = din("conv_w", [DEPTH, 248, 128])
    convb_in = din("conv_b", [DEPTH, 8, 128])
    cvlg_in = din("conv_ln_g", [DEPTH, 8, 128])
    cvlb_in = din("conv_ln_b", [DEPTH, 8, 128])
    qng_in = din("q_norm_g", [DEPTH, 128])
    kng_in = din("k_norm_g", [DEPTH, 128])
    ib_in = din("mlstm_i_bias", [DEPTH, 16])
    fb_in = din("mlstm_f_bias", [DEPTH, 16])
    mhg_in = din("mh_norm_g", [DEPTH, 8, 128])
    wout_in = din("w_out", [DEPTH, D, D])
    consts_in = din("consts", [128, 5 * 128])
    rope_in = din("rope", [128, 8, 128])
    sel_in = din("sel", [128, 12])
    y_out = nc.dram_tensor("y", [NL, D], F32, kind="ExternalOutput").ap()

    ada_part = nc.dram_tensor("ada_part", [4, 3 * D], F32)
    ada_full = nc.dram_tensor("ada_full", [4, 3 * D], F32)
    dbg_ada = dscr("dbg_ada", [4, 3 * D]) if "dbg_ada" in debug else None
    xs = dscr("xs", [NT, D])
    Ptok = dscr("Ptok", [NT, NTOKC])
    Pch = dscr("Pch", [NCHR, NT])
    Ych = dscr("Ych", [1024, NT])
    exp_bufs = [nc.dram_tensor("exp_buf%d" % c, [EXC_ROWS[c], 512], F32) for c in range(5)]
    gat_bufs = [nc.dram_tensor("gat_buf%d" % c, [4 * EXC_ROWS[c], 512], F32) for c in range(5)]
    dbg_gat = None
    dbg_mixT = dscr("dbg_mixT", [128, 32 * NT], BF16) if "dbg_mixT" in debug else None
    dbg_HT = dscr("dbg_HT", [128, 32 * NT], BF16) if "dbg_HT" in debug else None

    exp_flat = [b_.ap().rearrange("r c -> (r c)") for b_ in exp_bufs]
    gat_flat = [b_.ap().rearrange("r c -> (r c)") for b_ in gat_bufs]

    def exr(c, off, n):
        return exp_flat[c][off:off + n]

    def gar(c, r, off, n):
        o = r * EXC_ROWS[c] * 512 + off
        return gat_flat[c][o:o + n]

    GROUPS = [[0, 1, 2, 3], [4, 5, 6, 7]]

    with ExitStack() as es:
        ARN = 50 * 1024
        arena_t = es.enter_context(nc.sbuf_tensor("arena", [128, ARN], F32))
        A = Arena(arena_t[:, :], ARN)
        psum_t = es.enter_context(nc.psum_tensor("psum", [128, 4096], F32))
        PS = psum_t[:, :]
        kb = KB(nc, es)
        A.kb = kb
        BK = [Res("bank%d" % i) for i in range(8)]

        def bank(i, n=512):
            return PS[:, i * 512:i * 512 + n]

        def v3(ap, a):
            return ap.rearrange("p (a b) -> p a b", a=a)

        CONST = A.f32(4 * 128)
        IDF = CONST[:, 0:128]
        ONF = CONST[:, 128:256]
        TRIF = CONST[:, 256:384]
        TRIB = CONST[:, 384:512]
        IDB = A.bf16(128)
        ONB = A.bf16(128)
        SEL = A.f32(12)
        ROPE = A.f32(8 * 128)
        rCONST = Res("const")
        kb.dma("sp", CONST, consts_in[:, 0:512], writes=[rCONST])
        kb.dma("sp", SEL, sel_in[:, :], writes=[rCONST])
        kb.dma("sp", ROPE, rope_in.rearrange("p a b -> p (a b)"), writes=[rCONST])
        kb.op("dve", lambda e: e.tensor_copy(IDB, IDF), reads=[rCONST], writes=[rCONST])
        kb.op("dve", lambda e: e.tensor_copy(ONB, ONF), reads=[rCONST], writes=[rCONST])
        ROPE3 = v3(ROPE, 8)
        kb.barrier()

        rBIGd = Res("bigd")
        rBIGa = Res("biga")
        MIXd = dscr("MIXd", [32 * 128, NT], BF16)
        MST = [A.bf16(NT) for _ in range(2)]
        rMST = [Res("mst0"), Res("mst1")]
        mst_i = [0]

        def mix_stage():
            i = mst_i[0] % 2
            mst_i[0] += 1
            return MST[i], rMST[i]

        def mix_store(kc, stg, rstg):
            kb.dma("sp", MIXd.ap()[kc * 128:(kc + 1) * 128, 0:NTL_cur[0]], stg[:, 0:NTL_cur[0]], reads=[rstg])
        NTL_cur = [NT]
        dumped = set()

        def dump(name, ap2d, shape, dt=F32, reads=()):
            if name not in debug or name in dumped:
                return
            dumped.add(name)
            dtn = nc.dram_tensor(name, list(shape), dt, kind="ExternalOutput")
            kb.dma("sp", dtn.ap(), ap2d, reads=list(reads))

        def transpose_rows(src, R, dst, bk, rsrc, rdst):
            kb.op("pe", lambda e: e.matmul(bank(bk, R), lhsT=src[0:R, :], rhs=IDF[0:R, 0:R], start=True, stop=True),
                  reads=[rsrc, rCONST], writes=[BK[bk]])
            kb.op("dve", lambda e: e.tensor_copy(dst, bank(bk, R)), reads=[BK[bk]], writes=[rdst])

        m0 = A.mark()
        CCT = A.f32(16)
        rCCT = Res("cct")
        kb.dma("sp", v3(CCT, 8), ccT_in.rearrange("(kc kp) r -> kp kc r", kp=128), writes=[rCCT])
        kb.op("act", lambda e: e.activation(out=CCT, in_=CCT, func=AF.Silu), reads=[rCCT], writes=[rCCT])
        CCT3 = v3(CCT, 8)
        WA = [A.f32(4096) for _ in range(3)]
        rWA = [Res("wa%d" % i) for i in range(3)]
        AST = [A.f32(4096) for _ in range(2)]
        rAST = [Res("ast%d" % i) for i in range(2)]
        it = 0
        for l in range(DEPTH):
            for third in range(3):
                c0 = third * 4096
                for kc in range(8):
                    b = it % 3
                    it += 1
                    kb.dma("sp", WA[b], wada_in[l][kc * 128:(kc + 1) * 128, c0:c0 + 4096], writes=[rWA[b]])
                    for n in range(8):
                        kb.op("pe", (lambda b=b, kc=kc, n=n: lambda e: e.matmul(
                            bank(n)[0:2, :], lhsT=CCT3[:, kc, :], rhs=WA[b][:, n * 512:(n + 1) * 512],
                            start=(kc == 0), stop=(kc == 7)))(), reads=[rCCT, rWA[b]], writes=[BK[n]])
                ab = third % 2
                for n in range(8):
                    if n % 2 == 0:
                        kb.op("act", (lambda ab=ab, n=n: lambda e: e.copy(AST[ab][0:2, n * 512:(n + 1) * 512], bank(n)[0:2, :]))(),
                              reads=[BK[n]], writes=[rAST[ab]])
                    else:
                        kb.op("dve", (lambda ab=ab, n=n: lambda e: e.tensor_copy(AST[ab][0:2, n * 512:(n + 1) * 512], bank(n)[0:2, :]))(),
                              reads=[BK[n]], writes=[rAST[ab]])
                kb.dma("sp", ada_part.ap()[2 * l:2 * l + 2, c0:c0 + 4096], AST[ab][0:2, :], reads=[rAST[ab]])
        kb.collective("AllReduce", ALU.add, GROUPS, ada_part.ap().opt(), ada_full.ap().opt())
        A.release(m0)
        if dbg_ada is not None:
            kb.dma("sp", dbg_ada.ap(), ada_full.ap())
            kb.barrier()
        if stop_after == "ada":
            nlayers = 0

        def emit_layer(l):
            last = (l == DEPTH - 1)
            NTL = NL if last else NT
            TL = NTL // 128
            wl = win_in[l]
            lay_mark = A.mark()

            def tok_src(t, c0, c1, l=l):
                if l == 0:
                    if t < 8:
                        return x_in[t * 128:(t + 1) * 128, c0:c1]
                    return ctx_in[(t - 8) * 128:(t - 7) * 128, c0:c1]
                return xs.ap()[t * 128:(t + 1) * 128, c0:c1]

            R1 = A.f32(128)
            R2 = A.f32(128)
            R3 = A.f32(128)
            R4a = A.f32(128)
            R4b = A.f32(128)
            rR = Res("R")
            af = ada_full.ap()
            srcs1 = [af[2 * l:2 * l + 1, D:2 * D], af[2 * l:2 * l + 1, 0:D],
                     af[2 * l + 1:2 * l + 2, D:2 * D], af[2 * l + 1:2 * l + 2, 0:D]]
            for i, s_ in enumerate(srcs1):
                kb.dma("sp", R1[32 * i:32 * i + 32, :], s_.rearrange("o (a b) -> (o a) b", b=128), writes=[rR])
            srcs2 = [bada_in[l:l + 1, D:2 * D], bada_in[l:l + 1, 0:D], normg_in[l:l + 1, :]]
            for i, s_ in enumerate(srcs2):
                kb.dma("sp", R2[32 * i:32 * i + 32, :], s_.rearrange("o (a b) -> (o a) b", b=128), writes=[rR])
            for i, s_ in enumerate([convb_in[l], cvlg_in[l], cvlb_in[l], mhg_in[l]]):
                kb.dma("sp", R3[8 * i:8 * i + 8, :], s_, writes=[rR])
            kb.dma("sp", R4a[0:124, :], convw_in[l][0:124, :], writes=[rR])
            kb.dma("sp", R4b[0:124, :], convw_in[l][124:248, :], writes=[rR])
            PT1 = A.f32(128)
            PT2 = A.f32(96)
            PT3 = A.f32(32)
            CW = A.f32(248)
            rPT = Res("PT")
            transpose_rows(R1, 128, PT1, 0, rR, rPT)
            transpose_rows(R2, 96, PT2, 1, rR, rPT)
            transpose_rows(R3, 32, PT3, 2, rR, rPT)
            transpose_rows(R4a, 124, CW[:, 0:124], 3, rR, rPT)
            transpose_rows(R4b, 124, CW[:, 124:248], 4, rR, rPT)
            MOD = A.f32(128)
            for j in range(2):
                kb.op("dve", (lambda j=j: lambda e: e.scalar_tensor_tensor(
                    out=MOD[:, 64 * j:64 * j + 32], in0=PT1[:, 64 * j:64 * j + 32], scalar=1.0, in1=PT2[:, 0:32],
                    op0=ALU.add, op1=ALU.add))(), reads=[rPT], writes=[rPT])
                kb.op("dve", (lambda j=j: lambda e: e.tensor_tensor(
                    out=MOD[:, 64 * j:64 * j + 32], in0=MOD[:, 64 * j:64 * j + 32], in1=PT2[:, 64:96],
                    op=ALU.mult))(), reads=[rPT], writes=[rPT])
                kb.op("dve", (lambda j=j: lambda e: e.tensor_tensor(
                    out=MOD[:, 64 * j + 32:64 * j + 64], in0=PT1[:, 64 * j + 32:64 * j + 64], in1=PT2[:, 32:64],
                    op=ALU.add))(), reads=[rPT], writes=[rPT])
            kb.barrier()
            par_mark = A.mark()
            NTL_cur[0] = NTL
            BIG = A.f32(16 * NT)
            BIGB = BIG.bitcast(BF16)
            HT = v3(BIGB, 32)
            big_mark = A.mark()

            XT = [A.f32(D) for _ in range(2)]
            rXT = [Res("xt0"), Res("xt1")]
            JUNK = A.bf16(D)
            rJ = Res("junk")
            SS = A.f32(8)
            rSS = Res("ss")
            for t in range(10):
                b = t % 2
                kb.dma("sp", XT[b], tok_src(t, 0, D), writes=[rXT[b]])
                kb.op("act", (lambda b=b: lambda e: e.activation(out=JUNK, in_=XT[b], func=AF.Square,
                                                                   accum_out=SS[:, 0:1]))(),
                      reads=[rXT[b]], writes=[rJ, rSS])
                kb.op("dve", lambda e: e.tensor_scalar(SS[:, 1:2], SS[:, 0:1], 1.0 / D, EPS, ALU.mult, ALU.add),
                      reads=[rSS], writes=[rSS])
                kb.op("act", lambda e: e.sqrt(SS[:, 3:4], SS[:, 1:2]), reads=[rSS], writes=[rSS])
                kb.op("dve", lambda e: e.reciprocal(SS[:, 2:3], SS[:, 3:4]), reads=[rSS], writes=[rSS])
                kb.op("act", (lambda b=b: lambda e: e.activation(out=XT[b], in_=XT[b], func=AF.Copy,
                                                                   scale=SS[:, 2:3]))(),
                      reads=[rXT[b], rSS], writes=[rXT[b]])
                mo = 0 if t < 8 else 64
                for g4 in range(8):
                    bk = g4 % 4
                    for j in range(4):
                        kc = g4 * 4 + j
                        kb.op("pe", (lambda b=b, kc=kc, bk=bk, j=j: lambda e: e.matmul(
                            bank(bk)[:, j * 128:(j + 1) * 128], lhsT=XT[b][:, kc * 128:(kc + 1) * 128], rhs=IDF,
                            start=True, stop=True))(), reads=[rXT[b], rCONST], writes=[BK[bk]])
                    for j in range(4):
                        kc = g4 * 4 + j
                        dst = HT[:, kc, t * 128:(t + 1) * 128]
                        src = bank(bk)[:, j * 128:(j + 1) * 128]
                        if g4 % 2 == 0:
                            kb.op("dve", (lambda dst=dst, src=src, kc=kc, mo=mo: lambda e: e.tensor_scalar(
                                dst, src, MOD[:, mo + kc:mo + kc + 1], MOD[:, mo + 32 + kc:mo + 33 + kc],
                                ALU.mult, ALU.add))(), reads=[BK[bk], rPT], writes=[rBIGd])
                        else:
                            kb.op("act", (lambda dst=dst, src=src, kc=kc, mo=mo: lambda e: e.activation(
                                out=dst, in_=src, func=AF.Identity, scale=MOD[:, mo + kc:mo + kc + 1],
                                bias=MOD[:, mo + 32 + kc:mo + 33 + kc]))(), reads=[BK[bk], rPT], writes=[rBIGa])
            kb.barrier()
            if dbg_HT is not None and l == 0:
                kb.dma("sp", dbg_HT.ap(), BIGB, reads=[rBIGd, rBIGa])
                kb.barrier()
            A.release(big_mark)
            if stop_after == "norm":
                return

            WT = [A.bf16(32 * 512) for _ in range(2)]
            rWT = [Res("wt0"), Res("wt1")]
            STG = [A.f32(NT) for _ in range(2)]
            rSTG = [Res("stg0"), Res("stg1")]
            tiles = []
            for name, c0, ncol, d0 in TOKB:
                for j in range(0, ncol, 512):
                    tiles.append(("tok", name, c0 + j, min(512, ncol - j), d0 + j))
            for name, c0, r0 in CHB:
                for j in range(0, 1024, 512):
                    tiles.append(("ch", name, c0 + j, 512, r0 + j))
            wsrc = wl.rearrange("(kc kp) n -> kp kc n", kp=128)

            def load_w(src3, c0, ncol, b):
                w3 = v3(WT[b], 32)
                for g in range(4):
                    kb.dma("pool", w3[:, g * 8:(g + 1) * 8, 0:ncol], src3[:, g * 8:(g + 1) * 8, c0:c0 + ncol],
                           writes=[rWT[b]])

            stg_i = 0
            ev_i = 0
            load_w(wsrc, tiles[0][2], tiles[0][3], 0)
            for ti, (kind, name, c0, ncol, d0) in enumerate(tiles):
                b = ti % 2
                if ti + 1 < len(tiles):
                    load_w(wsrc, tiles[ti + 1][2], tiles[ti + 1][3], 1 - b)
                w3 = v3(WT[b], 32)
                if kind == "tok":
                    nt_ = 8 if (last and name in ("Av", "Cq")) else 10
                    for t in range(nt_):
                        bk = 6 + (t % 2)
                        for kc in range(32):
                            kb.op("pe", (lambda bk=bk, kc=kc, t=t, w3=w3, ncol=ncol: lambda e: e.matmul(
                                bank(bk, ncol), lhsT=HT[:, kc, t * 128:(t + 1) * 128], rhs=w3[:, kc, 0:ncol],
                                start=(kc == 0), stop=(kc == 31)))(), reads=[rBIGd, rBIGa, rWT[b]], writes=[BK[bk]])
                        sb = stg_i % 2
                        stg_i += 1
                        eng = "act" if ev_i % 2 == 0 else "dve"
                        ev_i += 1
                        if eng == "act":
                            kb.op("act", (lambda sb=sb, bk=bk, ncol=ncol: lambda e: e.copy(
                                STG[sb][:, 0:ncol], bank(bk, ncol)))(), reads=[BK[bk]], writes=[rSTG[sb]])
                        else:
                            kb.op("dve", (lambda sb=sb, bk=bk, ncol=ncol: lambda e: e.tensor_copy(
                                STG[sb][:, 0:ncol], bank(bk, ncol)))(), reads=[BK[bk]], writes=[rSTG[sb]])
                        kb.dma("sp", Ptok.ap()[t * 128:(t + 1) * 128, d0:d0 + ncol], STG[sb][:, 0:ncol],
                               reads=[rSTG[sb]])
                else:
                    ntok = NTL
                    tts = [(0, 512), (512, 512)] + ([(1024, 256)] if ntok > 1024 else [])
                    for cb in range(4):
                        bset = 3 * (cb % 2)
                        for kc in range(32):
                            for i, (t0, tn) in enumerate(tts):
                                kb.op("pe", (lambda bset=bset, i=i, kc=kc, cb=cb, t0=t0, tn=tn, w3=w3: lambda e: e.matmul(
                                    bank(bset + i, tn), lhsT=w3[:, kc, cb * 128:(cb + 1) * 128],
                                    rhs=HT[:, kc, t0:t0 + tn], start=(kc == 0), stop=(kc == 31)))(),
                                    reads=[rBIGd, rBIGa, rWT[b]], writes=[BK[bset + i]])
                        sb = stg_i % 2
                        stg_i += 1
                        for i, (t0, tn) in enumerate(tts):
                            eng = "act" if ev_i % 2 == 0 else "dve"
                            ev_i += 1
                            if eng == "act":
                                kb.op("act", (lambda sb=sb, bset=bset, i=i, t0=t0, tn=tn: lambda e: e.copy(
                                    STG[sb][:, t0:t0 + tn], bank(bset + i, tn)))(),
                                    reads=[BK[bset + i]], writes=[rSTG[sb]])
                            else:
                                kb.op("dve", (lambda sb=sb, bset=bset, i=i, t0=t0, tn=tn: lambda e: e.tensor_copy(
                                    STG[sb][:, t0:t0 + tn], bank(bset + i, tn)))(),
                                    reads=[BK[bset + i]], writes=[rSTG[sb]])
                        kb.dma("sp", Pch.ap()[d0 + cb * 128:d0 + (cb + 1) * 128, 0:ntok], STG[sb][:, 0:ntok],
                               reads=[rSTG[sb]])
            kb.barrier()
            A.release(par_mark)
            if stop_after == "gemm1":
                return
            tts = [(0, 512), (512, 512)] + ([(1024, 256)] if NTL > 1024 else [])
            KTC = A.bf16(2 * 256)
            KTC3 = v3(KTC, 2)
            rKTC = Res("ktc")
            IBFB = A.f32(32)
            GQK = A.f32(256)
            rPB = Res("pb")
            kb.dma("sp", IBFB[:, 0:16], ib_in[l:l + 1, :].partition_broadcast(128), writes=[rPB])
            kb.dma("sp", IBFB[:, 16:32], fb_in[l:l + 1, :].partition_broadcast(128), writes=[rPB])
            kb.dma("sp", GQK[:, 0:128], qng_in[l:l + 1, :].partition_broadcast(128), writes=[rPB])
            kb.dma("sp", GQK[:, 128:256], kng_in[l:l + 1, :].partition_broadcast(128), writes=[rPB])
            pers2_mark = A.mark()
            HS = A.f32(8 * NT)
            HS3 = v3(HS, 8)
            rHS = Res("hs")
            GP = A.f32(10 * 5 * 16)
            GP4 = GP.rearrange("p (t k g) -> p t k g", t=10, k=5)
            rGP = Res("gp")
            CS = A.f32(16 * 256)
            CS3 = v3(CS, 16)
            rCS = Res("cs")
            CB = A.bf16(16 * 256)
            CB3 = v3(CB, 16)
            rCB = Res("cb")
            STFB = A.f32(16 * 256)
            rSTFB = Res("stfb")
            pers_mark = A.mark()

            def bc(ap, shape):
                return ap.broadcast_to(list(shape))

            def rsqrt_ops(dst, src, scale, tmp, rr, rw):
                kb.op("dve", lambda e: e.tensor_scalar(tmp, src, scale, EPS, ALU.mult, ALU.add), reads=rr, writes=rw)
                kb.op("act", lambda e: e.sqrt(tmp, tmp), reads=rw, writes=rw)
                kb.op("dve", lambda e: e.reciprocal(dst, tmp), reads=rw, writes=rw)

            def rope_ops(src, dst, H, t, T1, T2, rs, rd, rt):
                s5 = src.rearrange("p (h a c j) -> p h a c j", h=H, a=2, c=2)
                d5 = dst.rearrange("p (h a c j) -> p h a c j", h=H, a=2, c=2)
                x1, x2 = s5[:, :, :, 0, :], s5[:, :, :, 1, :]
                o1, o2 = d5[:, :, :, 0, :], d5[:, :, :, 1, :]
                cs = bc(ROPE3[:, t, 0:64].rearrange("p (a j) -> p a j", a=2).unsqueeze(1), [128, H, 2, 32])
                sn = bc(ROPE3[:, t, 64:128].rearrange("p (a j) -> p a j", a=2).unsqueeze(1), [128, H, 2, 32])
                t1 = T1.rearrange("p (h a j) -> p h a j", h=H, a=2)
                t2 = T2.rearrange("p (h a j) -> p h a j", h=H, a=2)
                kb.op("dve", lambda e: e.tensor_tensor(o1, x1, cs, ALU.mult), reads=[rs, rCONST], writes=[rd])
                kb.op("pool", lambda e: e.tensor_tensor(t1, x2, sn, ALU.mult), reads=[rs, rCONST], writes=[rt])
                kb.op("dve", lambda e: e.tensor_tensor(o1, o1, t1, ALU.subtract), reads=[rt], writes=[rd])
                kb.op("dve", lambda e: e.tensor_tensor(o2, x2, cs, ALU.mult), reads=[rs, rCONST], writes=[rd])
                kb.op("pool", lambda e: e.tensor_tensor(t2, x1, sn, ALU.mult), reads=[rs, rCONST], writes=[rt])
                kb.op("dve", lambda e: e.tensor_tensor(o2, o2, t2, ALU.add), reads=[rt], writes=[rd])

            def qk_norm_rope(src, H, gq, t, NRM, ROT, TS, SSQ, T1, T2, rsrc, rw):
                kb.op("act", lambda e: e.activation(out=TS, in_=src, func=AF.Square), reads=[rsrc], writes=[rw])
                kb.op("dve", lambda e: e.tensor_reduce(out=SSQ[:, 0:H], in_=v3(TS, H), axis=AX.X, op=ALU.add),
                      reads=[rw], writes=[rw])
                rsqrt_ops(SSQ[:, 16:16 + H], SSQ[:, 0:H], 1.0 / 128, SSQ[:, 8:8 + H], [rw], [rw])
                kb.op("dve", lambda e: e.tensor_tensor(v3(NRM, H), v3(src, H), bc(SSQ[:, 16:16 + H].unsqueeze(2), [128, H, 128]),
                                                       ALU.mult), reads=[rsrc, rw], writes=[rw])
                kb.op("dve", lambda e: e.tensor_tensor(v3(NRM, H), v3(NRM, H), bc(gq.unsqueeze(1), [128, H, 128]), ALU.mult),
                      reads=[rw, rPB], writes=[rw])
                if t < 8:
                    rope_ops(NRM, ROT, H, t, T1, T2, rw, rw, rw)
                else:
                    kb.op("dve", lambda e: e.tensor_copy(ROT, NRM), reads=[rw], writes=[rw])

            KV = [A.f32(512) for _ in range(2)]
            rKV = [Res("kv0"), Res("kv1")]
            KTE = A.f32(2 * 1024)
            KTE3 = v3(KTE, 2)
            rKTE = Res("kte")
            KWS = []
            for i_ in range(2):
                KWS.append(dict(KN=A.f32(256), KR=A.f32(256), KTS=A.f32(256), KT1=A.f32(128), KT2=A.f32(128),
                                KRB=A.bf16(256), SSK=A.f32(24), r=Res("kw%d" % i_)))
            for t in range(10):
                b = t % 2
                W_ = KWS[b]
                bk_ = 4 * b
                kb.dma("sp", KV[b], Ptok.ap()[t * 128:(t + 1) * 128, 2048:2560], writes=[rKV[b]])
                qk_norm_rope(KV[b][:, 0:256], 2, GQK[:, 128:256], t, W_["KN"], W_["KR"], W_["KTS"], W_["SSK"], W_["KT1"], W_["KT2"],
                             rKV[b], W_["r"])
                kb.op("act", (lambda W_=W_: lambda e: e.copy(W_["KRB"], W_["KR"]))(), reads=[W_["r"]], writes=[W_["r"]])
                for g in range(2):
                    kb.op("pe", (lambda g=g, W_=W_, bk_=bk_: lambda e: e.matmul(
                        bank(bk_)[:, g * 128:(g + 1) * 128], lhsT=W_["KRB"][:, g * 128:(g + 1) * 128], rhs=IDB,
                        start=True, stop=True))(), reads=[W_["r"], rCONST], writes=[BK[bk_]])
                if t < 8:
                    kb.op("act", (lambda t=t, bk_=bk_: lambda e: e.copy(KTE3[:, :, t * 128:(t + 1) * 128], v3(bank(bk_, 256), 2)))(),
                          reads=[BK[bk_]], writes=[rKTE])
                else:
                    kb.op("act", (lambda t=t, bk_=bk_: lambda e: e.copy(KTC3[:, :, (t - 8) * 128:(t - 7) * 128],
                                                                      v3(bank(bk_, 256), 2)))(), reads=[BK[bk_]], writes=[rKTC])
            kb.dma("sp", exr(0, 0, 128 * 2048).rearrange("(d x) -> d x", d=128), KTE, reads=[rKTE])
            kb.dma("sp", exr(1, 0, 1024 * 256).rearrange("(t c) -> t c", c=256), Ptok.ap()[0:1024, 2304:2560])
            A.release(pers_mark)

            ATb = [A.f32(NT) for _ in range(2)]
            GTb = [A.f32(NT) for _ in range(2)]
            rATb = [Res("at0"), Res("at1")]
            rGTb = [Res("gt0"), Res("gt1")]
            halo_ex = exr(4, EX_HALO_OFF, 128 * 240).rearrange("(c g j) -> c g j", c=128, g=8)
            for cg in range(8):
                b = cg % 2
                kb.dma("sp", ATb[b][:, 0:NTL], Pch.ap()[CR["Ba"] + cg * 128:CR["Ba"] + (cg + 1) * 128, 0:NTL], writes=[rATb[b]])
                kb.dma("sp", GTb[b][:, 0:NTL], Pch.ap()[CR["Bg"] + cg * 128:CR["Bg"] + (cg + 1) * 128, 0:NTL], writes=[rGTb[b]])
                kb.op("act", (lambda b=b: lambda e: e.activation(out=GTb[b][:, 0:NTL], in_=GTb[b][:, 0:NTL], func=AF.Sigmoid))(),
                      reads=[rGTb[b]], writes=[rGTb[b]])
                kb.op("dve", (lambda b=b: lambda e: e.tensor_tensor(ATb[b][:, 0:NTL], ATb[b][:, 0:NTL], GTb[b][:, 0:NTL], ALU.mult))(),
                      reads=[rGTb[b], rATb[b]], writes=[rATb[b]])
                kb.dma("sp", Ych.ap()[cg * 128:(cg + 1) * 128, 0:NTL], ATb[b][:, 0:NTL], reads=[rATb[b]])
                kb.dma("sp", halo_ex[:, cg, 0:15], ATb[b][:, 0:15], reads=[rATb[b]])
                kb.dma("sp", halo_ex[:, cg, 15:30], ATb[b][:, 1009:1024], reads=[rATb[b]])
            A.release(pers_mark)

            GG = A.f32(320)
            XF = A.f32(160)
            IG = A.f32(160)
            LF = A.f32(160)
            rGG = Res("gg")
            G5 = GG.rearrange("p (t d i h) -> p t d i h", t=10, d=2, i=2)
            x4 = lambda ap: ap.rearrange("p (t d h) -> p t d h", t=10, d=2)
            kb.dma("sp", v3(GG, 10), Ptok.ap()[:, 4608:4640].rearrange("(t p) c -> p t c", p=128), writes=[rGG])
            kb.op("dve", lambda e: e.tensor_tensor(x4(XF), G5[:, :, :, 1, :], bc(v3(IBFB[:, 16:32], 2).unsqueeze(1), [128, 10, 2, 8]),
                                                   ALU.add), reads=[rGG, rPB], writes=[rGG])
            kb.op("dve", lambda e: e.tensor_tensor(x4(IG), G5[:, :, :, 0, :], bc(v3(IBFB[:, 0:16], 2).unsqueeze(1), [128, 10, 2, 8]),
                                                   ALU.add), reads=[rGG, rPB], writes=[rGG])
            kb.op("act", lambda e: e.activation(out=LF, in_=XF, func=AF.Exp, scale=-1.0), reads=[rGG], writes=[rGG])
            kb.op("dve", lambda e: e.tensor_scalar_add(LF, LF, 1.0), reads=[rGG], writes=[rGG])
            kb.op("act", lambda e: e.activation(out=LF, in_=LF, func=AF.Ln), reads=[rGG], writes=[rGG])
            kb.op("dve", lambda e: e.tensor_scalar_mul(LF, LF, -1.0), reads=[rGG], writes=[rGG])
            LF3 = v3(LF, 10)
            for t in range(10):
                kb.op("pe", (lambda t=t: lambda e: e.matmul(bank(1)[:, t * 32:t * 32 + 8], lhsT=TRIF, rhs=LF3[:, t, 0:8],
                                                            start=True, stop=True))(), reads=[rGG, rCONST], writes=[BK[1]])
                kb.op("pe", (lambda t=t: lambda e: e.matmul(bank(1)[:, t * 32 + 8:t * 32 + 16], lhsT=TRIB, rhs=LF3[:, t, 8:16],
                                                            start=True, stop=True))(), reads=[rGG, rCONST], writes=[BK[1]])
                kb.op("pe", (lambda t=t: lambda e: e.matmul(bank(1)[:, t * 32 + 16:t * 32 + 32], lhsT=ONF, rhs=LF3[:, t, 0:16],
                                                            start=True, stop=True))(), reads=[rGG, rCONST], writes=[BK[1]])
            PS1 = v3(bank(1, 320), 10)
            kb.op("dve", lambda e: e.tensor_copy(GP4[:, :, 0, :], PS1[:, :, 0:16]), reads=[BK[1]], writes=[rGP])
            kb.op("dve", lambda e: e.tensor_copy(GP4[:, :, 3, :], PS1[:, :, 16:32]), reads=[BK[1]], writes=[rGP])
            kb.op("dve", lambda e: e.scalar_tensor_tensor(out=GP4[:, :, 1, :], in0=v3(IG, 10), scalar=LNSC, in1=GP4[:, :, 0, :],
                                                          op0=ALU.add, op1=ALU.subtract), reads=[rGG, rGP], writes=[rGP])
            kb.op("dve", lambda e: e.tensor_tensor(v3(XF, 10), GP4[:, :, 3, :], GP4[:, :, 1, :], ALU.add), reads=[rGP], writes=[rGG])
            kb.op("act", lambda e: e.activation(out=GP4[:, :, 2, :], in_=v3(XF, 10), func=AF.Exp), reads=[rGG], writes=[rGP])
            kb.op("act", lambda e: e.activation(out=GP4[:, :, 4, :], in_=GP4[:, :, 3, :], func=AF.Exp), reads=[rGP], writes=[rGP])
            dump("d_GP", GP, [128, 800], reads=[rGP])
            A.release(pers_mark)

            KTOK = [A.f32(1024) for _ in range(2)]
            VTOK = [A.f32(1024) for _ in range(2)]
            rKTOK = [Res("ktok0"), Res("ktok1")]
            rVTOK = [Res("vtok0"), Res("vtok1")]
            KTLs = [A.bf16(1024) for _ in range(2)]
            rKTLs = [Res("ktl0"), Res("ktl1")]
            VA = [A.bf16(8 * 256) for _ in range(2)]
            rVA = [Res("va0"), Res("va1")]
            QTb = A.bf16(1024)
            KTb = A.bf16(1024)
            rQKb = Res("qkb")
            DG = A.f32(1024)
            rDG = Res("dg")
            EB = A.f32(1024)
            rEB = Res("eb")
            WTm = A.f32(1024)
            rWTm = Res("wtm")
            PTb = A.bf16(1024)
            rPTb = Res("ptb")
            QTL = A.bf16(1024)
            rQTL = Res("qtl")
            DN = A.f32(1024)
            rDN = Res("dn")
            TMPH = A.f32(1024)
            rTMPH = Res("tmph")
            for b in range(2):
                kb.op("pool", (lambda b=b: lambda e: e.memset(v3(VA[b], 8)[:, :, 128:256], 1.0))(), writes=[rVA[b]])
            scan_ctr = [0]

            def scan(chunks, dr, emit):
                g0 = dr * 8
                MASK = TRIF if dr == 0 else TRIB
                for t in chunks:
                    b = scan_ctr[0] % 2
                    scan_ctr[0] += 1
                    va3 = v3(VA[b], 8)
                    KTL = KTLs[b]
                    rKTL = rKTLs[b]
                    stb = 2 if (emit or b == 0) else 4
                    kb.dma("sp", KTOK[b], Ptok.ap()[t * 128:(t + 1) * 128, 2560:3584], writes=[rKTOK[b]])
                    kb.dma("sp", VTOK[b], Ptok.ap()[t * 128:(t + 1) * 128, 3584:4608], writes=[rVTOK[b]])
                    kb.op("dve", (lambda b=b, t=t, KTL=KTL: lambda e: e.tensor_tensor(
                        v3(KTL, 8), v3(KTOK[b], 8), bc(GP4[:, t, 2, g0:g0 + 8].unsqueeze(2), [128, 8, 128]), ALU.mult))(),
                        reads=[rKTOK[b], rGP], writes=[rKTL])
                    kb.op("act", (lambda b=b, va3=va3: lambda e: e.copy(va3[:, :, 0:128], v3(VTOK[b], 8)))(),
                          reads=[rVTOK[b]], writes=[rVA[b]])
                    if emit:
                        kb.dma("pool", v3(QTb, 8), Pch.ap()[CR["Dq"]:CR["Dq"] + 1024, t * 128:(t + 1) * 128].rearrange(
                            "(h d) t -> d h t", d=128), writes=[rQKb])
                        kb.dma("pool", v3(KTb, 8), Pch.ap()[CR["DkT"]:CR["DkT"] + 1024, t * 128:(t + 1) * 128].rearrange(
                            "(h d) t -> d h t", d=128), writes=[rQKb])
                        for h in range(8):
                            kb.op("pe", (lambda h=h: lambda e: e.matmul(
                                PS[:, h * 128:(h + 1) * 128], lhsT=v3(KTb, 8)[:, h, :], rhs=v3(QTb, 8)[:, h, :],
                                start=True, stop=True))(), reads=[rQKb], writes=[BK[h // 4]])
                        kb.op("dve", (lambda t=t: lambda e: e.tensor_tensor(
                            v3(DG, 8), bc(IDF.unsqueeze(1), [128, 8, 128]),
                            bc(GP4[:, t, 0, g0:g0 + 8].unsqueeze(2), [128, 8, 128]), ALU.mult))(),
                            reads=[rGP, rCONST], writes=[rDG])
                        for j in range(2):
                            kb.op("pe", (lambda j=j: lambda e: e.matmul(
                                PS[:, 1024 + j * 512:1536 + j * 512], lhsT=ONF, rhs=DG[:, j * 512:(j + 1) * 512],
                                start=True, stop=True))(), reads=[rDG, rCONST], writes=[BK[2 + j]])
                            kb.op("act", (lambda j=j: lambda e: e.activation(
                                out=EB[:, j * 512:(j + 1) * 512], in_=PS[:, 1024 + j * 512:1536 + j * 512], func=AF.Exp))(),
                                reads=[BK[2 + j]], writes=[rEB])
                        for h in range(8):
                            kb.op("act", (lambda h=h, t=t: lambda e: e.activation(
                                out=WTm[:, h * 128:(h + 1) * 128], in_=PS[:, 1024 + h * 128:1152 + h * 128], func=AF.Exp,
                                bias=GP4[:, t, 1, g0 + h:g0 + h + 1]))(), reads=[BK[2 + h // 4], rGP], writes=[rWTm])
                        kb.op("pool", lambda e: e.tensor_tensor(v3(WTm, 8), v3(WTm, 8), bc(MASK.unsqueeze(1), [128, 8, 128]),
                                                                ALU.mult), reads=[rCONST], writes=[rWTm])
                        kb.op("dve", lambda e: e.tensor_tensor(PTb, PS[:, 0:1024], WTm, ALU.mult),
                              reads=[BK[0], BK[1], rWTm], writes=[rPTb])
                        kb.op("pool", lambda e: e.tensor_tensor(QTL, QTb, EB, ALU.mult), reads=[rQKb, rEB], writes=[rQTL])
                        for h in range(8):
                            kb.op("pe", (lambda h=h, va3=va3: lambda e: e.matmul(
                                PS[:, 2048 + h * 128:2176 + h * 128], lhsT=va3[:, h, 0:128], rhs=v3(PTb, 8)[:, h, :],
                                start=True, stop=False))(), reads=[rVA[b], rPTb], writes=[BK[4 + h // 4]])
                            kb.op("pe", (lambda h=h: lambda e: e.matmul(
                                PS[:, 2048 + h * 128:2176 + h * 128], lhsT=CB3[:, g0 + h, 0:128], rhs=v3(QTL, 8)[:, h, :],
                                start=False, stop=True))(), reads=[rCB, rQTL], writes=[BK[4 + h // 4]])
                        for h in range(8):
                            kb.op("pe", (lambda h=h: lambda e: e.matmul(
                                PS[:, 3072 + h * 128:3200 + h * 128], lhsT=ONB, rhs=v3(PTb, 8)[:, h, :],
                                start=True, stop=False))(), reads=[rCONST, rPTb], writes=[BK[6 + h // 4]])
                            kb.op("pe", (lambda h=h: lambda e: e.matmul(
                                PS[:, 3072 + h * 128:3200 + h * 128], lhsT=CB3[:, g0 + h, 128:256], rhs=v3(QTL, 8)[:, h, :],
                                start=False, stop=True))(), reads=[rCB, rQTL], writes=[BK[6 + h // 4]])
                        kb.op("act", lambda e: e.activation(out=DN, in_=PS[:, 3072:4096], func=AF.Abs),
                              reads=[BK[6], BK[7]], writes=[rDN])
                        kb.op("dve", lambda e: e.tensor_scalar_max(DN, DN, 1.0), reads=[rDN], writes=[rDN])
                        kb.op("dve", lambda e: e.reciprocal(DN, DN), reads=[rDN], writes=[rDN])
                        hs_sl = HS3[:, :, t * 128:(t + 1) * 128]
                        if dr == 0:
                            kb.op("dve", (lambda hs_sl=hs_sl: lambda e: e.tensor_tensor(hs_sl, v3(PS[:, 2048:3072], 8), v3(DN, 8),
                                                                                      ALU.mult))(),
                                  reads=[BK[4], BK[5], rDN], writes=[rHS])
                        else:
                            kb.op("dve", lambda e: e.tensor_tensor(TMPH, PS[:, 2048:3072], DN, ALU.mult),
                                  reads=[BK[4], BK[5], rDN], writes=[rTMPH])
                            kb.op("pool", (lambda hs_sl=hs_sl: lambda e: e.tensor_tensor(hs_sl, hs_sl, v3(TMPH, 8), ALU.add))(),
                                  reads=[rTMPH, rHS], writes=[rHS])
                    kb.op("dve", (lambda t=t: lambda e: e.tensor_tensor(
                        CS3[:, g0:g0 + 8, :], CS3[:, g0:g0 + 8, :], bc(GP4[:, t, 4, g0:g0 + 8].unsqueeze(2), [128, 8, 256]),
                        ALU.mult))(), reads=[rGP, rCS], writes=[rCS])
                    for half in range(2):
                        for hh in range(4):
                            h = half * 4 + hh
                            kb.op("pe", (lambda h=h, hh=hh, va3=va3, KTL=KTL, stb=stb: lambda e: e.matmul(
                                PS[:, stb * 512 + hh * 256:stb * 512 + 256 + hh * 256], lhsT=v3(KTL, 8)[:, h, :], rhs=va3[:, h, :],
                                start=True, stop=True))(), reads=[rKTL, rVA[b]], writes=[BK[stb + hh // 2]])
                        kb.op("dve", (lambda half=half, stb=stb: lambda e: e.tensor_tensor(
                            CS3[:, g0 + half * 4:g0 + half * 4 + 4, :], CS3[:, g0 + half * 4:g0 + half * 4 + 4, :],
                            v3(PS[:, stb * 512:stb * 512 + 1024], 4), ALU.add))(), reads=[BK[stb], BK[stb + 1], rCS], writes=[rCS])
                    if emit:
                        kb.op("act", lambda e: e.copy(CB3[:, g0:g0 + 8, :], CS3[:, g0:g0 + 8, :]), reads=[rCS], writes=[rCB])

            def zero_state():
                kb.op("dve", lambda e: e.memset(CS, 0.0), writes=[rCS])
                kb.op("pool", lambda e: e.memset(CB, 0.0), writes=[rCB])

            zero_state()
            scan([8, 9], 0, not last)
            scan([9, 8], 1, not last)
            kb.op("dve", lambda e: e.tensor_copy(STFB, CS), reads=[rCS], writes=[rSTFB])
            dump("d_STFB", STFB, [128, 4096], reads=[rSTFB])
            dump("d_HSctx", HS, [128, 8 * NT], reads=[rHS])
            zero_state()
            scan(list(range(8)), 0, False)
            scan(list(range(7, -1, -1)), 1, False)
            for dr_ in range(2):
                kb.dma("sp", exr(2 + dr_, 0, 128 * 2048).rearrange("(p x) -> p x", p=128), CS[:, dr_ * 2048:(dr_ + 1) * 2048],
                       reads=[rCS])
            DTT = A.f32(16)
            rDTT = Res("dtt")
            kb.op("dve", lambda e: e.tensor_reduce(out=DTT, in_=GP4[:, 0:8, 3, :].rearrange("p t g -> p g t"), axis=AX.X,
                                                   op=ALU.add), reads=[rGP], writes=[rDTT])
            kb.dma("sp", exr(4, EX_DT_OFF, 128 * 16).rearrange("(p x) -> p x", p=128), DTT, reads=[rDTT])
            scan_mark = A.mark()

            if stop_after == "pre":
                return
            for c_ in range(5):
                kb.collective("AllGather", ALU.bypass, GROUPS, exp_bufs[c_].ap().opt(), gat_bufs[c_].ap().opt())

            if stop_after == "gather":
                return
            SG = [A.f32(8 * 256) for _ in range(4)]
            rSG = Res("sg")
            DJ = A.f32(64)
            FF = A.f32(8 * 256)
            rFF = Res("ff")
            for r in range(4):
                kb.dma("sp", DJ[:, r * 16:(r + 1) * 16], gar(4, r, EX_DT_OFF, 128 * 16).rearrange("(p x) -> p x", p=128), writes=[rSG])
            kb.op("act", lambda e: e.activation(out=DJ, in_=DJ, func=AF.Exp), reads=[rSG], writes=[rSG])
            for dr in range(2):
                for r in range(4):
                    kb.dma("sp", SG[r], gar(2 + dr, r, 0, 128 * 2048).rearrange("(p x) -> p x", p=128), writes=[rSG])
                CSd = CS[:, dr * 2048:(dr + 1) * 2048]
                kb.op("dve", (lambda dr=dr: lambda e: e.tensor_copy(FF, STFB[:, dr * 2048:(dr + 1) * 2048]))(),
                      reads=[rSTFB], writes=[rFF])
                order = [0, 1, 2] if dr == 0 else [3, 2, 1]
                fs = 0 if dr == 0 else 3
                kb.op("dve", (lambda CSd=CSd, fs=fs: lambda e: e.tensor_scalar_mul(CSd, FF, SEL[:, fs:fs + 1]))(),
                      reads=[rFF, rCONST], writes=[rCS])
                for j in order:
                    for h in range(8):
                        kb.op("dve", (lambda j=j, h=h, dr=dr: lambda e: e.scalar_tensor_tensor(
                            out=v3(FF, 8)[:, h, :], in0=v3(FF, 8)[:, h, :],
                            scalar=DJ[:, j * 16 + dr * 8 + h:j * 16 + dr * 8 + h + 1], in1=v3(SG[j], 8)[:, h, :],
                            op0=ALU.mult, op1=ALU.add))(), reads=[rSG, rFF], writes=[rFF])
                    nx = j + 1 if dr == 0 else j - 1
                    kb.op("dve", (lambda CSd=CSd, nx=nx: lambda e: e.scalar_tensor_tensor(
                        out=CSd, in0=FF, scalar=SEL[:, nx:nx + 1], in1=CSd, op0=ALU.mult, op1=ALU.add))(),
                        reads=[rFF, rCONST, rCS], writes=[rCS])
            kb.op("act", lambda e: e.copy(CB, CS), reads=[rCS], writes=[rCB])
            dump("d_INIT", CS, [128, 4096], reads=[rCS])
            scan(list(range(8)), 0, True)
            scan(list(range(7, -1, -1)), 1, True)
            dump("d_HS", HS, [128, 8 * NT], reads=[rHS])
            A.release(pers_mark)
            OTb = [A.f32(NT) for _ in range(2)]
            ZTb = [A.f32(NT) for _ in range(2)]
            rOTb = [Res("ot0"), Res("ot1")]
            rZTb = [Res("zt0"), Res("zt1")]
            SQ = A.f32(NT)
            rSQ = Res("sq")
            RS = A.f32(NT)
            rRS = Res("rs")
            for h in range(8):
                b = h % 2
                kb.dma("sp", OTb[b][:, 0:NTL], Pch.ap()[CR["Do"] + h * 128:CR["Do"] + (h + 1) * 128, 0:NTL], writes=[rOTb[b]])
                kb.dma("sp", ZTb[b][:, 0:NTL], Pch.ap()[CR["Dz"] + h * 128:CR["Dz"] + (h + 1) * 128, 0:NTL], writes=[rZTb[b]])
                kb.op("act", (lambda h=h: lambda e: e.activation(out=SQ[:, 0:NTL], in_=HS3[:, h, 0:NTL], func=AF.Square))(),
                      reads=[rHS], writes=[rSQ])
                for i, (t0, tn) in enumerate(tts):
                    kb.op("pe", (lambda i=i, t0=t0, tn=tn: lambda e: e.matmul(bank(i, tn), lhsT=ONF, rhs=SQ[:, t0:t0 + tn],
                                                                              start=True, stop=True))(),
                          reads=[rSQ, rCONST], writes=[BK[i]])
                    kb.op("dve", (lambda i=i, t0=t0, tn=tn: lambda e: e.tensor_scalar(RS[:, t0:t0 + tn], bank(i, tn), 1.0 / 128, EPS,
                                                                                    ALU.mult, ALU.add))(),
                          reads=[BK[i]], writes=[rRS])
                kb.op("act", lambda e: e.sqrt(RS[:, 0:NTL], RS[:, 0:NTL]), reads=[rRS], writes=[rRS])
                kb.op("dve", lambda e: e.reciprocal(RS[:, 0:NTL], RS[:, 0:NTL]), reads=[rRS], writes=[rRS])
                kb.op("dve", (lambda h=h: lambda e: e.tensor_tensor(SQ[:, 0:NTL], HS3[:, h, 0:NTL], RS[:, 0:NTL], ALU.mult))(),
                      reads=[rHS, rRS, rSQ], writes=[rSQ])
                kb.op("act", (lambda b=b: lambda e: e.activation(out=OTb[b][:, 0:NTL], in_=OTb[b][:, 0:NTL], func=AF.Sigmoid))(),
                      reads=[rOTb[b]], writes=[rOTb[b]])
                kb.op("act", (lambda b=b: lambda e: e.activation(out=ZTb[b][:, 0:NTL], in_=ZTb[b][:, 0:NTL], func=AF.Silu))(),
                      reads=[rZTb[b]], writes=[rZTb[b]])
                kb.op("pool", (lambda b=b: lambda e: e.tensor_tensor(SQ[:, 0:NTL], SQ[:, 0:NTL], OTb[b][:, 0:NTL], ALU.mult))(),
                      reads=[rOTb[b], rSQ], writes=[rSQ])
                stg, rstg = mix_stage()
                kb.op("dve", (lambda b=b, h=h, stg=stg: lambda e: e.scalar_tensor_tensor(
                    out=stg[:, 0:NTL], in0=SQ[:, 0:NTL], scalar=PT3[:, 24 + h:25 + h], in1=ZTb[b][:, 0:NTL],
                    op0=ALU.mult, op1=ALU.mult))(), reads=[rSQ, rZTb[b], rPT], writes=[rstg])
                mix_store(24 + h, stg, rstg)
            kb.barrier()
            A.release(pers2_mark)
            LNG = A.f32(1024)
            LNB = A.f32(1024)
            SBI = A.f32(1024)
            rAP = Res("ap")
            kb.dma("sp", LNG, sglg_in[l:l + 1, :].partition_broadcast(128), writes=[rAP])
            kb.dma("sp", LNB, sglb_in[l:l + 1, :].partition_broadcast(128), writes=[rAP])
            kb.dma("sp", SBI, sgub_in[l:l + 1, :].partition_broadcast(128), writes=[rAP])
            WSF = [A.f32(128) for _ in range(2)]
            rWSF = [Res("wsf0"), Res("wsf1")]
            WST = A.bf16(1024)
            rWST = Res("wst")
            for h in range(8):
                b = h % 2
                kb.dma("sp", WSF[b], sguw_in[l, h], writes=[rWSF[b]])
                kb.op("pe", (lambda b=b: lambda e: e.matmul(bank(b, 128), lhsT=WSF[b], rhs=IDF, start=True, stop=True))(),
                      reads=[rWSF[b], rCONST], writes=[BK[b]])
                kb.op("dve", (lambda b=b, h=h: lambda e: e.tensor_copy(WST[:, h * 128:(h + 1) * 128], bank(b, 128)))(),
                      reads=[BK[b]], writes=[rWST])
            VN = A.bf16(TL * 1024)
            VN3 = v3(VN, TL)
            rVN = Res("vn")
            VT = [A.f32(1024) for _ in range(2)]
            rVT = [Res("vt0"), Res("vt1")]
            AJ = A.f32(1024)
            rAJ = Res("aj")
            ST = A.f32(16)
            rST = Res("st")
            for t in range(TL):
                b = t % 2
                kb.dma("sp", VT[b], Ptok.ap()[t * 128:(t + 1) * 128, 0:1024], writes=[rVT[b]])
                kb.op("act", (lambda b=b: lambda e: e.activation(out=AJ, in_=VT[b], func=AF.Copy, accum_out=ST[:, 0:1]))(),
                      reads=[rVT[b]], writes=[rAJ, rST])
                kb.op("act", (lambda b=b: lambda e: e.activation(out=AJ, in_=VT[b], func=AF.Square, accum_out=ST[:, 1:2]))(),
                      reads=[rVT[b]], writes=[rAJ, rST])
                kb.op("dve", lambda e: e.tensor_scalar_mul(ST[:, 2:4], ST[:, 0:2], 1.0 / 1024), reads=[rST], writes=[rST])
                kb.op("dve", lambda e: e.tensor_tensor(ST[:, 4:5], ST[:, 2:3], ST[:, 2:3], ALU.mult), reads=[rST], writes=[rST])
                kb.op("dve", lambda e: e.tensor_tensor(ST[:, 5:6], ST[:, 3:4], ST[:, 4:5], ALU.subtract), reads=[rST], writes=[rST])
                rsqrt_ops(ST[:, 7:8], ST[:, 5:6], 1.0, ST[:, 6:7], [rST], [rST])
                kb.op("dve", (lambda b=b: lambda e: e.tensor_scalar(VT[b], VT[b], ST[:, 2:3], ST[:, 7:8], ALU.subtract, ALU.mult))(),
                      reads=[rST, rVT[b]], writes=[rVT[b]])
                kb.op("pool", (lambda b=b: lambda e: e.tensor_tensor(VT[b], VT[b], LNG, ALU.mult))(), reads=[rAP, rVT[b]], writes=[rVT[b]])
                kb.op("dve", (lambda b=b, t=t: lambda e: e.tensor_tensor(VN3[:, t, :], VT[b], LNB, ALU.add))(),
                      reads=[rAP, rVT[b]], writes=[rVN])
            UT = [A.f32(NT) for _ in range(2)]
            ZT2 = [A.f32(NT) for _ in range(2)]
            rUT = [Res("ut0"), Res("ut1")]
            rZT2 = [Res("zt20"), Res("zt21")]
            TMA = A.f32(NT)
            rTMA = Res("tma")
            for h in range(8):
                b = h % 2
                kb.dma("sp", UT[b][:, 0:NTL], Pch.ap()[CR["Au"] + h * 128:CR["Au"] + (h + 1) * 128, 0:NTL], writes=[rUT[b]])
                kb.dma("sp", ZT2[b][:, 0:NTL], Pch.ap()[CR["Az"] + h * 128:CR["Az"] + (h + 1) * 128, 0:NTL], writes=[rZT2[b]])
                kb.op("act", (lambda b=b: lambda e: e.activation(out=ZT2[b][:, 0:NTL], in_=ZT2[b][:, 0:NTL], func=AF.Silu))(),
                      reads=[rZT2[b]], writes=[rZT2[b]])
                kb.op("pool", (lambda b=b: lambda e: e.tensor_tensor(UT[b][:, 0:NTL], UT[b][:, 0:NTL], ZT2[b][:, 0:NTL], ALU.mult))(),
                      reads=[rZT2[b], rUT[b]], writes=[rUT[b]])
                for t in range(TL):
                    kb.op("pe", (lambda h=h, t=t: lambda e: e.matmul(
                        PS[:, t * 128:(t + 1) * 128], lhsT=VN3[:, t, h * 128:(h + 1) * 128], rhs=WST[:, h * 128:(h + 1) * 128],
                        start=True, stop=True))(), reads=[rVN, rWST], writes=[BK[t // 4]])
                kb.op("dve", (lambda h=h: lambda e: e.tensor_tensor(
                    v3(TMA[:, 0:NTL], TL), v3(PS[:, 0:NTL], TL), bc(SBI[:, h * 128:(h + 1) * 128].unsqueeze(1), [128, TL, 128]),
                    ALU.add))(), reads=[BK[0], BK[1], BK[2], rAP], writes=[rTMA])
                stg, rstg = mix_stage()
                kb.op("dve", (lambda b=b, stg=stg: lambda e: e.tensor_tensor(stg[:, 0:NTL], TMA[:, 0:NTL], UT[b][:, 0:NTL], ALU.mult))(),
                      reads=[rTMA, rUT[b]], writes=[rstg])
                mix_store(h, stg, rstg)
            kb.barrier()
            A.release(pers2_mark)

            HG = A.f32(4 * 240)
            rHG = Res("hg")
            for r in range(4):
                kb.dma("sp", HG[:, r * 240:(r + 1) * 240], gar(4, r, EX_HALO_OFF, 128 * 240).rearrange("(c x) -> c x", c=128), writes=[rHG])
            HG4 = HG.rearrange("p (r g j) -> p r g j", r=4, g=8)
            LH = A.f32(8 * 15)
            RH = A.f32(8 * 15)
            rLR = Res("lr")
            for r in range(4):
                for (dst, so, j0) in ((LH, 4, 15), (RH, 8, 0)):
                    src = HG4[:, r, :, j0:j0 + 15]
                    if r == 0:
                        kb.op("dve", (lambda dst=dst, src=src, so=so, r=r: lambda e: e.tensor_scalar_mul(
                            v3(dst, 8), src, SEL[:, so + r:so + r + 1]))(), reads=[rHG, rCONST], writes=[rLR])
                    else:
                        kb.op("dve", (lambda dst=dst, src=src, so=so, r=r: lambda e: e.scalar_tensor_tensor(
                            out=v3(dst, 8), in0=src, scalar=SEL[:, so + r:so + r + 1], in1=v3(dst, 8), op0=ALU.mult, op1=ALU.add))(),
                            reads=[rHG, rCONST, rLR], writes=[rLR])
            CONV = A.f32(8 * NT)
            CONV3 = v3(CONV, 8)
            rCONV = [Res("conv%d" % i) for i in range(8)]
            YP = [A.bf16(1054 + 286) for _ in range(2)]
            rYP = [Res("yp0"), Res("yp1")]
            DIAG = [A.bf16(31 * 128) for _ in range(2)]
            rDIAG = [Res("diag0"), Res("diag1")]
            CWr = A.f32(248)
            rCWr = Res("cwr")
            kb.op("dve", lambda e: e.tensor_copy(v3(CWr, 8), v3(CW, 31).rearrange("p k g -> p g k")), reads=[rPT], writes=[rCWr])
            SQB = A.f32(NT)
            rSQB = Res("sqb")
            for b in range(2):
                kb.op("pool", (lambda b=b: lambda e: e.memset(YP[b], 0.0))(), writes=[rYP[b]])
            cvb = 0
            for cg in range(8):
                b = cg % 2
                kb.dma("pool", YP[b][:, 15:1039], Ych.ap()[cg * 128:(cg + 1) * 128, 0:1024], writes=[rYP[b]])
                if not last:
                    kb.dma("pool", YP[b][:, 1054 + 15:1054 + 271], Ych.ap()[cg * 128:(cg + 1) * 128, 1024:1280], writes=[rYP[b]])
                kb.op("dve", (lambda b=b, cg=cg: lambda e: e.tensor_copy(YP[b][:, 0:15], v3(LH, 8)[:, cg, :]))(), reads=[rLR], writes=[rYP[b]])
                kb.op("dve", (lambda b=b, cg=cg: lambda e: e.tensor_copy(YP[b][:, 1039:1054], v3(RH, 8)[:, cg, :]))(), reads=[rLR], writes=[rYP[b]])
                kb.op("dve", (lambda b=b, cg=cg: lambda e: e.tensor_tensor(
                    v3(DIAG[b], 31), bc(IDF.unsqueeze(1), [128, 31, 128]), bc(v3(CWr, 8)[:, cg, :].unsqueeze(2), [128, 31, 128]),
                    ALU.mult))(), reads=[rCWr, rCONST], writes=[rDIAG[b]])
                for (ys, co, n) in [(0, 0, 512), (512, 512, 512)] + ([] if last else [(1054, 1024, 256)]):
                    bk = 6 + (cvb % 2)
                    cvb += 1
                    for k in range(31):
                        kb.op("pe", (lambda b=b, bk=bk, k=k, ys=ys, n=n: lambda e: e.matmul(
                            bank(bk, n), lhsT=v3(DIAG[b], 31)[:, k, :], rhs=YP[b][:, ys + k:ys + k + n],
                            start=(k == 0), stop=(k == 30)))(), reads=[rDIAG[b], rYP[b]], writes=[BK[bk]])
                    kb.op("act", (lambda bk=bk, cg=cg, co=co, n=n: lambda e: e.activation(
                        out=CONV3[:, cg, co:co + n], in_=bank(bk, n), func=AF.Identity, bias=PT3[:, cg:cg + 1]))(),
                        reads=[BK[bk], rPT], writes=[rCONV[cg]])
                kb.op("act", (lambda cg=cg: lambda e: e.activation(out=SQB[:, 0:NTL], in_=CONV3[:, cg, 0:NTL], func=AF.Square))(),
                      reads=[rCONV[cg]], writes=[rSQB])
                for i, (t0, tn) in enumerate(tts):
                    kb.op("pe", (lambda cg=cg, i=i, t0=t0, tn=tn: lambda e: e.matmul(
                        bank(i, tn), lhsT=ONF, rhs=CONV3[:, cg, t0:t0 + tn], start=(cg == 0), stop=(cg == 7)))(),
                        reads=[rCONV[cg], rCONST], writes=[BK[i]])
                    kb.op("pe", (lambda cg=cg, i=i, t0=t0, tn=tn: lambda e: e.matmul(
                        bank(3 + i, tn), lhsT=ONF, rhs=SQB[:, t0:t0 + tn], start=(cg == 0), stop=(cg == 7)))(),
                        reads=[rSQB, rCONST], writes=[BK[3 + i]])
            dump("d_CONV", CONV, [128, 8 * NT], reads=rCONV)
            MEAN = A.f32(NT)
            RSTD = A.f32(NT)
            MSQ = A.f32(NT)
            rMS = Res("ms")
            for i, (t0, tn) in enumerate(tts):
                kb.op("dve", (lambda i=i, t0=t0, tn=tn: lambda e: e.tensor_scalar_mul(MEAN[:, t0:t0 + tn], bank(i, tn), 1.0 / 1024))(),
                      reads=[BK[i]], writes=[rMS])
                kb.op("dve", (lambda i=i, t0=t0, tn=tn: lambda e: e.tensor_scalar_mul(RSTD[:, t0:t0 + tn], bank(3 + i, tn), 1.0 / 1024))(),
                      reads=[BK[3 + i]], writes=[rMS])
            kb.op("dve", lambda e: e.tensor_tensor(MSQ[:, 0:NTL], MEAN[:, 0:NTL], MEAN[:, 0:NTL], ALU.mult), reads=[rMS], writes=[rMS])
            kb.op("dve", lambda e: e.tensor_tensor(RSTD[:, 0:NTL], RSTD[:, 0:NTL], MSQ[:, 0:NTL], ALU.subtract), reads=[rMS], writes=[rMS])
            rsqrt_ops(RSTD[:, 0:NTL], RSTD[:, 0:NTL], 1.0, MSQ[:, 0:NTL], [rMS], [rMS])
            dump("d_MEAN", MEAN, [128, NT], reads=[rMS])
            dump("d_RSTD", RSTD, [128, NT], reads=[rMS])
            ZB = [A.f32(NT) for _ in range(2)]
            rZB = [Res("zb0"), Res("zb1")]
            for cg in range(8):
                b = cg % 2
                kb.dma("sp", ZB[b][:, 0:NTL], Pch.ap()[CR["Bz"] + cg * 128:CR["Bz"] + (cg + 1) * 128, 0:NTL], writes=[rZB[b]])
                cv = CONV3[:, cg, 0:NTL]
                kb.op("dve", (lambda cv=cv: lambda e: e.tensor_tensor(cv, cv, MEAN[:, 0:NTL], ALU.subtract))(), reads=[rMS, rCONV[cg]], writes=[rCONV[cg]])
                kb.op("pool", (lambda cv=cv: lambda e: e.tensor_tensor(cv, cv, RSTD[:, 0:NTL], ALU.mult))(), reads=[rMS, rCONV[cg]], writes=[rCONV[cg]])
                kb.op("act", (lambda cv=cv, cg=cg: lambda e: e.activation(out=cv, in_=cv, func=AF.Silu, scale=PT3[:, 8 + cg:9 + cg],
                                                                          bias=PT3[:, 16 + cg:17 + cg]))(), reads=[rPT, rCONV[cg]], writes=[rCONV[cg]])
                kb.op("act", (lambda b=b: lambda e: e.activation(out=ZB[b][:, 0:NTL], in_=ZB[b][:, 0:NTL], func=AF.Silu))(),
                      reads=[rZB[b]], writes=[rZB[b]])
                stg, rstg = mix_stage()
                kb.op("dve", (lambda cv=cv, b=b, stg=stg: lambda e: e.tensor_tensor(stg[:, 0:NTL], cv, ZB[b][:, 0:NTL], ALU.mult))(),
                      reads=[rCONV[cg], rZB[b]], writes=[rstg])
                mix_store(8 + cg, stg, rstg)
            kb.barrier()
            A.release(pers2_mark)

            QT = A.bf16(8 * NT)
            QT3 = v3(QT, 8)
            rQT = Res("qt")
            QF = [A.f32(1024) for _ in range(2)]
            rQF = [Res("qf0"), Res("qf1")]
            QWS = []
            for i_ in range(2):
                QWS.append(dict(QN=A.f32(1024), QR=A.f32(1024), QTS=A.f32(1024), QT1=A.f32(512), QT2=A.f32(512),
                                QRB=A.bf16(1024), SSQ=A.f32(24), r=Res("qw%d" % i_)))
            for t in range(TL):
                b = t % 2
                W_ = QWS[b]
                pb0 = 4 * b
                kb.dma("sp", QF[b], Ptok.ap()[t * 128:(t + 1) * 128, 1024:2048], writes=[rQF[b]])
                qk_norm_rope(QF[b], 8, GQK[:, 0:128], t, W_["QN"], W_["QR"], W_["QTS"], W_["SSQ"], W_["QT1"], W_["QT2"], rQF[b], W_["r"])
                kb.op("act", (lambda W_=W_: lambda e: e.copy(W_["QRB"], W_["QR"]))(), reads=[W_["r"]], writes=[W_["r"]])
                for h in range(8):
                    kb.op("pe", (lambda h=h, W_=W_, pb0=pb0: lambda e: e.matmul(
                        PS[:, pb0 * 512 + h * 128:pb0 * 512 + (h + 1) * 128], lhsT=W_["QRB"][:, h * 128:(h + 1) * 128],
                        rhs=IDB, start=True, stop=True))(), reads=[W_["r"], rCONST], writes=[BK[pb0 + h // 4]])
                kb.op("act", (lambda t=t, pb0=pb0: lambda e: e.copy(QT3[:, :, t * 128:(t + 1) * 128],
                                                                  v3(PS[:, pb0 * 512:pb0 * 512 + 1024], 8)))(),
                      reads=[BK[pb0], BK[pb0 + 1]], writes=[rQT])
            KTA = A.bf16(2 * 4352)
            KTA3 = v3(KTA, 2)
            VAL = A.bf16(34 * 256)
            VAL3 = v3(VAL, 34)
            rKVA = Res("kva")
            for r in range(4):
                kb.dma("pool", KTA3[:, :, r * 1024:(r + 1) * 1024],
                       gar(0, r, 0, 128 * 2048).rearrange("(d g t) -> d g t", d=128, g=2), writes=[rKVA])
                kb.dma("pool", VAL3[:, r * 8:(r + 1) * 8, :],
                       gar(1, r, 0, 1024 * 256).rearrange("(kt p c) -> p kt c", p=128, c=256), writes=[rKVA])
            kb.op("dve", lambda e: e.tensor_copy(KTA3[:, :, 4096:4352], KTC3), reads=[rKTC], writes=[rKVA])
            kb.dma("pool", VAL3[:, 32:34, :], Ptok.ap()[1024:1280, 2304:2560].rearrange("(kt p) c -> p kt c", p=128), writes=[rKVA])
            dump("d_KTA", KTA, [128, 2 * 4352], BF16, reads=[rKVA])
            dump("d_VAL", VAL, [128, 34 * 256], BF16, reads=[rKVA])
            dump("d_QT", QT, [128, 8 * NT], BF16, reads=[rQT])
            SZ = [A.f32(NT) for _ in range(2)]
            rSZ = [Res("sz0"), Res("sz1")]
            PTA = [A.bf16(512) for _ in range(4)]
            rPTA = [Res("pta%d" % i) for i in range(4)]
            RL = A.f32(512)
            rRL = Res("rl")
            OA = A.f32(512)
            rOA = Res("oa")
            SC = 128.0 ** -0.5
            pcount = 0
            for h in range(8):
                g = h // 4
                b = h % 2
                kb.dma("sp", SZ[b][:, 0:NTL], Pch.ap()[CR["Cz"] + h * 128:CR["Cz"] + (h + 1) * 128, 0:NTL], writes=[rSZ[b]])
                kb.op("act", (lambda b=b: lambda e: e.activation(out=SZ[b][:, 0:NTL], in_=SZ[b][:, 0:NTL], func=AF.Silu))(),
                      reads=[rSZ[b]], writes=[rSZ[b]])
                stg, rstg = mix_stage()
                qtiles = [(0, 512, list(range(34))), (512, 512, list(range(34)))] + ([] if last else [(1024, 256, [32, 33])])
                for (q0, qn, kts) in qtiles:
                    nk = len(kts)

                    def emit_s(ki, q0=q0, qn=qn, kts=kts, g=g, h=h):
                        sb = 2 + (ki % 3)
                        kt = kts[ki]
                        kb.op("pe", (lambda: lambda e: e.matmul(
                            bank(sb, qn), lhsT=KTA3[:, g, kt * 128:(kt + 1) * 128], rhs=QT3[:, h, q0:q0 + qn],
                            start=True, stop=True))(), reads=[rKVA, rQT], writes=[BK[sb]])
                    emit_s(0)
                    if nk > 1:
                        emit_s(1)
                    for ki, kt in enumerate(kts):
                        sb = 2 + (ki % 3)
                        pb = pcount % 4
                        pcount += 1
                        if ki + 2 < nk:
                            emit_s(ki + 2)
                        kb.op("act", (lambda sb=sb, pb=pb, qn=qn: lambda e: e.activation(
                            out=PTA[pb][:, 0:qn], in_=bank(sb, qn), func=AF.Exp, scale=SC))(), reads=[BK[sb]], writes=[rPTA[pb]])
                        kb.op("pe", (lambda pb=pb, kt=kt, g=g, qn=qn, ki=ki, nk=nk: lambda e: e.matmul(
                            bank(0, qn), lhsT=VAL3[:, kt, g * 128:(g + 1) * 128], rhs=PTA[pb][:, 0:qn],
                            start=(ki == 0), stop=(ki == nk - 1)))(), reads=[rKVA, rPTA[pb]], writes=[BK[0]])
                        kb.op("pe", (lambda pb=pb, qn=qn, ki=ki, nk=nk: lambda e: e.matmul(
                            bank(1, qn), lhsT=ONB, rhs=PTA[pb][:, 0:qn], start=(ki == 0), stop=(ki == nk - 1)))(),
                            reads=[rCONST, rPTA[pb]], writes=[BK[1]])
                    kb.op("dve", (lambda qn=qn: lambda e: e.reciprocal(RL[:, 0:qn], bank(1, qn)))(), reads=[BK[1]], writes=[rRL])
                    kb.op("dve", (lambda qn=qn: lambda e: e.tensor_tensor(OA[:, 0:qn], bank(0, qn), RL[:, 0:qn], ALU.mult))(),
                          reads=[BK[0], rRL], writes=[rOA])
                    kb.op("pool", (lambda qn=qn, q0=q0, b=b, stg=stg: lambda e: e.tensor_tensor(
                        stg[:, q0:q0 + qn], OA[:, 0:qn], SZ[b][:, q0:q0 + qn], ALU.mult))(), reads=[rOA, rSZ[b]], writes=[rstg])
                mix_store(16 + h, stg, rstg)
            kb.barrier()
            A.release(pers_mark)

            A.release(par_mark)
            BIG2 = A.f32(16 * NT)
            MX = v3(BIG2.bitcast(BF16), 32)
            rMX = Res("mx")
            for kc in range(32):
                kb.dma("sp", MX[:, kc, 0:NTL], MIXd.ap()[kc * 128:(kc + 1) * 128, 0:NTL], writes=[rMX])
            GLn = [A.f32(512) for _ in range(2)]
            GCn = [A.f32(512) for _ in range(2)]
            GBn = [A.f32(512) for _ in range(2)]
            rGn = [Res("gn0"), Res("gn1")]
            WT2 = [A.bf16(32 * 512) for _ in range(2)]
            rWT2 = [Res("w2t0"), Res("w2t1")]
            XO = [A.f32(512) for _ in range(3)]
            rXO = [Res("xo%d" % i) for i in range(3)]
            OO = [A.f32(512) for _ in range(3)]
            rOO = [Res("oo%d" % i) for i in range(3)]
            wsrc2 = wout_in[l].rearrange("(kc kp) n -> kp kc n", kp=128)

            def load_w2(n, b):
                w3 = v3(WT2[b], 32)
                for g in range(4):
                    kb.dma("pool", w3[:, g * 8:(g + 1) * 8, :], wsrc2[:, g * 8:(g + 1) * 8, n * 512:(n + 1) * 512], writes=[rWT2[b]])
            load_w2(0, 0)
            oc = 0
            for n in range(8):
                b = n % 2
                if n + 1 < 8:
                    load_w2(n + 1, 1 - b)
                w3 = v3(WT2[b], 32)
                cs_ = slice(2 * D + n * 512, 2 * D + (n + 1) * 512)
                kb.dma("sp", GLn[b], af[2 * l:2 * l + 1, cs_].partition_broadcast(128), writes=[rGn[b]])
                kb.dma("sp", GBn[b], bada_in[l:l + 1, cs_].partition_broadcast(128), writes=[rGn[b]])
                kb.op("dve", (lambda b=b: lambda e: e.tensor_tensor(GLn[b], GLn[b], GBn[b], ALU.add))(), reads=[rGn[b]], writes=[rGn[b]])
                if not last:
                    kb.dma("sp", GCn[b], af[2 * l + 1:2 * l + 2, cs_].partition_broadcast(128), writes=[rGn[b]])
                    kb.op("dve", (lambda b=b: lambda e: e.tensor_tensor(GCn[b], GCn[b], GBn[b], ALU.add))(), reads=[rGn[b]], writes=[rGn[b]])
                for t in range(TL):
                    bk = t % 2
                    ob = oc % 3
                    oc += 1
                    kb.dma("sp", XO[ob], tok_src(t, n * 512, (n + 1) * 512), writes=[rXO[ob]])
                    for kc in range(32):
                        kb.op("pe", (lambda bk=bk, kc=kc, t=t, w3=w3: lambda e: e.matmul(
                            bank(bk), lhsT=MX[:, kc, t * 128:(t + 1) * 128], rhs=w3[:, kc, :], start=(kc == 0), stop=(kc == 31)))(),
                            reads=[rMX, rWT2[b]], writes=[BK[bk]])
                    Gt = GLn[b] if t < 8 else GCn[b]
                    kb.op("dve", (lambda bk=bk, ob=ob, Gt=Gt: lambda e: e.tensor_tensor(
                        OO[ob], bank(bk), Gt, ALU.mult))(), reads=[BK[bk], rGn[b]], writes=[rOO[ob]])
                    kb.op("pool", (lambda ob=ob: lambda e: e.tensor_tensor(OO[ob], OO[ob], XO[ob], ALU.add))(),
                          reads=[rXO[ob], rOO[ob]], writes=[rOO[ob]])
                    if last:
                        dst = y_out[t * 128:(t + 1) * 128, n * 512:(n + 1) * 512]
                    else:
                        dst = xs.ap()[t * 128:(t + 1) * 128, n * 512:(n + 1) * 512]
                    kb.dma("sp", dst, OO[ob], reads=[rOO[ob]])
            kb.barrier()
            A.release(lay_mark)

        for l_ in range(nlayers):
            emit_layer(l_)

        kb.barrier()
        block = es.enter_context(nc.Block())
        kb.replay(block)
    return nc


def LAYER_BODY_2(env):
    pass


def _consts():
    c = np.zeros((128, 5 * 128), np.float32)
    c[:, 0:128] = np.eye(128, dtype=np.float32)
    c[:, 128:256] = 1.0
    s = np.arange(128)[:, None]
    l_ = np.arange(128)[None, :]
    c[:, 256:384] = (s <= l_).astype(np.float32)
    c[:, 384:512] = (s >= l_).astype(np.float32)
    return c


def _rope_table(seg):
    n = np.arange(seg * 1024, (seg + 1) * 1024)
    row = (n // 64).astype(np.float32)
    col = (n % 64).astype(np.float32)
    freq = (10000.0 ** (-np.arange(32, dtype=np.float32) / 32)).astype(np.float32)
    ang = np.stack([row, col], -1)[..., None] * freq
    cs = np.concatenate([np.cos(ang).reshape(1024, 64), np.sin(ang).reshape(1024, 64)], -1).astype(np.float32)
    return np.ascontiguousarray(cs.reshape(8, 128, 128).transpose(1, 0, 2))


def make_in_maps(inputs):
    f = lambda a: np.ascontiguousarray(np.asarray(a, dtype=np.float32))
    x = f(inputs["x"]); c = f(inputs["c"]); ctx = f(inputs["ctx"]); c_ctx = f(inputs["c_ctx"])
    w_ada = f(inputs["w_ada"])
    shared = {
        "b_ada": f(inputs["b_ada"]), "norm_g": f(inputs["norm_g"]), "w_in": f(inputs["w_in"]),
        "sgu_w": f(inputs["sgu_w"]), "sgu_b": f(inputs["sgu_b"]).reshape(DEPTH, 1024),
        "sgu_ln_g": f(inputs["sgu_ln_g"]), "sgu_ln_b": f(inputs["sgu_ln_b"]),
        "conv_w": f(inputs["conv_w"]).reshape(DEPTH, 248, 128),
        "conv_b": f(inputs["conv_b"]).reshape(DEPTH, 8, 128),
        "conv_ln_g": f(inputs["conv_ln_g"]).reshape(DEPTH, 8, 128),
        "conv_ln_b": f(inputs["conv_ln_b"]).reshape(DEPTH, 8, 128),
        "q_norm_g": f(inputs["q_norm_g"]), "k_norm_g": f(inputs["k_norm_g"]),
        "mlstm_i_bias": f(inputs["mlstm_i_bias"]).reshape(DEPTH, 16),
        "mlstm_f_bias": f(inputs["mlstm_f_bias"]).reshape(DEPTH, 16),
        "mh_norm_g": f(inputs["mh_norm_g"]).reshape(DEPTH, 8, 128),
        "w_out": f(inputs["w_out"]), "consts": _consts(),
    }
    maps = []
    for core in range(8):
        b, s = core // 4, core % 4
        m = dict(shared)
        m["x"] = np.ascontiguousarray(x[b, s * 1024:(s + 1) * 1024])
        m["ctx"] = np.ascontiguousarray(ctx[b])
        cc = np.stack([c[b], c_ctx], 0)
        m["ccT"] = np.ascontiguousarray(cc[:, s * 1024:(s + 1) * 1024].T)
        m["w_ada_s"] = np.ascontiguousarray(w_ada[:, s * 1024:(s + 1) * 1024, :])
        m["rope"] = _rope_table(s)
        sel = np.zeros((128, 12), np.float32)
        sel[:, s] = 1.0
        if s - 1 >= 0:
            sel[:, 4 + s - 1] = 1.0
        if s + 1 <= 3:
            sel[:, 8 + s + 1] = 1.0
        m["sel"] = sel
        maps.append(m)
    return maps


def kernel(**inputs):
    nc = build_program()
    maps = make_in_maps(inputs)
    res = run_bass_kernel_spmd(nc, maps, core_ids=list(range(8)))
    out = np.empty((2, 4096, D), np.float32)
    for core in range(8):
        b, s = core // 4, core % 4
        out[b, s * 1024:(s + 1) * 1024] = res.results[core]["y"]
    return out
```

```python
import math
import numpy as np
import ml_dtypes
from contextlib import ExitStack
import concourse.bass as bass
import concourse.mybir as mybir
from concourse.bass_utils import run_bass_kernel_spmd

F32 = mybir.dt.float32
BF16 = mybir.dt.bfloat16
AF = mybir.ActivationFunctionType
ALU = mybir.AluOpType
AX = mybir.AxisListType

D = 4096
PIN = 13856
DEPTH = 2
NT = 1280
NL = 1024
EPS = 1e-6
LNSC = math.log(128.0 ** -0.5)

TOKB = [("Av", 1024, 1024, 0), ("Cq", 6144, 1024, 1024), ("Ckv", 7168, 512, 2048),
        ("Dk", 9728, 1024, 2560), ("Dv", 10752, 1024, 3584), ("G", 13824, 32, 4608)]
NTOKC = 4640
TC = {n: d for n, _, _, d in TOKB}
CHB = [("Au", 0, 0), ("Az", 2048, 1024), ("Ba", 3072, 2048), ("Bg", 4096, 3072), ("Bz", 5120, 4096),
       ("Cz", 7680, 5120), ("Dq", 8704, 6144), ("DkT", 9728, 7168), ("Do", 11776, 8192), ("Dz", 12800, 9216)]
NCHR = 10240
CR = {n: r for n, _, r in CHB}

EXC_ROWS = [512, 512, 512, 512, 4, 60]
EX_HALO_OFF = 0
EX_DT_OFF = 128 * 240


class Res:
    __slots__ = ("name", "w", "r")

    def __init__(self, name):
        self.name = name
        self.w = None
        self.r = {}


class KB:
    CE = ["pe", "act", "dve", "pool"]

    def __init__(self, nc, es):
        self.nc = nc
        self.engs = ["pe", "act", "dve", "pool", "sp"]
        self.q = {e: [] for e in self.engs}
        self.sem = {e: es.enter_context(nc.semaphore("s_" + e)) for e in self.CE}
        self.cnt = {e: 0 for e in self.CE}
        self.known = {e: {} for e in self.engs}
        self.dsem = {"sp": [es.enter_context(nc.semaphore("d_sp%d" % i)) for i in range(16)],
                     "pool": [es.enter_context(nc.semaphore("d_pl%d" % i)) for i in range(8)]}
        self.dcnt = {"sp": 0, "pool": 0}
        self.duse = {k: [0] * len(v) for k, v in self.dsem.items()}
        self.dlast = {k: [None] * len(v) for k, v in self.dsem.items()}
        self.ccsem = es.enter_context(nc.semaphore("s_cc"))
        self.cccnt = 0
        self.cclast = None

    def _wait(self, eng, ev):
        if ev is None:
            return
        key, sem, val = ev
        if self.known[eng].get(key, 0) >= val:
            return
        self.known[eng][key] = val
        self.q[eng].append(("w", sem, val))

    def _deps(self, eng, reads, writes):
        deps = {}

        def add(ev):
            if ev is None:
                return
            k = ev[0]
            if k not in deps or deps[k][2] < ev[2]:
                deps[k] = ev
        for r in reads:
            add(r.w)
        for w in writes:
            add(w.w)
            for ev in w.r.values():
                add(ev)
        for k, ev in deps.items():
            if eng == "pe" and k == "pe":
                continue
            self._wait(eng, ev)

    def _mark(self, ev, reads, writes):
        for r in reads:
            r.r[ev[0]] = ev
        for w in writes:
            w.w = ev
            w.r = {}

    def op(self, eng, fn, reads=(), writes=()):
        self._deps(eng, reads, writes)
        self.cnt[eng] += 1
        ev = (eng, self.sem[eng], self.cnt[eng])
        self.q[eng].append(("o", fn, self.sem[eng]))
        self._mark(ev, reads, writes)
        return ev

    def dma(self, qn, out, in_, reads=(), writes=()):
        self._deps(qn, reads, writes)
        n = len(self.dsem[qn])
        slot = self.dcnt[qn] % n
        self.dcnt[qn] += 1
        self._wait(qn, self.dlast[qn][slot])
        self.duse[qn][slot] += 1
        sem = self.dsem[qn][slot]
        ev = ("%s%d" % (qn, slot), sem, 16 * self.duse[qn][slot])
        self.dlast[qn][slot] = ev
        self.q[qn].append(("d", out, in_, sem))
        self._mark(ev, reads, writes)
        return ev

    def all_events(self):
        evs = [(e, self.sem[e], self.cnt[e]) for e in self.CE if self.cnt[e] > 0]
        for qn in self.dlast:
            evs += [ev for ev in self.dlast[qn] if ev is not None]
        if self.cclast is not None:
            evs.append(self.cclast)
        return evs

    def barrier(self):
        evs = self.all_events()
        for eng in self.engs:
            for ev in evs:
                if eng == "pe" and ev[0] == "pe":
                    continue
                self._wait(eng, ev)

    def collective(self, kind, alu, groups, in_ap, out_ap):
        self.barrier()
        self.cccnt += 1
        self.q["pool"].append(("c", kind, alu, groups, in_ap, out_ap))
        self.cclast = ("cc", self.ccsem, self.cccnt)
        self.barrier()

    def issue_collectives(self, kind, alu, groups, pairs):
        self.barrier()
        for in_ap, out_ap in pairs:
            self.cccnt += 1
            self.q["pool"].append(("c", kind, alu, groups, in_ap, out_ap))
        self.cclast = ("cc", self.ccsem, self.cccnt)

    def replay(self, block):
        def mk(eng):
            def f(e):
                for it in self.q[eng]:
                    k = it[0]
                    if k == "w":
                        e.wait_ge(it[1], it[2])
                    elif k == "o":
                        it[1](e).then_inc(it[2], 1)
                    elif k == "d":
                        e.dma_start(out=it[1], in_=it[2]).then_inc(it[3], 16)
                    elif k == "c":
                        e.collective_compute(it[1], it[2], replica_groups=it[3], ins=[it[4]],
                                             outs=[it[5]]).then_inc(self.ccsem)
            return f
        block.sync(mk("sp"))
        block.gpsimd(mk("pool"))
        block.scalar(mk("act"))
        block.vector(mk("dve"))
        block.tensor(mk("pe"))


class Arena:
    def __init__(self, ap, nfloats):
        self.ap = ap
        self.n = nfloats
        self.top = 0
        self.kb = None

    def mark(self):
        return self.top

    def release(self, m):
        if self.kb is not None:
            self.kb.barrier()
        self.top = m

    def f32(self, n):
        n4 = (n + 7) // 8 * 8
        assert self.top + n4 <= self.n, ("SBUF arena overflow", self.top, n4, self.n)
        v = self.ap[:, self.top:self.top + n]
        self.top += n4
        return v

    def bf16(self, n):
        nf = (n + 1) // 2
        return self.f32(nf).bitcast(BF16)[:, 0:n]


def build_program(debug=None, stop_after=None, nlayers=DEPTH):
    debug = debug or set()
    nc = bass.Bass("TRN2", target_bir_lowering=False)

    def din(name, shape, dt=F32):
        return nc.dram_tensor(name, list(shape), dt, kind="ExternalInput").ap()

    def dscr(name, shape, dt=F32):
        kind = "ExternalOutput" if name in debug else "Internal"
        return nc.dram_tensor(name, list(shape), dt, kind=kind)

    x_in = din("x", [NL, D])
    ctx_in = din("ctx", [256, D])
    ccT_in = din("ccT", [1024, 2])
    wada_in = din("w_ada_s", [DEPTH, 1024, 3 * D])
    bada_in = din("b_ada", [DEPTH, 3 * D])
    normg_in = din("norm_g", [DEPTH, D])
    win_in = din("w_in", [DEPTH, D, PIN])
    sguw_in = din("sgu_w", [DEPTH, 8, 128, 128])
    sgub_in = din("sgu_b", [DEPTH, 1024])
    sglg_in = din("sgu_ln_g", [DEPTH, 1024])
    sglb_in = din("sgu_ln_b", [DEPTH, 1024])
    convw_in = din("conv_w", [DEPTH, 248, 128])
    convb_in = din("conv_b", [DEPTH, 8, 128])
    cvlg_in = din("conv_ln_g", [DEPTH, 8, 128])
    cvlb_in = din("conv_ln_b", [DEPTH, 8, 128])
    qng_in = din("q_norm_g", [DEPTH, 128])
    kng_in = din("k_norm_g", [DEPTH, 128])
    ib_in = din("mlstm_i_bias", [DEPTH, 16])
    fb_in = din("mlstm_f_bias", [DEPTH, 16])
    mhg_in = din("mh_norm_g", [DEPTH, 8, 128])
    wout_in = din("w_out", [DEPTH, D, D])
    consts_in = din("consts", [128, 5 * 128])
    rope_in = din("rope", [128, 8, 128])
    sel_in = din("sel", [128, 12])
    y_out = nc.dram_tensor("y", [NL, D], F32, kind="ExternalOutput").ap()

    ada_part = nc.dram_tensor("ada_part", [4, 3 * D], F32)
    ada_full = nc.dram_tensor("ada_full", [4, 3 * D], F32)
    dbg_ada = dscr("dbg_ada", [4, 3 * D]) if "dbg_ada" in debug else None
    xs = dscr("xs", [NT, D])
    Ptok = dscr("Ptok", [NT, NTOKC])
    Pch = dscr("Pch", [NCHR, NT])
    Ych = dscr("Ych", [1024, NT])
    exp_bufs = [nc.dram_tensor("exp_buf%d" % c, [EXC_ROWS[c], 512], F32) for c in range(6)]
    gat_bufs = [nc.dram_tensor("gat_buf%d" % c, [4 * EXC_ROWS[c], 512], F32) for c in range(6)]
    dbg_gat = None
    dbg_mixT = dscr("dbg_mixT", [128, 32 * NT], BF16) if "dbg_mixT" in debug else None
    dbg_HT = dscr("dbg_HT", [128, 32 * NT], BF16) if "dbg_HT" in debug else None

    exp_flat = [b_.ap().rearrange("r c -> (r c)") for b_ in exp_bufs]
    gat_flat = [b_.ap().rearrange("r c -> (r c)") for b_ in gat_bufs]

    def exr(c, off, n):
        return exp_flat[c][off:off + n]

    def gar(c, r, off, n):
        o = r * EXC_ROWS[c] * 512 + off
        return gat_flat[c][o:o + n]

    GROUPS = [[0, 1, 2, 3], [4, 5, 6, 7]]

    with ExitStack() as es:
        ARN = 50 * 1024
        arena_t = es.enter_context(nc.sbuf_tensor("arena", [128, ARN], F32))
        A = Arena(arena_t[:, :], ARN)
        psum_t = es.enter_context(nc.psum_tensor("psum", [128, 4096], F32))
        PS = psum_t[:, :]
        kb = KB(nc, es)
        A.kb = kb
        BK = [Res("bank%d" % i) for i in range(8)]

        def bank(i, n=512):
            return PS[:, i * 512:i * 512 + n]

        def v3(ap, a):
            return ap.rearrange("p (a b) -> p a b", a=a)

        CONST = A.f32(4 * 128)
        IDF = CONST[:, 0:128]
        ONF = CONST[:, 128:256]
        TRIF = CONST[:, 256:384]
        TRIB = CONST[:, 384:512]
        IDB = A.bf16(128)
        ONB = A.bf16(128)
        SEL = A.f32(12)
        ROPE = A.f32(8 * 128)
        rCONST = Res("const")
        kb.dma("sp", CONST, consts_in[:, 0:512], writes=[rCONST])
        kb.dma("sp", SEL, sel_in[:, :], writes=[rCONST])
        kb.dma("sp", ROPE, rope_in.rearrange("p a b -> p (a b)"), writes=[rCONST])
        kb.op("dve", lambda e: e.tensor_copy(IDB, IDF), reads=[rCONST], writes=[rCONST])
        kb.op("dve", lambda e: e.tensor_copy(ONB, ONF), reads=[rCONST], writes=[rCONST])
        ROPE3 = v3(ROPE, 8)
        kb.barrier()

        rBIGd = Res("bigd")
        rBIGa = Res("biga")
        MIXd = dscr("MIXd", [32 * 128, NT], BF16)
        MST = [A.bf16(NT) for _ in range(2)]
        rMST = [Res("mst0"), Res("mst1")]
        mst_i = [0]

        def mix_stage():
            i = mst_i[0] % 2
            mst_i[0] += 1
            return MST[i], rMST[i]

        def mix_store(kc, stg, rstg):
            kb.dma("sp", MIXd.ap()[kc * 128:(kc + 1) * 128, 0:NTL_cur[0]], stg[:, 0:NTL_cur[0]], reads=[rstg])
        NTL_cur = [NT]
        dumped = set()

        def dump(name, ap2d, shape, dt=F32, reads=()):
            if name not in debug or name in dumped:
                return
            dumped.add(name)
            dtn = nc.dram_tensor(name, list(shape), dt, kind="ExternalOutput")
            kb.dma("sp", dtn.ap(), ap2d, reads=list(reads))

        def transpose_rows(src, R, dst, bk, rsrc, rdst):
            kb.op("pe", lambda e: e.matmul(bank(bk, R), lhsT=src[0:R, :], rhs=IDF[0:R, 0:R], start=True, stop=True),
                  reads=[rsrc, rCONST], writes=[BK[bk]])
            kb.op("dve", lambda e: e.tensor_copy(dst, bank(bk, R)), reads=[BK[bk]], writes=[rdst])

        m0 = A.mark()
        CCT = A.f32(16)
        rCCT = Res("cct")
        kb.dma("sp", v3(CCT, 8), ccT_in.rearrange("(kc kp) r -> kp kc r", kp=128), writes=[rCCT])
        kb.op("act", lambda e: e.activation(out=CCT, in_=CCT, func=AF.Silu), reads=[rCCT], writes=[rCCT])
        CCT3 = v3(CCT, 8)
        WA = [A.f32(4096) for _ in range(3)]
        rWA = [Res("wa%d" % i) for i in range(3)]
        AST = [A.f32(4096) for _ in range(2)]
        rAST = [Res("ast%d" % i) for i in range(2)]
        it = 0
        for l in range(DEPTH):
            for third in range(3):
                c0 = third * 4096
                for kc in range(8):
                    b = it % 3
                    it += 1
                    kb.dma("sp", WA[b], wada_in[l][kc * 128:(kc + 1) * 128, c0:c0 + 4096], writes=[rWA[b]])
                    for n in range(8):
                        kb.op("pe", (lambda b=b, kc=kc, n=n: lambda e: e.matmul(
                            bank(n)[0:2, :], lhsT=CCT3[:, kc, :], rhs=WA[b][:, n * 512:(n + 1) * 512],
                            start=(kc == 0), stop=(kc == 7)))(), reads=[rCCT, rWA[b]], writes=[BK[n]])
                ab = third % 2
                for n in range(8):
                    if n % 2 == 0:
                        kb.op("act", (lambda ab=ab, n=n: lambda e: e.copy(AST[ab][0:2, n * 512:(n + 1) * 512], bank(n)[0:2, :]))(),
                              reads=[BK[n]], writes=[rAST[ab]])
                    else:
                        kb.op("dve", (lambda ab=ab, n=n: lambda e: e.tensor_copy(AST[ab][0:2, n * 512:(n + 1) * 512], bank(n)[0:2, :]))(),
                              reads=[BK[n]], writes=[rAST[ab]])
                kb.dma("sp", ada_part.ap()[2 * l:2 * l + 2, c0:c0 + 4096], AST[ab][0:2, :], reads=[rAST[ab]])
        kb.collective("AllReduce", ALU.add, GROUPS, ada_part.ap().opt(), ada_full.ap().opt())
        A.release(m0)
        if dbg_ada is not None:
            kb.dma("sp", dbg_ada.ap(), ada_full.ap())
            kb.barrier()
        if stop_after == "ada":
            nlayers = 0

        def emit_layer(l):
            last = (l == DEPTH - 1)
            NTL = NL if last else NT
            TL = NTL // 128
            wl = win_in[l]
            lay_mark = A.mark()

            def tok_src(t, c0, c1, l=l):
                if l == 0:
                    if t < 8:
                        return x_in[t * 128:(t + 1) * 128, c0:c1]
                    return ctx_in[(t - 8) * 128:(t - 7) * 128, c0:c1]
                return xs.ap()[t * 128:(t + 1) * 128, c0:c1]

            R1 = A.f32(128)
            R2 = A.f32(128)
            R3 = A.f32(128)
            R4a = A.f32(128)
            R4b = A.f32(128)
            rR = Res("R")
            af = ada_full.ap()
            srcs1 = [af[2 * l:2 * l + 1, D:2 * D], af[2 * l:2 * l + 1, 0:D],
                     af[2 * l + 1:2 * l + 2, D:2 * D], af[2 * l + 1:2 * l + 2, 0:D]]
            for i, s_ in enumerate(srcs1):
                kb.dma("sp", R1[32 * i:32 * i + 32, :], s_.rearrange("o (a b) -> (o a) b", b=128), writes=[rR])
            srcs2 = [bada_in[l:l + 1, D:2 * D], bada_in[l:l + 1, 0:D], normg_in[l:l + 1, :]]
            for i, s_ in enumerate(srcs2):
                kb.dma("sp", R2[32 * i:32 * i + 32, :], s_.rearrange("o (a b) -> (o a) b", b=128), writes=[rR])
            for i, s_ in enumerate([convb_in[l], cvlg_in[l], cvlb_in[l], mhg_in[l]]):
                kb.dma("sp", R3[8 * i:8 * i + 8, :], s_, writes=[rR])
            kb.dma("sp", R4a[0:124, :], convw_in[l][0:124, :], writes=[rR])
            kb.dma("sp", R4b[0:124, :], convw_in[l][124:248, :], writes=[rR])
            PT1 = A.f32(128)
            PT2 = A.f32(96)
            PT3 = A.f32(32)
            CW = A.f32(248)
            rPT = Res("PT")
            transpose_rows(R1, 128, PT1, 0, rR, rPT)
            transpose_rows(R2, 96, PT2, 1, rR, rPT)
            transpose_rows(R3, 32, PT3, 2, rR, rPT)
            transpose_rows(R4a, 124, CW[:, 0:124], 3, rR, rPT)
            transpose_rows(R4b, 124, CW[:, 124:248], 4, rR, rPT)
            MOD = A.f32(128)
            for j in range(2):
                kb.op("dve", (lambda j=j: lambda e: e.scalar_tensor_tensor(
                    out=MOD[:, 64 * j:64 * j + 32], in0=PT1[:, 64 * j:64 * j + 32], scalar=1.0, in1=PT2[:, 0:32],
                    op0=ALU.add, op1=ALU.add))(), reads=[rPT], writes=[rPT])
                kb.op("dve", (lambda j=j: lambda e: e.tensor_tensor(
                    out=MOD[:, 64 * j:64 * j + 32], in0=MOD[:, 64 * j:64 * j + 32], in1=PT2[:, 64:96],
                    op=ALU.mult))(), reads=[rPT], writes=[rPT])
                kb.op("dve", (lambda j=j: lambda e: e.tensor_tensor(
                    out=MOD[:, 64 * j + 32:64 * j + 64], in0=PT1[:, 64 * j + 32:64 * j + 64], in1=PT2[:, 32:64],
                    op=ALU.add))(), reads=[rPT], writes=[rPT])
            kb.barrier()
            par_mark = A.mark()
            NTL_cur[0] = NTL
            BIG = A.f32(16 * NT)
            BIGB = BIG.bitcast(BF16)
            HT = v3(BIGB, 32)
            big_mark = A.mark()

            XT = [A.f32(D) for _ in range(2)]
            rXT = [Res("xt0"), Res("xt1")]
            JUNK = A.bf16(D)
            rJ = Res("junk")
            SS = A.f32(8)
            rSS = Res("ss")
            for t in range(10):
                b = t % 2
                kb.dma("sp", XT[b], tok_src(t, 0, D), writes=[rXT[b]])
                kb.op("act", (lambda b=b: lambda e: e.activation(out=JUNK, in_=XT[b], func=AF.Square,
                                                                   accum_out=SS[:, 0:1]))(),
                      reads=[rXT[b]], writes=[rJ, rSS])
                kb.op("dve", lambda e: e.tensor_scalar(SS[:, 1:2], SS[:, 0:1], 1.0 / D, EPS, ALU.mult, ALU.add),
                      reads=[rSS], writes=[rSS])
                kb.op("act", lambda e: e.sqrt(SS[:, 3:4], SS[:, 1:2]), reads=[rSS], writes=[rSS])
                kb.op("dve", lambda e: e.reciprocal(SS[:, 2:3], SS[:, 3:4]), reads=[rSS], writes=[rSS])
                kb.op("act", (lambda b=b: lambda e: e.activation(out=XT[b], in_=XT[b], func=AF.Copy,
                                                                   scale=SS[:, 2:3]))(),
                      reads=[rXT[b], rSS], writes=[rXT[b]])
                mo = 0 if t < 8 else 64
                for g4 in range(8):
                    bk = g4 % 4
                    for j in range(4):
                        kc = g4 * 4 + j
                        kb.op("pe", (lambda b=b, kc=kc, bk=bk, j=j: lambda e: e.matmul(
                            bank(bk)[:, j * 128:(j + 1) * 128], lhsT=XT[b][:, kc * 128:(kc + 1) * 128], rhs=IDF,
                            start=True, stop=True))(), reads=[rXT[b], rCONST], writes=[BK[bk]])
                    for j in range(4):
                        kc = g4 * 4 + j
                        dst = HT[:, kc, t * 128:(t + 1) * 128]
                        src = bank(bk)[:, j * 128:(j + 1) * 128]
                        if g4 % 2 == 0:
                            kb.op("dve", (lambda dst=dst, src=src, kc=kc, mo=mo: lambda e: e.tensor_scalar(
                                dst, src, MOD[:, mo + kc:mo + kc + 1], MOD[:, mo + 32 + kc:mo + 33 + kc],
                                ALU.mult, ALU.add))(), reads=[BK[bk], rPT], writes=[rBIGd])
                        else:
                            kb.op("act", (lambda dst=dst, src=src, kc=kc, mo=mo: lambda e: e.activation(
                                out=dst, in_=src, func=AF.Identity, scale=MOD[:, mo + kc:mo + kc + 1],
                                bias=MOD[:, mo + 32 + kc:mo + 33 + kc]))(), reads=[BK[bk], rPT], writes=[rBIGa])
            kb.barrier()
            if dbg_HT is not None and l == 0:
                kb.dma("sp", dbg_HT.ap(), BIGB, reads=[rBIGd, rBIGa])
                kb.barrier()
            A.release(big_mark)
            if stop_after == "norm":
                return

            WT = [A.bf16(32 * 512) for _ in range(2)]
            rWT = [Res("wt0"), Res("wt1")]
            STG = [A.f32(NT) for _ in range(2)]
            rSTG = [Res("stg0"), Res("stg1")]
            tiles = []
            for name, c0, ncol, d0 in TOKB:
                for j in range(0, ncol, 512):
                    tiles.append(("tok", name, c0 + j, min(512, ncol - j), d0 + j))
            for name, c0, r0 in CHB:
                for j in range(0, 1024, 512):
                    tiles.append(("ch", name, c0 + j, 512, r0 + j))
            wsrc = wl.rearrange("(kc kp) n -> kp kc n", kp=128)

            def load_w(src3, c0, ncol, b):
                w3 = v3(WT[b], 32)
                for g in range(4):
                    kb.dma("pool", w3[:, g * 8:(g + 1) * 8, 0:ncol], src3[:, g * 8:(g + 1) * 8, c0:c0 + ncol],
                           writes=[rWT[b]])

            stg_i = 0
            ev_i = 0
            load_w(wsrc, tiles[0][2], tiles[0][3], 0)
            for ti, (kind, name, c0, ncol, d0) in enumerate(tiles):
                b = ti % 2
                if ti + 1 < len(tiles):
                    load_w(wsrc, tiles[ti + 1][2], tiles[ti + 1][3], 1 - b)
                w3 = v3(WT[b], 32)
                if kind == "tok":
                    nt_ = 8 if (last and name in ("Av", "Cq")) else 10
                    for t in range(nt_):
                        bk = 6 + (t % 2)
                        for kc in range(32):
                            kb.op("pe", (lambda bk=bk, kc=kc, t=t, w3=w3, ncol=ncol: lambda e: e.matmul(
                                bank(bk, ncol), lhsT=HT[:, kc, t * 128:(t + 1) * 128], rhs=w3[:, kc, 0:ncol],
                                start=(kc == 0), stop=(kc == 31)))(), reads=[rBIGd, rBIGa, rWT[b]], writes=[BK[bk]])
                        sb = stg_i % 2
                        stg_i += 1
                        eng = "act" if ev_i % 2 == 0 else "dve"
                        ev_i += 1
                        if eng == "act":
                            kb.op("act", (lambda sb=sb, bk=bk, ncol=ncol: lambda e: e.copy(
                                STG[sb][:, 0:ncol], bank(bk, ncol)))(), reads=[BK[bk]], writes=[rSTG[sb]])
                        else:
                            kb.op("dve", (lambda sb=sb, bk=bk, ncol=ncol: lambda e: e.tensor_copy(
                                STG[sb][:, 0:ncol], bank(bk, ncol)))(), reads=[BK[bk]], writes=[rSTG[sb]])
                        kb.dma("sp", Ptok.ap()[t * 128:(t + 1) * 128, d0:d0 + ncol], STG[sb][:, 0:ncol],
                               reads=[rSTG[sb]])
                else:
                    ntok = NTL
                    tts = [(0, 512), (512, 512)] + ([(1024, 256)] if ntok > 1024 else [])
                    for cb in range(4):
                        bset = 3 * (cb % 2)
                        for kc in range(32):
                            for i, (t0, tn) in enumerate(tts):
                                kb.op("pe", (lambda bset=bset, i=i, kc=kc, cb=cb, t0=t0, tn=tn, w3=w3: lambda e: e.matmul(
                                    bank(bset + i, tn), lhsT=w3[:, kc, cb * 128:(cb + 1) * 128],
                                    rhs=HT[:, kc, t0:t0 + tn], start=(kc == 0), stop=(kc == 31)))(),
                                    reads=[rBIGd, rBIGa, rWT[b]], writes=[BK[bset + i]])
                        sb = stg_i % 2
                        stg_i += 1
                        for i, (t0, tn) in enumerate(tts):
                            eng = "act" if ev_i % 2 == 0 else "dve"
                            ev_i += 1
                            if eng == "act":
                                kb.op("act", (lambda sb=sb, bset=bset, i=i, t0=t0, tn=tn: lambda e: e.copy(
                                    STG[sb][:, t0:t0 + tn], bank(bset + i, tn)))(),
                                    reads=[BK[bset + i]], writes=[rSTG[sb]])
                            else:
                                kb.op("dve", (lambda sb=sb, bset=bset, i=i, t0=t0, tn=tn: lambda e: e.tensor_copy(
                                    STG[sb][:, t0:t0 + tn], bank(bset + i, tn)))(),
                                    reads=[BK[bset + i]], writes=[rSTG[sb]])
                        kb.dma("sp", Pch.ap()[d0 + cb * 128:d0 + (cb + 1) * 128, 0:ntok], STG[sb][:, 0:ntok],
                               reads=[rSTG[sb]])
            kb.barrier()
            A.release(par_mark)
            if stop_after == "gemm1":
                return
            tts = [(0, 512), (512, 512)] + ([(1024, 256)] if NTL > 1024 else [])
            KTC = A.bf16(2 * 256)
            KTC3 = v3(KTC, 2)
            rKTC = Res("ktc")
            IBFB = A.f32(32)
            GQK = A.f32(256)
            rPB = Res("pb")
            kb.dma("sp", IBFB[:, 0:16], ib_in[l:l + 1, :].partition_broadcast(128), writes=[rPB])
            kb.dma("sp", IBFB[:, 16:32], fb_in[l:l + 1, :].partition_broadcast(128), writes=[rPB])
            kb.dma("sp", GQK[:, 0:128], qng_in[l:l + 1, :].partition_broadcast(128), writes=[rPB])
            kb.dma("sp", GQK[:, 128:256], kng_in[l:l + 1, :].partition_broadcast(128), writes=[rPB])
            pers2_mark = A.mark()
            HS = A.f32(8 * NT)
            HS3 = v3(HS, 8)
            rHS = Res("hs")
            GP = A.f32(10 * 5 * 16)
            GP4 = GP.rearrange("p (t k g) -> p t k g", t=10, k=5)
            rGP = Res("gp")
            CS = A.f32(16 * 256)
            CS3 = v3(CS, 16)
            rCS = Res("cs")
            CB = A.bf16(16 * 256)
            CB3 = v3(CB, 16)
            rCB = Res("cb")
            STFB = A.f32(16 * 256)
            rSTFB = Res("stfb")
            pers_mark = A.mark()

            def bc(ap, shape):
                return ap.broadcast_to(list(shape))

            def rsqrt_ops(dst, src, scale, tmp, rr, rw):
                kb.op("dve", lambda e: e.tensor_scalar(tmp, src, scale, EPS, ALU.mult, ALU.add), reads=rr, writes=rw)
                kb.op("act", lambda e: e.sqrt(tmp, tmp), reads=rw, writes=rw)
                kb.op("dve", lambda e: e.reciprocal(dst, tmp), reads=rw, writes=rw)

            def rope_ops(src, dst, H, t, T1, T2, rs, rd, rt):
                s5 = src.rearrange("p (h a c j) -> p h a c j", h=H, a=2, c=2)
                d5 = dst.rearrange("p (h a c j) -> p h a c j", h=H, a=2, c=2)
                x1, x2 = s5[:, :, :, 0, :], s5[:, :, :, 1, :]
                o1, o2 = d5[:, :, :, 0, :], d5[:, :, :, 1, :]
                cs = bc(ROPE3[:, t, 0:64].rearrange("p (a j) -> p a j", a=2).unsqueeze(1), [128, H, 2, 32])
                sn = bc(ROPE3[:, t, 64:128].rearrange("p (a j) -> p a j", a=2).unsqueeze(1), [128, H, 2, 32])
                t1 = T1.rearrange("p (h a j) -> p h a j", h=H, a=2)
                t2 = T2.rearrange("p (h a j) -> p h a j", h=H, a=2)
                kb.op("dve", lambda e: e.tensor_tensor(o1, x1, cs, ALU.mult), reads=[rs, rCONST], writes=[rd])
                kb.op("pool", lambda e: e.tensor_tensor(t1, x2, sn, ALU.mult), reads=[rs, rCONST], writes=[rt])
                kb.op("dve", lambda e: e.tensor_tensor(o1, o1, t1, ALU.subtract), reads=[rt], writes=[rd])
                kb.op("dve", lambda e: e.tensor_tensor(o2, x2, cs, ALU.mult), reads=[rs, rCONST], writes=[rd])
                kb.op("pool", lambda e: e.tensor_tensor(t2, x1, sn, ALU.mult), reads=[rs, rCONST], writes=[rt])
                kb.op("dve", lambda e: e.tensor_tensor(o2, o2, t2, ALU.add), reads=[rt], writes=[rd])

            def qk_norm_rope(src, H, gq, t, NRM, ROT, TS, SSQ, T1, T2, rsrc, rw):
                kb.op("act", lambda e: e.activation(out=TS, in_=src, func=AF.Square), reads=[rsrc], writes=[rw])
                kb.op("dve", lambda e: e.tensor_reduce(out=SSQ[:, 0:H], in_=v3(TS, H), axis=AX.X, op=ALU.add),
                      reads=[rw], writes=[rw])
                rsqrt_ops(SSQ[:, 16:16 + H], SSQ[:, 0:H], 1.0 / 128, SSQ[:, 8:8 + H], [rw], [rw])
                kb.op("dve", lambda e: e.tensor_tensor(v3(NRM, H), v3(src, H), bc(SSQ[:, 16:16 + H].unsqueeze(2), [128, H, 128]),
                                                       ALU.mult), reads=[rsrc, rw], writes=[rw])
                kb.op("dve", lambda e: e.tensor_tensor(v3(NRM, H), v3(NRM, H), bc(gq.unsqueeze(1), [128, H, 128]), ALU.mult),
                      reads=[rw, rPB], writes=[rw])
                if t < 8:
                    rope_ops(NRM, ROT, H, t, T1, T2, rw, rw, rw)
                else:
                    kb.op("dve", lambda e: e.tensor_copy(ROT, NRM), reads=[rw], writes=[rw])

            GG = A.f32(320)
            XF = A.f32(160)
            IG = A.f32(160)
            LF = A.f32(160)
            rGG = Res("gg")
            G5 = GG.rearrange("p (t d i h) -> p t d i h", t=10, d=2, i=2)
            x4 = lambda ap: ap.rearrange("p (t d h) -> p t d h", t=10, d=2)
            kb.dma("sp", v3(GG, 10), Ptok.ap()[:, 4608:4640].rearrange("(t p) c -> p t c", p=128), writes=[rGG])
            kb.op("dve", lambda e: e.tensor_tensor(x4(XF), G5[:, :, :, 1, :], bc(v3(IBFB[:, 16:32], 2).unsqueeze(1), [128, 10, 2, 8]),
                                                   ALU.add), reads=[rGG, rPB], writes=[rGG])
            kb.op("dve", lambda e: e.tensor_tensor(x4(IG), G5[:, :, :, 0, :], bc(v3(IBFB[:, 0:16], 2).unsqueeze(1), [128, 10, 2, 8]),
                                                   ALU.add), reads=[rGG, rPB], writes=[rGG])
            kb.op("act", lambda e: e.activation(out=LF, in_=XF, func=AF.Exp, scale=-1.0), reads=[rGG], writes=[rGG])
            kb.op("dve", lambda e: e.tensor_scalar_add(LF, LF, 1.0), reads=[rGG], writes=[rGG])
            kb.op("act", lambda e: e.activation(out=LF, in_=LF, func=AF.Ln), reads=[rGG], writes=[rGG])
            kb.op("dve", lambda e: e.tensor_scalar_mul(LF, LF, -1.0), reads=[rGG], writes=[rGG])
            LF3 = v3(LF, 10)
            for t in range(10):
                kb.op("pe", (lambda t=t: lambda e: e.matmul(bank(1)[:, t * 32:t * 32 + 8], lhsT=TRIF, rhs=LF3[:, t, 0:8],
                                                            start=True, stop=True))(), reads=[rGG, rCONST], writes=[BK[1]])
                kb.op("pe", (lambda t=t: lambda e: e.matmul(bank(1)[:, t * 32 + 8:t * 32 + 16], lhsT=TRIB, rhs=LF3[:, t, 8:16],
                                                            start=True, stop=True))(), reads=[rGG, rCONST], writes=[BK[1]])
                kb.op("pe", (lambda t=t: lambda e: e.matmul(bank(1)[:, t * 32 + 16:t * 32 + 32], lhsT=ONF, rhs=LF3[:, t, 0:16],
                                                            start=True, stop=True))(), reads=[rGG, rCONST], writes=[BK[1]])
            PS1 = v3(bank(1, 320), 10)
            kb.op("dve", lambda e: e.tensor_copy(GP4[:, :, 0, :], PS1[:, :, 0:16]), reads=[BK[1]], writes=[rGP])
            kb.op("dve", lambda e: e.tensor_copy(GP4[:, :, 3, :], PS1[:, :, 16:32]), reads=[BK[1]], writes=[rGP])
            kb.op("dve", lambda e: e.scalar_tensor_tensor(out=GP4[:, :, 1, :], in0=v3(IG, 10), scalar=LNSC, in1=GP4[:, :, 0, :],
                                                          op0=ALU.add, op1=ALU.subtract), reads=[rGG, rGP], writes=[rGP])
            kb.op("dve", lambda e: e.tensor_tensor(v3(XF, 10), GP4[:, :, 3, :], GP4[:, :, 1, :], ALU.add), reads=[rGP], writes=[rGG])
            kb.op("act", lambda e: e.activation(out=GP4[:, :, 2, :], in_=v3(XF, 10), func=AF.Exp), reads=[rGG], writes=[rGP])
            kb.op("act", lambda e: e.activation(out=GP4[:, :, 4, :], in_=GP4[:, :, 3, :], func=AF.Exp), reads=[rGP], writes=[rGP])
            dump("d_GP", GP, [128, 800], reads=[rGP])
            A.release(pers_mark)

            KTOK = [A.f32(1024) for _ in range(2)]
            VTOK = [A.f32(1024) for _ in range(2)]
            rKTOK = [Res("ktok0"), Res("ktok1")]
            rVTOK = [Res("vtok0"), Res("vtok1")]
            KTLs = [A.bf16(1024) for _ in range(2)]
            rKTLs = [Res("ktl0"), Res("ktl1")]
            VA = [A.bf16(8 * 256) for _ in range(2)]
            rVA = [Res("va0"), Res("va1")]
            QTb = A.bf16(1024)
            KTb = A.bf16(1024)
            rQKb = Res("qkb")
            DG = A.f32(1024)
            rDG = Res("dg")
            EB = A.f32(1024)
            rEB = Res("eb")
            WTm = A.f32(1024)
            rWTm = Res("wtm")
            PTb = A.bf16(1024)
            rPTb = Res("ptb")
            QTL = A.bf16(1024)
            rQTL = Res("qtl")
            DN = A.f32(1024)
            rDN = Res("dn")
            TMPH = A.f32(1024)
            rTMPH = Res("tmph")
            for b in range(2):
                kb.op("pool", (lambda b=b: lambda e: e.memset(v3(VA[b], 8)[:, :, 128:256], 1.0))(), writes=[rVA[b]])
            scan_ctr = [0]

            def scan(chunks, dr, emit):
                g0 = dr * 8
                MASK = TRIF if dr == 0 else TRIB
                for t in chunks:
                    b = scan_ctr[0] % 2
                    scan_ctr[0] += 1
                    va3 = v3(VA[b], 8)
                    KTL = KTLs[b]
                    rKTL = rKTLs[b]
                    stb = 2 if (emit or b == 0) else 4
                    kb.dma("sp", KTOK[b], Ptok.ap()[t * 128:(t + 1) * 128, 2560:3584], writes=[rKTOK[b]])
                    kb.dma("sp", VTOK[b], Ptok.ap()[t * 128:(t + 1) * 128, 3584:4608], writes=[rVTOK[b]])
                    kb.op("dve", (lambda b=b, t=t, KTL=KTL: lambda e: e.tensor_tensor(
                        v3(KTL, 8), v3(KTOK[b], 8), bc(GP4[:, t, 2, g0:g0 + 8].unsqueeze(2), [128, 8, 128]), ALU.mult))(),
                        reads=[rKTOK[b], rGP], writes=[rKTL])
                    kb.op("act", (lambda b=b, va3=va3: lambda e: e.copy(va3[:, :, 0:128], v3(VTOK[b], 8)))(),
                          reads=[rVTOK[b]], writes=[rVA[b]])
                    if emit:
                        kb.dma("pool", v3(QTb, 8), Pch.ap()[CR["Dq"]:CR["Dq"] + 1024, t * 128:(t + 1) * 128].rearrange(
                            "(h d) t -> d h t", d=128), writes=[rQKb])
                        kb.dma("pool", v3(KTb, 8), Pch.ap()[CR["DkT"]:CR["DkT"] + 1024, t * 128:(t + 1) * 128].rearrange(
                            "(h d) t -> d h t", d=128), writes=[rQKb])
                        for h in range(8):
                            kb.op("pe", (lambda h=h: lambda e: e.matmul(
                                PS[:, h * 128:(h + 1) * 128], lhsT=v3(KTb, 8)[:, h, :], rhs=v3(QTb, 8)[:, h, :],
                                start=True, stop=True))(), reads=[rQKb], writes=[BK[h // 4]])
                        kb.op("dve", (lambda t=t: lambda e: e.tensor_tensor(
                            v3(DG, 8), bc(IDF.unsqueeze(1), [128, 8, 128]),
                            bc(GP4[:, t, 0, g0:g0 + 8].unsqueeze(2), [128, 8, 128]), ALU.mult))(),
                            reads=[rGP, rCONST], writes=[rDG])
                        for j in range(2):
                            kb.op("pe", (lambda j=j: lambda e: e.matmul(
                                PS[:, 1024 + j * 512:1536 + j * 512], lhsT=ONF, rhs=DG[:, j * 512:(j + 1) * 512],
                                start=True, stop=True))(), reads=[rDG, rCONST], writes=[BK[2 + j]])
                            kb.op("act", (lambda j=j: lambda e: e.activation(
                                out=EB[:, j * 512:(j + 1) * 512], in_=PS[:, 1024 + j * 512:1536 + j * 512], func=AF.Exp))(),
                                reads=[BK[2 + j]], writes=[rEB])
                        for h in range(8):
                            kb.op("act", (lambda h=h, t=t: lambda e: e.activation(
                                out=WTm[:, h * 128:(h + 1) * 128], in_=PS[:, 1024 + h * 128:1152 + h * 128], func=AF.Exp,
                                bias=GP4[:, t, 1, g0 + h:g0 + h + 1]))(), reads=[BK[2 + h // 4], rGP], writes=[rWTm])
                        kb.op("pool", lambda e: e.tensor_tensor(v3(WTm, 8), v3(WTm, 8), bc(MASK.unsqueeze(1), [128, 8, 128]),
                                                                ALU.mult), reads=[rCONST], writes=[rWTm])
                        kb.op("dve", lambda e: e.tensor_tensor(PTb, PS[:, 0:1024], WTm, ALU.mult),
                              reads=[BK[0], BK[1], rWTm], writes=[rPTb])
                        kb.op("pool", lambda e: e.tensor_tensor(QTL, QTb, EB, ALU.mult), reads=[rQKb, rEB], writes=[rQTL])
                        for h in range(8):
                            kb.op("pe", (lambda h=h, va3=va3: lambda e: e.matmul(
                                PS[:, 2048 + h * 128:2176 + h * 128], lhsT=va3[:, h, 0:128], rhs=v3(PTb, 8)[:, h, :],
                                start=True, stop=False))(), reads=[rVA[b], rPTb], writes=[BK[4 + h // 4]])
                            kb.op("pe", (lambda h=h: lambda e: e.matmul(
                                PS[:, 2048 + h * 128:2176 + h * 128], lhsT=CB3[:, g0 + h, 0:128], rhs=v3(QTL, 8)[:, h, :],
                                start=False, stop=True))(), reads=[rCB, rQTL], writes=[BK[4 + h // 4]])
                        for h in range(8):
                            kb.op("pe", (lambda h=h: lambda e: e.matmul(
                                PS[:, 3072 + h * 128:3200 + h * 128], lhsT=ONB, rhs=v3(PTb, 8)[:, h, :],
                                start=True, stop=False))(), reads=[rCONST, rPTb], writes=[BK[6 + h // 4]])
                            kb.op("pe", (lambda h=h: lambda e: e.matmul(
                                PS[:, 3072 + h * 128:3200 + h * 128], lhsT=CB3[:, g0 + h, 128:256], rhs=v3(QTL, 8)[:, h, :],
                                start=False, stop=True))(), reads=[rCB, rQTL], writes=[BK[6 + h // 4]])
                        kb.op("act", lambda e: e.activation(out=DN, in_=PS[:, 3072:4096], func=AF.Abs),
                              reads=[BK[6], BK[7]], writes=[rDN])
                        kb.op("dve", lambda e: e.tensor_scalar_max(DN, DN, 1.0), reads=[rDN], writes=[rDN])
                        kb.op("dve", lambda e: e.reciprocal(DN, DN), reads=[rDN], writes=[rDN])
                        hs_sl = HS3[:, :, t * 128:(t + 1) * 128]
                        if dr == 0:
                            kb.op("dve", (lambda hs_sl=hs_sl: lambda e: e.tensor_tensor(hs_sl, v3(PS[:, 2048:3072], 8), v3(DN, 8),
                                                                                      ALU.mult))(),
                                  reads=[BK[4], BK[5], rDN], writes=[rHS])
                        else:
                            kb.op("dve", lambda e: e.tensor_tensor(TMPH, PS[:, 2048:3072], DN, ALU.mult),
                                  reads=[BK[4], BK[5], rDN], writes=[rTMPH])
                            kb.op("pool", (lambda hs_sl=hs_sl: lambda e: e.tensor_tensor(hs_sl, hs_sl, v3(TMPH, 8), ALU.add))(),
                                  reads=[rTMPH, rHS], writes=[rHS])
                    kb.op("dve", (lambda t=t: lambda e: e.tensor_tensor(
                        CS3[:, g0:g0 + 8, :], CS3[:, g0:g0 + 8, :], bc(GP4[:, t, 4, g0:g0 + 8].unsqueeze(2), [128, 8, 256]),
                        ALU.mult))(), reads=[rGP, rCS], writes=[rCS])
                    for half in range(2):
                        for hh in range(4):
                            h = half * 4 + hh
                            kb.op("pe", (lambda h=h, hh=hh, va3=va3, KTL=KTL, stb=stb: lambda e: e.matmul(
                                PS[:, stb * 512 + hh * 256:stb * 512 + 256 + hh * 256], lhsT=v3(KTL, 8)[:, h, :], rhs=va3[:, h, :],
                                start=True, stop=True))(), reads=[rKTL, rVA[b]], writes=[BK[stb + hh // 2]])
                        kb.op("dve", (lambda half=half, stb=stb: lambda e: e.tensor_tensor(
                            CS3[:, g0 + half * 4:g0 + half * 4 + 4, :], CS3[:, g0 + half * 4:g0 + half * 4 + 4, :],
                            v3(PS[:, stb * 512:stb * 512 + 1024], 4), ALU.add))(), reads=[BK[stb], BK[stb + 1], rCS], writes=[rCS])
                    if emit:
                        kb.op("act", lambda e: e.copy(CB3[:, g0:g0 + 8, :], CS3[:, g0:g0 + 8, :]), reads=[rCS], writes=[rCB])

            def zero_state():
                kb.op("dve", lambda e: e.memset(CS, 0.0), writes=[rCS])
                kb.op("pool", lambda e: e.memset(CB, 0.0), writes=[rCB])

            zero_state()
            scan([8, 9], 0, not last)
            scan([9, 8], 1, not last)
            kb.op("dve", lambda e: e.tensor_copy(STFB, CS), reads=[rCS], writes=[rSTFB])
            dump("d_STFB", STFB, [128, 4096], reads=[rSTFB])
            dump("d_HSctx", HS, [128, 8 * NT], reads=[rHS])
            zero_state()
            scan(list(range(8)), 0, False)
            scan(list(range(7, -1, -1)), 1, False)
            for dr_ in range(2):
                kb.dma("sp", exr(2 + dr_, 0, 128 * 2048).rearrange("(p x) -> p x", p=128), CS[:, dr_ * 2048:(dr_ + 1) * 2048],
                       reads=[rCS])
            DTT = A.f32(16)
            rDTT = Res("dtt")
            kb.op("dve", lambda e: e.tensor_reduce(out=DTT, in_=GP4[:, 0:8, 3, :].rearrange("p t g -> p g t"), axis=AX.X,
                                                   op=ALU.add), reads=[rGP], writes=[rDTT])
            kb.dma("sp", exr(4, 0, 128 * 16).rearrange("(p x) -> p x", p=128), DTT, reads=[rDTT])
            scan_mark = A.mark()

            if stop_after == "pre":
                return
            kb.issue_collectives("AllGather", ALU.bypass, GROUPS,
                                 [(exp_bufs[c_].ap().opt(), gat_bufs[c_].ap().opt()) for c_ in (2, 3, 4)])
            m2_mark = A.mark()
            KV = [A.f32(512) for _ in range(2)]
            rKV = [Res("kv0"), Res("kv1")]
            KTE = A.f32(2 * 1024)
            KTE3 = v3(KTE, 2)
            rKTE = Res("kte")
            KWS = []
            for i_ in range(2):
                KWS.append(dict(KN=A.f32(256), KR=A.f32(256), KTS=A.f32(256), KT1=A.f32(128), KT2=A.f32(128),
                                KRB=A.bf16(256), SSK=A.f32(24), r=Res("kw%d" % i_)))
            for t in range(10):
                b = t % 2
                W_ = KWS[b]
                bk_ = 4 * b
                kb.dma("sp", KV[b], Ptok.ap()[t * 128:(t + 1) * 128, 2048:2560], writes=[rKV[b]])
                qk_norm_rope(KV[b][:, 0:256], 2, GQK[:, 128:256], t, W_["KN"], W_["KR"], W_["KTS"], W_["SSK"], W_["KT1"], W_["KT2"],
                             rKV[b], W_["r"])
                kb.op("act", (lambda W_=W_: lambda e: e.copy(W_["KRB"], W_["KR"]))(), reads=[W_["r"]], writes=[W_["r"]])
                for g in range(2):
                    kb.op("pe", (lambda g=g, W_=W_, bk_=bk_: lambda e: e.matmul(
                        bank(bk_)[:, g * 128:(g + 1) * 128], lhsT=W_["KRB"][:, g * 128:(g + 1) * 128], rhs=IDB,
                        start=True, stop=True))(), reads=[W_["r"], rCONST], writes=[BK[bk_]])
                if t < 8:
                    kb.op("act", (lambda t=t, bk_=bk_: lambda e: e.copy(KTE3[:, :, t * 128:(t + 1) * 128], v3(bank(bk_, 256), 2)))(),
                          reads=[BK[bk_]], writes=[rKTE])
                else:
                    kb.op("act", (lambda t=t, bk_=bk_: lambda e: e.copy(KTC3[:, :, (t - 8) * 128:(t - 7) * 128],
                                                                      v3(bank(bk_, 256), 2)))(), reads=[BK[bk_]], writes=[rKTC])
            kb.dma("sp", exr(0, 0, 128 * 2048).rearrange("(d x) -> d x", d=128), KTE, reads=[rKTE])
            kb.dma("sp", exr(1, 0, 1024 * 256).rearrange("(t c) -> t c", c=256), Ptok.ap()[0:1024, 2304:2560])
            A.release(m2_mark)

            ATb = [A.f32(NT) for _ in range(2)]
            GTb = [A.f32(NT) for _ in range(2)]
            rATb = [Res("at0"), Res("at1")]
            rGTb = [Res("gt0"), Res("gt1")]
            halo_ex = exr(5, 0, 128 * 240).rearrange("(c g j) -> c g j", c=128, g=8)
            for cg in range(8):
                b = cg % 2
                kb.dma("sp", ATb[b][:, 0:NTL], Pch.ap()[CR["Ba"] + cg * 128:CR["Ba"] + (cg + 1) * 128, 0:NTL], writes=[rATb[b]])
                kb.dma("sp", GTb[b][:, 0:NTL], Pch.ap()[CR["Bg"] + cg * 128:CR["Bg"] + (cg + 1) * 128, 0:NTL], writes=[rGTb[b]])
                kb.op("act", (lambda b=b: lambda e: e.activation(out=GTb[b][:, 0:NTL], in_=GTb[b][:, 0:NTL], func=AF.Sigmoid))(),
                      reads=[rGTb[b]], writes=[rGTb[b]])
                kb.op("dve", (lambda b=b: lambda e: e.tensor_tensor(ATb[b][:, 0:NTL], ATb[b][:, 0:NTL], GTb[b][:, 0:NTL], ALU.mult))(),
                      reads=[rGTb[b], rATb[b]], writes=[rATb[b]])
                kb.dma("sp", Ych.ap()[cg * 128:(cg + 1) * 128, 0:NTL], ATb[b][:, 0:NTL], reads=[rATb[b]])
                kb.dma("sp", halo_ex[:, cg, 0:15], ATb[b][:, 0:15], reads=[rATb[b]])
                kb.dma("sp", halo_ex[:, cg, 15:30], ATb[b][:, 1009:1024], reads=[rATb[b]])
            A.release(m2_mark)

            kb.issue_collectives("AllGather", ALU.bypass, GROUPS,
                                 [(exp_bufs[c_].ap().opt(), gat_bufs[c_].ap().opt()) for c_ in (0, 1, 5)])
            if stop_after == "gather":
                return
            SG = [A.f32(8 * 256) for _ in range(4)]
            rSG = Res("sg")
            DJ = A.f32(64)
            FF = A.f32(8 * 256)
            rFF = Res("ff")
            for r in range(4):
                kb.dma("sp", DJ[:, r * 16:(r + 1) * 16], gar(4, r, 0, 128 * 16).rearrange("(p x) -> p x", p=128), writes=[rSG])
            kb.op("act", lambda e: e.activation(out=DJ, in_=DJ, func=AF.Exp), reads=[rSG], writes=[rSG])
            for dr in range(2):
                for r in range(4):
                    kb.dma("sp", SG[r], gar(2 + dr, r, 0, 128 * 2048).rearrange("(p x) -> p x", p=128), writes=[rSG])
                CSd = CS[:, dr * 2048:(dr + 1) * 2048]
                kb.op("dve", (lambda dr=dr: lambda e: e.tensor_copy(FF, STFB[:, dr * 2048:(dr + 1) * 2048]))(),
                      reads=[rSTFB], writes=[rFF])
                order = [0, 1, 2] if dr == 0 else [3, 2, 1]
                fs = 0 if dr == 0 else 3
                kb.op("dve", (lambda CSd=CSd, fs=fs: lambda e: e.tensor_scalar_mul(CSd, FF, SEL[:, fs:fs + 1]))(),
                      reads=[rFF, rCONST], writes=[rCS])
                for j in order:
                    for h in range(8):
                        kb.op("dve", (lambda j=j, h=h, dr=dr: lambda e: e.scalar_tensor_tensor(
                            out=v3(FF, 8)[:, h, :], in0=v3(FF, 8)[:, h, :],
                            scalar=DJ[:, j * 16 + dr * 8 + h:j * 16 + dr * 8 + h + 1], in1=v3(SG[j], 8)[:, h, :],
                            op0=ALU.mult, op1=ALU.add))(), reads=[rSG, rFF], writes=[rFF])
                    nx = j + 1 if dr == 0 else j - 1
                    kb.op("dve", (lambda CSd=CSd, nx=nx: lambda e: e.scalar_tensor_tensor(
                        out=CSd, in0=FF, scalar=SEL[:, nx:nx + 1], in1=CSd, op0=ALU.mult, op1=ALU.add))(),
                        reads=[rFF, rCONST, rCS], writes=[rCS])
            kb.op("act", lambda e: e.copy(CB, CS), reads=[rCS], writes=[rCB])
            dump("d_INIT", CS, [128, 4096], reads=[rCS])
            scan(list(range(8)), 0, True)
            scan(list(range(7, -1, -1)), 1, True)
            dump("d_HS", HS, [128, 8 * NT], reads=[rHS])
            A.release(pers_mark)
            OTb = [A.f32(NT) for _ in range(2)]
            ZTb = [A.f32(NT) for _ in range(2)]
            rOTb = [Res("ot0"), Res("ot1")]
            rZTb = [Res("zt0"), Res("zt1")]
            SQ = A.f32(NT)
            rSQ = Res("sq")
            RS = A.f32(NT)
            rRS = Res("rs")
            for h in range(8):
                b = h % 2
                kb.dma("sp", OTb[b][:, 0:NTL], Pch.ap()[CR["Do"] + h * 128:CR["Do"] + (h + 1) * 128, 0:NTL], writes=[rOTb[b]])
                kb.dma("sp", ZTb[b][:, 0:NTL], Pch.ap()[CR["Dz"] + h * 128:CR["Dz"] + (h + 1) * 128, 0:NTL], writes=[rZTb[b]])
                kb.op("act", (lambda h=h: lambda e: e.activation(out=SQ[:, 0:NTL], in_=HS3[:, h, 0:NTL], func=AF.Square))(),
                      reads=[rHS], writes=[rSQ])
                for i, (t0, tn) in enumerate(tts):
                    kb.op("pe", (lambda i=i, t0=t0, tn=tn: lambda e: e.matmul(bank(i, tn), lhsT=ONF, rhs=SQ[:, t0:t0 + tn],
                                                                              start=True, stop=True))(),
                          reads=[rSQ, rCONST], writes=[BK[i]])
                    kb.op("dve", (lambda i=i, t0=t0, tn=tn: lambda e: e.tensor_scalar(RS[:, t0:t0 + tn], bank(i, tn), 1.0 / 128, EPS,
                                                                                    ALU.mult, ALU.add))(),
                          reads=[BK[i]], writes=[rRS])
                kb.op("act", lambda e: e.sqrt(RS[:, 0:NTL], RS[:, 0:NTL]), reads=[rRS], writes=[rRS])
                kb.op("dve", lambda e: e.reciprocal(RS[:, 0:NTL], RS[:, 0:NTL]), reads=[rRS], writes=[rRS])
                kb.op("dve", (lambda h=h: lambda e: e.tensor_tensor(SQ[:, 0:NTL], HS3[:, h, 0:NTL], RS[:, 0:NTL], ALU.mult))(),
                      reads=[rHS, rRS, rSQ], writes=[rSQ])
                kb.op("act", (lambda b=b: lambda e: e.activation(out=OTb[b][:, 0:NTL], in_=OTb[b][:, 0:NTL], func=AF.Sigmoid))(),
                      reads=[rOTb[b]], writes=[rOTb[b]])
                kb.op("act", (lambda b=b: lambda e: e.activation(out=ZTb[b][:, 0:NTL], in_=ZTb[b][:, 0:NTL], func=AF.Silu))(),
                      reads=[rZTb[b]], writes=[rZTb[b]])
                kb.op("pool", (lambda b=b: lambda e: e.tensor_tensor(SQ[:, 0:NTL], SQ[:, 0:NTL], OTb[b][:, 0:NTL], ALU.mult))(),
                      reads=[rOTb[b], rSQ], writes=[rSQ])
                stg, rstg = mix_stage()
                kb.op("dve", (lambda b=b, h=h, stg=stg: lambda e: e.scalar_tensor_tensor(
                    out=stg[:, 0:NTL], in0=SQ[:, 0:NTL], scalar=PT3[:, 24 + h:25 + h], in1=ZTb[b][:, 0:NTL],
                    op0=ALU.mult, op1=ALU.mult))(), reads=[rSQ, rZTb[b], rPT], writes=[rstg])
                mix_store(24 + h, stg, rstg)
            kb.barrier()
            A.release(pers2_mark)
            LNG = A.f32(1024)
            LNB = A.f32(1024)
            SBI = A.f32(1024)
            rAP = Res("ap")
            kb.dma("sp", LNG, sglg_in[l:l + 1, :].partition_broadcast(128), writes=[rAP])
            kb.dma("sp", LNB, sglb_in[l:l + 1, :].partition_broadcast(128), writes=[rAP])
            kb.dma("sp", SBI, sgub_in[l:l + 1, :].partition_broadcast(128), writes=[rAP])
            WSF = [A.f32(128) for _ in range(2)]
            rWSF = [Res("wsf0"), Res("wsf1")]
            WST = A.bf16(1024)
            rWST = Res("wst")
            for h in range(8):
                b = h % 2
                kb.dma("sp", WSF[b], sguw_in[l, h], writes=[rWSF[b]])
                kb.op("pe", (lambda b=b: lambda e: e.matmul(bank(b, 128), lhsT=WSF[b], rhs=IDF, start=True, stop=True))(),
                      reads=[rWSF[b], rCONST], writes=[BK[b]])
                kb.op("dve", (lambda b=b, h=h: lambda e: e.tensor_copy(WST[:, h * 128:(h + 1) * 128], bank(b, 128)))(),
                      reads=[BK[b]], writes=[rWST])
            VN = A.bf16(TL * 1024)
            VN3 = v3(VN, TL)
            rVN = Res("vn")
            VT = [A.f32(1024) for _ in range(2)]
            rVT = [Res("vt0"), Res("vt1")]
            AJ = A.f32(1024)
            rAJ = Res("aj")
            ST = A.f32(16)
            rST = Res("st")
            for t in range(TL):
                b = t % 2
                kb.dma("sp", VT[b], Ptok.ap()[t * 128:(t + 1) * 128, 0:1024], writes=[rVT[b]])
                kb.op("act", (lambda b=b: lambda e: e.activation(out=AJ, in_=VT[b], func=AF.Copy, accum_out=ST[:, 0:1]))(),
                      reads=[rVT[b]], writes=[rAJ, rST])
                kb.op("act", (lambda b=b: lambda e: e.activation(out=AJ, in_=VT[b], func=AF.Square, accum_out=ST[:, 1:2]))(),
                      reads=[rVT[b]], writes=[rAJ, rST])
                kb.op("dve", lambda e: e.tensor_scalar_mul(ST[:, 2:4], ST[:, 0:2], 1.0 / 1024), reads=[rST], writes=[rST])
                kb.op("dve", lambda e: e.tensor_tensor(ST[:, 4:5], ST[:, 2:3], ST[:, 2:3], ALU.mult), reads=[rST], writes=[rST])
                kb.op("dve", lambda e: e.tensor_tensor(ST[:, 5:6], ST[:, 3:4], ST[:, 4:5], ALU.subtract), reads=[rST], writes=[rST])
                rsqrt_ops(ST[:, 7:8], ST[:, 5:6], 1.0, ST[:, 6:7], [rST], [rST])
                kb.op("dve", (lambda b=b: lambda e: e.tensor_scalar(VT[b], VT[b], ST[:, 2:3], ST[:, 7:8], ALU.subtract, ALU.mult))(),
                      reads=[rST, rVT[b]], writes=[rVT[b]])
                kb.op("pool", (lambda b=b: lambda e: e.tensor_tensor(VT[b], VT[b], LNG, ALU.mult))(), reads=[rAP, rVT[b]], writes=[rVT[b]])
                kb.op("dve", (lambda b=b, t=t: lambda e: e.tensor_tensor(VN3[:, t, :], VT[b], LNB, ALU.add))(),
                      reads=[rAP, rVT[b]], writes=[rVN])
            UT = [A.f32(NT) for _ in range(2)]
            ZT2 = [A.f32(NT) for _ in range(2)]
            rUT = [Res("ut0"), Res("ut1")]
            rZT2 = [Res("zt20"), Res("zt21")]
            TMA = A.f32(NT)
            rTMA = Res("tma")
            for h in range(8):
                b = h % 2
                kb.dma("sp", UT[b][:, 0:NTL], Pch.ap()[CR["Au"] + h * 128:CR["Au"] + (h + 1) * 128, 0:NTL], writes=[rUT[b]])
                kb.dma("sp", ZT2[b][:, 0:NTL], Pch.ap()[CR["Az"] + h * 128:CR["Az"] + (h + 1) * 128, 0:NTL], writes=[rZT2[b]])
                kb.op("act", (lambda b=b: lambda e: e.activation(out=ZT2[b][:, 0:NTL], in_=ZT2[b][:, 0:NTL], func=AF.Silu))(),
                      reads=[rZT2[b]], writes=[rZT2[b]])
                kb.op("pool", (lambda b=b: lambda e: e.tensor_tensor(UT[b][:, 0:NTL], UT[b][:, 0:NTL], ZT2[b][:, 0:NTL], ALU.mult))(),
                      reads=[rZT2[b], rUT[b]], writes=[rUT[b]])
                for t in range(TL):
                    kb.op("pe", (lambda h=h, t=t: lambda e: e.matmul(
                        PS[:, t * 128:(t + 1) * 128], lhsT=VN3[:, t, h * 128:(h + 1) * 128], rhs=WST[:, h * 128:(h + 1) * 128],
                        start=True, stop=True))(), reads=[rVN, rWST], writes=[BK[t // 4]])
                kb.op("dve", (lambda h=h: lambda e: e.tensor_tensor(
                    v3(TMA[:, 0:NTL], TL), v3(PS[:, 0:NTL], TL), bc(SBI[:, h * 128:(h + 1) * 128].unsqueeze(1), [128, TL, 128]),
                    ALU.add))(), reads=[BK[0], BK[1], BK[2], rAP], writes=[rTMA])
                stg, rstg = mix_stage()
                kb.op("dve", (lambda b=b, stg=stg: lambda e: e.tensor_tensor(stg[:, 0:NTL], TMA[:, 0:NTL], UT[b][:, 0:NTL], ALU.mult))(),
                      reads=[rTMA, rUT[b]], writes=[rstg])
                mix_store(h, stg, rstg)
            kb.barrier()
            A.release(pers2_mark)

            HG = A.f32(4 * 240)
            rHG = Res("hg")
            for r in range(4):
                kb.dma("sp", HG[:, r * 240:(r + 1) * 240], gar(5, r, 0, 128 * 240).rearrange("(c x) -> c x", c=128), writes=[rHG])
            HG4 = HG.rearrange("p (r g j) -> p r g j", r=4, g=8)
            LH = A.f32(8 * 15)
            RH = A.f32(8 * 15)
            rLR = Res("lr")
            for r in range(4):
                for (dst, so, j0) in ((LH, 4, 15), (RH, 8, 0)):
                    src = HG4[:, r, :, j0:j0 + 15]
                    if r == 0:
                        kb.op("dve", (lambda dst=dst, src=src, so=so, r=r: lambda e: e.tensor_scalar_mul(
                            v3(dst, 8), src, SEL[:, so + r:so + r + 1]))(), reads=[rHG, rCONST], writes=[rLR])
                    else:
                        kb.op("dve", (lambda dst=dst, src=src, so=so, r=r: lambda e: e.scalar_tensor_tensor(
                            out=v3(dst, 8), in0=src, scalar=SEL[:, so + r:so + r + 1], in1=v3(dst, 8), op0=ALU.mult, op1=ALU.add))(),
                            reads=[rHG, rCONST, rLR], writes=[rLR])
            CONV = A.f32(8 * NT)
            CONV3 = v3(CONV, 8)
            rCONV = [Res("conv%d" % i) for i in range(8)]
            YP = [A.bf16(1054 + 286) for _ in range(2)]
            rYP = [Res("yp0"), Res("yp1")]
            DIAG = [A.bf16(31 * 128) for _ in range(2)]
            rDIAG = [Res("diag0"), Res("diag1")]
            CWr = A.f32(248)
            rCWr = Res("cwr")
            kb.op("dve", lambda e: e.tensor_copy(v3(CWr, 8), v3(CW, 31).rearrange("p k g -> p g k")), reads=[rPT], writes=[rCWr])
            SQB = A.f32(NT)
            rSQB = Res("sqb")
            for b in range(2):
                kb.op("pool", (lambda b=b: lambda e: e.memset(YP[b], 0.0))(), writes=[rYP[b]])
            cvb = 0
            for cg in range(8):
                b = cg % 2
                kb.dma("pool", YP[b][:, 15:1039], Ych.ap()[cg * 128:(cg + 1) * 128, 0:1024], writes=[rYP[b]])
                if not last:
                    kb.dma("pool", YP[b][:, 1054 + 15:1054 + 271], Ych.ap()[cg * 128:(cg + 1) * 128, 1024:1280], writes=[rYP[b]])
                kb.op("dve", (lambda b=b, cg=cg: lambda e: e.tensor_copy(YP[b][:, 0:15], v3(LH, 8)[:, cg, :]))(), reads=[rLR], writes=[rYP[b]])
                kb.op("dve", (lambda b=b, cg=cg: lambda e: e.tensor_copy(YP[b][:, 1039:1054], v3(RH, 8)[:, cg, :]))(), reads=[rLR], writes=[rYP[b]])
                kb.op("dve", (lambda b=b, cg=cg: lambda e: e.tensor_tensor(
                    v3(DIAG[b], 31), bc(IDF.unsqueeze(1), [128, 31, 128]), bc(v3(CWr, 8)[:, cg, :].unsqueeze(2), [128, 31, 128]),
                    ALU.mult))(), reads=[rCWr, rCONST], writes=[rDIAG[b]])
                for (ys, co, n) in [(0, 0, 512), (512, 512, 512)] + ([] if last else [(1054, 1024, 256)]):
                    bk = 6 + (cvb % 2)
                    cvb += 1
                    for k in range(31):
                        kb.op("pe", (lambda b=b, bk=bk, k=k, ys=ys, n=n: lambda e: e.matmul(
                            bank(bk, n), lhsT=v3(DIAG[b], 31)[:, k, :], rhs=YP[b][:, ys + k:ys + k + n],
                            start=(k == 0), stop=(k == 30)))(), reads=[rDIAG[b], rYP[b]], writes=[BK[bk]])
                    kb.op("act", (lambda bk=bk, cg=cg, co=co, n=n: lambda e: e.activation(
                        out=CONV3[:, cg, co:co + n], in_=bank(bk, n), func=AF.Identity, bias=PT3[:, cg:cg + 1]))(),
                        reads=[BK[bk], rPT], writes=[rCONV[cg]])
                kb.op("act", (lambda cg=cg: lambda e: e.activation(out=SQB[:, 0:NTL], in_=CONV3[:, cg, 0:NTL], func=AF.Square))(),
                      reads=[rCONV[cg]], writes=[rSQB])
                for i, (t0, tn) in enumerate(tts):
                    kb.op("pe", (lambda cg=cg, i=i, t0=t0, tn=tn: lambda e: e.matmul(
                        bank(i, tn), lhsT=ONF, rhs=CONV3[:, cg, t0:t0 + tn], start=(cg == 0), stop=(cg == 7)))(),
                        reads=[rCONV[cg], rCONST], writes=[BK[i]])
                    kb.op("pe", (lambda cg=cg, i=i, t0=t0, tn=tn: lambda e: e.matmul(
                        bank(3 + i, tn), lhsT=ONF, rhs=SQB[:, t0:t0 + tn], start=(cg == 0), stop=(cg == 7)))(),
                        reads=[rSQB, rCONST], writes=[BK[3 + i]])
            dump("d_CONV", CONV, [128, 8 * NT], reads=rCONV)
            MEAN = A.f32(NT)
            RSTD = A.f32(NT)
            MSQ = A.f32(NT)
            rMS = Res("ms")
            for i, (t0, tn) in enumerate(tts):
                kb.op("dve", (lambda i=i, t0=t0, tn=tn: lambda e: e.tensor_scalar_mul(MEAN[:, t0:t0 + tn], bank(i, tn), 1.0 / 1024))(),
                      reads=[BK[i]], writes=[rMS])
                kb.op("dve", (lambda i=i, t0=t0, tn=tn: lambda e: e.tensor_scalar_mul(RSTD[:, t0:t0 + tn], bank(3 + i, tn), 1.0 / 1024))(),
                      reads=[BK[3 + i]], writes=[rMS])
            kb.op("dve", lambda e: e.tensor_tensor(MSQ[:, 0:NTL], MEAN[:, 0:NTL], MEAN[:, 0:NTL], ALU.mult), reads=[rMS], writes=[rMS])
            kb.op("dve", lambda e: e.tensor_tensor(RSTD[:, 0:NTL], RSTD[:, 0:NTL], MSQ[:, 0:NTL], ALU.subtract), reads=[rMS], writes=[rMS])
            rsqrt_ops(RSTD[:, 0:NTL], RSTD[:, 0:NTL], 1.0, MSQ[:, 0:NTL], [rMS], [rMS])
            dump("d_MEAN", MEAN, [128, NT], reads=[rMS])
            dump("d_RSTD", RSTD, [128, NT], reads=[rMS])
            ZB = [A.f32(NT) for _ in range(2)]
            rZB = [Res("zb0"), Res("zb1")]
            for cg in range(8):
                b = cg % 2
                kb.dma("sp", ZB[b][:, 0:NTL], Pch.ap()[CR["Bz"] + cg * 128:CR["Bz"] + (cg + 1) * 128, 0:NTL], writes=[rZB[b]])
                cv = CONV3[:, cg, 0:NTL]
                kb.op("dve", (lambda cv=cv: lambda e: e.tensor_tensor(cv, cv, MEAN[:, 0:NTL], ALU.subtract))(), reads=[rMS, rCONV[cg]], writes=[rCONV[cg]])
                kb.op("pool", (lambda cv=cv: lambda e: e.tensor_tensor(cv, cv, RSTD[:, 0:NTL], ALU.mult))(), reads=[rMS, rCONV[cg]], writes=[rCONV[cg]])
                kb.op("act", (lambda cv=cv, cg=cg: lambda e: e.activation(out=cv, in_=cv, func=AF.Silu, scale=PT3[:, 8 + cg:9 + cg],
                                                                          bias=PT3[:, 16 + cg:17 + cg]))(), reads=[rPT, rCONV[cg]], writes=[rCONV[cg]])
                kb.op("act", (lambda b=b: lambda e: e.activation(out=ZB[b][:, 0:NTL], in_=ZB[b][:, 0:NTL], func=AF.Silu))(),
                      reads=[rZB[b]], writes=[rZB[b]])
                stg, rstg = mix_stage()
                kb.op("dve", (lambda cv=cv, b=b, stg=stg: lambda e: e.tensor_tensor(stg[:, 0:NTL], cv, ZB[b][:, 0:NTL], ALU.mult))(),
                      reads=[rCONV[cg], rZB[b]], writes=[rstg])
                mix_store(8 + cg, stg, rstg)
            kb.barrier()
            A.release(pers2_mark)

            QT = A.bf16(8 * NT)
            QT3 = v3(QT, 8)
            rQT = Res("qt")
            QF = [A.f32(1024) for _ in range(2)]
            rQF = [Res("qf0"), Res("qf1")]
            QWS = []
            for i_ in range(2):
                QWS.append(dict(QN=A.f32(1024), QR=A.f32(1024), QTS=A.f32(1024), QT1=A.f32(512), QT2=A.f32(512),
                                QRB=A.bf16(1024), SSQ=A.f32(24), r=Res("qw%d" % i_)))
            for t in range(TL):
                b = t % 2
                W_ = QWS[b]
                pb0 = 4 * b
                kb.dma("sp", QF[b], Ptok.ap()[t * 128:(t + 1) * 128, 1024:2048], writes=[rQF[b]])
                qk_norm_rope(QF[b], 8, GQK[:, 0:128], t, W_["QN"], W_["QR"], W_["QTS"], W_["SSQ"], W_["QT1"], W_["QT2"], rQF[b], W_["r"])
                kb.op("act", (lambda W_=W_: lambda e: e.copy(W_["QRB"], W_["QR"]))(), reads=[W_["r"]], writes=[W_["r"]])
                for h in range(8):
                    kb.op("pe", (lambda h=h, W_=W_, pb0=pb0: lambda e: e.matmul(
                        PS[:, pb0 * 512 + h * 128:pb0 * 512 + (h + 1) * 128], lhsT=W_["QRB"][:, h * 128:(h + 1) * 128],
                        rhs=IDB, start=True, stop=True))(), reads=[W_["r"], rCONST], writes=[BK[pb0 + h // 4]])
                kb.op("act", (lambda t=t, pb0=pb0: lambda e: e.copy(QT3[:, :, t * 128:(t + 1) * 128],
                                                                  v3(PS[:, pb0 * 512:pb0 * 512 + 1024], 8)))(),
                      reads=[BK[pb0], BK[pb0 + 1]], writes=[rQT])
            KTA = A.bf16(2 * 4352)
            KTA3 = v3(KTA, 2)
            VAL = A.bf16(34 * 256)
            VAL3 = v3(VAL, 34)
            rKVA = Res("kva")
            for r in range(4):
                kb.dma("pool", KTA3[:, :, r * 1024:(r + 1) * 1024],
                       gar(0, r, 0, 128 * 2048).rearrange("(d g t) -> d g t", d=128, g=2), writes=[rKVA])
                kb.dma("pool", VAL3[:, r * 8:(r + 1) * 8, :],
                       gar(1, r, 0, 1024 * 256).rearrange("(kt p c) -> p kt c", p=128, c=256), writes=[rKVA])
            kb.op("dve", lambda e: e.tensor_copy(KTA3[:, :, 4096:4352], KTC3), reads=[rKTC], writes=[rKVA])
            kb.dma("pool", VAL3[:, 32:34, :], Ptok.ap()[1024:1280, 2304:2560].rearrange("(kt p) c -> p kt c", p=128), writes=[rKVA])
            dump("d_KTA", KTA, [128, 2 * 4352], BF16, reads=[rKVA])
            dump("d_VAL", VAL, [128, 34 * 256], BF16, reads=[rKVA])
            dump("d_QT", QT, [128, 8 * NT], BF16, reads=[rQT])
            SZ = [A.f32(NT) for _ in range(2)]
            rSZ = [Res("sz0"), Res("sz1")]
            PTA = [A.bf16(512) for _ in range(4)]
            rPTA = [Res("pta%d" % i) for i in range(4)]
            RL = A.f32(512)
            rRL = Res("rl")
            OA = A.f32(512)
            rOA = Res("oa")
            SC = 128.0 ** -0.5
            pcount = 0
            for h in range(8):
                g = h // 4
                b = h % 2
                kb.dma("sp", SZ[b][:, 0:NTL], Pch.ap()[CR["Cz"] + h * 128:CR["Cz"] + (h + 1) * 128, 0:NTL], writes=[rSZ[b]])
                kb.op("act", (lambda b=b: lambda e: e.activation(out=SZ[b][:, 0:NTL], in_=SZ[b][:, 0:NTL], func=AF.Silu))(),
                      reads=[rSZ[b]], writes=[rSZ[b]])
                stg, rstg = mix_stage()
                qtiles = [(0, 512, list(range(34))), (512, 512, list(range(34)))] + ([] if last else [(1024, 256, [32, 33])])
                for (q0, qn, kts) in qtiles:
                    nk = len(kts)

                    def emit_s(ki, q0=q0, qn=qn, kts=kts, g=g, h=h):
                        sb = 2 + (ki % 3)
                        kt = kts[ki]
                        kb.op("pe", (lambda: lambda e: e.matmul(
                            bank(sb, qn), lhsT=KTA3[:, g, kt * 128:(kt + 1) * 128], rhs=QT3[:, h, q0:q0 + qn],
                            start=True, stop=True))(), reads=[rKVA, rQT], writes=[BK[sb]])
                    emit_s(0)
                    if nk > 1:
                        emit_s(1)
                    for ki, kt in enumerate(kts):
                        sb = 2 + (ki % 3)
                        pb = pcount % 4
                        pcount += 1
                        if ki + 2 < nk:
                            emit_s(ki + 2)
                        kb.op("act", (lambda sb=sb, pb=pb, qn=qn: lambda e: e.activation(
                            out=PTA[pb][:, 0:qn], in_=bank(sb, qn), func=AF.Exp, scale=SC))(), reads=[BK[sb]], writes=[rPTA[pb]])
                        kb.op("pe", (lambda pb=pb, kt=kt, g=g, qn=qn, ki=ki, nk=nk: lambda e: e.matmul(
                            bank(0, qn), lhsT=VAL3[:, kt, g * 128:(g + 1) * 128], rhs=PTA[pb][:, 0:qn],
                            start=(ki == 0), stop=(ki == nk - 1)))(), reads=[rKVA, rPTA[pb]], writes=[BK[0]])
                        kb.op("pe", (lambda pb=pb, qn=qn, ki=ki, nk=nk: lambda e: e.matmul(
                            bank(1, qn), lhsT=ONB, rhs=PTA[pb][:, 0:qn], start=(ki == 0), stop=(ki == nk - 1)))(),
                            reads=[rCONST, rPTA[pb]], writes=[BK[1]])
                    kb.op("dve", (lambda qn=qn: lambda e: e.reciprocal(RL[:, 0:qn], bank(1, qn)))(), reads=[BK[1]], writes=[rRL])
                    kb.op("dve", (lambda qn=qn: lambda e: e.tensor_tensor(OA[:, 0:qn], bank(0, qn), RL[:, 0:qn], ALU.mult))(),
                          reads=[BK[0], rRL], writes=[rOA])
                    kb.op("pool", (lambda qn=qn, q0=q0, b=b, stg=stg: lambda e: e.tensor_tensor(
                        stg[:, q0:q0 + qn], OA[:, 0:qn], SZ[b][:, q0:q0 + qn], ALU.mult))(), reads=[rOA, rSZ[b]], writes=[rstg])
                mix_store(16 + h, stg, rstg)
            kb.barrier()
            A.release(pers_mark)

            A.release(par_mark)
            BIG2 = A.f32(16 * NT)
            MX = v3(BIG2.bitcast(BF16), 32)
            rMX = Res("mx")
            for kc in range(32):
                kb.dma("sp", MX[:, kc, 0:NTL], MIXd.ap()[kc * 128:(kc + 1) * 128, 0:NTL], writes=[rMX])
            GLn = [A.f32(512) for _ in range(2)]
            GCn = [A.f32(512) for _ in range(2)]
            GBn = [A.f32(512) for _ in range(2)]
            rGn = [Res("gn0"), Res("gn1")]
            WT2 = [A.bf16(32 * 512) for _ in range(2)]
            rWT2 = [Res("w2t0"), Res("w2t1")]
            XO = [A.f32(512) for _ in range(3)]
            rXO = [Res("xo%d" % i) for i in range(3)]
            OO = [A.f32(512) for _ in range(3)]
            rOO = [Res("oo%d" % i) for i in range(3)]
            wsrc2 = wout_in[l].rearrange("(kc kp) n -> kp kc n", kp=128)

            def load_w2(n, b):
                w3 = v3(WT2[b], 32)
                for g in range(4):
                    kb.dma("pool", w3[:, g * 8:(g + 1) * 8, :], wsrc2[:, g * 8:(g + 1) * 8, n * 512:(n + 1) * 512], writes=[rWT2[b]])
            load_w2(0, 0)
            oc = 0
            for n in range(8):
                b = n % 2
                if n + 1 < 8:
                    load_w2(n + 1, 1 - b)
                w3 = v3(WT2[b], 32)
                cs_ = slice(2 * D + n * 512, 2 * D + (n + 1) * 512)
                kb.dma("sp", GLn[b], af[2 * l:2 * l + 1, cs_].partition_broadcast(128), writes=[rGn[b]])
                kb.dma("sp", GBn[b], bada_in[l:l + 1, cs_].partition_broadcast(128), writes=[rGn[b]])
                kb.op("dve", (lambda b=b: lambda e: e.tensor_tensor(GLn[b], GLn[b], GBn[b], ALU.add))(), reads=[rGn[b]], writes=[rGn[b]])
                if not last:
                    kb.dma("sp", GCn[b], af[2 * l + 1:2 * l + 2, cs_].partition_broadcast(128), writes=[rGn[b]])
                    kb.op("dve", (lambda b=b: lambda e: e.tensor_tensor(GCn[b], GCn[b], GBn[b], ALU.add))(), reads=[rGn[b]], writes=[rGn[b]])
                for t in range(TL):
                    bk = t % 2
                    ob = oc % 3
                    oc += 1
                    kb.dma("sp", XO[ob], tok_src(t, n * 512, (n + 1) * 512), writes=[rXO[ob]])
                    for kc in range(32):
                        kb.op("pe", (lambda bk=bk, kc=kc, t=t, w3=w3: lambda e: e.matmul(
                            bank(bk), lhsT=MX[:, kc, t * 128:(t + 1) * 128], rhs=w3[:, kc, :], start=(kc == 0), stop=(kc == 31)))(),
                            reads=[rMX, rWT2[b]], writes=[BK[bk]])
                    Gt = GLn[b] if t < 8 else GCn[b]
                    kb.op("dve", (lambda bk=bk, ob=ob, Gt=Gt: lambda e: e.tensor_tensor(
                        OO[ob], bank(bk), Gt, ALU.mult))(), reads=[BK[bk], rGn[b]], writes=[rOO[ob]])
                    kb.op("pool", (lambda ob=ob: lambda e: e.tensor_tensor(OO[ob], OO[ob], XO[ob], ALU.add))(),
                          reads=[rXO[ob], rOO[ob]], writes=[rOO[ob]])
                    if last:
                        dst = y_out[t * 128:(t + 1) * 128, n * 512:(n + 1) * 512]
                    else:
                        dst = xs.ap()[t * 128:(t + 1) * 128, n * 512:(n + 1) * 512]
                    kb.dma("sp", dst, OO[ob], reads=[rOO[ob]])
            kb.barrier()
            A.release(lay_mark)

        for l_ in range(nlayers):
            emit_layer(l_)

        kb.barrier()
        block = es.enter_context(nc.Block())
        kb.replay(block)
    return nc


def LAYER_BODY_2(env):
    pass


def _consts():
    c = np.zeros((128, 5 * 128), np.float32)
    c[:, 0:128] = np.eye(128, dtype=np.float32)
    c[:, 128:256] = 1.0
    s = np.arange(128)[:, None]
    l_ = np.arange(128)[None, :]
    c[:, 256:384] = (s <= l_).astype(np.float32)
    c[:, 384:512] = (s >= l_).astype(np.float32)
    return c


def _rope_table(seg):
    n = np.arange(seg * 1024, (seg + 1) * 1024)
    row = (n // 64).astype(np.float32)
    col = (n % 64).astype(np.float32)
    freq = (10000.0 ** (-np.arange(32, dtype=np.float32) / 32)).astype(np.float32)
    ang = np.stack([row, col], -1)[..., None] * freq
    cs = np.concatenate([np.cos(ang).reshape(1024, 64), np.sin(ang).reshape(1024, 64)], -1).astype(np.float32)
    return np.ascontiguousarray(cs.reshape(8, 128, 128).transpose(1, 0, 2))


def make_in_maps(inputs):
    f = lambda a: np.ascontiguousarray(np.asarray(a, dtype=np.float32))
    x = f(inputs["x"]); c = f(inputs["c"]); ctx = f(inputs["ctx"]); c_ctx = f(inputs["c_ctx"])
    w_ada = f(inputs["w_ada"])
    shared = {
        "b_ada": f(inputs["b_ada"]), "norm_g": f(inputs["norm_g"]), "w_in": f(inputs["w_in"]),
        "sgu_w": f(inputs["sgu_w"]), "sgu_b": f(inputs["sgu_b"]).reshape(DEPTH, 1024),
        "sgu_ln_g": f(inputs["sgu_ln_g"]), "sgu_ln_b": f(inputs["sgu_ln_b"]),
        "conv_w": f(inputs["conv_w"]).reshape(DEPTH, 248, 128),
        "conv_b": f(inputs["conv_b"]).reshape(DEPTH, 8, 128),
        "conv_ln_g": f(inputs["conv_ln_g"]).reshape(DEPTH, 8, 128),
        "conv_ln_b": f(inputs["conv_ln_b"]).reshape(DEPTH, 8, 128),
        "q_norm_g": f(inputs["q_norm_g"]), "k_norm_g": f(inputs["k_norm_g"]),
        "mlstm_i_bias": f(inputs["mlstm_i_bias"]).reshape(DEPTH, 16),
        "mlstm_f_bias": f(inputs["mlstm_f_bias"]).reshape(DEPTH, 16),
        "mh_norm_g": f(inputs["mh_norm_g"]).reshape(DEPTH, 8, 128),
        "w_out": f(inputs["w_out"]), "consts": _consts(),
    }
    maps = []
    for core in range(8):
        b, s = core // 4, core % 4
        m = dict(shared)
        m["x"] = np.ascontiguousarray(x[b, s * 1024:(s + 1) * 1024])
        m["ctx"] = np.ascontiguousarray(ctx[b])
        cc = np.stack([c[b], c_ctx], 0)
        m["ccT"] = np.ascontiguousarray(cc[:, s * 1024:(s + 1) * 1024].T)
        m["w_ada_s"] = np.ascontiguousarray(w_ada[:, s * 1024:(s + 1) * 1024, :])
        m["rope"] = _rope_table(s)
        sel = np.zeros((128, 12), np.float32)
        sel[:, s] = 1.0
        if s - 1 >= 0:
            sel[:, 4 + s - 1] = 1.0
        if s + 1 <= 3:
            sel[:, 8 + s + 1] = 1.0
        m["sel"] = sel
        maps.append(m)
    return maps


def kernel(**inputs):
    nc = build_program()
    maps = make_in_maps(inputs)
    res = run_bass_kernel_spmd(nc, maps, core_ids=list(range(8)))
    out = np.empty((2, 4096, D), np.float32)
    for core in range(8):
        b, s = core // 4, core % 4
        out[b, s * 1024:(s + 1) * 1024] = res.results[core]["y"]
    return out
```

```python
import math
import numpy as np
import ml_dtypes
from contextlib import ExitStack
import concourse.bass as bass
import concourse.mybir as mybir
from concourse.bass_utils import run_bass_kernel_spmd

F32 = mybir.dt.float32
BF16 = mybir.dt.bfloat16
AF = mybir.ActivationFunctionType
ALU = mybir.AluOpType
AX = mybir.AxisListType

D = 4096
PIN = 13856
DEPTH = 2
NT = 1280
NL = 1024
EPS = 1e-6
LNSC = math.log(128.0 ** -0.5)

TOKB = [("Av", 1024, 1024, 0), ("Cq", 6144, 1024, 1024), ("Ckv", 7168, 512, 2048),
        ("Dk", 9728, 1024, 2560), ("Dv", 10752, 1024, 3584), ("G", 13824, 32, 4608)]
NTOKC = 4640
TC = {n: d for n, _, _, d in TOKB}
CHB = [("Au", 0, 0), ("Az", 2048, 1024), ("Ba", 3072, 2048), ("Bg", 4096, 3072), ("Bz", 5120, 4096),
       ("Cz", 7680, 5120), ("Dq", 8704, 6144), ("DkT", 9728, 7168), ("Do", 11776, 8192), ("Dz", 12800, 9216)]
NCHR = 10240
CR = {n: r for n, _, r in CHB}

EXC_ROWS = [512, 512, 512, 512, 4, 60]
EX_HALO_OFF = 0
EX_DT_OFF = 128 * 240


class Res:
    __slots__ = ("name", "w", "r")

    def __init__(self, name):
        self.name = name
        self.w = None
        self.r = {}


class KB:
    CE = ["pe", "act", "dve", "pool"]

    def __init__(self, nc, es):
        self.nc = nc
        self.engs = ["pe", "act", "dve", "pool", "sp"]
        self.q = {e: [] for e in self.engs}
        self.sem = {e: es.enter_context(nc.semaphore("s_" + e)) for e in self.CE}
        self.cnt = {e: 0 for e in self.CE}
        self.known = {e: {} for e in self.engs}
        self.dsem = {"sp": [es.enter_context(nc.semaphore("d_sp%d" % i)) for i in range(16)],
                     "pool": [es.enter_context(nc.semaphore("d_pl%d" % i)) for i in range(8)]}
        self.dcnt = {"sp": 0, "pool": 0}
        self.duse = {k: [0] * len(v) for k, v in self.dsem.items()}
        self.dlast = {k: [None] * len(v) for k, v in self.dsem.items()}
        self.ccsem = es.enter_context(nc.semaphore("s_cc"))
        self.cccnt = 0
        self.cclast = None

    def _wait(self, eng, ev):
        if ev is None:
            return
        key, sem, val = ev
        if self.known[eng].get(key, 0) >= val:
            return
        self.known[eng][key] = val
        self.q[eng].append(("w", sem, val))

    def _deps(self, eng, reads, writes):
        deps = {}

        def add(ev):
            if ev is None:
                return
            k = ev[0]
            if k not in deps or deps[k][2] < ev[2]:
                deps[k] = ev
        for r in reads:
            add(r.w)
        for w in writes:
            add(w.w)
            for ev in w.r.values():
                add(ev)
        for k, ev in deps.items():
            if eng == "pe" and k == "pe":
                continue
            self._wait(eng, ev)

    def _mark(self, ev, reads, writes):
        for r in reads:
            r.r[ev[0]] = ev
        for w in writes:
            w.w = ev
            w.r = {}

    def op(self, eng, fn, reads=(), writes=()):
        self._deps(eng, reads, writes)
        self.cnt[eng] += 1
        ev = (eng, self.sem[eng], self.cnt[eng])
        self.q[eng].append(("o", fn, self.sem[eng]))
        self._mark(ev, reads, writes)
        return ev

    def dma(self, qn, out, in_, reads=(), writes=()):
        self._deps(qn, reads, writes)
        n = len(self.dsem[qn])
        slot = self.dcnt[qn] % n
        self.dcnt[qn] += 1
        self._wait(qn, self.dlast[qn][slot])
        self.duse[qn][slot] += 1
        sem = self.dsem[qn][slot]
        ev = ("%s%d" % (qn, slot), sem, 16 * self.duse[qn][slot])
        self.dlast[qn][slot] = ev
        self.q[qn].append(("d", out, in_, sem))
        self._mark(ev, reads, writes)
        return ev

    def all_events(self):
        evs = [(e, self.sem[e], self.cnt[e]) for e in self.CE if self.cnt[e] > 0]
        for qn in self.dlast:
            evs += [ev for ev in self.dlast[qn] if ev is not None]
        if self.cclast is not None:
            evs.append(self.cclast)
        return evs

    def barrier(self):
        evs = self.all_events()
        for eng in self.engs:
            for ev in evs:
                if eng == "pe" and ev[0] == "pe":
                    continue
                self._wait(eng, ev)

    def collective(self, kind, alu, groups, in_ap, out_ap):
        self.barrier()
        self.cccnt += 1
        self.q["pool"].append(("c", kind, alu, groups, in_ap, out_ap))
        self.cclast = ("cc", self.ccsem, self.cccnt)
        self.barrier()

    def issue_collectives(self, kind, alu, groups, pairs):
        self.barrier()
        for in_ap, out_ap in pairs:
            self.cccnt += 1
            self.q["pool"].append(("c", kind, alu, groups, in_ap, out_ap))
        self.cclast = ("cc", self.ccsem, self.cccnt)

    def replay(self, block):
        def mk(eng):
            def f(e):
                for it in self.q[eng]:
                    k = it[0]
                    if k == "w":
                        e.wait_ge(it[1], it[2])
                    elif k == "o":
                        it[1](e).then_inc(it[2], 1)
                    elif k == "d":
                        e.dma_start(out=it[1], in_=it[2]).then_inc(it[3], 16)
                    elif k == "c":
                        e.collective_compute(it[1], it[2], replica_groups=it[3], ins=[it[4]],
                                             outs=[it[5]]).then_inc(self.ccsem)
            return f
        block.sync(mk("sp"))
        block.gpsimd(mk("pool"))
        block.scalar(mk("act"))
        block.vector(mk("dve"))
        block.tensor(mk("pe"))


class Arena:
    def __init__(self, ap, nfloats):
        self.ap = ap
        self.n = nfloats
        self.top = 0
        self.kb = None

    def mark(self):
        return self.top

    def release(self, m):
        if self.kb is not None:
            self.kb.barrier()
        self.top = m

    def f32(self, n):
        n4 = (n + 7) // 8 * 8
        assert self.top + n4 <= self.n, ("SBUF arena overflow", self.top, n4, self.n)
        v = self.ap[:, self.top:self.top + n]
        self.top += n4
        return v

    def bf16(self, n):
        nf = (n + 1) // 2
        return self.f32(nf).bitcast(BF16)[:, 0:n]


def build_program(debug=None, stop_after=None, nlayers=DEPTH):
    debug = debug or set()
    nc = bass.Bass("TRN2", target_bir_lowering=False)

    def din(name, shape, dt=F32):
        return nc.dram_tensor(name, list(shape), dt, kind="ExternalInput").ap()

    def dscr(name, shape, dt=F32):
        kind = "ExternalOutput" if name in debug else "Internal"
        return nc.dram_tensor(name, list(shape), dt, kind=kind)

    x_in = din("x", [NL, D])
    ctx_in = din("ctx", [256, D])
    ccT_in = din("ccT", [1024, 2])
    wada_in = din("w_ada_s", [DEPTH, 1024, 3 * D])
    bada_in = din("b_ada", [DEPTH, 3 * D])
    normg_in = din("norm_g", [DEPTH, D])
    win_in = din("w_in", [DEPTH, D, PIN])
    sguw_in = din("sgu_w", [DEPTH, 8, 128, 128])
    sgub_in = din("sgu_b", [DEPTH, 1024])
    sglg_in = din("sgu_ln_g", [DEPTH, 1024])
    sglb_in = din("sgu_ln_b", [DEPTH, 1024])
    convw_in = din("conv_w", [DEPTH, 248, 128])
    convb_in = din("conv_b", [DEPTH, 8, 128])
    cvlg_in = din("conv_ln_g", [DEPTH, 8, 128])
    cvlb_in = din("conv_ln_b", [DEPTH, 8, 128])
    qng_in = din("q_norm_g", [DEPTH, 128])
    kng_in = din("k_norm_g", [DEPTH, 128])
    ib_in = din("mlstm_i_bias", [DEPTH, 16])
    fb_in = din("mlstm_f_bias", [DEPTH, 16])
    mhg_in = din("mh_norm_g", [DEPTH, 8, 128])
    wout_in = din("w_out", [DEPTH, D, D])
    consts_in = din("consts", [128, 5 * 128])
    rope_in = din("rope", [128, 8, 128])
    sel_in = din("sel", [128, 12])
    y_out = nc.dram_tensor("y", [NL, D], F32, kind="ExternalOutput").ap()

    ada_part = nc.dram_tensor("ada_part", [4, 3 * D], F32)
    ada_full = nc.dram_tensor("ada_full", [4, 3 * D], F32)
    dbg_ada = dscr("dbg_ada", [4, 3 * D]) if "dbg_ada" in debug else None
    xs = dscr("xs", [NT, D])
    Ptok = dscr("Ptok", [NT, NTOKC])
    Pch = dscr("Pch", [NCHR, NT])
    Ych = dscr("Ych", [1024, NT])
    exp_bufs = [nc.dram_tensor("exp_buf%d" % c, [EXC_ROWS[c], 512], F32) for c in range(6)]
    gat_bufs = [nc.dram_tensor("gat_buf%d" % c, [4 * EXC_ROWS[c], 512], F32) for c in range(6)]
    dbg_gat = None
    dbg_mixT = dscr("dbg_mixT", [128, 32 * NT], BF16) if "dbg_mixT" in debug else None
    dbg_HT = dscr("dbg_HT", [128, 32 * NT], BF16) if "dbg_HT" in debug else None

    exp_flat = [b_.ap().rearrange("r c -> (r c)") for b_ in exp_bufs]
    gat_flat = [b_.ap().rearrange("r c -> (r c)") for b_ in gat_bufs]

    def exr(c, off, n):
        return exp_flat[c][off:off + n]

    def gar(c, r, off, n):
        o = r * EXC_ROWS[c] * 512 + off
        return gat_flat[c][o:o + n]

    GROUPS = [[0, 1, 2, 3], [4, 5, 6, 7]]

    with ExitStack() as es:
        ARN = 50 * 1024
        arena_t = es.enter_context(nc.sbuf_tensor("arena", [128, ARN], F32))
        A = Arena(arena_t[:, :], ARN)
        psum_t = es.enter_context(nc.psum_tensor("psum", [128, 4096], F32))
        PS = psum_t[:, :]
        kb = KB(nc, es)
        A.kb = kb
        BK = [Res("bank%d" % i) for i in range(8)]

        def bank(i, n=512):
            return PS[:, i * 512:i * 512 + n]

        def v3(ap, a):
            return ap.rearrange("p (a b) -> p a b", a=a)

        CONST = A.f32(4 * 128)
        IDF = CONST[:, 0:128]
        ONF = CONST[:, 128:256]
        TRIF = CONST[:, 256:384]
        TRIB = CONST[:, 384:512]
        IDB = A.bf16(128)
        ONB = A.bf16(128)
        SEL = A.f32(12)
        ROPE = A.f32(8 * 128)
        rCONST = Res("const")
        kb.dma("sp", CONST, consts_in[:, 0:512], writes=[rCONST])
        kb.dma("sp", SEL, sel_in[:, :], writes=[rCONST])
        kb.dma("sp", ROPE, rope_in.rearrange("p a b -> p (a b)"), writes=[rCONST])
        kb.op("dve", lambda e: e.tensor_copy(IDB, IDF), reads=[rCONST], writes=[rCONST])
        kb.op("dve", lambda e: e.tensor_copy(ONB, ONF), reads=[rCONST], writes=[rCONST])
        ROPE3 = v3(ROPE, 8)
        kb.barrier()

        rBIGd = Res("bigd")
        rBIGa = Res("biga")
        MIXd = dscr("MIXd", [32 * 128, NT], BF16)
        MST = [A.bf16(NT) for _ in range(2)]
        rMST = [Res("mst0"), Res("mst1")]
        mst_i = [0]

        def mix_stage():
            i = mst_i[0] % 2
            mst_i[0] += 1
            return MST[i], rMST[i]

        def mix_store(kc, stg, rstg):
            kb.dma("sp", MIXd.ap()[kc * 128:(kc + 1) * 128, 0:NTL_cur[0]], stg[:, 0:NTL_cur[0]], reads=[rstg])
        NTL_cur = [NT]
        dumped = set()

        def dump(name, ap2d, shape, dt=F32, reads=()):
            if name not in debug or name in dumped:
                return
            dumped.add(name)
            dtn = nc.dram_tensor(name, list(shape), dt, kind="ExternalOutput")
            kb.dma("sp", dtn.ap(), ap2d, reads=list(reads))

        def transpose_rows(src, R, dst, bk, rsrc, rdst):
            kb.op("pe", lambda e: e.matmul(bank(bk, R), lhsT=src[0:R, :], rhs=IDF[0:R, 0:R], start=True, stop=True),
                  reads=[rsrc, rCONST], writes=[BK[bk]])
            kb.op("dve", lambda e: e.tensor_copy(dst, bank(bk, R)), reads=[BK[bk]], writes=[rdst])

        m0 = A.mark()
        CCT = A.f32(16)
        rCCT = Res("cct")
        kb.dma("sp", v3(CCT, 8), ccT_in.rearrange("(kc kp) r -> kp kc r", kp=128), writes=[rCCT])
        kb.op("act", lambda e: e.activation(out=CCT, in_=CCT, func=AF.Silu), reads=[rCCT], writes=[rCCT])
        CCT3 = v3(CCT, 8)
        WA = [A.f32(4096) for _ in range(3)]
        rWA = [Res("wa%d" % i) for i in range(3)]
        AST = [A.f32(4096) for _ in range(2)]
        rAST = [Res("ast%d" % i) for i in range(2)]
        it = 0
        for l in range(DEPTH):
            for third in range(3):
                c0 = third * 4096
                for kc in range(8):
                    b = it % 3
                    it += 1
                    kb.dma("sp", WA[b], wada_in[l][kc * 128:(kc + 1) * 128, c0:c0 + 4096], writes=[rWA[b]])
                    for n in range(8):
                        kb.op("pe", (lambda b=b, kc=kc, n=n: lambda e: e.matmul(
                            bank(n)[0:2, :], lhsT=CCT3[:, kc, :], rhs=WA[b][:, n * 512:(n + 1) * 512],
                            start=(kc == 0), stop=(kc == 7)))(), reads=[rCCT, rWA[b]], writes=[BK[n]])
                ab = third % 2
                for n in range(8):
                    if n % 2 == 0:
                        kb.op("act", (lambda ab=ab, n=n: lambda e: e.copy(AST[ab][0:2, n * 512:(n + 1) * 512], bank(n)[0:2, :]))(),
                              reads=[BK[n]], writes=[rAST[ab]])
                    else:
                        kb.op("dve", (lambda ab=ab, n=n: lambda e: e.tensor_copy(AST[ab][0:2, n * 512:(n + 1) * 512], bank(n)[0:2, :]))(),
                              reads=[BK[n]], writes=[rAST[ab]])
                kb.dma("sp", ada_part.ap()[2 * l:2 * l + 2, c0:c0 + 4096], AST[ab][0:2, :], reads=[rAST[ab]])
        kb.collective("AllReduce", ALU.add, GROUPS, ada_part.ap().opt(), ada_full.ap().opt())
        A.release(m0)
        if dbg_ada is not None:
            kb.dma("sp", dbg_ada.ap(), ada_full.ap())
            kb.barrier()
        if stop_after == "ada":
            nlayers = 0

        def emit_layer(l):
            last = (l == DEPTH - 1)
            NTL = NL if last else NT
            TL = NTL // 128
            wl = win_in[l]
            lay_mark = A.mark()

            def tok_src(t, c0, c1, l=l):
                if l == 0:
                    if t < 8:
                        return x_in[t * 128:(t + 1) * 128, c0:c1]
                    return ctx_in[(t - 8) * 128:(t - 7) * 128, c0:c1]
                return xs.ap()[t * 128:(t + 1) * 128, c0:c1]

            R1 = A.f32(128)
            R2 = A.f32(128)
            R3 = A.f32(128)
            R4a = A.f32(128)
            R4b = A.f32(128)
            rR = Res("R")
            af = ada_full.ap()
            srcs1 = [af[2 * l:2 * l + 1, D:2 * D], af[2 * l:2 * l + 1, 0:D],
                     af[2 * l + 1:2 * l + 2, D:2 * D], af[2 * l + 1:2 * l + 2, 0:D]]
            for i, s_ in enumerate(srcs1):
                kb.dma("sp", R1[32 * i:32 * i + 32, :], s_.rearrange("o (a b) -> (o a) b", b=128), writes=[rR])
            srcs2 = [bada_in[l:l + 1, D:2 * D], bada_in[l:l + 1, 0:D], normg_in[l:l + 1, :]]
            for i, s_ in enumerate(srcs2):
                kb.dma("sp", R2[32 * i:32 * i + 32, :], s_.rearrange("o (a b) -> (o a) b", b=128), writes=[rR])
            for i, s_ in enumerate([convb_in[l], cvlg_in[l], cvlb_in[l], mhg_in[l]]):
                kb.dma("sp", R3[8 * i:8 * i + 8, :], s_, writes=[rR])
            kb.dma("sp", R4a[0:124, :], convw_in[l][0:124, :], writes=[rR])
            kb.dma("sp", R4b[0:124, :], convw_in[l][124:248, :], writes=[rR])
            PT1 = A.f32(128)
            PT2 = A.f32(96)
            PT3 = A.f32(32)
            CW = A.f32(248)
            rPT = Res("PT")
            transpose_rows(R1, 128, PT1, 0, rR, rPT)
            transpose_rows(R2, 96, PT2, 1, rR, rPT)
            transpose_rows(R3, 32, PT3, 2, rR, rPT)
            transpose_rows(R4a, 124, CW[:, 0:124], 3, rR, rPT)
            transpose_rows(R4b, 124, CW[:, 124:248], 4, rR, rPT)
            MOD = A.f32(128)
            for j in range(2):
                kb.op("dve", (lambda j=j: lambda e: e.scalar_tensor_tensor(
                    out=MOD[:, 64 * j:64 * j + 32], in0=PT1[:, 64 * j:64 * j + 32], scalar=1.0, in1=PT2[:, 0:32],
                    op0=ALU.add, op1=ALU.add))(), reads=[rPT], writes=[rPT])
                kb.op("dve", (lambda j=j: lambda e: e.tensor_tensor(
                    out=MOD[:, 64 * j:64 * j + 32], in0=MOD[:, 64 * j:64 * j + 32], in1=PT2[:, 64:96],
                    op=ALU.mult))(), reads=[rPT], writes=[rPT])
                kb.op("dve", (lambda j=j: lambda e: e.tensor_tensor(
                    out=MOD[:, 64 * j + 32:64 * j + 64], in0=PT1[:, 64 * j + 32:64 * j + 64], in1=PT2[:, 32:64],
                    op=ALU.add))(), reads=[rPT], writes=[rPT])
            kb.barrier()
            par_mark = A.mark()
            NTL_cur[0] = NTL
            BIG = A.f32(16 * NT)
            BIGB = BIG.bitcast(BF16)
            HT = v3(BIGB, 32)
            WT0 = A.bf16(32 * 512)
            rWT = [Res("wt0"), Res("wt1")]
            tiles = []
            for name, c0, ncol, d0 in TOKB:
                for j in range(0, ncol, 512):
                    tiles.append(("tok", name, c0 + j, min(512, ncol - j), d0 + j))
            for name, c0, r0 in CHB:
                for j in range(0, 1024, 512):
                    tiles.append(("ch", name, c0 + j, 512, r0 + j))
            wsrc = wl.rearrange("(kc kp) n -> kp kc n", kp=128)
            for g_ in range(4):
                kb.dma("pool", v3(WT0, 32)[:, g_ * 8:(g_ + 1) * 8, 0:tiles[0][3]],
                       wsrc[:, g_ * 8:(g_ + 1) * 8, tiles[0][2]:tiles[0][2] + tiles[0][3]], writes=[rWT[0]])
            big_mark = A.mark()

            XT = [A.f32(D) for _ in range(2)]
            rXT = [Res("xt0"), Res("xt1")]
            JUNK = A.bf16(D)
            rJ = Res("junk")
            SS = A.f32(8)
            rSS = Res("ss")
            for t in range(10):
                b = t % 2
                kb.dma("sp", XT[b], tok_src(t, 0, D), writes=[rXT[b]])
                kb.op("act", (lambda b=b: lambda e: e.activation(out=JUNK, in_=XT[b], func=AF.Square,
                                                                   accum_out=SS[:, 0:1]))(),
                      reads=[rXT[b]], writes=[rJ, rSS])
                kb.op("dve", lambda e: e.tensor_scalar(SS[:, 1:2], SS[:, 0:1], 1.0 / D, EPS, ALU.mult, ALU.add),
                      reads=[rSS], writes=[rSS])
                kb.op("act", lambda e: e.sqrt(SS[:, 3:4], SS[:, 1:2]), reads=[rSS], writes=[rSS])
                kb.op("dve", lambda e: e.reciprocal(SS[:, 2:3], SS[:, 3:4]), reads=[rSS], writes=[rSS])
                kb.op("act", (lambda b=b: lambda e: e.activation(out=XT[b], in_=XT[b], func=AF.Copy,
                                                                   scale=SS[:, 2:3]))(),
                      reads=[rXT[b], rSS], writes=[rXT[b]])
                mo = 0 if t < 8 else 64
                for g4 in range(8):
                    bk = g4 % 4
                    for j in range(4):
                        kc = g4 * 4 + j
                        kb.op("pe", (lambda b=b, kc=kc, bk=bk, j=j: lambda e: e.matmul(
                            bank(bk)[:, j * 128:(j + 1) * 128], lhsT=XT[b][:, kc * 128:(kc + 1) * 128], rhs=IDF,
                            start=True, stop=True))(), reads=[rXT[b], rCONST], writes=[BK[bk]])
                    for j in range(4):
                        kc = g4 * 4 + j
                        dst = HT[:, kc, t * 128:(t + 1) * 128]
                        src = bank(bk)[:, j * 128:(j + 1) * 128]
                        if g4 % 2 == 0:
                            kb.op("dve", (lambda dst=dst, src=src, kc=kc, mo=mo: lambda e: e.tensor_scalar(
                                dst, src, MOD[:, mo + kc:mo + kc + 1], MOD[:, mo + 32 + kc:mo + 33 + kc],
                                ALU.mult, ALU.add))(), reads=[BK[bk], rPT], writes=[rBIGd])
                        else:
                            kb.op("act", (lambda dst=dst, src=src, kc=kc, mo=mo: lambda e: e.activation(
                                out=dst, in_=src, func=AF.Identity, scale=MOD[:, mo + kc:mo + kc + 1],
                                bias=MOD[:, mo + 32 + kc:mo + 33 + kc]))(), reads=[BK[bk], rPT], writes=[rBIGa])
            kb.barrier()
            if dbg_HT is not None and l == 0:
                kb.dma("sp", dbg_HT.ap(), BIGB, reads=[rBIGd, rBIGa])
                kb.barrier()
            A.release(big_mark)
            if stop_after == "norm":
                return

            WT = [WT0, A.bf16(32 * 512)]
            STG = [A.f32(NT) for _ in range(2)]
            rSTG = [Res("stg0"), Res("stg1")]
            def load_w(src3, c0, ncol, b):
                w3 = v3(WT[b], 32)
                for g in range(4):
                    kb.dma("pool", w3[:, g * 8:(g + 1) * 8, 0:ncol], src3[:, g * 8:(g + 1) * 8, c0:c0 + ncol],
                           writes=[rWT[b]])

            stg_i = 0
            ev_i = 0
            for ti, (kind, name, c0, ncol, d0) in enumerate(tiles):
                b = ti % 2
                if ti + 1 < len(tiles):
                    load_w(wsrc, tiles[ti + 1][2], tiles[ti + 1][3], 1 - b)
                w3 = v3(WT[b], 32)
                if kind == "tok":
                    nt_ = 8 if (last and name in ("Av", "Cq")) else 10
                    for t in range(nt_):
                        bk = 6 + (t % 2)
                        for kc in range(32):
                            kb.op("pe", (lambda bk=bk, kc=kc, t=t, w3=w3, ncol=ncol: lambda e: e.matmul(
                                bank(bk, ncol), lhsT=HT[:, kc, t * 128:(t + 1) * 128], rhs=w3[:, kc, 0:ncol],
                                start=(kc == 0), stop=(kc == 31)))(), reads=[rBIGd, rBIGa, rWT[b]], writes=[BK[bk]])
                        sb = stg_i % 2
                        stg_i += 1
                        eng = "act" if ev_i % 2 == 0 else "dve"
                        ev_i += 1
                        if eng == "act":
                            kb.op("act", (lambda sb=sb, bk=bk, ncol=ncol: lambda e: e.copy(
                                STG[sb][:, 0:ncol], bank(bk, ncol)))(), reads=[BK[bk]], writes=[rSTG[sb]])
                        else:
                            kb.op("dve", (lambda sb=sb, bk=bk, ncol=ncol: lambda e: e.tensor_copy(
                                STG[sb][:, 0:ncol], bank(bk, ncol)))(), reads=[BK[bk]], writes=[rSTG[sb]])
                        kb.dma("sp", Ptok.ap()[t * 128:(t + 1) * 128, d0:d0 + ncol], STG[sb][:, 0:ncol],
                               reads=[rSTG[sb]])
                else:
                    ntok = NTL
                    tts = [(0, 512), (512, 512)] + ([(1024, 256)] if ntok > 1024 else [])
                    for cb in range(4):
                        bset = 3 * (cb % 2)
                        for kc in range(32):
                            for i, (t0, tn) in enumerate(tts):
                                kb.op("pe", (lambda bset=bset, i=i, kc=kc, cb=cb, t0=t0, tn=tn, w3=w3: lambda e: e.matmul(
                                    bank(bset + i, tn), lhsT=w3[:, kc, cb * 128:(cb + 1) * 128],
                                    rhs=HT[:, kc, t0:t0 + tn], start=(kc == 0), stop=(kc == 31)))(),
                                    reads=[rBIGd, rBIGa, rWT[b]], writes=[BK[bset + i]])
                        sb = stg_i % 2
                        stg_i += 1
                        for i, (t0, tn) in enumerate(tts):
                            eng = "act" if ev_i % 2 == 0 else "dve"
                            ev_i += 1
                            if eng == "act":
                                kb.op("act", (lambda sb=sb, bset=bset, i=i, t0=t0, tn=tn: lambda e: e.copy(
                                    STG[sb][:, t0:t0 + tn], bank(bset + i, tn)))(),
                                    reads=[BK[bset + i]], writes=[rSTG[sb]])
                            else:
                                kb.op("dve", (lambda sb=sb, bset=bset, i=i, t0=t0, tn=tn: lambda e: e.tensor_copy(
                                    STG[sb][:, t0:t0 + tn], bank(bset + i, tn)))(),
                                    reads=[BK[bset + i]], writes=[rSTG[sb]])
                        kb.dma("sp", Pch.ap()[d0 + cb * 128:d0 + (cb + 1) * 128, 0:ntok], STG[sb][:, 0:ntok],
                               reads=[rSTG[sb]])
            kb.barrier()
            A.release(par_mark)
            if stop_after == "gemm1":
                return
            tts = [(0, 512), (512, 512)] + ([(1024, 256)] if NTL > 1024 else [])
            KTC = A.bf16(2 * 256)
            KTC3 = v3(KTC, 2)
            rKTC = Res("ktc")
            IBFB = A.f32(32)
            GQK = A.f32(256)
            rPB = Res("pb")
            kb.dma("sp", IBFB[:, 0:16], ib_in[l:l + 1, :].partition_broadcast(128), writes=[rPB])
            kb.dma("sp", IBFB[:, 16:32], fb_in[l:l + 1, :].partition_broadcast(128), writes=[rPB])
            kb.dma("sp", GQK[:, 0:128], qng_in[l:l + 1, :].partition_broadcast(128), writes=[rPB])
            kb.dma("sp", GQK[:, 128:256], kng_in[l:l + 1, :].partition_broadcast(128), writes=[rPB])
            pers2_mark = A.mark()
            HS = A.f32(8 * NT)
            HS3 = v3(HS, 8)
            rHS = Res("hs")
            GP = A.f32(10 * 5 * 16)
            GP4 = GP.rearrange("p (t k g) -> p t k g", t=10, k=5)
            rGP = Res("gp")
            CS = A.f32(16 * 256)
            CS3 = v3(CS, 16)
            rCS = Res("cs")
            CB = A.bf16(16 * 256)
            CB3 = v3(CB, 16)
            rCB = Res("cb")
            STFB = A.f32(16 * 256)
            rSTFB = Res("stfb")
            pers_mark = A.mark()

            def bc(ap, shape):
                return ap.broadcast_to(list(shape))

            def rsqrt_ops(dst, src, scale, tmp, rr, rw):
                kb.op("dve", lambda e: e.tensor_scalar(tmp, src, scale, EPS, ALU.mult, ALU.add), reads=rr, writes=rw)
                kb.op("act", lambda e: e.sqrt(tmp, tmp), reads=rw, writes=rw)
                kb.op("dve", lambda e: e.reciprocal(dst, tmp), reads=rw, writes=rw)

            def rope_ops(src, dst, H, t, T1, T2, rs, rd, rt):
                s5 = src.rearrange("p (h a c j) -> p h a c j", h=H, a=2, c=2)
                d5 = dst.rearrange("p (h a c j) -> p h a c j", h=H, a=2, c=2)
                x1, x2 = s5[:, :, :, 0, :], s5[:, :, :, 1, :]
                o1, o2 = d5[:, :, :, 0, :], d5[:, :, :, 1, :]
                cs = bc(ROPE3[:, t, 0:64].rearrange("p (a j) -> p a j", a=2).unsqueeze(1), [128, H, 2, 32])
                sn = bc(ROPE3[:, t, 64:128].rearrange("p (a j) -> p a j", a=2).unsqueeze(1), [128, H, 2, 32])
                t1 = T1.rearrange("p (h a j) -> p h a j", h=H, a=2)
                t2 = T2.rearrange("p (h a j) -> p h a j", h=H, a=2)
                kb.op("dve", lambda e: e.tensor_tensor(o1, x1, cs, ALU.mult), reads=[rs, rCONST], writes=[rd])
                kb.op("pool", lambda e: e.tensor_tensor(t1, x2, sn, ALU.mult), reads=[rs, rCONST], writes=[rt])
                kb.op("dve", lambda e: e.tensor_tensor(o1, o1, t1, ALU.subtract), reads=[rt], writes=[rd])
                kb.op("dve", lambda e: e.tensor_tensor(o2, x2, cs, ALU.mult), reads=[rs, rCONST], writes=[rd])
                kb.op("pool", lambda e: e.tensor_tensor(t2, x1, sn, ALU.mult), reads=[rs, rCONST], writes=[rt])
                kb.op("dve", lambda e: e.tensor_tensor(o2, o2, t2, ALU.add), reads=[rt], writes=[rd])

            def qk_norm_rope(src, H, gq, t, NRM, ROT, TS, SSQ, T1, T2, rsrc, rw):
                kb.op("act", lambda e: e.activation(out=TS, in_=src, func=AF.Square), reads=[rsrc], writes=[rw])
                kb.op("dve", lambda e: e.tensor_reduce(out=SSQ[:, 0:H], in_=v3(TS, H), axis=AX.X, op=ALU.add),
                      reads=[rw], writes=[rw])
                rsqrt_ops(SSQ[:, 16:16 + H], SSQ[:, 0:H], 1.0 / 128, SSQ[:, 8:8 + H], [rw], [rw])
                kb.op("dve", lambda e: e.tensor_tensor(v3(NRM, H), v3(src, H), bc(SSQ[:, 16:16 + H].unsqueeze(2), [128, H, 128]),
                                                       ALU.mult), reads=[rsrc, rw], writes=[rw])
                kb.op("dve", lambda e: e.tensor_tensor(v3(NRM, H), v3(NRM, H), bc(gq.unsqueeze(1), [128, H, 128]), ALU.mult),
                      reads=[rw, rPB], writes=[rw])
                if t < 8:
                    rope_ops(NRM, ROT, H, t, T1, T2, rw, rw, rw)
                else:
                    kb.op("dve", lambda e: e.tensor_copy(ROT, NRM), reads=[rw], writes=[rw])

            GG = A.f32(320)
            XF = A.f32(160)
            IG = A.f32(160)
            LF = A.f32(160)
            rGG = Res("gg")
            G5 = GG.rearrange("p (t d i h) -> p t d i h", t=10, d=2, i=2)
            x4 = lambda ap: ap.rearrange("p (t d h) -> p t d h", t=10, d=2)
            kb.dma("sp", v3(GG, 10), Ptok.ap()[:, 4608:4640].rearrange("(t p) c -> p t c", p=128), writes=[rGG])
            kb.op("dve", lambda e: e.tensor_tensor(x4(XF), G5[:, :, :, 1, :], bc(v3(IBFB[:, 16:32], 2).unsqueeze(1), [128, 10, 2, 8]),
                                                   ALU.add), reads=[rGG, rPB], writes=[rGG])
            kb.op("dve", lambda e: e.tensor_tensor(x4(IG), G5[:, :, :, 0, :], bc(v3(IBFB[:, 0:16], 2).unsqueeze(1), [128, 10, 2, 8]),
                                                   ALU.add), reads=[rGG, rPB], writes=[rGG])
            kb.op("act", lambda e: e.activation(out=LF, in_=XF, func=AF.Exp, scale=-1.0), reads=[rGG], writes=[rGG])
            kb.op("dve", lambda e: e.tensor_scalar_add(LF, LF, 1.0), reads=[rGG], writes=[rGG])
            kb.op("act", lambda e: e.activation(out=LF, in_=LF, func=AF.Ln), reads=[rGG], writes=[rGG])
            kb.op("dve", lambda e: e.tensor_scalar_mul(LF, LF, -1.0), reads=[rGG], writes=[rGG])
            LF3 = v3(LF, 10)
            for t in range(10):
                kb.op("pe", (lambda t=t: lambda e: e.matmul(bank(1)[:, t * 32:t * 32 + 8], lhsT=TRIF, rhs=LF3[:, t, 0:8],
                                                            start=True, stop=True))(), reads=[rGG, rCONST], writes=[BK[1]])
                kb.op("pe", (lambda t=t: lambda e: e.matmul(bank(1)[:, t * 32 + 8:t * 32 + 16], lhsT=TRIB, rhs=LF3[:, t, 8:16],
                                                            start=True, stop=True))(), reads=[rGG, rCONST], writes=[BK[1]])
                kb.op("pe", (lambda t=t: lambda e: e.matmul(bank(1)[:, t * 32 + 16:t * 32 + 32], lhsT=ONF, rhs=LF3[:, t, 0:16],
                                                            start=True, stop=True))(), reads=[rGG, rCONST], writes=[BK[1]])
            PS1 = v3(bank(1, 320), 10)
            kb.op("dve", lambda e: e.tensor_copy(GP4[:, :, 0, :], PS1[:, :, 0:16]), reads=[BK[1]], writes=[rGP])
            kb.op("dve", lambda e: e.tensor_copy(GP4[:, :, 3, :], PS1[:, :, 16:32]), reads=[BK[1]], writes=[rGP])
            kb.op("dve", lambda e: e.scalar_tensor_tensor(out=GP4[:, :, 1, :], in0=v3(IG, 10), scalar=LNSC, in1=GP4[:, :, 0, :],
                                                          op0=ALU.add, op1=ALU.subtract), reads=[rGG, rGP], writes=[rGP])
            kb.op("dve", lambda e: e.tensor_tensor(v3(XF, 10), GP4[:, :, 3, :], GP4[:, :, 1, :], ALU.add), reads=[rGP], writes=[rGG])
            kb.op("act", lambda e: e.activation(out=GP4[:, :, 2, :], in_=v3(XF, 10), func=AF.Exp), reads=[rGG], writes=[rGP])
            kb.op("act", lambda e: e.activation(out=GP4[:, :, 4, :], in_=GP4[:, :, 3, :], func=AF.Exp), reads=[rGP], writes=[rGP])
            dump("d_GP", GP, [128, 800], reads=[rGP])
            A.release(pers_mark)

            KTOK = [A.f32(1024) for _ in range(2)]
            VTOK = [A.f32(1024) for _ in range(2)]
            rKTOK = [Res("ktok0"), Res("ktok1")]
            rVTOK = [Res("vtok0"), Res("vtok1")]
            KTLs = [A.bf16(1024) for _ in range(2)]
            rKTLs = [Res("ktl0"), Res("ktl1")]
            VA = [A.bf16(8 * 256) for _ in range(2)]
            rVA = [Res("va0"), Res("va1")]
            QTb = A.bf16(1024)
            KTb = A.bf16(1024)
            rQKb = Res("qkb")
            DG = A.f32(1024)
            rDG = Res("dg")
            EB = A.f32(1024)
            rEB = Res("eb")
            WTm = A.f32(1024)
            rWTm = Res("wtm")
            PTb = A.bf16(1024)
            rPTb = Res("ptb")
            QTL = A.bf16(1024)
            rQTL = Res("qtl")
            DN = A.f32(1024)
            rDN = Res("dn")
            TMPH = A.f32(1024)
            rTMPH = Res("tmph")
            for b in range(2):
                kb.op("pool", (lambda b=b: lambda e: e.memset(v3(VA[b], 8)[:, :, 128:256], 1.0))(), writes=[rVA[b]])
            scan_ctr = [0]

            def scan(chunks, dr, emit):
                g0 = dr * 8
                MASK = TRIF if dr == 0 else TRIB
                for t in chunks:
                    b = scan_ctr[0] % 2
                    scan_ctr[0] += 1
                    va3 = v3(VA[b], 8)
                    KTL = KTLs[b]
                    rKTL = rKTLs[b]
                    stb = 2 if (emit or b == 0) else 4
                    kb.dma("sp", KTOK[b], Ptok.ap()[t * 128:(t + 1) * 128, 2560:3584], writes=[rKTOK[b]])
                    kb.dma("sp", VTOK[b], Ptok.ap()[t * 128:(t + 1) * 128, 3584:4608], writes=[rVTOK[b]])
                    kb.op("dve", (lambda b=b, t=t, KTL=KTL: lambda e: e.tensor_tensor(
                        v3(KTL, 8), v3(KTOK[b], 8), bc(GP4[:, t, 2, g0:g0 + 8].unsqueeze(2), [128, 8, 128]), ALU.mult))(),
                        reads=[rKTOK[b], rGP], writes=[rKTL])
                    kb.op("act", (lambda b=b, va3=va3: lambda e: e.copy(va3[:, :, 0:128], v3(VTOK[b], 8)))(),
                          reads=[rVTOK[b]], writes=[rVA[b]])
                    if emit:
                        kb.dma("pool", v3(QTb, 8), Pch.ap()[CR["Dq"]:CR["Dq"] + 1024, t * 128:(t + 1) * 128].rearrange(
                            "(h d) t -> d h t", d=128), writes=[rQKb])
                        kb.dma("pool", v3(KTb, 8), Pch.ap()[CR["DkT"]:CR["DkT"] + 1024, t * 128:(t + 1) * 128].rearrange(
                            "(h d) t -> d h t", d=128), writes=[rQKb])
                        for h in range(8):
                            kb.op("pe", (lambda h=h: lambda e: e.matmul(
                                PS[:, h * 128:(h + 1) * 128], lhsT=v3(KTb, 8)[:, h, :], rhs=v3(QTb, 8)[:, h, :],
                                start=True, stop=True))(), reads=[rQKb], writes=[BK[h // 4]])
                        kb.op("dve", (lambda t=t: lambda e: e.tensor_tensor(
                            v3(DG, 8), bc(IDF.unsqueeze(1), [128, 8, 128]),
                            bc(GP4[:, t, 0, g0:g0 + 8].unsqueeze(2), [128, 8, 128]), ALU.mult))(),
                            reads=[rGP, rCONST], writes=[rDG])
                        for j in range(2):
                            kb.op("pe", (lambda j=j: lambda e: e.matmul(
                                PS[:, 1024 + j * 512:1536 + j * 512], lhsT=ONF, rhs=DG[:, j * 512:(j + 1) * 512],
                                start=True, stop=True))(), reads=[rDG, rCONST], writes=[BK[2 + j]])
                            kb.op("act", (lambda j=j: lambda e: e.activation(
                                out=EB[:, j * 512:(j + 1) * 512], in_=PS[:, 1024 + j * 512:1536 + j * 512], func=AF.Exp))(),
                                reads=[BK[2 + j]], writes=[rEB])
                        for h in range(8):
                            kb.op("act", (lambda h=h, t=t: lambda e: e.activation(
                                out=WTm[:, h * 128:(h + 1) * 128], in_=PS[:, 1024 + h * 128:1152 + h * 128], func=AF.Exp,
                                bias=GP4[:, t, 1, g0 + h:g0 + h + 1]))(), reads=[BK[2 + h // 4], rGP], writes=[rWTm])
                        kb.op("pool", lambda e: e.tensor_tensor(v3(WTm, 8), v3(WTm, 8), bc(MASK.unsqueeze(1), [128, 8, 128]),
                                                                ALU.mult), reads=[rCONST], writes=[rWTm])
                        kb.op("dve", lambda e: e.tensor_tensor(PTb, PS[:, 0:1024], WTm, ALU.mult),
                              reads=[BK[0], BK[1], rWTm], writes=[rPTb])
                        kb.op("pool", lambda e: e.tensor_tensor(QTL, QTb, EB, ALU.mult), reads=[rQKb, rEB], writes=[rQTL])
                        for h in range(8):
                            kb.op("pe", (lambda h=h, va3=va3: lambda e: e.matmul(
                                PS[:, 2048 + h * 128:2176 + h * 128], lhsT=va3[:, h, 0:128], rhs=v3(PTb, 8)[:, h, :],
                                start=True, stop=False))(), reads=[rVA[b], rPTb], writes=[BK[4 + h // 4]])
                            kb.op("pe", (lambda h=h: lambda e: e.matmul(
                                PS[:, 2048 + h * 128:2176 + h * 128], lhsT=CB3[:, g0 + h, 0:128], rhs=v3(QTL, 8)[:, h, :],
                                start=False, stop=True))(), reads=[rCB, rQTL], writes=[BK[4 + h // 4]])
                        for h in range(8):
                            kb.op("pe", (lambda h=h: lambda e: e.matmul(
                                PS[:, 3072 + h * 128:3200 + h * 128], lhsT=ONB, rhs=v3(PTb, 8)[:, h, :],
                                start=True, stop=False))(), reads=[rCONST, rPTb], writes=[BK[6 + h // 4]])
                            kb.op("pe", (lambda h=h: lambda e: e.matmul(
                                PS[:, 3072 + h * 128:3200 + h * 128], lhsT=CB3[:, g0 + h, 128:256], rhs=v3(QTL, 8)[:, h, :],
                                start=False, stop=True))(), reads=[rCB, rQTL], writes=[BK[6 + h // 4]])
                        kb.op("act", lambda e: e.activation(out=DN, in_=PS[:, 3072:4096], func=AF.Abs),
                              reads=[BK[6], BK[7]], writes=[rDN])
                        kb.op("dve", lambda e: e.tensor_scalar_max(DN, DN, 1.0), reads=[rDN], writes=[rDN])
                        kb.op("dve", lambda e: e.reciprocal(DN, DN), reads=[rDN], writes=[rDN])
                        hs_sl = HS3[:, :, t * 128:(t + 1) * 128]
                        if dr == 0:
                            kb.op("dve", (lambda hs_sl=hs_sl: lambda e: e.tensor_tensor(hs_sl, v3(PS[:, 2048:3072], 8), v3(DN, 8),
                                                                                      ALU.mult))(),
                                  reads=[BK[4], BK[5], rDN], writes=[rHS])
                        else:
                            kb.op("dve", lambda e: e.tensor_tensor(TMPH, PS[:, 2048:3072], DN, ALU.mult),
                                  reads=[BK[4], BK[5], rDN], writes=[rTMPH])
                            kb.op("pool", (lambda hs_sl=hs_sl: lambda e: e.tensor_tensor(hs_sl, hs_sl, v3(TMPH, 8), ALU.add))(),
                                  reads=[rTMPH, rHS], writes=[rHS])
                    kb.op("dve", (lambda t=t: lambda e: e.tensor_tensor(
                        CS3[:, g0:g0 + 8, :], CS3[:, g0:g0 + 8, :], bc(GP4[:, t, 4, g0:g0 + 8].unsqueeze(2), [128, 8, 256]),
                        ALU.mult))(), reads=[rGP, rCS], writes=[rCS])
                    for half in range(2):
                        for hh in range(4):
                            h = half * 4 + hh
                            kb.op("pe", (lambda h=h, hh=hh, va3=va3, KTL=KTL, stb=stb: lambda e: e.matmul(
                                PS[:, stb * 512 + hh * 256:stb * 512 + 256 + hh * 256], lhsT=v3(KTL, 8)[:, h, :], rhs=va3[:, h, :],
                                start=True, stop=True))(), reads=[rKTL, rVA[b]], writes=[BK[stb + hh // 2]])
                        kb.op("dve", (lambda half=half, stb=stb: lambda e: e.tensor_tensor(
                            CS3[:, g0 + half * 4:g0 + half * 4 + 4, :], CS3[:, g0 + half * 4:g0 + half * 4 + 4, :],
                            v3(PS[:, stb * 512:stb * 512 + 1024], 4), ALU.add))(), reads=[BK[stb], BK[stb + 1], rCS], writes=[rCS])
                    if emit:
                        kb.op("act", lambda e: e.copy(CB3[:, g0:g0 + 8, :], CS3[:, g0:g0 + 8, :]), reads=[rCS], writes=[rCB])

            def zero_state():
                kb.op("dve", lambda e: e.memset(CS, 0.0), writes=[rCS])
                kb.op("pool", lambda e: e.memset(CB, 0.0), writes=[rCB])

            zero_state()
            scan([8, 9], 0, not last)
            scan([9, 8], 1, not last)
            kb.op("dve", lambda e: e.tensor_copy(STFB, CS), reads=[rCS], writes=[rSTFB])
            dump("d_STFB", STFB, [128, 4096], reads=[rSTFB])
            dump("d_HSctx", HS, [128, 8 * NT], reads=[rHS])
            zero_state()
            scan(list(range(8)), 0, False)
            scan(list(range(7, -1, -1)), 1, False)
            for dr_ in range(2):
                kb.dma("sp", exr(2 + dr_, 0, 128 * 2048).rearrange("(p x) -> p x", p=128), CS[:, dr_ * 2048:(dr_ + 1) * 2048],
                       reads=[rCS])
            DTT = A.f32(16)
            rDTT = Res("dtt")
            kb.op("dve", lambda e: e.tensor_reduce(out=DTT, in_=GP4[:, 0:8, 3, :].rearrange("p t g -> p g t"), axis=AX.X,
                                                   op=ALU.add), reads=[rGP], writes=[rDTT])
            kb.dma("sp", exr(4, 0, 128 * 16).rearrange("(p x) -> p x", p=128), DTT, reads=[rDTT])
            scan_mark = A.mark()

            if stop_after == "pre":
                return
            kb.issue_collectives("AllGather", ALU.bypass, GROUPS,
                                 [(exp_bufs[c_].ap().opt(), gat_bufs[c_].ap().opt()) for c_ in (2, 3, 4)])
            m2_mark = A.mark()
            KV = [A.f32(512) for _ in range(2)]
            rKV = [Res("kv0"), Res("kv1")]
            KTE = A.f32(2 * 1024)
            KTE3 = v3(KTE, 2)
            rKTE = Res("kte")
            KWS = []
            for i_ in range(2):
                KWS.append(dict(KN=A.f32(256), KR=A.f32(256), KTS=A.f32(256), KT1=A.f32(128), KT2=A.f32(128),
                                KRB=A.bf16(256), SSK=A.f32(24), r=Res("kw%d" % i_)))
            for t in range(10):
                b = t % 2
                W_ = KWS[b]
                bk_ = 4 * b
                kb.dma("sp", KV[b], Ptok.ap()[t * 128:(t + 1) * 128, 2048:2560], writes=[rKV[b]])
                qk_norm_rope(KV[b][:, 0:256], 2, GQK[:, 128:256], t, W_["KN"], W_["KR"], W_["KTS"], W_["SSK"], W_["KT1"], W_["KT2"],
                             rKV[b], W_["r"])
                kb.op("act", (lambda W_=W_: lambda e: e.copy(W_["KRB"], W_["KR"]))(), reads=[W_["r"]], writes=[W_["r"]])
                for g in range(2):
                    kb.op("pe", (lambda g=g, W_=W_, bk_=bk_: lambda e: e.matmul(
                        bank(bk_)[:, g * 128:(g + 1) * 128], lhsT=W_["KRB"][:, g * 128:(g + 1) * 128], rhs=IDB,
                        start=True, stop=True))(), reads=[W_["r"], rCONST], writes=[BK[bk_]])
                if t < 8:
                    kb.op("act", (lambda t=t, bk_=bk_: lambda e: e.copy(KTE3[:, :, t * 128:(t + 1) * 128], v3(bank(bk_, 256), 2)))(),
                          reads=[BK[bk_]], writes=[rKTE])
                else:
                    kb.op("act", (lambda t=t, bk_=bk_: lambda e: e.copy(KTC3[:, :, (t - 8) * 128:(t - 7) * 128],
                                                                      v3(bank(bk_, 256), 2)))(), reads=[BK[bk_]], writes=[rKTC])
            kb.dma("sp", exr(0, 0, 128 * 2048).rearrange("(d x) -> d x", d=128), KTE, reads=[rKTE])
            kb.dma("sp", exr(1, 0, 1024 * 256).rearrange("(t c) -> t c", c=256), Ptok.ap()[0:1024, 2304:2560])
            A.release(m2_mark)

            ATb = [A.f32(NT) for _ in range(2)]
            GTb = [A.f32(NT) for _ in range(2)]
            rATb = [Res("at0"), Res("at1")]
            rGTb = [Res("gt0"), Res("gt1")]
            halo_ex = exr(5, 0, 128 * 240).rearrange("(c g j) -> c g j", c=128, g=8)
            for cg in range(8):
                b = cg % 2
                kb.dma("sp", ATb[b][:, 0:NTL], Pch.ap()[CR["Ba"] + cg * 128:CR["Ba"] + (cg + 1) * 128, 0:NTL], writes=[rATb[b]])
                kb.dma("sp", GTb[b][:, 0:NTL], Pch.ap()[CR["Bg"] + cg * 128:CR["Bg"] + (cg + 1) * 128, 0:NTL], writes=[rGTb[b]])
                kb.op("act", (lambda b=b: lambda e: e.activation(out=GTb[b][:, 0:NTL], in_=GTb[b][:, 0:NTL], func=AF.Sigmoid))(),
                      reads=[rGTb[b]], writes=[rGTb[b]])
                kb.op("dve", (lambda b=b: lambda e: e.tensor_tensor(ATb[b][:, 0:NTL], ATb[b][:, 0:NTL], GTb[b][:, 0:NTL], ALU.mult))(),
                      reads=[rGTb[b], rATb[b]], writes=[rATb[b]])
                kb.dma("sp", Ych.ap()[cg * 128:(cg + 1) * 128, 0:NTL], ATb[b][:, 0:NTL], reads=[rATb[b]])
                kb.dma("sp", halo_ex[:, cg, 0:15], ATb[b][:, 0:15], reads=[rATb[b]])
                kb.dma("sp", halo_ex[:, cg, 15:30], ATb[b][:, 1009:1024], reads=[rATb[b]])
            A.release(m2_mark)

            kb.issue_collectives("AllGather", ALU.bypass, GROUPS,
                                 [(exp_bufs[c_].ap().opt(), gat_bufs[c_].ap().opt()) for c_ in (0, 1, 5)])
            if stop_after == "gather":
                return
            SG = [A.f32(8 * 256) for _ in range(4)]
            rSG = Res("sg")
            DJ = A.f32(64)
            FF = A.f32(8 * 256)
            rFF = Res("ff")
            for r in range(4):
                kb.dma("sp", DJ[:, r * 16:(r + 1) * 16], gar(4, r, 0, 128 * 16).rearrange("(p x) -> p x", p=128), writes=[rSG])
            kb.op("act", lambda e: e.activation(out=DJ, in_=DJ, func=AF.Exp), reads=[rSG], writes=[rSG])
            for dr in range(2):
                for r in range(4):
                    kb.dma("sp", SG[r], gar(2 + dr, r, 0, 128 * 2048).rearrange("(p x) -> p x", p=128), writes=[rSG])
                CSd = CS[:, dr * 2048:(dr + 1) * 2048]
                kb.op("dve", (lambda dr=dr: lambda e: e.tensor_copy(FF, STFB[:, dr * 2048:(dr + 1) * 2048]))(),
                      reads=[rSTFB], writes=[rFF])
                order = [0, 1, 2] if dr == 0 else [3, 2, 1]
                fs = 0 if dr == 0 else 3
                kb.op("dve", (lambda CSd=CSd, fs=fs: lambda e: e.tensor_scalar_mul(CSd, FF, SEL[:, fs:fs + 1]))(),
                      reads=[rFF, rCONST], writes=[rCS])
                for j in order:
                    for h in range(8):
                        kb.op("dve", (lambda j=j, h=h, dr=dr: lambda e: e.scalar_tensor_tensor(
                            out=v3(FF, 8)[:, h, :], in0=v3(FF, 8)[:, h, :],
                            scalar=DJ[:, j * 16 + dr * 8 + h:j * 16 + dr * 8 + h + 1], in1=v3(SG[j], 8)[:, h, :],
                            op0=ALU.mult, op1=ALU.add))(), reads=[rSG, rFF], writes=[rFF])
                    nx = j + 1 if dr == 0 else j - 1
                    kb.op("dve", (lambda CSd=CSd, nx=nx: lambda e: e.scalar_tensor_tensor(
                        out=CSd, in0=FF, scalar=SEL[:, nx:nx + 1], in1=CSd, op0=ALU.mult, op1=ALU.add))(),
                        reads=[rFF, rCONST, rCS], writes=[rCS])
            kb.op("act", lambda e: e.copy(CB, CS), reads=[rCS], writes=[rCB])
            dump("d_INIT", CS, [128, 4096], reads=[rCS])
            scan(list(range(8)), 0, True)
            scan(list(range(7, -1, -1)), 1, True)
            dump("d_HS", HS, [128, 8 * NT], reads=[rHS])
            A.release(pers_mark)
            OTb = [A.f32(NT) for _ in range(2)]
            ZTb = [A.f32(NT) for _ in range(2)]
            rOTb = [Res("ot0"), Res("ot1")]
            rZTb = [Res("zt0"), Res("zt1")]
            SQ = A.f32(NT)
            rSQ = Res("sq")
            RS = A.f32(NT)
            rRS = Res("rs")
            for h in range(8):
                b = h % 2
                kb.dma("sp", OTb[b][:, 0:NTL], Pch.ap()[CR["Do"] + h * 128:CR["Do"] + (h + 1) * 128, 0:NTL], writes=[rOTb[b]])
                kb.dma("sp", ZTb[b][:, 0:NTL], Pch.ap()[CR["Dz"] + h * 128:CR["Dz"] + (h + 1) * 128, 0:NTL], writes=[rZTb[b]])
                kb.op("act", (lambda h=h: lambda e: e.activation(out=SQ[:, 0:NTL], in_=HS3[:, h, 0:NTL], func=AF.Square))(),
                      reads=[rHS], writes=[rSQ])
                for i, (t0, tn) in enumerate(tts):
                    kb.op("pe", (lambda i=i, t0=t0, tn=tn: lambda e: e.matmul(bank(i, tn), lhsT=ONF, rhs=SQ[:, t0:t0 + tn],
                                                                              start=True, stop=True))(),
                          reads=[rSQ, rCONST], writes=[BK[i]])
                    kb.op("dve", (lambda i=i, t0=t0, tn=tn: lambda e: e.tensor_scalar(RS[:, t0:t0 + tn], bank(i, tn), 1.0 / 128, EPS,
                                                                                    ALU.mult, ALU.add))(),
                          reads=[BK[i]], writes=[rRS])
                kb.op("act", lambda e: e.sqrt(RS[:, 0:NTL], RS[:, 0:NTL]), reads=[rRS], writes=[rRS])
                kb.op("dve", lambda e: e.reciprocal(RS[:, 0:NTL], RS[:, 0:NTL]), reads=[rRS], writes=[rRS])
                kb.op("dve", (lambda h=h: lambda e: e.tensor_tensor(SQ[:, 0:NTL], HS3[:, h, 0:NTL], RS[:, 0:NTL], ALU.mult))(),
                      reads=[rHS, rRS, rSQ], writes=[rSQ])
                kb.op("act", (lambda b=b: lambda e: e.activation(out=OTb[b][:, 0:NTL], in_=OTb[b][:, 0:NTL], func=AF.Sigmoid))(),
                      reads=[rOTb[b]], writes=[rOTb[b]])
                kb.op("act", (lambda b=b: lambda e: e.activation(out=ZTb[b][:, 0:NTL], in_=ZTb[b][:, 0:NTL], func=AF.Silu))(),
                      reads=[rZTb[b]], writes=[rZTb[b]])
                kb.op("pool", (lambda b=b: lambda e: e.tensor_tensor(SQ[:, 0:NTL], SQ[:, 0:NTL], OTb[b][:, 0:NTL], ALU.mult))(),
                      reads=[rOTb[b], rSQ], writes=[rSQ])
                stg, rstg = mix_stage()
                kb.op("dve", (lambda b=b, h=h, stg=stg: lambda e: e.scalar_tensor_tensor(
                    out=stg[:, 0:NTL], in0=SQ[:, 0:NTL], scalar=PT3[:, 24 + h:25 + h], in1=ZTb[b][:, 0:NTL],
                    op0=ALU.mult, op1=ALU.mult))(), reads=[rSQ, rZTb[b], rPT], writes=[rstg])
                mix_store(24 + h, stg, rstg)
            kb.barrier()
            A.release(pers2_mark)
            LNG = A.f32(1024)
            LNB = A.f32(1024)
            SBI = A.f32(1024)
            rAP = Res("ap")
            kb.dma("sp", LNG, sglg_in[l:l + 1, :].partition_broadcast(128), writes=[rAP])
            kb.dma("sp", LNB, sglb_in[l:l + 1, :].partition_broadcast(128), writes=[rAP])
            kb.dma("sp", SBI, sgub_in[l:l + 1, :].partition_broadcast(128), writes=[rAP])
            WSF = [A.f32(128) for _ in range(2)]
            rWSF = [Res("wsf0"), Res("wsf1")]
            WST = A.bf16(1024)
            rWST = Res("wst")
            for h in range(8):
                b = h % 2
                kb.dma("sp", WSF[b], sguw_in[l, h], writes=[rWSF[b]])
                kb.op("pe", (lambda b=b: lambda e: e.matmul(bank(b, 128), lhsT=WSF[b], rhs=IDF, start=True, stop=True))(),
                      reads=[rWSF[b], rCONST], writes=[BK[b]])
                kb.op("dve", (lambda b=b, h=h: lambda e: e.tensor_copy(WST[:, h * 128:(h + 1) * 128], bank(b, 128)))(),
                      reads=[BK[b]], writes=[rWST])
            VN = A.bf16(TL * 1024)
            VN3 = v3(VN, TL)
            rVN = Res("vn")
            VT = [A.f32(1024) for _ in range(2)]
            rVT = [Res("vt0"), Res("vt1")]
            AJ = A.f32(1024)
            rAJ = Res("aj")
            ST = A.f32(16)
            rST = Res("st")
            for t in range(TL):
                b = t % 2
                kb.dma("sp", VT[b], Ptok.ap()[t * 128:(t + 1) * 128, 0:1024], writes=[rVT[b]])
                kb.op("act", (lambda b=b: lambda e: e.activation(out=AJ, in_=VT[b], func=AF.Copy, accum_out=ST[:, 0:1]))(),
                      reads=[rVT[b]], writes=[rAJ, rST])
                kb.op("act", (lambda b=b: lambda e: e.activation(out=AJ, in_=VT[b], func=AF.Square, accum_out=ST[:, 1:2]))(),
                      reads=[rVT[b]], writes=[rAJ, rST])
                kb.op("dve", lambda e: e.tensor_scalar_mul(ST[:, 2:4], ST[:, 0:2], 1.0 / 1024), reads=[rST], writes=[rST])
                kb.op("dve", lambda e: e.tensor_tensor(ST[:, 4:5], ST[:, 2:3], ST[:, 2:3], ALU.mult), reads=[rST], writes=[rST])
                kb.op("dve", lambda e: e.tensor_tensor(ST[:, 5:6], ST[:, 3:4], ST[:, 4:5], ALU.subtract), reads=[rST], writes=[rST])
                rsqrt_ops(ST[:, 7:8], ST[:, 5:6], 1.0, ST[:, 6:7], [rST], [rST])
                kb.op("dve", (lambda b=b: lambda e: e.tensor_scalar(VT[b], VT[b], ST[:, 2:3], ST[:, 7:8], ALU.subtract, ALU.mult))(),
                      reads=[rST, rVT[b]], writes=[rVT[b]])
                kb.op("pool", (lambda b=b: lambda e: e.tensor_tensor(VT[b], VT[b], LNG, ALU.mult))(), reads=[rAP, rVT[b]], writes=[rVT[b]])
                kb.op("dve", (lambda b=b, t=t: lambda e: e.tensor_tensor(VN3[:, t, :], VT[b], LNB, ALU.add))(),
                      reads=[rAP, rVT[b]], writes=[rVN])
            UT = [A.f32(NT) for _ in range(2)]
            ZT2 = [A.f32(NT) for _ in range(2)]
            rUT = [Res("ut0"), Res("ut1")]
            rZT2 = [Res("zt20"), Res("zt21")]
            TMA = A.f32(NT)
            rTMA = Res("tma")
            for h in range(8):
                b = h % 2
                kb.dma("sp", UT[b][:, 0:NTL], Pch.ap()[CR["Au"] + h * 128:CR["Au"] + (h + 1) * 128, 0:NTL], writes=[rUT[b]])
                kb.dma("sp", ZT2[b][:, 0:NTL], Pch.ap()[CR["Az"] + h * 128:CR["Az"] + (h + 1) * 128, 0:NTL], writes=[rZT2[b]])
                kb.op("act", (lambda b=b: lambda e: e.activation(out=ZT2[b][:, 0:NTL], in_=ZT2[b][:, 0:NTL], func=AF.Silu))(),
                      reads=[rZT2[b]], writes=[rZT2[b]])
                kb.op("pool", (lambda b=b: lambda e: e.tensor_tensor(UT[b][:, 0:NTL], UT[b][:, 0:NTL], ZT2[b][:, 0:NTL], ALU.mult))(),
                      reads=[rZT2[b], rUT[b]], writes=[rUT[b]])
                for t in range(TL):
                    kb.op("pe", (lambda h=h, t=t: lambda e: e.matmul(
                        PS[:, t * 128:(t + 1) * 128], lhsT=VN3[:, t, h * 128:(h + 1) * 128], rhs=WST[:, h * 128:(h + 1) * 128],
                        start=True, stop=True))(), reads=[rVN, rWST], writes=[BK[t // 4]])
                kb.op("dve", (lambda h=h: lambda e: e.tensor_tensor(
                    v3(TMA[:, 0:NTL], TL), v3(PS[:, 0:NTL], TL), bc(SBI[:, h * 128:(h + 1) * 128].unsqueeze(1), [128, TL, 128]),
                    ALU.add))(), reads=[BK[0], BK[1], BK[2], rAP], writes=[rTMA])
                stg, rstg = mix_stage()
                kb.op("dve", (lambda b=b, stg=stg: lambda e: e.tensor_tensor(stg[:, 0:NTL], TMA[:, 0:NTL], UT[b][:, 0:NTL], ALU.mult))(),
                      reads=[rTMA, rUT[b]], writes=[rstg])
                mix_store(h, stg, rstg)
            kb.barrier()
            A.release(pers2_mark)

            HG = A.f32(4 * 240)
            rHG = Res("hg")
            for r in range(4):
                kb.dma("sp", HG[:, r * 240:(r + 1) * 240], gar(5, r, 0, 128 * 240).rearrange("(c x) -> c x", c=128), writes=[rHG])
            HG4 = HG.rearrange("p (r g j) -> p r g j", r=4, g=8)
            LH = A.f32(8 * 15)
            RH = A.f32(8 * 15)
            rLR = Res("lr")
            for r in range(4):
                for (dst, so, j0) in ((LH, 4, 15), (RH, 8, 0)):
                    src = HG4[:, r, :, j0:j0 + 15]
                    if r == 0:
                        kb.op("dve", (lambda dst=dst, src=src, so=so, r=r: lambda e: e.tensor_scalar_mul(
                            v3(dst, 8), src, SEL[:, so + r:so + r + 1]))(), reads=[rHG, rCONST], writes=[rLR])
                    else:
                        kb.op("dve", (lambda dst=dst, src=src, so=so, r=r: lambda e: e.scalar_tensor_tensor(
                            out=v3(dst, 8), in0=src, scalar=SEL[:, so + r:so + r + 1], in1=v3(dst, 8), op0=ALU.mult, op1=ALU.add))(),
                            reads=[rHG, rCONST, rLR], writes=[rLR])
            CONV = A.f32(8 * NT)
            CONV3 = v3(CONV, 8)
            rCONV = [Res("conv%d" % i) for i in range(8)]
            YP = [A.bf16(1054 + 286) for _ in range(2)]
            rYP = [Res("yp0"), Res("yp1")]
            DIAG = [A.bf16(31 * 128) for _ in range(2)]
            rDIAG = [Res("diag0"), Res("diag1")]
            CWr = A.f32(248)
            rCWr = Res("cwr")
            kb.op("dve", lambda e: e.tensor_copy(v3(CWr, 8), v3(CW, 31).rearrange("p k g -> p g k")), reads=[rPT], writes=[rCWr])
            SQB = A.f32(NT)
            rSQB = Res("sqb")
            for b in range(2):
                kb.op("pool", (lambda b=b: lambda e: e.memset(YP[b], 0.0))(), writes=[rYP[b]])
            cvb = 0
            for cg in range(8):
                b = cg % 2
                kb.dma("pool", YP[b][:, 15:1039], Ych.ap()[cg * 128:(cg + 1) * 128, 0:1024], writes=[rYP[b]])
                if not last:
                    kb.dma("pool", YP[b][:, 1054 + 15:1054 + 271], Ych.ap()[cg * 128:(cg + 1) * 128, 1024:1280], writes=[rYP[b]])
                kb.op("dve", (lambda b=b, cg=cg: lambda e: e.tensor_copy(YP[b][:, 0:15], v3(LH, 8)[:, cg, :]))(), reads=[rLR], writes=[rYP[b]])
                kb.op("dve", (lambda b=b, cg=cg: lambda e: e.tensor_copy(YP[b][:, 1039:1054], v3(RH, 8)[:, cg, :]))(), reads=[rLR], writes=[rYP[b]])
                kb.op("dve", (lambda b=b, cg=cg: lambda e: e.tensor_tensor(
                    v3(DIAG[b], 31), bc(IDF.unsqueeze(1), [128, 31, 128]), bc(v3(CWr, 8)[:, cg, :].unsqueeze(2), [128, 31, 128]),
                    ALU.mult))(), reads=[rCWr, rCONST], writes=[rDIAG[b]])
                for (ys, co, n) in [(0, 0, 512), (512, 512, 512)] + ([] if last else [(1054, 1024, 256)]):
                    bk = 6 + (cvb % 2)
                    cvb += 1
                    for k in range(31):
                        kb.op("pe", (lambda b=b, bk=bk, k=k, ys=ys, n=n: lambda e: e.matmul(
                            bank(bk, n), lhsT=v3(DIAG[b], 31)[:, k, :], rhs=YP[b][:, ys + k:ys + k + n],
                            start=(k == 0), stop=(k == 30)))(), reads=[rDIAG[b], rYP[b]], writes=[BK[bk]])
                    kb.op("act", (lambda bk=bk, cg=cg, co=co, n=n: lambda e: e.activation(
                        out=CONV3[:, cg, co:co + n], in_=bank(bk, n), func=AF.Identity, bias=PT3[:, cg:cg + 1]))(),
                        reads=[BK[bk], rPT], writes=[rCONV[cg]])
                kb.op("act", (lambda cg=cg: lambda e: e.activation(out=SQB[:, 0:NTL], in_=CONV3[:, cg, 0:NTL], func=AF.Square))(),
                      reads=[rCONV[cg]], writes=[rSQB])
                for i, (t0, tn) in enumerate(tts):
                    kb.op("pe", (lambda cg=cg, i=i, t0=t0, tn=tn: lambda e: e.matmul(
                        bank(i, tn), lhsT=ONF, rhs=CONV3[:, cg, t0:t0 + tn], start=(cg == 0), stop=(cg == 7)))(),
                        reads=[rCONV[cg], rCONST], writes=[BK[i]])
                    kb.op("pe", (lambda cg=cg, i=i, t0=t0, tn=tn: lambda e: e.matmul(
                        bank(3 + i, tn), lhsT=ONF, rhs=SQB[:, t0:t0 + tn], start=(cg == 0), stop=(cg == 7)))(),
                        reads=[rSQB, rCONST], writes=[BK[3 + i]])
            dump("d_CONV", CONV, [128, 8 * NT], reads=rCONV)
            MEAN = A.f32(NT)
            RSTD = A.f32(NT)
            MSQ = A.f32(NT)
            rMS = Res("ms")
            for i, (t0, tn) in enumerate(tts):
                kb.op("dve", (lambda i=i, t0=t0, tn=tn: lambda e: e.tensor_scalar_mul(MEAN[:, t0:t0 + tn], bank(i, tn), 1.0 / 1024))(),
                      reads=[BK[i]], writes=[rMS])
                kb.op("dve", (lambda i=i, t0=t0, tn=tn: lambda e: e.tensor_scalar_mul(RSTD[:, t0:t0 + tn], bank(3 + i, tn), 1.0 / 1024))(),
                      reads=[BK[3 + i]], writes=[rMS])
            kb.op("dve", lambda e: e.tensor_tensor(MSQ[:, 0:NTL], MEAN[:, 0:NTL], MEAN[:, 0:NTL], ALU.mult), reads=[rMS], writes=[rMS])
            kb.op("dve", lambda e: e.tensor_tensor(RSTD[:, 0:NTL], RSTD[:, 0:NTL], MSQ[:, 0:NTL], ALU.subtract), reads=[rMS], writes=[rMS])
            rsqrt_ops(RSTD[:, 0:NTL], RSTD[:, 0:NTL], 1.0, MSQ[:, 0:NTL], [rMS], [rMS])
            dump("d_MEAN", MEAN, [128, NT], reads=[rMS])
            dump("d_RSTD", RSTD, [128, NT], reads=[rMS])
            ZB = [A.f32(NT) for _ in range(2)]
            rZB = [Res("zb0"), Res("zb1")]
            for cg in range(8):
                b = cg % 2
                kb.dma("sp", ZB[b][:, 0:NTL], Pch.ap()[CR["Bz"] + cg * 128:CR["Bz"] + (cg + 1) * 128, 0:NTL], writes=[rZB[b]])
                cv = CONV3[:, cg, 0:NTL]
                kb.op("dve", (lambda cv=cv: lambda e: e.tensor_tensor(cv, cv, MEAN[:, 0:NTL], ALU.subtract))(), reads=[rMS, rCONV[cg]], writes=[rCONV[cg]])
                kb.op("pool", (lambda cv=cv: lambda e: e.tensor_tensor(cv, cv, RSTD[:, 0:NTL], ALU.mult))(), reads=[rMS, rCONV[cg]], writes=[rCONV[cg]])
                kb.op("act", (lambda cv=cv, cg=cg: lambda e: e.activation(out=cv, in_=cv, func=AF.Silu, scale=PT3[:, 8 + cg:9 + cg],
                                                                          bias=PT3[:, 16 + cg:17 + cg]))(), reads=[rPT, rCONV[cg]], writes=[rCONV[cg]])
                kb.op("act", (lambda b=b: lambda e: e.activation(out=ZB[b][:, 0:NTL], in_=ZB[b][:, 0:NTL], func=AF.Silu))(),
                      reads=[rZB[b]], writes=[rZB[b]])
                stg, rstg = mix_stage()
                kb.op("dve", (lambda cv=cv, b=b, stg=stg: lambda e: e.tensor_tensor(stg[:, 0:NTL], cv, ZB[b][:, 0:NTL], ALU.mult))(),
                      reads=[rCONV[cg], rZB[b]], writes=[rstg])
                mix_store(8 + cg, stg, rstg)
            kb.barrier()
            A.release(pers2_mark)

            W2PRE = A.bf16(32 * 512)
            rWT2 = [Res("w2t0"), Res("w2t1")]
            wsrc2 = wout_in[l].rearrange("(kc kp) n -> kp kc n", kp=128)
            for g_ in range(4):
                kb.dma("pool", v3(W2PRE, 32)[:, g_ * 8:(g_ + 1) * 8, :], wsrc2[:, g_ * 8:(g_ + 1) * 8, 0:512], writes=[rWT2[0]])
            w2_mark = A.mark()
            QT = A.bf16(8 * NT)
            QT3 = v3(QT, 8)
            rQT = Res("qt")
            QF = [A.f32(1024) for _ in range(2)]
            rQF = [Res("qf0"), Res("qf1")]
            QWS = []
            for i_ in range(2):
                QWS.append(dict(QN=A.f32(1024), QR=A.f32(1024), QTS=A.f32(1024), QT1=A.f32(512), QT2=A.f32(512),
                                QRB=A.bf16(1024), SSQ=A.f32(24), r=Res("qw%d" % i_)))
            for t in range(TL):
                b = t % 2
                W_ = QWS[b]
                pb0 = 4 * b
                kb.dma("sp", QF[b], Ptok.ap()[t * 128:(t + 1) * 128, 1024:2048], writes=[rQF[b]])
                qk_norm_rope(QF[b], 8, GQK[:, 0:128], t, W_["QN"], W_["QR"], W_["QTS"], W_["SSQ"], W_["QT1"], W_["QT2"], rQF[b], W_["r"])
                kb.op("act", (lambda W_=W_: lambda e: e.copy(W_["QRB"], W_["QR"]))(), reads=[W_["r"]], writes=[W_["r"]])
                for h in range(8):
                    kb.op("pe", (lambda h=h, W_=W_, pb0=pb0: lambda e: e.matmul(
                        PS[:, pb0 * 512 + h * 128:pb0 * 512 + (h + 1) * 128], lhsT=W_["QRB"][:, h * 128:(h + 1) * 128],
                        rhs=IDB, start=True, stop=True))(), reads=[W_["r"], rCONST], writes=[BK[pb0 + h // 4]])
                kb.op("act", (lambda t=t, pb0=pb0: lambda e: e.copy(QT3[:, :, t * 128:(t + 1) * 128],
                                                                  v3(PS[:, pb0 * 512:pb0 * 512 + 1024], 8)))(),
                      reads=[BK[pb0], BK[pb0 + 1]], writes=[rQT])
            KTA = A.bf16(2 * 4352)
            KTA3 = v3(KTA, 2)
            VAL = A.bf16(34 * 256)
            VAL3 = v3(VAL, 34)
            rKVA = Res("kva")
            for r in range(4):
                kb.dma("pool", KTA3[:, :, r * 1024:(r + 1) * 1024],
                       gar(0, r, 0, 128 * 2048).rearrange("(d g t) -> d g t", d=128, g=2), writes=[rKVA])
                kb.dma("pool", VAL3[:, r * 8:(r + 1) * 8, :],
                       gar(1, r, 0, 1024 * 256).rearrange("(kt p c) -> p kt c", p=128, c=256), writes=[rKVA])
            kb.op("dve", lambda e: e.tensor_copy(KTA3[:, :, 4096:4352], KTC3), reads=[rKTC], writes=[rKVA])
            kb.dma("pool", VAL3[:, 32:34, :], Ptok.ap()[1024:1280, 2304:2560].rearrange("(kt p) c -> p kt c", p=128), writes=[rKVA])
            dump("d_KTA", KTA, [128, 2 * 4352], BF16, reads=[rKVA])
            dump("d_VAL", VAL, [128, 34 * 256], BF16, reads=[rKVA])
            dump("d_QT", QT, [128, 8 * NT], BF16, reads=[rQT])
            SZ = [A.f32(NT) for _ in range(2)]
            rSZ = [Res("sz0"), Res("sz1")]
            PTA = [A.bf16(512) for _ in range(4)]
            rPTA = [Res("pta%d" % i) for i in range(4)]
            RL = A.f32(512)
            rRL = Res("rl")
            OA = A.f32(512)
            rOA = Res("oa")
            SC = 128.0 ** -0.5
            pcount = 0
            for h in range(8):
                g = h // 4
                b = h % 2
                kb.dma("sp", SZ[b][:, 0:NTL], Pch.ap()[CR["Cz"] + h * 128:CR["Cz"] + (h + 1) * 128, 0:NTL], writes=[rSZ[b]])
                kb.op("act", (lambda b=b: lambda e: e.activation(out=SZ[b][:, 0:NTL], in_=SZ[b][:, 0:NTL], func=AF.Silu))(),
                      reads=[rSZ[b]], writes=[rSZ[b]])
                stg, rstg = mix_stage()
                qtiles = [(0, 512, list(range(34))), (512, 512, list(range(34)))] + ([] if last else [(1024, 256, [32, 33])])
                for (q0, qn, kts) in qtiles:
                    nk = len(kts)

                    def emit_s(ki, q0=q0, qn=qn, kts=kts, g=g, h=h):
                        sb = 2 + (ki % 3)
                        kt = kts[ki]
                        kb.op("pe", (lambda: lambda e: e.matmul(
                            bank(sb, qn), lhsT=KTA3[:, g, kt * 128:(kt + 1) * 128], rhs=QT3[:, h, q0:q0 + qn],
                            start=True, stop=True))(), reads=[rKVA, rQT], writes=[BK[sb]])
                    emit_s(0)
                    if nk > 1:
                        emit_s(1)
                    for ki, kt in enumerate(kts):
                        sb = 2 + (ki % 3)
                        pb = pcount % 4
                        pcount += 1
                        if ki + 2 < nk:
                            emit_s(ki + 2)
                        kb.op("act", (lambda sb=sb, pb=pb, qn=qn: lambda e: e.activation(
                            out=PTA[pb][:, 0:qn], in_=bank(sb, qn), func=AF.Exp, scale=SC))(), reads=[BK[sb]], writes=[rPTA[pb]])
                        kb.op("pe", (lambda pb=pb, kt=kt, g=g, qn=qn, ki=ki, nk=nk: lambda e: e.matmul(
                            bank(0, qn), lhsT=VAL3[:, kt, g * 128:(g + 1) * 128], rhs=PTA[pb][:, 0:qn],
                            start=(ki == 0), stop=(ki == nk - 1)))(), reads=[rKVA, rPTA[pb]], writes=[BK[0]])
                        kb.op("pe", (lambda pb=pb, qn=qn, ki=ki, nk=nk: lambda e: e.matmul(
                            bank(1, qn), lhsT=ONB, rhs=PTA[pb][:, 0:qn], start=(ki == 0), stop=(ki == nk - 1)))(),
                            reads=[rCONST, rPTA[pb]], writes=[BK[1]])
                    kb.op("dve", (lambda qn=qn: lambda e: e.reciprocal(RL[:, 0:qn], bank(1, qn)))(), reads=[BK[1]], writes=[rRL])
                    kb.op("dve", (lambda qn=qn: lambda e: e.tensor_tensor(OA[:, 0:qn], bank(0, qn), RL[:, 0:qn], ALU.mult))(),
                          reads=[BK[0], rRL], writes=[rOA])
                    kb.op("pool", (lambda qn=qn, q0=q0, b=b, stg=stg: lambda e: e.tensor_tensor(
                        stg[:, q0:q0 + qn], OA[:, 0:qn], SZ[b][:, q0:q0 + qn], ALU.mult))(), reads=[rOA, rSZ[b]], writes=[rstg])
                mix_store(16 + h, stg, rstg)
            kb.barrier()

            A.release(w2_mark)
            BIG2 = A.f32(16 * NT)
            MX = v3(BIG2.bitcast(BF16), 32)
            rMX = Res("mx")
            for kc in range(32):
                kb.dma("sp", MX[:, kc, 0:NTL], MIXd.ap()[kc * 128:(kc + 1) * 128, 0:NTL], writes=[rMX])
            GLn = [A.f32(512) for _ in range(2)]
            GCn = [A.f32(512) for _ in range(2)]
            GBn = [A.f32(512) for _ in range(2)]
            rGn = [Res("gn0"), Res("gn1")]
            WT2 = [W2PRE, A.bf16(32 * 512)]
            XO = [A.f32(512) for _ in range(3)]
            rXO = [Res("xo%d" % i) for i in range(3)]
            OO = [A.f32(512) for _ in range(3)]
            rOO = [Res("oo%d" % i) for i in range(3)]
            def load_w2(n, b):
                w3 = v3(WT2[b], 32)
                for g in range(4):
                    kb.dma("pool", w3[:, g * 8:(g + 1) * 8, :], wsrc2[:, g * 8:(g + 1) * 8, n * 512:(n + 1) * 512], writes=[rWT2[b]])
            oc = 0
            for n in range(8):
                b = n % 2
                if n + 1 < 8:
                    load_w2(n + 1, 1 - b)
                w3 = v3(WT2[b], 32)
                cs_ = slice(2 * D + n * 512, 2 * D + (n + 1) * 512)
                kb.dma("sp", GLn[b], af[2 * l:2 * l + 1, cs_].partition_broadcast(128), writes=[rGn[b]])
                kb.dma("sp", GBn[b], bada_in[l:l + 1, cs_].partition_broadcast(128), writes=[rGn[b]])
                kb.op("dve", (lambda b=b: lambda e: e.tensor_tensor(GLn[b], GLn[b], GBn[b], ALU.add))(), reads=[rGn[b]], writes=[rGn[b]])
                if not last:
                    kb.dma("sp", GCn[b], af[2 * l + 1:2 * l + 2, cs_].partition_broadcast(128), writes=[rGn[b]])
                    kb.op("dve", (lambda b=b: lambda e: e.tensor_tensor(GCn[b], GCn[b], GBn[b], ALU.add))(), reads=[rGn[b]], writes=[rGn[b]])
                for t in range(TL):
                    bk = t % 2
                    ob = oc % 3
                    oc += 1
                    kb.dma("sp", XO[ob], tok_src(t, n * 512, (n + 1) * 512), writes=[rXO[ob]])
                    for kc in range(32):
                        kb.op("pe", (lambda bk=bk, kc=kc, t=t, w3=w3: lambda e: e.matmul(
                            bank(bk), lhsT=MX[:, kc, t * 128:(t + 1) * 128], rhs=w3[:, kc, :], start=(kc == 0), stop=(kc == 31)))(),
                            reads=[rMX, rWT2[b]], writes=[BK[bk]])
                    Gt = GLn[b] if t < 8 else GCn[b]
                    kb.op("dve", (lambda bk=bk, ob=ob, Gt=Gt: lambda e: e.tensor_tensor(
                        OO[ob], bank(bk), Gt, ALU.mult))(), reads=[BK[bk], rGn[b]], writes=[rOO[ob]])
                    kb.op("pool", (lambda ob=ob: lambda e: e.tensor_tensor(OO[ob], OO[ob], XO[ob], ALU.add))(),
                          reads=[rXO[ob], rOO[ob]], writes=[rOO[ob]])
                    if last:
                        dst = y_out[t * 128:(t + 1) * 128, n * 512:(n + 1) * 512]
                    else:
                        dst = xs.ap()[t * 128:(t + 1) * 128, n * 512:(n + 1) * 512]
                    kb.dma("sp", dst, OO[ob], reads=[rOO[ob]])
            kb.barrier()
            A.release(lay_mark)

        for l_ in range(nlayers):
            emit_layer(l_)

        kb.barrier()
        block = es.enter_context(nc.Block())
        kb.replay(block)
    return nc


def LAYER_BODY_2(env):
    pass


def _consts():
    c = np.zeros((128, 5 * 128), np.float32)
    c[:, 0:128] = np.eye(128, dtype=np.float32)
    c[:, 128:256] = 1.0
    s = np.arange(128)[:, None]
    l_ = np.arange(128)[None, :]
    c[:, 256:384] = (s <= l_).astype(np.float32)
    c[:, 384:512] = (s >= l_).astype(np.float32)
    return c


def _rope_table(seg):
    n = np.arange(seg * 1024, (seg + 1) * 1024)
    row = (n // 64).astype(np.float32)
    col = (n % 64).astype(np.float32)
    freq = (10000.0 ** (-np.arange(32, dtype=np.float32) / 32)).astype(np.float32)
    ang = np.stack([row, col], -1)[..., None] * freq
    cs = np.concatenate([np.cos(ang).reshape(1024, 64), np.sin(ang).reshape(1024, 64)], -1).astype(np.float32)
    return np.ascontiguousarray(cs.reshape(8, 128, 128).transpose(1, 0, 2))


def make_in_maps(inputs):
    f = lambda a: np.ascontiguousarray(np.asarray(a, dtype=np.float32))
    x = f(inputs["x"]); c = f(inputs["c"]); ctx = f(inputs["ctx"]); c_ctx = f(inputs["c_ctx"])
    w_ada = f(inputs["w_ada"])
    shared = {
        "b_ada": f(inputs["b_ada"]), "norm_g": f(inputs["norm_g"]), "w_in": f(inputs["w_in"]),
        "sgu_w": f(inputs["sgu_w"]), "sgu_b": f(inputs["sgu_b"]).reshape(DEPTH, 1024),
        "sgu_ln_g": f(inputs["sgu_ln_g"]), "sgu_ln_b": f(inputs["sgu_ln_b"]),
        "conv_w": f(inputs["conv_w"]).reshape(DEPTH, 248, 128),
        "conv_b": f(inputs["conv_b"]).reshape(DEPTH, 8, 128),
        "conv_ln_g": f(inputs["conv_ln_g"]).reshape(DEPTH, 8, 128),
        "conv_ln_b": f(inputs["conv_ln_b"]).reshape(DEPTH, 8, 128),
        "q_norm_g": f(inputs["q_norm_g"]), "k_norm_g": f(inputs["k_norm_g"]),
        "mlstm_i_bias": f(inputs["mlstm_i_bias"]).reshape(DEPTH, 16),
        "mlstm_f_bias": f(inputs["mlstm_f_bias"]).reshape(DEPTH, 16),
        "mh_norm_g": f(inputs["mh_norm_g"]).reshape(DEPTH, 8, 128),
        "w_out": f(inputs["w_out"]), "consts": _consts(),
    }
    maps = []
    for core in range(8):
        b, s = core // 4, core % 4
        m = dict(shared)
        m["x"] = np.ascontiguousarray(x[b, s * 1024:(s + 1) * 1024])
        m["ctx"] = np.ascontiguousarray(ctx[b])
        cc = np.stack([c[b], c_ctx], 0)
        m["ccT"] = np.ascontiguousarray(cc[:, s * 1024:(s + 1) * 1024].T)
        m["w_ada_s"] = np.ascontiguousarray(w_ada[:, s * 1024:(s + 1) * 1024, :])
        m["rope"] = _rope_table(s)
        sel = np.zeros((128, 12), np.float32)
        sel[:, s] = 1.0
        if s - 1 >= 0:
            sel[:, 4 + s - 1] = 1.0
        if s + 1 <= 3:
            sel[:, 8 + s + 1] = 1.0
        m["sel"] = sel
        maps.append(m)
    return maps


def kernel(**inputs):
    nc = build_program()
    maps = make_in_maps(inputs)
    res = run_bass_kernel_spmd(nc, maps, core_ids=list(range(8)))
    out = np.empty((2, 4096, D), np.float32)
    for core in range(8):
        b, s = core // 4, core % 4
        out[b, s * 1024:(s + 1) * 1024] = res.results[core]["y"]
    return out
```

```python
import math
import numpy as np
import ml_dtypes
from contextlib import ExitStack
import concourse.bass as bass
import concourse.mybir as mybir
from concourse.bass_utils import run_bass_kernel_spmd

F32 = mybir.dt.float32
BF16 = mybir.dt.bfloat16
AF = mybir.ActivationFunctionType
ALU = mybir.AluOpType
AX = mybir.AxisListType

D = 4096
PIN = 13856
DEPTH = 2
NT = 1280
NL = 1024
EPS = 1e-6
LNSC = math.log(128.0 ** -0.5)

TOKB = [("Av", 1024, 1024, 0), ("Cq", 6144, 1024, 1024), ("Ckv", 7168, 512, 2048),
        ("Dk", 9728, 1024, 2560), ("Dv", 10752, 1024, 3584), ("G", 13824, 32, 4608)]
NTOKC = 4640
TC = {n: d for n, _, _, d in TOKB}
CHB = [("Au", 0, 0), ("Az", 2048, 1024), ("Ba", 3072, 2048), ("Bg", 4096, 3072), ("Bz", 5120, 4096),
       ("Cz", 7680, 5120), ("Dq", 8704, 6144), ("DkT", 9728, 7168), ("Do", 11776, 8192), ("Dz", 12800, 9216)]
NCHR = 10240
CR = {n: r for n, _, r in CHB}

EXC_ROWS = [512, 512, 512, 512, 4, 60]
EX_HALO_OFF = 0
EX_DT_OFF = 128 * 240


class Res:
    __slots__ = ("name", "w", "r")

    def __init__(self, name):
        self.name = name
        self.w = None
        self.r = {}


class KB:
    CE = ["pe", "act", "dve", "pool"]

    def __init__(self, nc, es):
        self.nc = nc
        self.engs = ["pe", "act", "dve", "pool", "sp"]
        self.q = {e: [] for e in self.engs}
        self.sem = {e: es.enter_context(nc.semaphore("s_" + e)) for e in self.CE}
        self.cnt = {e: 0 for e in self.CE}
        self.known = {e: {} for e in self.engs}
        self.dsem = {"sp": [es.enter_context(nc.semaphore("d_sp%d" % i)) for i in range(16)],
                     "pool": [es.enter_context(nc.semaphore("d_pl%d" % i)) for i in range(8)]}
        self.dcnt = {"sp": 0, "pool": 0}
        self.duse = {k: [0] * len(v) for k, v in self.dsem.items()}
        self.dlast = {k: [None] * len(v) for k, v in self.dsem.items()}
        self.ccsem = es.enter_context(nc.semaphore("s_cc"))
        self.cccnt = 0
        self.cclast = None

    def _wait(self, eng, ev):
        if ev is None:
            return
        key, sem, val = ev
        if self.known[eng].get(key, 0) >= val:
            return
        self.known[eng][key] = val
        self.q[eng].append(("w", sem, val))

    def _deps(self, eng, reads, writes):
        deps = {}

        def add(ev):
            if ev is None:
                return
            k = ev[0]
            if k not in deps or deps[k][2] < ev[2]:
                deps[k] = ev
        for r in reads:
            add(r.w)
        for w in writes:
            add(w.w)
            for ev in w.r.values():
                add(ev)
        for k, ev in deps.items():
            if eng == "pe" and k == "pe":
                continue
            self._wait(eng, ev)

    def _mark(self, ev, reads, writes):
        for r in reads:
            r.r[ev[0]] = ev
        for w in writes:
            w.w = ev
            w.r = {}

    def op(self, eng, fn, reads=(), writes=()):
        self._deps(eng, reads, writes)
        self.cnt[eng] += 1
        ev = (eng, self.sem[eng], self.cnt[eng])
        self.q[eng].append(("o", fn, self.sem[eng]))
        self._mark(ev, reads, writes)
        return ev

    def dma(self, qn, out, in_, reads=(), writes=()):
        self._deps(qn, reads, writes)
        n = len(self.dsem[qn])
        slot = self.dcnt[qn] % n
        self.dcnt[qn] += 1
        self._wait(qn, self.dlast[qn][slot])
        self.duse[qn][slot] += 1
        sem = self.dsem[qn][slot]
        ev = ("%s%d" % (qn, slot), sem, 16 * self.duse[qn][slot])
        self.dlast[qn][slot] = ev
        self.q[qn].append(("d", out, in_, sem))
        self._mark(ev, reads, writes)
        return ev

    def all_events(self, include_cc=True):
        evs = [(e, self.sem[e], self.cnt[e]) for e in self.CE if self.cnt[e] > 0]
        for qn in self.dlast:
            evs += [ev for ev in self.dlast[qn] if ev is not None]
        if include_cc and self.cclast is not None:
            evs.append(self.cclast)
        return evs

    def barrier(self, include_cc=True):
        evs = self.all_events(include_cc)
        for eng in self.engs:
            for ev in evs:
                if eng == "pe" and ev[0] == "pe":
                    continue
                self._wait(eng, ev)

    def collective(self, kind, alu, groups, in_ap, out_ap):
        self.barrier()
        self.cccnt += 1
        self.q["pool"].append(("c", kind, alu, groups, in_ap, out_ap))
        self.cclast = ("cc", self.ccsem, self.cccnt)
        self.barrier()

    def issue_collectives(self, kind, alu, groups, pairs):
        self.barrier()
        for in_ap, out_ap in pairs:
            self.cccnt += 1
            self.q["pool"].append(("c", kind, alu, groups, in_ap, out_ap))
        self.cclast = ("cc", self.ccsem, self.cccnt)

    def replay(self, block):
        def mk(eng):
            def f(e):
                for it in self.q[eng]:
                    k = it[0]
                    if k == "w":
                        e.wait_ge(it[1], it[2])
                    elif k == "o":
                        it[1](e).then_inc(it[2], 1)
                    elif k == "d":
                        e.dma_start(out=it[1], in_=it[2]).then_inc(it[3], 16)
                    elif k == "c":
                        e.collective_compute(it[1], it[2], replica_groups=it[3], ins=[it[4]],
                                             outs=[it[5]]).then_inc(self.ccsem)
            return f
        block.sync(mk("sp"))
        block.gpsimd(mk("pool"))
        block.scalar(mk("act"))
        block.vector(mk("dve"))
        block.tensor(mk("pe"))


class Arena:
    def __init__(self, ap, nfloats):
        self.ap = ap
        self.n = nfloats
        self.top = 0
        self.kb = None

    def mark(self):
        return self.top

    def release(self, m):
        if self.kb is not None:
            self.kb.barrier(include_cc=False)
        self.top = m

    def f32(self, n):
        n4 = (n + 7) // 8 * 8
        assert self.top + n4 <= self.n, ("SBUF arena overflow", self.top, n4, self.n)
        v = self.ap[:, self.top:self.top + n]
        self.top += n4
        return v

    def bf16(self, n):
        nf = (n + 1) // 2
        return self.f32(nf).bitcast(BF16)[:, 0:n]


def build_program(debug=None, stop_after=None, nlayers=DEPTH):
    debug = debug or set()
    nc = bass.Bass("TRN2", target_bir_lowering=False)

    def din(name, shape, dt=F32):
        return nc.dram_tensor(name, list(shape), dt, kind="ExternalInput").ap()

    def dscr(name, shape, dt=F32):
        kind = "ExternalOutput" if name in debug else "Internal"
        return nc.dram_tensor(name, list(shape), dt, kind=kind)

    x_in = din("x", [NL, D])
    ctx_in = din("ctx", [256, D])
    ccT_in = din("ccT", [1024, 2])
    wada_in = din("w_ada_s", [DEPTH, 1024, 3 * D])
    bada_in = din("b_ada", [DEPTH, 3 * D])
    normg_in = din("norm_g", [DEPTH, D])
    win_in = din("w_in", [DEPTH, D, PIN])
    sguw_in = din("sgu_w", [DEPTH, 8, 128, 128])
    sgub_in = din("sgu_b", [DEPTH, 1024])
    sglg_in = din("sgu_ln_g", [DEPTH, 1024])
    sglb_in = din("sgu_ln_b", [DEPTH, 1024])
    convw_in = din("conv_w", [DEPTH, 248, 128])
    convb_in = din("conv_b", [DEPTH, 8, 128])
    cvlg_in = din("conv_ln_g", [DEPTH, 8, 128])
    cvlb_in = din("conv_ln_b", [DEPTH, 8, 128])
    qng_in = din("q_norm_g", [DEPTH, 128])
    kng_in = din("k_norm_g", [DEPTH, 128])
    ib_in = din("mlstm_i_bias", [DEPTH, 16])
    fb_in = din("mlstm_f_bias", [DEPTH, 16])
    mhg_in = din("mh_norm_g", [DEPTH, 8, 128])
    wout_in = din("w_out", [DEPTH, D, D])
    consts_in = din("consts", [128, 5 * 128])
    rope_in = din("rope", [128, 8, 128])
    sel_in = din("sel", [128, 12])
    y_out = nc.dram_tensor("y", [NL, D], F32, kind="ExternalOutput").ap()

    ada_part = nc.dram_tensor("ada_part", [4, 3 * D], F32)
    ada_full = nc.dram_tensor("ada_full", [4, 3 * D], F32)
    dbg_ada = dscr("dbg_ada", [4, 3 * D]) if "dbg_ada" in debug else None
    xs = dscr("xs", [NT, D])
    Ptok = dscr("Ptok", [NT, NTOKC])
    Pch = dscr("Pch", [NCHR, NT])
    Ych = dscr("Ych", [1024, NT])
    exp_bufs = [nc.dram_tensor("exp_buf%d" % c, [EXC_ROWS[c], 512], F32) for c in range(6)]
    gat_bufs = [nc.dram_tensor("gat_buf%d" % c, [4 * EXC_ROWS[c], 512], F32) for c in range(6)]
    dbg_gat = None
    dbg_mixT = dscr("dbg_mixT", [128, 32 * NT], BF16) if "dbg_mixT" in debug else None
    dbg_HT = dscr("dbg_HT", [128, 32 * NT], BF16) if "dbg_HT" in debug else None

    exp_flat = [b_.ap().rearrange("r c -> (r c)") for b_ in exp_bufs]
    gat_flat = [b_.ap().rearrange("r c -> (r c)") for b_ in gat_bufs]

    def exr(c, off, n):
        return exp_flat[c][off:off + n]

    def gar(c, r, off, n):
        o = r * EXC_ROWS[c] * 512 + off
        return gat_flat[c][o:o + n]

    GROUPS = [[0, 1, 2, 3], [4, 5, 6, 7]]

    with ExitStack() as es:
        ARN = 50 * 1024
        arena_t = es.enter_context(nc.sbuf_tensor("arena", [128, ARN], F32))
        A = Arena(arena_t[:, :], ARN)
        psum_t = es.enter_context(nc.psum_tensor("psum", [128, 4096], F32))
        PS = psum_t[:, :]
        kb = KB(nc, es)
        A.kb = kb
        BK = [Res("bank%d" % i) for i in range(8)]

        def bank(i, n=512):
            return PS[:, i * 512:i * 512 + n]

        def v3(ap, a):
            return ap.rearrange("p (a b) -> p a b", a=a)

        CONST = A.f32(4 * 128)
        IDF = CONST[:, 0:128]
        ONF = CONST[:, 128:256]
        TRIF = CONST[:, 256:384]
        TRIB = CONST[:, 384:512]
        IDB = A.bf16(128)
        ONB = A.bf16(128)
        SEL = A.f32(12)
        ROPE = A.f32(8 * 128)
        rCONST = Res("const")
        kb.dma("sp", CONST, consts_in[:, 0:512], writes=[rCONST])
        kb.dma("sp", SEL, sel_in[:, :], writes=[rCONST])
        kb.dma("sp", ROPE, rope_in.rearrange("p a b -> p (a b)"), writes=[rCONST])
        kb.op("dve", lambda e: e.tensor_copy(IDB, IDF), reads=[rCONST], writes=[rCONST])
        kb.op("dve", lambda e: e.tensor_copy(ONB, ONF), reads=[rCONST], writes=[rCONST])
        ROPE3 = v3(ROPE, 8)
        kb.barrier()

        rBIGd = Res("bigd")
        rBIGa = Res("biga")
        MIXd = dscr("MIXd", [32 * 128, NT], BF16)
        MST = [A.bf16(NT) for _ in range(2)]
        rMST = [Res("mst0"), Res("mst1")]
        mst_i = [0]

        def mix_stage():
            i = mst_i[0] % 2
            mst_i[0] += 1
            return MST[i], rMST[i]

        def mix_store(kc, stg, rstg):
            kb.dma("sp", MIXd.ap()[kc * 128:(kc + 1) * 128, 0:NTL_cur[0]], stg[:, 0:NTL_cur[0]], reads=[rstg])
        NTL_cur = [NT]
        dumped = set()

        def dump(name, ap2d, shape, dt=F32, reads=()):
            if name not in debug or name in dumped:
                return
            dumped.add(name)
            dtn = nc.dram_tensor(name, list(shape), dt, kind="ExternalOutput")
            kb.dma("sp", dtn.ap(), ap2d, reads=list(reads))

        def transpose_rows(src, R, dst, bk, rsrc, rdst):
            kb.op("pe", lambda e: e.matmul(bank(bk, R), lhsT=src[0:R, :], rhs=IDF[0:R, 0:R], start=True, stop=True),
                  reads=[rsrc, rCONST], writes=[BK[bk]])
            kb.op("dve", lambda e: e.tensor_copy(dst, bank(bk, R)), reads=[BK[bk]], writes=[rdst])

        m0 = A.mark()
        CCT = A.f32(16)
        rCCT = Res("cct")
        kb.dma("sp", v3(CCT, 8), ccT_in.rearrange("(kc kp) r -> kp kc r", kp=128), writes=[rCCT])
        kb.op("act", lambda e: e.activation(out=CCT, in_=CCT, func=AF.Silu), reads=[rCCT], writes=[rCCT])
        CCT3 = v3(CCT, 8)
        WA = [A.f32(4096) for _ in range(3)]
        rWA = [Res("wa%d" % i) for i in range(3)]
        AST = [A.f32(4096) for _ in range(2)]
        rAST = [Res("ast%d" % i) for i in range(2)]
        it = 0
        for l in range(DEPTH):
            for third in range(3):
                c0 = third * 4096
                for kc in range(8):
                    b = it % 3
                    it += 1
                    kb.dma("sp", WA[b], wada_in[l][kc * 128:(kc + 1) * 128, c0:c0 + 4096], writes=[rWA[b]])
                    for n in range(8):
                        kb.op("pe", (lambda b=b, kc=kc, n=n: lambda e: e.matmul(
                            bank(n)[0:2, :], lhsT=CCT3[:, kc, :], rhs=WA[b][:, n * 512:(n + 1) * 512],
                            start=(kc == 0), stop=(kc == 7)))(), reads=[rCCT, rWA[b]], writes=[BK[n]])
                ab = third % 2
                for n in range(8):
                    if n % 2 == 0:
                        kb.op("act", (lambda ab=ab, n=n: lambda e: e.copy(AST[ab][0:2, n * 512:(n + 1) * 512], bank(n)[0:2, :]))(),
                              reads=[BK[n]], writes=[rAST[ab]])
                    else:
                        kb.op("dve", (lambda ab=ab, n=n: lambda e: e.tensor_copy(AST[ab][0:2, n * 512:(n + 1) * 512], bank(n)[0:2, :]))(),
                              reads=[BK[n]], writes=[rAST[ab]])
                kb.dma("sp", ada_part.ap()[2 * l:2 * l + 2, c0:c0 + 4096], AST[ab][0:2, :], reads=[rAST[ab]])
        kb.collective("AllReduce", ALU.add, GROUPS, ada_part.ap().opt(), ada_full.ap().opt())
        A.release(m0)
        if dbg_ada is not None:
            kb.dma("sp", dbg_ada.ap(), ada_full.ap())
            kb.barrier()
        if stop_after == "ada":
            nlayers = 0

        def emit_layer(l):
            last = (l == DEPTH - 1)
            NTL = NL if last else NT
            TL = NTL // 128
            wl = win_in[l]
            lay_mark = A.mark()

            def tok_src(t, c0, c1, l=l):
                if l == 0:
                    if t < 8:
                        return x_in[t * 128:(t + 1) * 128, c0:c1]
                    return ctx_in[(t - 8) * 128:(t - 7) * 128, c0:c1]
                return xs.ap()[t * 128:(t + 1) * 128, c0:c1]

            R1 = A.f32(128)
            R2 = A.f32(128)
            R3 = A.f32(128)
            R4a = A.f32(128)
            R4b = A.f32(128)
            rR = Res("R")
            af = ada_full.ap()
            srcs1 = [af[2 * l:2 * l + 1, D:2 * D], af[2 * l:2 * l + 1, 0:D],
                     af[2 * l + 1:2 * l + 2, D:2 * D], af[2 * l + 1:2 * l + 2, 0:D]]
            for i, s_ in enumerate(srcs1):
                kb.dma("sp", R1[32 * i:32 * i + 32, :], s_.rearrange("o (a b) -> (o a) b", b=128), writes=[rR])
            srcs2 = [bada_in[l:l + 1, D:2 * D], bada_in[l:l + 1, 0:D], normg_in[l:l + 1, :]]
            for i, s_ in enumerate(srcs2):
                kb.dma("sp", R2[32 * i:32 * i + 32, :], s_.rearrange("o (a b) -> (o a) b", b=128), writes=[rR])
            for i, s_ in enumerate([convb_in[l], cvlg_in[l], cvlb_in[l], mhg_in[l]]):
                kb.dma("sp", R3[8 * i:8 * i + 8, :], s_, writes=[rR])
            kb.dma("sp", R4a[0:124, :], convw_in[l][0:124, :], writes=[rR])
            kb.dma("sp", R4b[0:124, :], convw_in[l][124:248, :], writes=[rR])
            PT1 = A.f32(128)
            PT2 = A.f32(96)
            PT3 = A.f32(32)
            CW = A.f32(248)
            rPT = Res("PT")
            transpose_rows(R1, 128, PT1, 0, rR, rPT)
            transpose_rows(R2, 96, PT2, 1, rR, rPT)
            transpose_rows(R3, 32, PT3, 2, rR, rPT)
            transpose_rows(R4a, 124, CW[:, 0:124], 3, rR, rPT)
            transpose_rows(R4b, 124, CW[:, 124:248], 4, rR, rPT)
            MOD = A.f32(128)
            for j in range(2):
                kb.op("dve", (lambda j=j: lambda e: e.scalar_tensor_tensor(
                    out=MOD[:, 64 * j:64 * j + 32], in0=PT1[:, 64 * j:64 * j + 32], scalar=1.0, in1=PT2[:, 0:32],
                    op0=ALU.add, op1=ALU.add))(), reads=[rPT], writes=[rPT])
                kb.op("dve", (lambda j=j: lambda e: e.tensor_tensor(
                    out=MOD[:, 64 * j:64 * j + 32], in0=MOD[:, 64 * j:64 * j + 32], in1=PT2[:, 64:96],
                    op=ALU.mult))(), reads=[rPT], writes=[rPT])
                kb.op("dve", (lambda j=j: lambda e: e.tensor_tensor(
                    out=MOD[:, 64 * j + 32:64 * j + 64], in0=PT1[:, 64 * j + 32:64 * j + 64], in1=PT2[:, 32:64],
                    op=ALU.add))(), reads=[rPT], writes=[rPT])
            kb.barrier()
            par_mark = A.mark()
            NTL_cur[0] = NTL
            BIG = A.f32(16 * NT)
            BIGB = BIG.bitcast(BF16)
            HT = v3(BIGB, 32)
            WT0 = A.bf16(32 * 512)
            rWT = [Res("wt0"), Res("wt1")]
            tiles = []
            for name, c0, ncol, d0 in TOKB:
                for j in range(0, ncol, 512):
                    tiles.append(("tok", name, c0 + j, min(512, ncol - j), d0 + j))
            for name, c0, r0 in CHB:
                for j in range(0, 1024, 512):
                    tiles.append(("ch", name, c0 + j, 512, r0 + j))
            wsrc = wl.rearrange("(kc kp) n -> kp kc n", kp=128)
            for g_ in range(4):
                kb.dma("pool", v3(WT0, 32)[:, g_ * 8:(g_ + 1) * 8, 0:tiles[0][3]],
                       wsrc[:, g_ * 8:(g_ + 1) * 8, tiles[0][2]:tiles[0][2] + tiles[0][3]], writes=[rWT[0]])
            big_mark = A.mark()

            XT = [A.f32(D) for _ in range(2)]
            rXT = [Res("xt0"), Res("xt1")]
            JUNK = A.bf16(D)
            rJ = Res("junk")
            SS = A.f32(8)
            rSS = Res("ss")
            for t in range(10):
                b = t % 2
                kb.dma("sp", XT[b], tok_src(t, 0, D), writes=[rXT[b]])
                kb.op("act", (lambda b=b: lambda e: e.activation(out=JUNK, in_=XT[b], func=AF.Square,
                                                                   accum_out=SS[:, 0:1]))(),
                      reads=[rXT[b]], writes=[rJ, rSS])
                kb.op("dve", lambda e: e.tensor_scalar(SS[:, 1:2], SS[:, 0:1], 1.0 / D, EPS, ALU.mult, ALU.add),
                      reads=[rSS], writes=[rSS])
                kb.op("act", lambda e: e.sqrt(SS[:, 3:4], SS[:, 1:2]), reads=[rSS], writes=[rSS])
                kb.op("dve", lambda e: e.reciprocal(SS[:, 2:3], SS[:, 3:4]), reads=[rSS], writes=[rSS])
                kb.op("act", (lambda b=b: lambda e: e.activation(out=XT[b], in_=XT[b], func=AF.Copy,
                                                                   scale=SS[:, 2:3]))(),
                      reads=[rXT[b], rSS], writes=[rXT[b]])
                mo = 0 if t < 8 else 64
                for g4 in range(8):
                    bk = g4 % 4
                    for j in range(4):
                        kc = g4 * 4 + j
                        kb.op("pe", (lambda b=b, kc=kc, bk=bk, j=j: lambda e: e.matmul(
                            bank(bk)[:, j * 128:(j + 1) * 128], lhsT=XT[b][:, kc * 128:(kc + 1) * 128], rhs=IDF,
                            start=True, stop=True))(), reads=[rXT[b], rCONST], writes=[BK[bk]])
                    for j in range(4):
                        kc = g4 * 4 + j
                        dst = HT[:, kc, t * 128:(t + 1) * 128]
                        src = bank(bk)[:, j * 128:(j + 1) * 128]
                        if g4 % 2 == 0:
                            kb.op("dve", (lambda dst=dst, src=src, kc=kc, mo=mo: lambda e: e.tensor_scalar(
                                dst, src, MOD[:, mo + kc:mo + kc + 1], MOD[:, mo + 32 + kc:mo + 33 + kc],
                                ALU.mult, ALU.add))(), reads=[BK[bk], rPT], writes=[rBIGd])
                        else:
                            kb.op("act", (lambda dst=dst, src=src, kc=kc, mo=mo: lambda e: e.activation(
                                out=dst, in_=src, func=AF.Identity, scale=MOD[:, mo + kc:mo + kc + 1],
                                bias=MOD[:, mo + 32 + kc:mo + 33 + kc]))(), reads=[BK[bk], rPT], writes=[rBIGa])
            kb.barrier()
            if dbg_HT is not None and l == 0:
                kb.dma("sp", dbg_HT.ap(), BIGB, reads=[rBIGd, rBIGa])
                kb.barrier()
            A.release(big_mark)
            if stop_after == "norm":
                return

            WT = [WT0, A.bf16(32 * 512)]
            STG = [A.f32(NT) for _ in range(2)]
            rSTG = [Res("stg0"), Res("stg1")]
            def load_w(src3, c0, ncol, b):
                w3 = v3(WT[b], 32)
                for g in range(4):
                    kb.dma("pool", w3[:, g * 8:(g + 1) * 8, 0:ncol], src3[:, g * 8:(g + 1) * 8, c0:c0 + ncol],
                           writes=[rWT[b]])

            stg_i = 0
            ev_i = 0
            for ti, (kind, name, c0, ncol, d0) in enumerate(tiles):
                b = ti % 2
                if ti + 1 < len(tiles):
                    load_w(wsrc, tiles[ti + 1][2], tiles[ti + 1][3], 1 - b)
                w3 = v3(WT[b], 32)
                if kind == "tok":
                    nt_ = 8 if (last and name in ("Av", "Cq")) else 10
                    for t in range(nt_):
                        bk = 6 + (t % 2)
                        for kc in range(32):
                            kb.op("pe", (lambda bk=bk, kc=kc, t=t, w3=w3, ncol=ncol: lambda e: e.matmul(
                                bank(bk, ncol), lhsT=HT[:, kc, t * 128:(t + 1) * 128], rhs=w3[:, kc, 0:ncol],
                                start=(kc == 0), stop=(kc == 31)))(), reads=[rBIGd, rBIGa, rWT[b]], writes=[BK[bk]])
                        sb = stg_i % 2
                        stg_i += 1
                        eng = "act" if ev_i % 2 == 0 else "dve"
                        ev_i += 1
                        if eng == "act":
                            kb.op("act", (lambda sb=sb, bk=bk, ncol=ncol: lambda e: e.copy(
                                STG[sb][:, 0:ncol], bank(bk, ncol)))(), reads=[BK[bk]], writes=[rSTG[sb]])
                        else:
                            kb.op("dve", (lambda sb=sb, bk=bk, ncol=ncol: lambda e: e.tensor_copy(
                                STG[sb][:, 0:ncol], bank(bk, ncol)))(), reads=[BK[bk]], writes=[rSTG[sb]])
                        kb.dma("sp", Ptok.ap()[t * 128:(t + 1) * 128, d0:d0 + ncol], STG[sb][:, 0:ncol],
                               reads=[rSTG[sb]])
                else:
                    ntok = NTL
                    tts = [(0, 512), (512, 512)] + ([(1024, 256)] if ntok > 1024 else [])
                    for cb in range(4):
                        bset = 3 * (cb % 2)
                        for kc in range(32):
                            for i, (t0, tn) in enumerate(tts):
                                kb.op("pe", (lambda bset=bset, i=i, kc=kc, cb=cb, t0=t0, tn=tn, w3=w3: lambda e: e.matmul(
                                    bank(bset + i, tn), lhsT=w3[:, kc, cb * 128:(cb + 1) * 128],
                                    rhs=HT[:, kc, t0:t0 + tn], start=(kc == 0), stop=(kc == 31)))(),
                                    reads=[rBIGd, rBIGa, rWT[b]], writes=[BK[bset + i]])
                        sb = stg_i % 2
                        stg_i += 1
                        for i, (t0, tn) in enumerate(tts):
                            eng = "act" if ev_i % 2 == 0 else "dve"
                            ev_i += 1
                            if eng == "act":
                                kb.op("act", (lambda sb=sb, bset=bset, i=i, t0=t0, tn=tn: lambda e: e.copy(
                                    STG[sb][:, t0:t0 + tn], bank(bset + i, tn)))(),
                                    reads=[BK[bset + i]], writes=[rSTG[sb]])
                            else:
                                kb.op("dve", (lambda sb=sb, bset=bset, i=i, t0=t0, tn=tn: lambda e: e.tensor_copy(
                                    STG[sb][:, t0:t0 + tn], bank(bset + i, tn)))(),
                                    reads=[BK[bset + i]], writes=[rSTG[sb]])
                        kb.dma("sp", Pch.ap()[d0 + cb * 128:d0 + (cb + 1) * 128, 0:ntok], STG[sb][:, 0:ntok],
                               reads=[rSTG[sb]])
            kb.barrier()
            A.release(par_mark)
            if stop_after == "gemm1":
                return
            tts = [(0, 512), (512, 512)] + ([(1024, 256)] if NTL > 1024 else [])
            KTC = A.bf16(2 * 256)
            KTC3 = v3(KTC, 2)
            rKTC = Res("ktc")
            IBFB = A.f32(32)
            GQK = A.f32(256)
            rPB = Res("pb")
            kb.dma("sp", IBFB[:, 0:16], ib_in[l:l + 1, :].partition_broadcast(128), writes=[rPB])
            kb.dma("sp", IBFB[:, 16:32], fb_in[l:l + 1, :].partition_broadcast(128), writes=[rPB])
            kb.dma("sp", GQK[:, 0:128], qng_in[l:l + 1, :].partition_broadcast(128), writes=[rPB])
            kb.dma("sp", GQK[:, 128:256], kng_in[l:l + 1, :].partition_broadcast(128), writes=[rPB])
            pers2_mark = A.mark()
            HS = A.f32(8 * NT)
            HS3 = v3(HS, 8)
            rHS = Res("hs")
            GP = A.f32(10 * 5 * 16)
            GP4 = GP.rearrange("p (t k g) -> p t k g", t=10, k=5)
            rGP = Res("gp")
            CS = A.f32(16 * 256)
            CS3 = v3(CS, 16)
            rCS = Res("cs")
            CB = A.bf16(16 * 256)
            CB3 = v3(CB, 16)
            rCB = Res("cb")
            STFB = A.f32(16 * 256)
            rSTFB = Res("stfb")
            pers_mark = A.mark()

            def bc(ap, shape):
                return ap.broadcast_to(list(shape))

            def rsqrt_ops(dst, src, scale, tmp, rr, rw):
                kb.op("dve", lambda e: e.tensor_scalar(tmp, src, scale, EPS, ALU.mult, ALU.add), reads=rr, writes=rw)
                kb.op("act", lambda e: e.sqrt(tmp, tmp), reads=rw, writes=rw)
                kb.op("dve", lambda e: e.reciprocal(dst, tmp), reads=rw, writes=rw)

            def rope_ops(src, dst, H, t, T1, T2, rs, rd, rt):
                s5 = src.rearrange("p (h a c j) -> p h a c j", h=H, a=2, c=2)
                d5 = dst.rearrange("p (h a c j) -> p h a c j", h=H, a=2, c=2)
                x1, x2 = s5[:, :, :, 0, :], s5[:, :, :, 1, :]
                o1, o2 = d5[:, :, :, 0, :], d5[:, :, :, 1, :]
                cs = bc(ROPE3[:, t, 0:64].rearrange("p (a j) -> p a j", a=2).unsqueeze(1), [128, H, 2, 32])
                sn = bc(ROPE3[:, t, 64:128].rearrange("p (a j) -> p a j", a=2).unsqueeze(1), [128, H, 2, 32])
                t1 = T1.rearrange("p (h a j) -> p h a j", h=H, a=2)
                t2 = T2.rearrange("p (h a j) -> p h a j", h=H, a=2)
                kb.op("dve", lambda e: e.tensor_tensor(o1, x1, cs, ALU.mult), reads=[rs, rCONST], writes=[rd])
                kb.op("pool", lambda e: e.tensor_tensor(t1, x2, sn, ALU.mult), reads=[rs, rCONST], writes=[rt])
                kb.op("dve", lambda e: e.tensor_tensor(o1, o1, t1, ALU.subtract), reads=[rt], writes=[rd])
                kb.op("dve", lambda e: e.tensor_tensor(o2, x2, cs, ALU.mult), reads=[rs, rCONST], writes=[rd])
                kb.op("pool", lambda e: e.tensor_tensor(t2, x1, sn, ALU.mult), reads=[rs, rCONST], writes=[rt])
                kb.op("dve", lambda e: e.tensor_tensor(o2, o2, t2, ALU.add), reads=[rt], writes=[rd])

            def qk_norm_rope(src, H, gq, t, NRM, ROT, TS, SSQ, T1, T2, rsrc, rw):
                kb.op("act", lambda e: e.activation(out=TS, in_=src, func=AF.Square), reads=[rsrc], writes=[rw])
                kb.op("dve", lambda e: e.tensor_reduce(out=SSQ[:, 0:H], in_=v3(TS, H), axis=AX.X, op=ALU.add),
                      reads=[rw], writes=[rw])
                rsqrt_ops(SSQ[:, 16:16 + H], SSQ[:, 0:H], 1.0 / 128, SSQ[:, 8:8 + H], [rw], [rw])
                kb.op("dve", lambda e: e.tensor_tensor(v3(NRM, H), v3(src, H), bc(SSQ[:, 16:16 + H].unsqueeze(2), [128, H, 128]),
                                                       ALU.mult), reads=[rsrc, rw], writes=[rw])
                kb.op("dve", lambda e: e.tensor_tensor(v3(NRM, H), v3(NRM, H), bc(gq.unsqueeze(1), [128, H, 128]), ALU.mult),
                      reads=[rw, rPB], writes=[rw])
                if t < 8:
                    rope_ops(NRM, ROT, H, t, T1, T2, rw, rw, rw)
                else:
                    kb.op("dve", lambda e: e.tensor_copy(ROT, NRM), reads=[rw], writes=[rw])

            GG = A.f32(320)
            XF = A.f32(160)
            IG = A.f32(160)
            LF = A.f32(160)
            rGG = Res("gg")
            G5 = GG.rearrange("p (t d i h) -> p t d i h", t=10, d=2, i=2)
            x4 = lambda ap: ap.rearrange("p (t d h) -> p t d h", t=10, d=2)
            kb.dma("sp", v3(GG, 10), Ptok.ap()[:, 4608:4640].rearrange("(t p) c -> p t c", p=128), writes=[rGG])
            kb.op("dve", lambda e: e.tensor_tensor(x4(XF), G5[:, :, :, 1, :], bc(v3(IBFB[:, 16:32], 2).unsqueeze(1), [128, 10, 2, 8]),
                                                   ALU.add), reads=[rGG, rPB], writes=[rGG])
            kb.op("dve", lambda e: e.tensor_tensor(x4(IG), G5[:, :, :, 0, :], bc(v3(IBFB[:, 0:16], 2).unsqueeze(1), [128, 10, 2, 8]),
                                                   ALU.add), reads=[rGG, rPB], writes=[rGG])
            kb.op("act", lambda e: e.activation(out=LF, in_=XF, func=AF.Exp, scale=-1.0), reads=[rGG], writes=[rGG])
            kb.op("dve", lambda e: e.tensor_scalar_add(LF, LF, 1.0), reads=[rGG], writes=[rGG])
            kb.op("act", lambda e: e.activation(out=LF, in_=LF, func=AF.Ln), reads=[rGG], writes=[rGG])
            kb.op("dve", lambda e: e.tensor_scalar_mul(LF, LF, -1.0), reads=[rGG], writes=[rGG])
            LF3 = v3(LF, 10)
            for t in range(10):
                kb.op("pe", (lambda t=t: lambda e: e.matmul(bank(1)[:, t * 32:t * 32 + 8], lhsT=TRIF, rhs=LF3[:, t, 0:8],
                                                            start=True, stop=True))(), reads=[rGG, rCONST], writes=[BK[1]])
                kb.op("pe", (lambda t=t: lambda e: e.matmul(bank(1)[:, t * 32 + 8:t * 32 + 16], lhsT=TRIB, rhs=LF3[:, t, 8:16],
                                                            start=True, stop=True))(), reads=[rGG, rCONST], writes=[BK[1]])
                kb.op("pe", (lambda t=t: lambda e: e.matmul(bank(1)[:, t * 32 + 16:t * 32 + 32], lhsT=ONF, rhs=LF3[:, t, 0:16],
                                                            start=True, stop=True))(), reads=[rGG, rCONST], writes=[BK[1]])
            PS1 = v3(bank(1, 320), 10)
            kb.op("dve", lambda e: e.tensor_copy(GP4[:, :, 0, :], PS1[:, :, 0:16]), reads=[BK[1]], writes=[rGP])
            kb.op("dve", lambda e: e.tensor_copy(GP4[:, :, 3, :], PS1[:, :, 16:32]), reads=[BK[1]], writes=[rGP])
            kb.op("dve", lambda e: e.scalar_tensor_tensor(out=GP4[:, :, 1, :], in0=v3(IG, 10), scalar=LNSC, in1=GP4[:, :, 0, :],
                                                          op0=ALU.add, op1=ALU.subtract), reads=[rGG, rGP], writes=[rGP])
            kb.op("dve", lambda e: e.tensor_tensor(v3(XF, 10), GP4[:, :, 3, :], GP4[:, :, 1, :], ALU.add), reads=[rGP], writes=[rGG])
            kb.op("act", lambda e: e.activation(out=GP4[:, :, 2, :], in_=v3(XF, 10), func=AF.Exp), reads=[rGG], writes=[rGP])
            kb.op("act", lambda e: e.activation(out=GP4[:, :, 4, :], in_=GP4[:, :, 3, :], func=AF.Exp), reads=[rGP], writes=[rGP])
            dump("d_GP", GP, [128, 800], reads=[rGP])
            A.release(pers_mark)

            KTOK = [A.f32(1024) for _ in range(2)]
            VTOK = [A.f32(1024) for _ in range(2)]
            rKTOK = [Res("ktok0"), Res("ktok1")]
            rVTOK = [Res("vtok0"), Res("vtok1")]
            KTLs = [A.bf16(1024) for _ in range(2)]
            rKTLs = [Res("ktl0"), Res("ktl1")]
            VA = [A.bf16(8 * 256) for _ in range(2)]
            rVA = [Res("va0"), Res("va1")]
            QTb = A.bf16(1024)
            KTb = A.bf16(1024)
            rQKb = Res("qkb")
            DG = A.f32(1024)
            rDG = Res("dg")
            EB = A.f32(1024)
            rEB = Res("eb")
            WTm = A.f32(1024)
            rWTm = Res("wtm")
            PTb = A.bf16(1024)
            rPTb = Res("ptb")
            QTL = A.bf16(1024)
            rQTL = Res("qtl")
            DN = A.f32(1024)
            rDN = Res("dn")
            TMPH = A.f32(1024)
            rTMPH = Res("tmph")
            for b in range(2):
                kb.op("pool", (lambda b=b: lambda e: e.memset(v3(VA[b], 8)[:, :, 128:256], 1.0))(), writes=[rVA[b]])
            scan_ctr = [0]

            def scan(chunks, dr, emit):
                g0 = dr * 8
                MASK = TRIF if dr == 0 else TRIB
                for t in chunks:
                    b = scan_ctr[0] % 2
                    scan_ctr[0] += 1
                    va3 = v3(VA[b], 8)
                    KTL = KTLs[b]
                    rKTL = rKTLs[b]
                    stb = 2 if (emit or b == 0) else 4
                    kb.dma("sp", KTOK[b], Ptok.ap()[t * 128:(t + 1) * 128, 2560:3584], writes=[rKTOK[b]])
                    kb.dma("sp", VTOK[b], Ptok.ap()[t * 128:(t + 1) * 128, 3584:4608], writes=[rVTOK[b]])
                    kb.op("dve", (lambda b=b, t=t, KTL=KTL: lambda e: e.tensor_tensor(
                        v3(KTL, 8), v3(KTOK[b], 8), bc(GP4[:, t, 2, g0:g0 + 8].unsqueeze(2), [128, 8, 128]), ALU.mult))(),
                        reads=[rKTOK[b], rGP], writes=[rKTL])
                    kb.op("act", (lambda b=b, va3=va3: lambda e: e.copy(va3[:, :, 0:128], v3(VTOK[b], 8)))(),
                          reads=[rVTOK[b]], writes=[rVA[b]])
                    if emit:
                        kb.dma("pool", v3(QTb, 8), Pch.ap()[CR["Dq"]:CR["Dq"] + 1024, t * 128:(t + 1) * 128].rearrange(
                            "(h d) t -> d h t", d=128), writes=[rQKb])
                        kb.dma("pool", v3(KTb, 8), Pch.ap()[CR["DkT"]:CR["DkT"] + 1024, t * 128:(t + 1) * 128].rearrange(
                            "(h d) t -> d h t", d=128), writes=[rQKb])
                        for h in range(8):
                            kb.op("pe", (lambda h=h: lambda e: e.matmul(
                                PS[:, h * 128:(h + 1) * 128], lhsT=v3(KTb, 8)[:, h, :], rhs=v3(QTb, 8)[:, h, :],
                                start=True, stop=True))(), reads=[rQKb], writes=[BK[h // 4]])
                        kb.op("dve", (lambda t=t: lambda e: e.tensor_tensor(
                            v3(DG, 8), bc(IDF.unsqueeze(1), [128, 8, 128]),
                            bc(GP4[:, t, 0, g0:g0 + 8].unsqueeze(2), [128, 8, 128]), ALU.mult))(),
                            reads=[rGP, rCONST], writes=[rDG])
                        for j in range(2):
                            kb.op("pe", (lambda j=j: lambda e: e.matmul(
                                PS[:, 1024 + j * 512:1536 + j * 512], lhsT=ONF, rhs=DG[:, j * 512:(j + 1) * 512],
                                start=True, stop=True))(), reads=[rDG, rCONST], writes=[BK[2 + j]])
                            kb.op("act", (lambda j=j: lambda e: e.activation(
                                out=EB[:, j * 512:(j + 1) * 512], in_=PS[:, 1024 + j * 512:1536 + j * 512], func=AF.Exp))(),
                                reads=[BK[2 + j]], writes=[rEB])
                        for h in range(8):
                            kb.op("act", (lambda h=h, t=t: lambda e: e.activation(
                                out=WTm[:, h * 128:(h + 1) * 128], in_=PS[:, 1024 + h * 128:1152 + h * 128], func=AF.Exp,
                                bias=GP4[:, t, 1, g0 + h:g0 + h + 1]))(), reads=[BK[2 + h // 4], rGP], writes=[rWTm])
                        kb.op("pool", lambda e: e.tensor_tensor(v3(WTm, 8), v3(WTm, 8), bc(MASK.unsqueeze(1), [128, 8, 128]),
                                                                ALU.mult), reads=[rCONST], writes=[rWTm])
                        kb.op("dve", lambda e: e.tensor_tensor(PTb, PS[:, 0:1024], WTm, ALU.mult),
                              reads=[BK[0], BK[1], rWTm], writes=[rPTb])
                        kb.op("pool", lambda e: e.tensor_tensor(QTL, QTb, EB, ALU.mult), reads=[rQKb, rEB], writes=[rQTL])
                        for h in range(8):
                            kb.op("pe", (lambda h=h, va3=va3: lambda e: e.matmul(
                                PS[:, 2048 + h * 128:2176 + h * 128], lhsT=va3[:, h, 0:128], rhs=v3(PTb, 8)[:, h, :],
                                start=True, stop=False))(), reads=[rVA[b], rPTb], writes=[BK[4 + h // 4]])
                            kb.op("pe", (lambda h=h: lambda e: e.matmul(
                                PS[:, 2048 + h * 128:2176 + h * 128], lhsT=CB3[:, g0 + h, 0:128], rhs=v3(QTL, 8)[:, h, :],
                                start=False, stop=True))(), reads=[rCB, rQTL], writes=[BK[4 + h // 4]])
                        for h in range(8):
                            kb.op("pe", (lambda h=h: lambda e: e.matmul(
                                PS[:, 3072 + h * 128:3200 + h * 128], lhsT=ONB, rhs=v3(PTb, 8)[:, h, :],
                                start=True, stop=False))(), reads=[rCONST, rPTb], writes=[BK[6 + h // 4]])
                            kb.op("pe", (lambda h=h: lambda e: e.matmul(
                                PS[:, 3072 + h * 128:3200 + h * 128], lhsT=CB3[:, g0 + h, 128:256], rhs=v3(QTL, 8)[:, h, :],
                                start=False, stop=True))(), reads=[rCB, rQTL], writes=[BK[6 + h // 4]])
                        kb.op("act", lambda e: e.activation(out=DN, in_=PS[:, 3072:4096], func=AF.Abs),
                              reads=[BK[6], BK[7]], writes=[rDN])
                        kb.op("dve", lambda e: e.tensor_scalar_max(DN, DN, 1.0), reads=[rDN], writes=[rDN])
                        kb.op("dve", lambda e: e.reciprocal(DN, DN), reads=[rDN], writes=[rDN])
                        hs_sl = HS3[:, :, t * 128:(t + 1) * 128]
                        if dr == 0:
                            kb.op("dve", (lambda hs_sl=hs_sl: lambda e: e.tensor_tensor(hs_sl, v3(PS[:, 2048:3072], 8), v3(DN, 8),
                                                                                      ALU.mult))(),
                                  reads=[BK[4], BK[5], rDN], writes=[rHS])
                        else:
                            kb.op("dve", lambda e: e.tensor_tensor(TMPH, PS[:, 2048:3072], DN, ALU.mult),
                                  reads=[BK[4], BK[5], rDN], writes=[rTMPH])
                            kb.op("pool", (lambda hs_sl=hs_sl: lambda e: e.tensor_tensor(hs_sl, hs_sl, v3(TMPH, 8), ALU.add))(),
                                  reads=[rTMPH, rHS], writes=[rHS])
                    kb.op("dve", (lambda t=t: lambda e: e.tensor_tensor(
                        CS3[:, g0:g0 + 8, :], CS3[:, g0:g0 + 8, :], bc(GP4[:, t, 4, g0:g0 + 8].unsqueeze(2), [128, 8, 256]),
                        ALU.mult))(), reads=[rGP, rCS], writes=[rCS])
                    for half in range(2):
                        for hh in range(4):
                            h = half * 4 + hh
                            kb.op("pe", (lambda h=h, hh=hh, va3=va3, KTL=KTL, stb=stb: lambda e: e.matmul(
                                PS[:, stb * 512 + hh * 256:stb * 512 + 256 + hh * 256], lhsT=v3(KTL, 8)[:, h, :], rhs=va3[:, h, :],
                                start=True, stop=True))(), reads=[rKTL, rVA[b]], writes=[BK[stb + hh // 2]])
                        kb.op("dve", (lambda half=half, stb=stb: lambda e: e.tensor_tensor(
                            CS3[:, g0 + half * 4:g0 + half * 4 + 4, :], CS3[:, g0 + half * 4:g0 + half * 4 + 4, :],
                            v3(PS[:, stb * 512:stb * 512 + 1024], 4), ALU.add))(), reads=[BK[stb], BK[stb + 1], rCS], writes=[rCS])
                    if emit:
                        kb.op("act", lambda e: e.copy(CB3[:, g0:g0 + 8, :], CS3[:, g0:g0 + 8, :]), reads=[rCS], writes=[rCB])

            def zero_state():
                kb.op("dve", lambda e: e.memset(CS, 0.0), writes=[rCS])
                kb.op("pool", lambda e: e.memset(CB, 0.0), writes=[rCB])

            zero_state()
            scan([8, 9], 0, not last)
            scan([9, 8], 1, not last)
            kb.op("dve", lambda e: e.tensor_copy(STFB, CS), reads=[rCS], writes=[rSTFB])
            dump("d_STFB", STFB, [128, 4096], reads=[rSTFB])
            dump("d_HSctx", HS, [128, 8 * NT], reads=[rHS])
            zero_state()
            scan(list(range(8)), 0, False)
            scan(list(range(7, -1, -1)), 1, False)
            for dr_ in range(2):
                kb.dma("sp", exr(2 + dr_, 0, 128 * 2048).rearrange("(p x) -> p x", p=128), CS[:, dr_ * 2048:(dr_ + 1) * 2048],
                       reads=[rCS])
            DTT = A.f32(16)
            rDTT = Res("dtt")
            kb.op("dve", lambda e: e.tensor_reduce(out=DTT, in_=GP4[:, 0:8, 3, :].rearrange("p t g -> p g t"), axis=AX.X,
                                                   op=ALU.add), reads=[rGP], writes=[rDTT])
            kb.dma("sp", exr(4, 0, 128 * 16).rearrange("(p x) -> p x", p=128), DTT, reads=[rDTT])
            scan_mark = A.mark()

            if stop_after == "pre":
                return
            kb.issue_collectives("AllGather", ALU.bypass, GROUPS,
                                 [(exp_bufs[c_].ap().opt(), gat_bufs[c_].ap().opt()) for c_ in (2, 3, 4)])
            m2_mark = A.mark()
            KV = [A.f32(512) for _ in range(2)]
            rKV = [Res("kv0"), Res("kv1")]
            KTE = A.f32(2 * 1024)
            KTE3 = v3(KTE, 2)
            rKTE = Res("kte")
            KWS = []
            for i_ in range(2):
                KWS.append(dict(KN=A.f32(256), KR=A.f32(256), KTS=A.f32(256), KT1=A.f32(128), KT2=A.f32(128),
                                KRB=A.bf16(256), SSK=A.f32(24), r=Res("kw%d" % i_)))
            for t in range(10):
                b = t % 2
                W_ = KWS[b]
                bk_ = 4 * b
                kb.dma("sp", KV[b], Ptok.ap()[t * 128:(t + 1) * 128, 2048:2560], writes=[rKV[b]])
                qk_norm_rope(KV[b][:, 0:256], 2, GQK[:, 128:256], t, W_["KN"], W_["KR"], W_["KTS"], W_["SSK"], W_["KT1"], W_["KT2"],
                             rKV[b], W_["r"])
                kb.op("act", (lambda W_=W_: lambda e: e.copy(W_["KRB"], W_["KR"]))(), reads=[W_["r"]], writes=[W_["r"]])
                for g in range(2):
                    kb.op("pe", (lambda g=g, W_=W_, bk_=bk_: lambda e: e.matmul(
                        bank(bk_)[:, g * 128:(g + 1) * 128], lhsT=W_["KRB"][:, g * 128:(g + 1) * 128], rhs=IDB,
                        start=True, stop=True))(), reads=[W_["r"], rCONST], writes=[BK[bk_]])
                if t < 8:
                    kb.op("act", (lambda t=t, bk_=bk_: lambda e: e.copy(KTE3[:, :, t * 128:(t + 1) * 128], v3(bank(bk_, 256), 2)))(),
                          reads=[BK[bk_]], writes=[rKTE])
                else:
                    kb.op("act", (lambda t=t, bk_=bk_: lambda e: e.copy(KTC3[:, :, (t - 8) * 128:(t - 7) * 128],
                                                                      v3(bank(bk_, 256), 2)))(), reads=[BK[bk_]], writes=[rKTC])
            kb.dma("sp", exr(0, 0, 128 * 2048).rearrange("(d x) -> d x", d=128), KTE, reads=[rKTE])
            kb.dma("sp", exr(1, 0, 1024 * 256).rearrange("(t c) -> t c", c=256), Ptok.ap()[0:1024, 2304:2560])
            A.release(m2_mark)

            ATb = [A.f32(NT) for _ in range(2)]
            GTb = [A.f32(NT) for _ in range(2)]
            rATb = [Res("at0"), Res("at1")]
            rGTb = [Res("gt0"), Res("gt1")]
            halo_ex = exr(5, 0, 128 * 240).rearrange("(c g j) -> c g j", c=128, g=8)
            for cg in range(8):
                b = cg % 2
                kb.dma("sp", ATb[b][:, 0:NTL], Pch.ap()[CR["Ba"] + cg * 128:CR["Ba"] + (cg + 1) * 128, 0:NTL], writes=[rATb[b]])
                kb.dma("sp", GTb[b][:, 0:NTL], Pch.ap()[CR["Bg"] + cg * 128:CR["Bg"] + (cg + 1) * 128, 0:NTL], writes=[rGTb[b]])
                kb.op("act", (lambda b=b: lambda e: e.activation(out=GTb[b][:, 0:NTL], in_=GTb[b][:, 0:NTL], func=AF.Sigmoid))(),
                      reads=[rGTb[b]], writes=[rGTb[b]])
                kb.op("dve", (lambda b=b: lambda e: e.tensor_tensor(ATb[b][:, 0:NTL], ATb[b][:, 0:NTL], GTb[b][:, 0:NTL], ALU.mult))(),
                      reads=[rGTb[b], rATb[b]], writes=[rATb[b]])
                kb.dma("sp", Ych.ap()[cg * 128:(cg + 1) * 128, 0:NTL], ATb[b][:, 0:NTL], reads=[rATb[b]])
                kb.dma("sp", halo_ex[:, cg, 0:15], ATb[b][:, 0:15], reads=[rATb[b]])
                kb.dma("sp", halo_ex[:, cg, 15:30], ATb[b][:, 1009:1024], reads=[rATb[b]])
            A.release(m2_mark)

            kb.barrier()
            if stop_after == "gather":
                return
            SG = [A.f32(8 * 256) for _ in range(4)]
            rSG = Res("sg")
            DJ = A.f32(64)
            FF = A.f32(8 * 256)
            rFF = Res("ff")
            for r in range(4):
                kb.dma("sp", DJ[:, r * 16:(r + 1) * 16], gar(4, r, 0, 128 * 16).rearrange("(p x) -> p x", p=128), writes=[rSG])
            kb.op("act", lambda e: e.activation(out=DJ, in_=DJ, func=AF.Exp), reads=[rSG], writes=[rSG])
            for dr in range(2):
                for r in range(4):
                    kb.dma("sp", SG[r], gar(2 + dr, r, 0, 128 * 2048).rearrange("(p x) -> p x", p=128), writes=[rSG])
                CSd = CS[:, dr * 2048:(dr + 1) * 2048]
                kb.op("dve", (lambda dr=dr: lambda e: e.tensor_copy(FF, STFB[:, dr * 2048:(dr + 1) * 2048]))(),
                      reads=[rSTFB], writes=[rFF])
                order = [0, 1, 2] if dr == 0 else [3, 2, 1]
                fs = 0 if dr == 0 else 3
                kb.op("dve", (lambda CSd=CSd, fs=fs: lambda e: e.tensor_scalar_mul(CSd, FF, SEL[:, fs:fs + 1]))(),
                      reads=[rFF, rCONST], writes=[rCS])
                for j in order:
                    for h in range(8):
                        kb.op("dve", (lambda j=j, h=h, dr=dr: lambda e: e.scalar_tensor_tensor(
                            out=v3(FF, 8)[:, h, :], in0=v3(FF, 8)[:, h, :],
                            scalar=DJ[:, j * 16 + dr * 8 + h:j * 16 + dr * 8 + h + 1], in1=v3(SG[j], 8)[:, h, :],
                            op0=ALU.mult, op1=ALU.add))(), reads=[rSG, rFF], writes=[rFF])
                    nx = j + 1 if dr == 0 else j - 1
                    kb.op("dve", (lambda CSd=CSd, nx=nx: lambda e: e.scalar_tensor_tensor(
                        out=CSd, in0=FF, scalar=SEL[:, nx:nx + 1], in1=CSd, op0=ALU.mult, op1=ALU.add))(),
                        reads=[rFF, rCONST, rCS], writes=[rCS])
            kb.op("act", lambda e: e.copy(CB, CS), reads=[rCS], writes=[rCB])
            dump("d_INIT", CS, [128, 4096], reads=[rCS])
            kb.issue_collectives("AllGather", ALU.bypass, GROUPS,
                                 [(exp_bufs[c_].ap().opt(), gat_bufs[c_].ap().opt()) for c_ in (0, 1, 5)])
            scan(list(range(8)), 0, True)
            scan(list(range(7, -1, -1)), 1, True)
            dump("d_HS", HS, [128, 8 * NT], reads=[rHS])
            A.release(pers_mark)
            OTb = [A.f32(NT) for _ in range(2)]
            ZTb = [A.f32(NT) for _ in range(2)]
            rOTb = [Res("ot0"), Res("ot1")]
            rZTb = [Res("zt0"), Res("zt1")]
            SQ = A.f32(NT)
            rSQ = Res("sq")
            RS = A.f32(NT)
            rRS = Res("rs")
            for h in range(8):
                b = h % 2
                kb.dma("sp", OTb[b][:, 0:NTL], Pch.ap()[CR["Do"] + h * 128:CR["Do"] + (h + 1) * 128, 0:NTL], writes=[rOTb[b]])
                kb.dma("sp", ZTb[b][:, 0:NTL], Pch.ap()[CR["Dz"] + h * 128:CR["Dz"] + (h + 1) * 128, 0:NTL], writes=[rZTb[b]])
                kb.op("act", (lambda h=h: lambda e: e.activation(out=SQ[:, 0:NTL], in_=HS3[:, h, 0:NTL], func=AF.Square))(),
                      reads=[rHS], writes=[rSQ])
                for i, (t0, tn) in enumerate(tts):
                    kb.op("pe", (lambda i=i, t0=t0, tn=tn: lambda e: e.matmul(bank(i, tn), lhsT=ONF, rhs=SQ[:, t0:t0 + tn],
                                                                              start=True, stop=True))(),
                          reads=[rSQ, rCONST], writes=[BK[i]])
                    kb.op("dve", (lambda i=i, t0=t0, tn=tn: lambda e: e.tensor_scalar(RS[:, t0:t0 + tn], bank(i, tn), 1.0 / 128, EPS,
                                                                                    ALU.mult, ALU.add))(),
                          reads=[BK[i]], writes=[rRS])
                kb.op("act", lambda e: e.sqrt(RS[:, 0:NTL], RS[:, 0:NTL]), reads=[rRS], writes=[rRS])
                kb.op("dve", lambda e: e.reciprocal(RS[:, 0:NTL], RS[:, 0:NTL]), reads=[rRS], writes=[rRS])
                kb.op("dve", (lambda h=h: lambda e: e.tensor_tensor(SQ[:, 0:NTL], HS3[:, h, 0:NTL], RS[:, 0:NTL], ALU.mult))(),
                      reads=[rHS, rRS, rSQ], writes=[rSQ])
                kb.op("act", (lambda b=b: lambda e: e.activation(out=OTb[b][:, 0:NTL], in_=OTb[b][:, 0:NTL], func=AF.Sigmoid))(),
                      reads=[rOTb[b]], writes=[rOTb[b]])
                kb.op("act", (lambda b=b: lambda e: e.activation(out=ZTb[b][:, 0:NTL], in_=ZTb[b][:, 0:NTL], func=AF.Silu))(),
                      reads=[rZTb[b]], writes=[rZTb[b]])
                kb.op("pool", (lambda b=b: lambda e: e.tensor_tensor(SQ[:, 0:NTL], SQ[:, 0:NTL], OTb[b][:, 0:NTL], ALU.mult))(),
                      reads=[rOTb[b], rSQ], writes=[rSQ])
                stg, rstg = mix_stage()
                kb.op("dve", (lambda b=b, h=h, stg=stg: lambda e: e.scalar_tensor_tensor(
                    out=stg[:, 0:NTL], in0=SQ[:, 0:NTL], scalar=PT3[:, 24 + h:25 + h], in1=ZTb[b][:, 0:NTL],
                    op0=ALU.mult, op1=ALU.mult))(), reads=[rSQ, rZTb[b], rPT], writes=[rstg])
                mix_store(24 + h, stg, rstg)
            kb.barrier()
            A.release(pers2_mark)
            LNG = A.f32(1024)
            LNB = A.f32(1024)
            SBI = A.f32(1024)
            rAP = Res("ap")
            kb.dma("sp", LNG, sglg_in[l:l + 1, :].partition_broadcast(128), writes=[rAP])
            kb.dma("sp", LNB, sglb_in[l:l + 1, :].partition_broadcast(128), writes=[rAP])
            kb.dma("sp", SBI, sgub_in[l:l + 1, :].partition_broadcast(128), writes=[rAP])
            WSF = [A.f32(128) for _ in range(2)]
            rWSF = [Res("wsf0"), Res("wsf1")]
            WST = A.bf16(1024)
            rWST = Res("wst")
            for h in range(8):
                b = h % 2
                kb.dma("sp", WSF[b], sguw_in[l, h], writes=[rWSF[b]])
                kb.op("pe", (lambda b=b: lambda e: e.matmul(bank(b, 128), lhsT=WSF[b], rhs=IDF, start=True, stop=True))(),
                      reads=[rWSF[b], rCONST], writes=[BK[b]])
                kb.op("dve", (lambda b=b, h=h: lambda e: e.tensor_copy(WST[:, h * 128:(h + 1) * 128], bank(b, 128)))(),
                      reads=[BK[b]], writes=[rWST])
            VN = A.bf16(TL * 1024)
            VN3 = v3(VN, TL)
            rVN = Res("vn")
            VT = [A.f32(1024) for _ in range(2)]
            rVT = [Res("vt0"), Res("vt1")]
            AJ = A.f32(1024)
            rAJ = Res("aj")
            ST = A.f32(16)
            rST = Res("st")
            for t in range(TL):
                b = t % 2
                kb.dma("sp", VT[b], Ptok.ap()[t * 128:(t + 1) * 128, 0:1024], writes=[rVT[b]])
                kb.op("act", (lambda b=b: lambda e: e.activation(out=AJ, in_=VT[b], func=AF.Copy, accum_out=ST[:, 0:1]))(),
                      reads=[rVT[b]], writes=[rAJ, rST])
                kb.op("act", (lambda b=b: lambda e: e.activation(out=AJ, in_=VT[b], func=AF.Square, accum_out=ST[:, 1:2]))(),
                      reads=[rVT[b]], writes=[rAJ, rST])
                kb.op("dve", lambda e: e.tensor_scalar_mul(ST[:, 2:4], ST[:, 0:2], 1.0 / 1024), reads=[rST], writes=[rST])
                kb.op("dve", lambda e: e.tensor_tensor(ST[:, 4:5], ST[:, 2:3], ST[:, 2:3], ALU.mult), reads=[rST], writes=[rST])
                kb.op("dve", lambda e: e.tensor_tensor(ST[:, 5:6], ST[:, 3:4], ST[:, 4:5], ALU.subtract), reads=[rST], writes=[rST])
                rsqrt_ops(ST[:, 7:8], ST[:, 5:6], 1.0, ST[:, 6:7], [rST], [rST])
                kb.op("dve", (lambda b=b: lambda e: e.tensor_scalar(VT[b], VT[b], ST[:, 2:3], ST[:, 7:8], ALU.subtract, ALU.mult))(),
                      reads=[rST, rVT[b]], writes=[rVT[b]])
                kb.op("pool", (lambda b=b: lambda e: e.tensor_tensor(VT[b], VT[b], LNG, ALU.mult))(), reads=[rAP, rVT[b]], writes=[rVT[b]])
                kb.op("dve", (lambda b=b, t=t: lambda e: e.tensor_tensor(VN3[:, t, :], VT[b], LNB, ALU.add))(),
                      reads=[rAP, rVT[b]], writes=[rVN])
            UT = [A.f32(NT) for _ in range(2)]
            ZT2 = [A.f32(NT) for _ in range(2)]
            rUT = [Res("ut0"), Res("ut1")]
            rZT2 = [Res("zt20"), Res("zt21")]
            TMA = A.f32(NT)
            rTMA = Res("tma")
            for h in range(8):
                b = h % 2
                kb.dma("sp", UT[b][:, 0:NTL], Pch.ap()[CR["Au"] + h * 128:CR["Au"] + (h + 1) * 128, 0:NTL], writes=[rUT[b]])
                kb.dma("sp", ZT2[b][:, 0:NTL], Pch.ap()[CR["Az"] + h * 128:CR["Az"] + (h + 1) * 128, 0:NTL], writes=[rZT2[b]])
                kb.op("act", (lambda b=b: lambda e: e.activation(out=ZT2[b][:, 0:NTL], in_=ZT2[b][:, 0:NTL], func=AF.Silu))(),
                      reads=[rZT2[b]], writes=[rZT2[b]])
                kb.op("pool", (lambda b=b: lambda e: e.tensor_tensor(UT[b][:, 0:NTL], UT[b][:, 0:NTL], ZT2[b][:, 0:NTL], ALU.mult))(),
                      reads=[rZT2[b], rUT[b]], writes=[rUT[b]])
                for t in range(TL):
                    kb.op("pe", (lambda h=h, t=t: lambda e: e.matmul(
                        PS[:, t * 128:(t + 1) * 128], lhsT=VN3[:, t, h * 128:(h + 1) * 128], rhs=WST[:, h * 128:(h + 1) * 128],
                        start=True, stop=True))(), reads=[rVN, rWST], writes=[BK[t // 4]])
                kb.op("dve", (lambda h=h: lambda e: e.tensor_tensor(
                    v3(TMA[:, 0:NTL], TL), v3(PS[:, 0:NTL], TL), bc(SBI[:, h * 128:(h + 1) * 128].unsqueeze(1), [128, TL, 128]),
                    ALU.add))(), reads=[BK[0], BK[1], BK[2], rAP], writes=[rTMA])
                stg, rstg = mix_stage()
                kb.op("dve", (lambda b=b, stg=stg: lambda e: e.tensor_tensor(stg[:, 0:NTL], TMA[:, 0:NTL], UT[b][:, 0:NTL], ALU.mult))(),
                      reads=[rTMA, rUT[b]], writes=[rstg])
                mix_store(h, stg, rstg)
            kb.barrier()
            A.release(pers2_mark)

            HG = A.f32(4 * 240)
            rHG = Res("hg")
            for r in range(4):
                kb.dma("sp", HG[:, r * 240:(r + 1) * 240], gar(5, r, 0, 128 * 240).rearrange("(c x) -> c x", c=128), writes=[rHG])
            HG4 = HG.rearrange("p (r g j) -> p r g j", r=4, g=8)
            LH = A.f32(8 * 15)
            RH = A.f32(8 * 15)
            rLR = Res("lr")
            for r in range(4):
                for (dst, so, j0) in ((LH, 4, 15), (RH, 8, 0)):
                    src = HG4[:, r, :, j0:j0 + 15]
                    if r == 0:
                        kb.op("dve", (lambda dst=dst, src=src, so=so, r=r: lambda e: e.tensor_scalar_mul(
                            v3(dst, 8), src, SEL[:, so + r:so + r + 1]))(), reads=[rHG, rCONST], writes=[rLR])
                    else:
                        kb.op("dve", (lambda dst=dst, src=src, so=so, r=r: lambda e: e.scalar_tensor_tensor(
                            out=v3(dst, 8), in0=src, scalar=SEL[:, so + r:so + r + 1], in1=v3(dst, 8), op0=ALU.mult, op1=ALU.add))(),
                            reads=[rHG, rCONST, rLR], writes=[rLR])
            CONV = A.f32(8 * NT)
            CONV3 = v3(CONV, 8)
            rCONV = [Res("conv%d" % i) for i in range(8)]
            YP = [A.bf16(1054 + 286) for _ in range(2)]
            rYP = [Res("yp0"), Res("yp1")]
            DIAG = [A.bf16(31 * 128) for _ in range(2)]
            rDIAG = [Res("diag0"), Res("diag1")]
            CWr = A.f32(248)
            rCWr = Res("cwr")
            kb.op("dve", lambda e: e.tensor_copy(v3(CWr, 8), v3(CW, 31).rearrange("p k g -> p g k")), reads=[rPT], writes=[rCWr])
            SQB = A.f32(NT)
            rSQB = Res("sqb")
            for b in range(2):
                kb.op("pool", (lambda b=b: lambda e: e.memset(YP[b], 0.0))(), writes=[rYP[b]])
            cvb = 0
            for cg in range(8):
                b = cg % 2
                kb.dma("pool", YP[b][:, 15:1039], Ych.ap()[cg * 128:(cg + 1) * 128, 0:1024], writes=[rYP[b]])
                if not last:
                    kb.dma("pool", YP[b][:, 1054 + 15:1054 + 271], Ych.ap()[cg * 128:(cg + 1) * 128, 1024:1280], writes=[rYP[b]])
                kb.op("dve", (lambda b=b, cg=cg: lambda e: e.tensor_copy(YP[b][:, 0:15], v3(LH, 8)[:, cg, :]))(), reads=[rLR], writes=[rYP[b]])
                kb.op("dve", (lambda b=b, cg=cg: lambda e: e.tensor_copy(YP[b][:, 1039:1054], v3(RH, 8)[:, cg, :]))(), reads=[rLR], writes=[rYP[b]])
                kb.op("dve", (lambda b=b, cg=cg: lambda e: e.tensor_tensor(
                    v3(DIAG[b], 31), bc(IDF.unsqueeze(1), [128, 31, 128]), bc(v3(CWr, 8)[:, cg, :].unsqueeze(2), [128, 31, 128]),
                    ALU.mult))(), reads=[rCWr, rCONST], writes=[rDIAG[b]])
                for (ys, co, n) in [(0, 0, 512), (512, 512, 512)] + ([] if last else [(1054, 1024, 256)]):
                    bk = 6 + (cvb % 2)
                    cvb += 1
                    for k in range(31):
                        kb.op("pe", (lambda b=b, bk=bk, k=k, ys=ys, n=n: lambda e: e.matmul(
                            bank(bk, n), lhsT=v3(DIAG[b], 31)[:, k, :], rhs=YP[b][:, ys + k:ys + k + n],
                            start=(k == 0), stop=(k == 30)))(), reads=[rDIAG[b], rYP[b]], writes=[BK[bk]])
                    kb.op("act", (lambda bk=bk, cg=cg, co=co, n=n: lambda e: e.activation(
                        out=CONV3[:, cg, co:co + n], in_=bank(bk, n), func=AF.Identity, bias=PT3[:, cg:cg + 1]))(),
                        reads=[BK[bk], rPT], writes=[rCONV[cg]])
                kb.op("act", (lambda cg=cg: lambda e: e.activation(out=SQB[:, 0:NTL], in_=CONV3[:, cg, 0:NTL], func=AF.Square))(),
                      reads=[rCONV[cg]], writes=[rSQB])
                for i, (t0, tn) in enumerate(tts):
                    kb.op("pe", (lambda cg=cg, i=i, t0=t0, tn=tn: lambda e: e.matmul(
                        bank(i, tn), lhsT=ONF, rhs=CONV3[:, cg, t0:t0 + tn], start=(cg == 0), stop=(cg == 7)))(),
                        reads=[rCONV[cg], rCONST], writes=[BK[i]])
                    kb.op("pe", (lambda cg=cg, i=i, t0=t0, tn=tn: lambda e: e.matmul(
                        bank(3 + i, tn), lhsT=ONF, rhs=SQB[:, t0:t0 + tn], start=(cg == 0), stop=(cg == 7)))(),
                        reads=[rSQB, rCONST], writes=[BK[3 + i]])
            dump("d_CONV", CONV, [128, 8 * NT], reads=rCONV)
            MEAN = A.f32(NT)
            RSTD = A.f32(NT)
            MSQ = A.f32(NT)
            rMS = Res("ms")
            for i, (t0, tn) in enumerate(tts):
                kb.op("dve", (lambda i=i, t0=t0, tn=tn: lambda e: e.tensor_scalar_mul(MEAN[:, t0:t0 + tn], bank(i, tn), 1.0 / 1024))(),
                      reads=[BK[i]], writes=[rMS])
                kb.op("dve", (lambda i=i, t0=t0, tn=tn: lambda e: e.tensor_scalar_mul(RSTD[:, t0:t0 + tn], bank(3 + i, tn), 1.0 / 1024))(),
                      reads=[BK[3 + i]], writes=[rMS])
            kb.op("dve", lambda e: e.tensor_tensor(MSQ[:, 0:NTL], MEAN[:, 0:NTL], MEAN[:, 0:NTL], ALU.mult), reads=[rMS], writes=[rMS])
            kb.op("dve", lambda e: e.tensor_tensor(RSTD[:, 0:NTL], RSTD[:, 0:NTL], MSQ[:, 0:NTL], ALU.subtract), reads=[rMS], writes=[rMS])
            rsqrt_ops(RSTD[:, 0:NTL], RSTD[:, 0:NTL], 1.0, MSQ[:, 0:NTL], [rMS], [rMS])
            dump("d_MEAN", MEAN, [128, NT], reads=[rMS])
            dump("d_RSTD", RSTD, [128, NT], reads=[rMS])
            ZB = [A.f32(NT) for _ in range(2)]
            rZB = [Res("zb0"), Res("zb1")]
            for cg in range(8):
                b = cg % 2
                kb.dma("sp", ZB[b][:, 0:NTL], Pch.ap()[CR["Bz"] + cg * 128:CR["Bz"] + (cg + 1) * 128, 0:NTL], writes=[rZB[b]])
                cv = CONV3[:, cg, 0:NTL]
                kb.op("dve", (lambda cv=cv: lambda e: e.tensor_tensor(cv, cv, MEAN[:, 0:NTL], ALU.subtract))(), reads=[rMS, rCONV[cg]], writes=[rCONV[cg]])
                kb.op("pool", (lambda cv=cv: lambda e: e.tensor_tensor(cv, cv, RSTD[:, 0:NTL], ALU.mult))(), reads=[rMS, rCONV[cg]], writes=[rCONV[cg]])
                kb.op("act", (lambda cv=cv, cg=cg: lambda e: e.activation(out=cv, in_=cv, func=AF.Silu, scale=PT3[:, 8 + cg:9 + cg],
                                                                          bias=PT3[:, 16 + cg:17 + cg]))(), reads=[rPT, rCONV[cg]], writes=[rCONV[cg]])
                kb.op("act", (lambda b=b: lambda e: e.activation(out=ZB[b][:, 0:NTL], in_=ZB[b][:, 0:NTL], func=AF.Silu))(),
                      reads=[rZB[b]], writes=[rZB[b]])
                stg, rstg = mix_stage()
                kb.op("dve", (lambda cv=cv, b=b, stg=stg: lambda e: e.tensor_tensor(stg[:, 0:NTL], cv, ZB[b][:, 0:NTL], ALU.mult))(),
                      reads=[rCONV[cg], rZB[b]], writes=[rstg])
                mix_store(8 + cg, stg, rstg)
            kb.barrier()
            A.release(pers2_mark)

            W2PRE = A.bf16(32 * 512)
            rWT2 = [Res("w2t0"), Res("w2t1")]
            wsrc2 = wout_in[l].rearrange("(kc kp) n -> kp kc n", kp=128)
            for g_ in range(4):
                kb.dma("pool", v3(W2PRE, 32)[:, g_ * 8:(g_ + 1) * 8, :], wsrc2[:, g_ * 8:(g_ + 1) * 8, 0:512], writes=[rWT2[0]])
            w2_mark = A.mark()
            QT = A.bf16(8 * NT)
            QT3 = v3(QT, 8)
            rQT = Res("qt")
            QF = [A.f32(1024) for _ in range(2)]
            rQF = [Res("qf0"), Res("qf1")]
            QWS = []
            for i_ in range(2):
                QWS.append(dict(QN=A.f32(1024), QR=A.f32(1024), QTS=A.f32(1024), QT1=A.f32(512), QT2=A.f32(512),
                                QRB=A.bf16(1024), SSQ=A.f32(24), r=Res("qw%d" % i_)))
            for t in range(TL):
                b = t % 2
                W_ = QWS[b]
                pb0 = 4 * b
                kb.dma("sp", QF[b], Ptok.ap()[t * 128:(t + 1) * 128, 1024:2048], writes=[rQF[b]])
                qk_norm_rope(QF[b], 8, GQK[:, 0:128], t, W_["QN"], W_["QR"], W_["QTS"], W_["SSQ"], W_["QT1"], W_["QT2"], rQF[b], W_["r"])
                kb.op("act", (lambda W_=W_: lambda e: e.copy(W_["QRB"], W_["QR"]))(), reads=[W_["r"]], writes=[W_["r"]])
                for h in range(8):
                    kb.op("pe", (lambda h=h, W_=W_, pb0=pb0: lambda e: e.matmul(
                        PS[:, pb0 * 512 + h * 128:pb0 * 512 + (h + 1) * 128], lhsT=W_["QRB"][:, h * 128:(h + 1) * 128],
                        rhs=IDB, start=True, stop=True))(), reads=[W_["r"], rCONST], writes=[BK[pb0 + h // 4]])
                kb.op("act", (lambda t=t, pb0=pb0: lambda e: e.copy(QT3[:, :, t * 128:(t + 1) * 128],
                                                                  v3(PS[:, pb0 * 512:pb0 * 512 + 1024], 8)))(),
                      reads=[BK[pb0], BK[pb0 + 1]], writes=[rQT])
            KTA = A.bf16(2 * 4352)
            KTA3 = v3(KTA, 2)
            VAL = A.bf16(34 * 256)
            VAL3 = v3(VAL, 34)
            rKVA = Res("kva")
            for r in range(4):
                kb.dma("pool", KTA3[:, :, r * 1024:(r + 1) * 1024],
                       gar(0, r, 0, 128 * 2048).rearrange("(d g t) -> d g t", d=128, g=2), writes=[rKVA])
                kb.dma("pool", VAL3[:, r * 8:(r + 1) * 8, :],
                       gar(1, r, 0, 1024 * 256).rearrange("(kt p c) -> p kt c", p=128, c=256), writes=[rKVA])
            kb.op("dve", lambda e: e.tensor_copy(KTA3[:, :, 4096:4352], KTC3), reads=[rKTC], writes=[rKVA])
            kb.dma("pool", VAL3[:, 32:34, :], Ptok.ap()[1024:1280, 2304:2560].rearrange("(kt p) c -> p kt c", p=128), writes=[rKVA])
            dump("d_KTA", KTA, [128, 2 * 4352], BF16, reads=[rKVA])
            dump("d_VAL", VAL, [128, 34 * 256], BF16, reads=[rKVA])
            dump("d_QT", QT, [128, 8 * NT], BF16, reads=[rQT])
            SZ = [A.f32(NT) for _ in range(2)]
            rSZ = [Res("sz0"), Res("sz1")]
            PTA = [A.bf16(512) for _ in range(4)]
            rPTA = [Res("pta%d" % i) for i in range(4)]
            RL = A.f32(512)
            rRL = Res("rl")
            OA = A.f32(512)
            rOA = Res("oa")
            SC = 128.0 ** -0.5
            pcount = 0
            for h in range(8):
                g = h // 4
                b = h % 2
                kb.dma("sp", SZ[b][:, 0:NTL], Pch.ap()[CR["Cz"] + h * 128:CR["Cz"] + (h + 1) * 128, 0:NTL], writes=[rSZ[b]])
                kb.op("act", (lambda b=b: lambda e: e.activation(out=SZ[b][:, 0:NTL], in_=SZ[b][:, 0:NTL], func=AF.Silu))(),
                      reads=[rSZ[b]], writes=[rSZ[b]])
                stg, rstg = mix_stage()
                qtiles = [(0, 512, list(range(34))), (512, 512, list(range(34)))] + ([] if last else [(1024, 256, [32, 33])])
                for (q0, qn, kts) in qtiles:
                    nk = len(kts)

                    def emit_s(ki, q0=q0, qn=qn, kts=kts, g=g, h=h):
                        sb = 2 + (ki % 3)
                        kt = kts[ki]
                        kb.op("pe", (lambda: lambda e: e.matmul(
                            bank(sb, qn), lhsT=KTA3[:, g, kt * 128:(kt + 1) * 128], rhs=QT3[:, h, q0:q0 + qn],
                            start=True, stop=True))(), reads=[rKVA, rQT], writes=[BK[sb]])
                    emit_s(0)
                    if nk > 1:
                        emit_s(1)
                    for ki, kt in enumerate(kts):
                        sb = 2 + (ki % 3)
                        pb = pcount % 4
                        pcount += 1
                        if ki + 2 < nk:
                            emit_s(ki + 2)
                        kb.op("act", (lambda sb=sb, pb=pb, qn=qn: lambda e: e.activation(
                            out=PTA[pb][:, 0:qn], in_=bank(sb, qn), func=AF.Exp, scale=SC))(), reads=[BK[sb]], writes=[rPTA[pb]])
                        kb.op("pe", (lambda pb=pb, kt=kt, g=g, qn=qn, ki=ki, nk=nk: lambda e: e.matmul(
                            bank(0, qn), lhsT=VAL3[:, kt, g * 128:(g + 1) * 128], rhs=PTA[pb][:, 0:qn],
                            start=(ki == 0), stop=(ki == nk - 1)))(), reads=[rKVA, rPTA[pb]], writes=[BK[0]])
                        kb.op("pe", (lambda pb=pb, qn=qn, ki=ki, nk=nk: lambda e: e.matmul(
                            bank(1, qn), lhsT=ONB, rhs=PTA[pb][:, 0:qn], start=(ki == 0), stop=(ki == nk - 1)))(),
                            reads=[rCONST, rPTA[pb]], writes=[BK[1]])
                    kb.op("dve", (lambda qn=qn: lambda e: e.reciprocal(RL[:, 0:qn], bank(1, qn)))(), reads=[BK[1]], writes=[rRL])
                    kb.op("dve", (lambda qn=qn: lambda e: e.tensor_tensor(OA[:, 0:qn], bank(0, qn), RL[:, 0:qn], ALU.mult))(),
                          reads=[BK[0], rRL], writes=[rOA])
                    kb.op("pool", (lambda qn=qn, q0=q0, b=b, stg=stg: lambda e: e.tensor_tensor(
                        stg[:, q0:q0 + qn], OA[:, 0:qn], SZ[b][:, q0:q0 + qn], ALU.mult))(), reads=[rOA, rSZ[b]], writes=[rstg])
                mix_store(16 + h, stg, rstg)
            kb.barrier()

            A.release(w2_mark)
            BIG2 = A.f32(16 * NT)
            MX = v3(BIG2.bitcast(BF16), 32)
            rMX = Res("mx")
            for kc in range(32):
                kb.dma("sp", MX[:, kc, 0:NTL], MIXd.ap()[kc * 128:(kc + 1) * 128, 0:NTL], writes=[rMX])
            GLn = [A.f32(512) for _ in range(2)]
            GCn = [A.f32(512) for _ in range(2)]
            GBn = [A.f32(512) for _ in range(2)]
            rGn = [Res("gn0"), Res("gn1")]
            WT2 = [W2PRE, A.bf16(32 * 512)]
            XO = [A.f32(512) for _ in range(3)]
            rXO = [Res("xo%d" % i) for i in range(3)]
            OO = [A.f32(512) for _ in range(3)]
            rOO = [Res("oo%d" % i) for i in range(3)]
            def load_w2(n, b):
                w3 = v3(WT2[b], 32)
                for g in range(4):
                    kb.dma("pool", w3[:, g * 8:(g + 1) * 8, :], wsrc2[:, g * 8:(g + 1) * 8, n * 512:(n + 1) * 512], writes=[rWT2[b]])
            oc = 0
            for n in range(8):
                b = n % 2
                if n + 1 < 8:
                    load_w2(n + 1, 1 - b)
                w3 = v3(WT2[b], 32)
                cs_ = slice(2 * D + n * 512, 2 * D + (n + 1) * 512)
                kb.dma("sp", GLn[b], af[2 * l:2 * l + 1, cs_].partition_broadcast(128), writes=[rGn[b]])
                kb.dma("sp", GBn[b], bada_in[l:l + 1, cs_].partition_broadcast(128), writes=[rGn[b]])
                kb.op("dve", (lambda b=b: lambda e: e.tensor_tensor(GLn[b], GLn[b], GBn[b], ALU.add))(), reads=[rGn[b]], writes=[rGn[b]])
                if not last:
                    kb.dma("sp", GCn[b], af[2 * l + 1:2 * l + 2, cs_].partition_broadcast(128), writes=[rGn[b]])
                    kb.op("dve", (lambda b=b: lambda e: e.tensor_tensor(GCn[b], GCn[b], GBn[b], ALU.add))(), reads=[rGn[b]], writes=[rGn[b]])
                for t in range(TL):
                    bk = t % 2
                    ob = oc % 3
                    oc += 1
                    kb.dma("sp", XO[ob], tok_src(t, n * 512, (n + 1) * 512), writes=[rXO[ob]])
                    for kc in range(32):
                        kb.op("pe", (lambda bk=bk, kc=kc, t=t, w3=w3: lambda e: e.matmul(
                            bank(bk), lhsT=MX[:, kc, t * 128:(t + 1) * 128], rhs=w3[:, kc, :], start=(kc == 0), stop=(kc == 31)))(),
                            reads=[rMX, rWT2[b]], writes=[BK[bk]])
                    Gt = GLn[b] if t < 8 else GCn[b]
                    kb.op("dve", (lambda bk=bk, ob=ob, Gt=Gt: lambda e: e.tensor_tensor(
                        OO[ob], bank(bk), Gt, ALU.mult))(), reads=[BK[bk], rGn[b]], writes=[rOO[ob]])
                    kb.op("pool", (lambda ob=ob: lambda e: e.tensor_tensor(OO[ob], OO[ob], XO[ob], ALU.add))(),
                          reads=[rXO[ob], rOO[ob]], writes=[rOO[ob]])
                    if last:
                        dst = y_out[t * 128:(t + 1) * 128, n * 512:(n + 1) * 512]
                    else:
                        dst = xs.ap()[t * 128:(t + 1) * 128, n * 512:(n + 1) * 512]
                    kb.dma("sp", dst, OO[ob], reads=[rOO[ob]])
            kb.barrier()
            A.release(lay_mark)

        for l_ in range(nlayers):
            emit_layer(l_)

        kb.barrier()
        block = es.enter_context(nc.Block())
        kb.replay(block)
    return nc


def LAYER_BODY_2(env):
    pass


def _consts():
    c = np.zeros((128, 5 * 128), np.float32)
    c[:, 0:128] = np.eye(128, dtype=np.float32)
    c[:, 128:256] = 1.0
    s = np.arange(128)[:, None]
    l_ = np.arange(128)[None, :]
    c[:, 256:384] = (s <= l_).astype(np.float32)
    c[:, 384:512] = (s >= l_).astype(np.float32)
    return c


def _rope_table(seg):
    n = np.arange(seg * 1024, (seg + 1) * 1024)
    row = (n // 64).astype(np.float32)
    col = (n % 64).astype(np.float32)
    freq = (10000.0 ** (-np.arange(32, dtype=np.float32) / 32)).astype(np.float32)
    ang = np.stack([row, col], -1)[..., None] * freq
    cs = np.concatenate([np.cos(ang).reshape(1024, 64), np.sin(ang).reshape(1024, 64)], -1).astype(np.float32)
    return np.ascontiguousarray(cs.reshape(8, 128, 128).transpose(1, 0, 2))


def make_in_maps(inputs):
    f = lambda a: np.ascontiguousarray(np.asarray(a, dtype=np.float32))
    x = f(inputs["x"]); c = f(inputs["c"]); ctx = f(inputs["ctx"]); c_ctx = f(inputs["c_ctx"])
    w_ada = f(inputs["w_ada"])
    shared = {
        "b_ada": f(inputs["b_ada"]), "norm_g": f(inputs["norm_g"]), "w_in": f(inputs["w_in"]),
        "sgu_w": f(inputs["sgu_w"]), "sgu_b": f(inputs["sgu_b"]).reshape(DEPTH, 1024),
        "sgu_ln_g": f(inputs["sgu_ln_g"]), "sgu_ln_b": f(inputs["sgu_ln_b"]),
        "conv_w": f(inputs["conv_w"]).reshape(DEPTH, 248, 128),
        "conv_b": f(inputs["conv_b"]).reshape(DEPTH, 8, 128),
        "conv_ln_g": f(inputs["conv_ln_g"]).reshape(DEPTH, 8, 128),
        "conv_ln_b": f(inputs["conv_ln_b"]).reshape(DEPTH, 8, 128),
        "q_norm_g": f(inputs["q_norm_g"]), "k_norm_g": f(inputs["k_norm_g"]),
        "mlstm_i_bias": f(inputs["mlstm_i_bias"]).reshape(DEPTH, 16),
        "mlstm_f_bias": f(inputs["mlstm_f_bias"]).reshape(DEPTH, 16),
        "mh_norm_g": f(inputs["mh_norm_g"]).reshape(DEPTH, 8, 128),
        "w_out": f(inputs["w_out"]), "consts": _consts(),
    }
    maps = []
    for core in range(8):
        b, s = core // 4, core % 4
        m = dict(shared)
        m["x"] = np.ascontiguousarray(x[b, s * 1024:(s + 1) * 1024])
        m["ctx"] = np.ascontiguousarray(ctx[b])
        cc = np.stack([c[b], c_ctx], 0)
        m["ccT"] = np.ascontiguousarray(cc[:, s * 1024:(s + 1) * 1024].T)
        m["w_ada_s"] = np.ascontiguousarray(w_ada[:, s * 1024:(s + 1) * 1024, :])
        m["rope"] = _rope_table(s)
        sel = np.zeros((128, 12), np.float32)
        sel[:, s] = 1.0
        if s - 1 >= 0:
            sel[:, 4 + s - 1] = 1.0
        if s + 1 <= 3:
            sel[:, 8 + s + 1] = 1.0
        m["sel"] = sel
        maps.append(m)
    return maps


def kernel(**inputs):
    nc = build_program()
    maps = make_in_maps(inputs)
    res = run_bass_kernel_spmd(nc, maps, core_ids=list(range(8)))
    out = np.empty((2, 4096, D), np.float32)
    for core in range(8):
        b, s = core // 4, core % 4
        out[b, s * 1024:(s + 1) * 1024] = res.results[core]["y"]
    return out
```

```python
import math
import numpy as np
import ml_dtypes
from contextlib import ExitStack
import concourse.bass as bass
import concourse.mybir as mybir
from concourse.bass_utils import run_bass_kernel_spmd

F32 = mybir.dt.float32
BF16 = mybir.dt.bfloat16
AF = mybir.ActivationFunctionType
ALU = mybir.AluOpType
AX = mybir.AxisListType

D = 4096
PIN = 13856
DEPTH = 2
NT = 1280
NL = 1024
EPS = 1e-6
LNSC = math.log(128.0 ** -0.5)

TOKB = [("Av", 1024, 1024, 0), ("Cq", 6144, 1024, 1024), ("Ckv", 7168, 512, 2048),
        ("Dk", 9728, 1024, 2560), ("Dv", 10752, 1024, 3584), ("G", 13824, 32, 4608)]
NTOKC = 4640
TC = {n: d for n, _, _, d in TOKB}
CHB = [("Au", 0, 0), ("Az", 2048, 1024), ("Ba", 3072, 2048), ("Bg", 4096, 3072), ("Bz", 5120, 4096),
       ("Cz", 7680, 5120), ("Dq", 8704, 6144), ("DkT", 9728, 7168), ("Do", 11776, 8192), ("Dz", 12800, 9216)]
NCHR = 10240
CR = {n: r for n, _, r in CHB}

EXC_ROWS = [512, 512, 512, 512, 4, 60]
EX_HALO_OFF = 0
EX_DT_OFF = 128 * 240


class Res:
    __slots__ = ("name", "w", "r")

    def __init__(self, name):
        self.name = name
        self.w = None
        self.r = {}


class KB:
    CE = ["pe", "act", "dve", "pool"]

    def __init__(self, nc, es):
        self.nc = nc
        self.engs = ["pe", "act", "dve", "pool", "sp"]
        self.q = {e: [] for e in self.engs}
        self.sem = {e: es.enter_context(nc.semaphore("s_" + e)) for e in self.CE}
        self.cnt = {e: 0 for e in self.CE}
        self.known = {e: {} for e in self.engs}
        self.dsem = {"sp": [es.enter_context(nc.semaphore("d_sp%d" % i)) for i in range(16)],
                     "pool": [es.enter_context(nc.semaphore("d_pl%d" % i)) for i in range(8)]}
        self.dcnt = {"sp": 0, "pool": 0}
        self.duse = {k: [0] * len(v) for k, v in self.dsem.items()}
        self.dlast = {k: [None] * len(v) for k, v in self.dsem.items()}
        self.ccsem = es.enter_context(nc.semaphore("s_cc"))
        self.cccnt = 0
        self.cclast = None

    def _wait(self, eng, ev):
        if ev is None:
            return
        key, sem, val = ev
        if self.known[eng].get(key, 0) >= val:
            return
        self.known[eng][key] = val
        self.q[eng].append(("w", sem, val))

    def _deps(self, eng, reads, writes):
        deps = {}

        def add(ev):
            if ev is None:
                return
            k = ev[0]
            if k not in deps or deps[k][2] < ev[2]:
                deps[k] = ev
        for r in reads:
            add(r.w)
        for w in writes:
            add(w.w)
            for ev in w.r.values():
                add(ev)
        for k, ev in deps.items():
            if eng == "pe" and k == "pe":
                continue
            self._wait(eng, ev)

    def _mark(self, ev, reads, writes):
        for r in reads:
            r.r[ev[0]] = ev
        for w in writes:
            w.w = ev
            w.r = {}

    def op(self, eng, fn, reads=(), writes=()):
        self._deps(eng, reads, writes)
        self.cnt[eng] += 1
        ev = (eng, self.sem[eng], self.cnt[eng])
        self.q[eng].append(("o", fn, self.sem[eng]))
        self._mark(ev, reads, writes)
        return ev

    def dma(self, qn, out, in_, reads=(), writes=()):
        self._deps(qn, reads, writes)
        n = len(self.dsem[qn])
        slot = self.dcnt[qn] % n
        self.dcnt[qn] += 1
        self._wait(qn, self.dlast[qn][slot])
        self.duse[qn][slot] += 1
        sem = self.dsem[qn][slot]
        ev = ("%s%d" % (qn, slot), sem, 16 * self.duse[qn][slot])
        self.dlast[qn][slot] = ev
        self.q[qn].append(("d", out, in_, sem))
        self._mark(ev, reads, writes)
        return ev

    def all_events(self, include_cc=True):
        evs = [(e, self.sem[e], self.cnt[e]) for e in self.CE if self.cnt[e] > 0]
        for qn in self.dlast:
            evs += [ev for ev in self.dlast[qn] if ev is not None]
        if include_cc and self.cclast is not None:
            evs.append(self.cclast)
        return evs

    def barrier(self, include_cc=True):
        evs = self.all_events(include_cc)
        for eng in self.engs:
            for ev in evs:
                if eng == "pe" and ev[0] == "pe":
                    continue
                self._wait(eng, ev)

    def collective(self, kind, alu, groups, in_ap, out_ap):
        self.barrier()
        self.cccnt += 1
        self.q["pool"].append(("c", kind, alu, groups, in_ap, out_ap))
        self.cclast = ("cc", self.ccsem, self.cccnt)
        self.barrier()

    def issue_collectives(self, kind, alu, groups, pairs):
        self.barrier()
        for in_ap, out_ap in pairs:
            self.cccnt += 1
            self.q["pool"].append(("c", kind, alu, groups, in_ap, out_ap))
        self.cclast = ("cc", self.ccsem, self.cccnt)

    def replay(self, block):
        def mk(eng):
            def f(e):
                for it in self.q[eng]:
                    k = it[0]
                    if k == "w":
                        e.wait_ge(it[1], it[2])
                    elif k == "o":
                        it[1](e).then_inc(it[2], 1)
                    elif k == "d":
                        e.dma_start(out=it[1], in_=it[2]).then_inc(it[3], 16)
                    elif k == "c":
                        e.collective_compute(it[1], it[2], replica_groups=it[3], ins=[it[4]],
                                             outs=[it[5]]).then_inc(self.ccsem)
            return f
        block.sync(mk("sp"))
        block.gpsimd(mk("pool"))
        block.scalar(mk("act"))
        block.vector(mk("dve"))
        block.tensor(mk("pe"))


class Arena:
    def __init__(self, ap, nfloats):
        self.ap = ap
        self.n = nfloats
        self.top = 0
        self.kb = None

    def mark(self):
        return self.top

    def release(self, m):
        if self.kb is not None:
            self.kb.barrier(include_cc=False)
        self.top = m

    def f32(self, n):
        n4 = (n + 7) // 8 * 8
        assert self.top + n4 <= self.n, ("SBUF arena overflow", self.top, n4, self.n)
        v = self.ap[:, self.top:self.top + n]
        self.top += n4
        return v

    def bf16(self, n):
        nf = (n + 1) // 2
        return self.f32(nf).bitcast(BF16)[:, 0:n]


def build_program(debug=None, stop_after=None, nlayers=DEPTH):
    debug = debug or set()
    nc = bass.Bass("TRN2", target_bir_lowering=False)

    def din(name, shape, dt=F32):
        return nc.dram_tensor(name, list(shape), dt, kind="ExternalInput").ap()

    def dscr(name, shape, dt=F32):
        kind = "ExternalOutput" if name in debug else "Internal"
        return nc.dram_tensor(name, list(shape), dt, kind=kind)

    x_in = din("x", [NL, D])
    ctx_in = din("ctx", [256, D])
    ccT_in = din("ccT", [1024, 2])
    wada_in = din("w_ada_s", [DEPTH, 1024, 3 * D])
    bada_in = din("b_ada", [DEPTH, 3 * D])
    normg_in = din("norm_g", [DEPTH, D])
    win_in = din("w_in", [DEPTH, D, PIN])
    sguw_in = din("sgu_w", [DEPTH, 8, 128, 128])
    sgub_in = din("sgu_b", [DEPTH, 1024])
    sglg_in = din("sgu_ln_g", [DEPTH, 1024])
    sglb_in = din("sgu_ln_b", [DEPTH, 1024])
    convw_in = din("conv_w", [DEPTH, 248, 128])
    convb_in = din("conv_b", [DEPTH, 8, 128])
    cvlg_in = din("conv_ln_g", [DEPTH, 8, 128])
    cvlb_in = din("conv_ln_b", [DEPTH, 8, 128])
    qng_in = din("q_norm_g", [DEPTH, 128])
    kng_in = din("k_norm_g", [DEPTH, 128])
    ib_in = din("mlstm_i_bias", [DEPTH, 16])
    fb_in = din("mlstm_f_bias", [DEPTH, 16])
    mhg_in = din("mh_norm_g", [DEPTH, 8, 128])
    wout_in = din("w_out", [DEPTH, D, D])
    consts_in = din("consts", [128, 5 * 128])
    rope_in = din("rope", [128, 8, 128])
    sel_in = din("sel", [128, 12])
    y_out = nc.dram_tensor("y", [NL, D], F32, kind="ExternalOutput").ap()

    ada_part = nc.dram_tensor("ada_part", [4, 3 * D], F32)
    ada_full = nc.dram_tensor("ada_full", [4, 3 * D], F32)
    dbg_ada = dscr("dbg_ada", [4, 3 * D]) if "dbg_ada" in debug else None
    xs = dscr("xs", [NT, D])
    Ptok = dscr("Ptok", [NT, NTOKC])
    Pch = dscr("Pch", [NCHR, NT])
    Ych = dscr("Ych", [1024, NT])
    exp_bufs = [nc.dram_tensor("exp_buf%d" % c, [EXC_ROWS[c], 512], F32) for c in range(6)]
    gat_bufs = [nc.dram_tensor("gat_buf%d" % c, [4 * EXC_ROWS[c], 512], F32) for c in range(6)]
    dbg_gat = None
    dbg_mixT = dscr("dbg_mixT", [128, 32 * NT], BF16) if "dbg_mixT" in debug else None
    dbg_HT = dscr("dbg_HT", [128, 32 * NT], BF16) if "dbg_HT" in debug else None

    exp_flat = [b_.ap().rearrange("r c -> (r c)") for b_ in exp_bufs]
    gat_flat = [b_.ap().rearrange("r c -> (r c)") for b_ in gat_bufs]

    def exr(c, off, n):
        return exp_flat[c][off:off + n]

    def gar(c, r, off, n):
        o = r * EXC_ROWS[c] * 512 + off
        return gat_flat[c][o:o + n]

    GROUPS = [[0, 1, 2, 3], [4, 5, 6, 7]]

    with ExitStack() as es:
        ARN = 50 * 1024
        arena_t = es.enter_context(nc.sbuf_tensor("arena", [128, ARN], F32))
        A = Arena(arena_t[:, :], ARN)
        psum_t = es.enter_context(nc.psum_tensor("psum", [128, 4096], F32))
        PS = psum_t[:, :]
        kb = KB(nc, es)
        A.kb = kb
        BK = [Res("bank%d" % i) for i in range(8)]

        def bank(i, n=512):
            return PS[:, i * 512:i * 512 + n]

        def v3(ap, a):
            return ap.rearrange("p (a b) -> p a b", a=a)

        CONST = A.f32(4 * 128)
        IDF = CONST[:, 0:128]
        ONF = CONST[:, 128:256]
        TRIF = CONST[:, 256:384]
        TRIB = CONST[:, 384:512]
        IDB = A.bf16(128)
        ONB = A.bf16(128)
        SEL = A.f32(12)
        ROPE = A.f32(8 * 128)
        rCONST = Res("const")
        kb.dma("sp", CONST, consts_in[:, 0:512], writes=[rCONST])
        kb.dma("sp", SEL, sel_in[:, :], writes=[rCONST])
        kb.dma("sp", ROPE, rope_in.rearrange("p a b -> p (a b)"), writes=[rCONST])
        kb.op("dve", lambda e: e.tensor_copy(IDB, IDF), reads=[rCONST], writes=[rCONST])
        kb.op("dve", lambda e: e.tensor_copy(ONB, ONF), reads=[rCONST], writes=[rCONST])
        ROPE3 = v3(ROPE, 8)
        kb.barrier()

        rBIGd = Res("bigd")
        rBIGa = Res("biga")
        MIXd = dscr("MIXd", [32 * 128, NT], BF16)
        MST = [A.bf16(NT) for _ in range(2)]
        rMST = [Res("mst0"), Res("mst1")]
        mst_i = [0]

        def mix_stage():
            i = mst_i[0] % 2
            mst_i[0] += 1
            return MST[i], rMST[i]

        def mix_store(kc, stg, rstg):
            kb.dma("sp", MIXd.ap()[kc * 128:(kc + 1) * 128, 0:NTL_cur[0]], stg[:, 0:NTL_cur[0]], reads=[rstg])
        NTL_cur = [NT]
        dumped = set()

        def dump(name, ap2d, shape, dt=F32, reads=()):
            if name not in debug or name in dumped:
                return
            dumped.add(name)
            dtn = nc.dram_tensor(name, list(shape), dt, kind="ExternalOutput")
            kb.dma("sp", dtn.ap(), ap2d, reads=list(reads))

        def transpose_rows(src, R, dst, bk, rsrc, rdst):
            kb.op("pe", lambda e: e.matmul(bank(bk, R), lhsT=src[0:R, :], rhs=IDF[0:R, 0:R], start=True, stop=True),
                  reads=[rsrc, rCONST], writes=[BK[bk]])
            kb.op("dve", lambda e: e.tensor_copy(dst, bank(bk, R)), reads=[BK[bk]], writes=[rdst])

        m0 = A.mark()
        CCT = A.f32(16)
        rCCT = Res("cct")
        kb.dma("sp", v3(CCT, 8), ccT_in.rearrange("(kc kp) r -> kp kc r", kp=128), writes=[rCCT])
        kb.op("act", lambda e: e.activation(out=CCT, in_=CCT, func=AF.Silu), reads=[rCCT], writes=[rCCT])
        CCT3 = v3(CCT, 8)
        WA = [A.f32(4096) for _ in range(3)]
        rWA = [Res("wa%d" % i) for i in range(3)]
        AST = [A.f32(4096) for _ in range(2)]
        rAST = [Res("ast%d" % i) for i in range(2)]
        it = 0
        for l in range(DEPTH):
            for third in range(3):
                c0 = third * 4096
                for kc in range(8):
                    b = it % 3
                    it += 1
                    kb.dma("sp", WA[b], wada_in[l][kc * 128:(kc + 1) * 128, c0:c0 + 4096], writes=[rWA[b]])
                    for n in range(8):
                        kb.op("pe", (lambda b=b, kc=kc, n=n: lambda e: e.matmul(
                            bank(n)[0:2, :], lhsT=CCT3[:, kc, :], rhs=WA[b][:, n * 512:(n + 1) * 512],
                            start=(kc == 0), stop=(kc == 7)))(), reads=[rCCT, rWA[b]], writes=[BK[n]])
                ab = third % 2
                for n in range(8):
                    if n % 2 == 0:
                        kb.op("act", (lambda ab=ab, n=n: lambda e: e.copy(AST[ab][0:2, n * 512:(n + 1) * 512], bank(n)[0:2, :]))(),
                              reads=[BK[n]], writes=[rAST[ab]])
                    else:
                        kb.op("dve", (lambda ab=ab, n=n: lambda e: e.tensor_copy(AST[ab][0:2, n * 512:(n + 1) * 512], bank(n)[0:2, :]))(),
                              reads=[BK[n]], writes=[rAST[ab]])
                kb.dma("sp", ada_part.ap()[2 * l:2 * l + 2, c0:c0 + 4096], AST[ab][0:2, :], reads=[rAST[ab]])
        kb.collective("AllReduce", ALU.add, GROUPS, ada_part.ap().opt(), ada_full.ap().opt())
        A.release(m0)
        if dbg_ada is not None:
            kb.dma("sp", dbg_ada.ap(), ada_full.ap())
            kb.barrier()
        if stop_after == "ada":
            nlayers = 0

        def emit_layer(l):
            last = (l == DEPTH - 1)
            NTL = NL if last else NT
            TL = NTL // 128
            wl = win_in[l]
            lay_mark = A.mark()

            def tok_src(t, c0, c1, l=l):
                if l == 0:
                    if t < 8:
                        return x_in[t * 128:(t + 1) * 128, c0:c1]
                    return ctx_in[(t - 8) * 128:(t - 7) * 128, c0:c1]
                return xs.ap()[t * 128:(t + 1) * 128, c0:c1]

            R1 = A.f32(128)
            R2 = A.f32(128)
            R3 = A.f32(128)
            R4a = A.f32(128)
            R4b = A.f32(128)
            rR = Res("R")
            af = ada_full.ap()
            srcs1 = [af[2 * l:2 * l + 1, D:2 * D], af[2 * l:2 * l + 1, 0:D],
                     af[2 * l + 1:2 * l + 2, D:2 * D], af[2 * l + 1:2 * l + 2, 0:D]]
            for i, s_ in enumerate(srcs1):
                kb.dma("sp", R1[32 * i:32 * i + 32, :], s_.rearrange("o (a b) -> (o a) b", b=128), writes=[rR])
            srcs2 = [bada_in[l:l + 1, D:2 * D], bada_in[l:l + 1, 0:D], normg_in[l:l + 1, :]]
            for i, s_ in enumerate(srcs2):
                kb.dma("sp", R2[32 * i:32 * i + 32, :], s_.rearrange("o (a b) -> (o a) b", b=128), writes=[rR])
            for i, s_ in enumerate([convb_in[l], cvlg_in[l], cvlb_in[l], mhg_in[l]]):
                kb.dma("sp", R3[8 * i:8 * i + 8, :], s_, writes=[rR])
            kb.dma("sp", R4a[0:124, :], convw_in[l][0:124, :], writes=[rR])
            kb.dma("sp", R4b[0:124, :], convw_in[l][124:248, :], writes=[rR])
            PT1 = A.f32(128)
            PT2 = A.f32(96)
            PT3 = A.f32(32)
            CW = A.f32(248)
            rPT = Res("PT")
            transpose_rows(R1, 128, PT1, 0, rR, rPT)
            transpose_rows(R2, 96, PT2, 1, rR, rPT)
            transpose_rows(R3, 32, PT3, 2, rR, rPT)
            transpose_rows(R4a, 124, CW[:, 0:124], 3, rR, rPT)
            transpose_rows(R4b, 124, CW[:, 124:248], 4, rR, rPT)
            MOD = A.f32(128)
            for j in range(2):
                kb.op("dve", (lambda j=j: lambda e: e.scalar_tensor_tensor(
                    out=MOD[:, 64 * j:64 * j + 32], in0=PT1[:, 64 * j:64 * j + 32], scalar=1.0, in1=PT2[:, 0:32],
                    op0=ALU.add, op1=ALU.add))(), reads=[rPT], writes=[rPT])
                kb.op("dve", (lambda j=j: lambda e: e.tensor_tensor(
                    out=MOD[:, 64 * j:64 * j + 32], in0=MOD[:, 64 * j:64 * j + 32], in1=PT2[:, 64:96],
                    op=ALU.mult))(), reads=[rPT], writes=[rPT])
                kb.op("dve", (lambda j=j: lambda e: e.tensor_tensor(
                    out=MOD[:, 64 * j + 32:64 * j + 64], in0=PT1[:, 64 * j + 32:64 * j + 64], in1=PT2[:, 32:64],
                    op=ALU.add))(), reads=[rPT], writes=[rPT])
            kb.barrier()
            par_mark = A.mark()
            NTL_cur[0] = NTL
            BIG = A.f32(16 * NT)
            BIGB = BIG.bitcast(BF16)
            HT = v3(BIGB, 32)
            WT0 = A.bf16(32 * 512)
            rWT = [Res("wt0"), Res("wt1")]
            tiles = []
            for name, c0, ncol, d0 in TOKB:
                for j in range(0, ncol, 512):
                    tiles.append(("tok", name, c0 + j, min(512, ncol - j), d0 + j))
            for name, c0, r0 in CHB:
                for j in range(0, 1024, 512):
                    tiles.append(("ch", name, c0 + j, 512, r0 + j))
            wsrc = wl.rearrange("(kc kp) n -> kp kc n", kp=128)
            for g_ in range(4):
                kb.dma("pool", v3(WT0, 32)[:, g_ * 8:(g_ + 1) * 8, 0:tiles[0][3]],
                       wsrc[:, g_ * 8:(g_ + 1) * 8, tiles[0][2]:tiles[0][2] + tiles[0][3]], writes=[rWT[0]])
            big_mark = A.mark()

            XT = [A.f32(D) for _ in range(2)]
            rXT = [Res("xt0"), Res("xt1")]
            JUNK = A.bf16(D)
            rJ = Res("junk")
            SS = A.f32(8)
            rSS = Res("ss")
            for t in range(10):
                b = t % 2
                kb.dma("sp", XT[b], tok_src(t, 0, D), writes=[rXT[b]])
                kb.op("act", (lambda b=b: lambda e: e.activation(out=JUNK, in_=XT[b], func=AF.Square,
                                                                   accum_out=SS[:, 0:1]))(),
                      reads=[rXT[b]], writes=[rJ, rSS])
                kb.op("dve", lambda e: e.tensor_scalar(SS[:, 1:2], SS[:, 0:1], 1.0 / D, EPS, ALU.mult, ALU.add),
                      reads=[rSS], writes=[rSS])
                kb.op("act", lambda e: e.sqrt(SS[:, 3:4], SS[:, 1:2]), reads=[rSS], writes=[rSS])
                kb.op("dve", lambda e: e.reciprocal(SS[:, 2:3], SS[:, 3:4]), reads=[rSS], writes=[rSS])
                kb.op("act", (lambda b=b: lambda e: e.activation(out=XT[b], in_=XT[b], func=AF.Copy,
                                                                   scale=SS[:, 2:3]))(),
                      reads=[rXT[b], rSS], writes=[rXT[b]])
                mo = 0 if t < 8 else 64
                for g4 in range(8):
                    bk = g4 % 4
                    for j in range(4):
                        kc = g4 * 4 + j
                        kb.op("pe", (lambda b=b, kc=kc, bk=bk, j=j: lambda e: e.matmul(
                            bank(bk)[:, j * 128:(j + 1) * 128], lhsT=XT[b][:, kc * 128:(kc + 1) * 128], rhs=IDF,
                            start=True, stop=True))(), reads=[rXT[b], rCONST], writes=[BK[bk]])
                    for j in range(4):
                        kc = g4 * 4 + j
                        dst = HT[:, kc, t * 128:(t + 1) * 128]
                        src = bank(bk)[:, j * 128:(j + 1) * 128]
                        if g4 % 2 == 0:
                            kb.op("dve", (lambda dst=dst, src=src, kc=kc, mo=mo: lambda e: e.tensor_scalar(
                                dst, src, MOD[:, mo + kc:mo + kc + 1], MOD[:, mo + 32 + kc:mo + 33 + kc],
                                ALU.mult, ALU.add))(), reads=[BK[bk], rPT], writes=[rBIGd])
                        else:
                            kb.op("act", (lambda dst=dst, src=src, kc=kc, mo=mo: lambda e: e.activation(
                                out=dst, in_=src, func=AF.Identity, scale=MOD[:, mo + kc:mo + kc + 1],
                                bias=MOD[:, mo + 32 + kc:mo + 33 + kc]))(), reads=[BK[bk], rPT], writes=[rBIGa])
            kb.barrier()
            if dbg_HT is not None and l == 0:
                kb.dma("sp", dbg_HT.ap(), BIGB, reads=[rBIGd, rBIGa])
                kb.barrier()
            A.release(big_mark)
            if stop_after == "norm":
                return

            WT = [WT0, A.bf16(32 * 512)]
            STG = [A.f32(NT) for _ in range(2)]
            rSTG = [Res("stg0"), Res("stg1")]
            def load_w(src3, c0, ncol, b):
                w3 = v3(WT[b], 32)
                for g in range(4):
                    kb.dma("pool", w3[:, g * 8:(g + 1) * 8, 0:ncol], src3[:, g * 8:(g + 1) * 8, c0:c0 + ncol],
                           writes=[rWT[b]])

            stg_i = 0
            ev_i = 0
            for ti, (kind, name, c0, ncol, d0) in enumerate(tiles):
                b = ti % 2
                if ti + 1 < len(tiles):
                    load_w(wsrc, tiles[ti + 1][2], tiles[ti + 1][3], 1 - b)
                w3 = v3(WT[b], 32)
                if kind == "tok":
                    nt_ = 8 if (last and name in ("Av", "Cq")) else 10
                    for t in range(nt_):
                        bk = 6 + (t % 2)
                        for kc in range(32):
                            kb.op("pe", (lambda bk=bk, kc=kc, t=t, w3=w3, ncol=ncol: lambda e: e.matmul(
                                bank(bk, ncol), lhsT=HT[:, kc, t * 128:(t + 1) * 128], rhs=w3[:, kc, 0:ncol],
                                start=(kc == 0), stop=(kc == 31)))(), reads=[rBIGd, rBIGa, rWT[b]], writes=[BK[bk]])
                        sb = stg_i % 2
                        stg_i += 1
                        eng = "act" if ev_i % 2 == 0 else "dve"
                        ev_i += 1
                        if eng == "act":
                            kb.op("act", (lambda sb=sb, bk=bk, ncol=ncol: lambda e: e.copy(
                                STG[sb][:, 0:ncol], bank(bk, ncol)))(), reads=[BK[bk]], writes=[rSTG[sb]])
                        else:
                            kb.op("dve", (lambda sb=sb, bk=bk, ncol=ncol: lambda e: e.tensor_copy(
                                STG[sb][:, 0:ncol], bank(bk, ncol)))(), reads=[BK[bk]], writes=[rSTG[sb]])
                        kb.dma("sp", Ptok.ap()[t * 128:(t + 1) * 128, d0:d0 + ncol], STG[sb][:, 0:ncol],
                               reads=[rSTG[sb]])
                else:
                    ntok = NTL
                    tts = [(0, 512), (512, 512)] + ([(1024, 256)] if ntok > 1024 else [])
                    for cb in range(4):
                        bset = 3 * (cb % 2)
                        for kc in range(32):
                            for i, (t0, tn) in enumerate(tts):
                                kb.op("pe", (lambda bset=bset, i=i, kc=kc, cb=cb, t0=t0, tn=tn, w3=w3: lambda e: e.matmul(
                                    bank(bset + i, tn), lhsT=w3[:, kc, cb * 128:(cb + 1) * 128],
                                    rhs=HT[:, kc, t0:t0 + tn], start=(kc == 0), stop=(kc == 31)))(),
                                    reads=[rBIGd, rBIGa, rWT[b]], writes=[BK[bset + i]])
                        sb = stg_i % 2
                        stg_i += 1
                        for i, (t0, tn) in enumerate(tts):
                            eng = "act" if ev_i % 2 == 0 else "dve"
                            ev_i += 1
                            if eng == "act":
                                kb.op("act", (lambda sb=sb, bset=bset, i=i, t0=t0, tn=tn: lambda e: e.copy(
                                    STG[sb][:, t0:t0 + tn], bank(bset + i, tn)))(),
                                    reads=[BK[bset + i]], writes=[rSTG[sb]])
                            else:
                                kb.op("dve", (lambda sb=sb, bset=bset, i=i, t0=t0, tn=tn: lambda e: e.tensor_copy(
                                    STG[sb][:, t0:t0 + tn], bank(bset + i, tn)))(),
                                    reads=[BK[bset + i]], writes=[rSTG[sb]])
                        kb.dma("sp", Pch.ap()[d0 + cb * 128:d0 + (cb + 1) * 128, 0:ntok], STG[sb][:, 0:ntok],
                               reads=[rSTG[sb]])
            kb.barrier()
            A.release(par_mark)
            if stop_after == "gemm1":
                return
            tts = [(0, 512), (512, 512)] + ([(1024, 256)] if NTL > 1024 else [])
            KTC = A.bf16(2 * 256)
            KTC3 = v3(KTC, 2)
            rKTC = Res("ktc")
            IBFB = A.f32(32)
            GQK = A.f32(256)
            rPB = Res("pb")
            kb.dma("sp", IBFB[:, 0:16], ib_in[l:l + 1, :].partition_broadcast(128), writes=[rPB])
            kb.dma("sp", IBFB[:, 16:32], fb_in[l:l + 1, :].partition_broadcast(128), writes=[rPB])
            kb.dma("sp", GQK[:, 0:128], qng_in[l:l + 1, :].partition_broadcast(128), writes=[rPB])
            kb.dma("sp", GQK[:, 128:256], kng_in[l:l + 1, :].partition_broadcast(128), writes=[rPB])
            pers2_mark = A.mark()
            HS = A.f32(8 * NT)
            HS3 = v3(HS, 8)
            rHS = Res("hs")
            GP = A.f32(10 * 5 * 16)
            GP4 = GP.rearrange("p (t k g) -> p t k g", t=10, k=5)
            rGP = Res("gp")
            CS = A.f32(16 * 256)
            CS3 = v3(CS, 16)
            rCS = Res("cs")
            CB = A.bf16(16 * 256)
            CB3 = v3(CB, 16)
            rCB = Res("cb")
            STFB = A.f32(16 * 256)
            rSTFB = Res("stfb")
            pers_mark = A.mark()

            def bc(ap, shape):
                return ap.broadcast_to(list(shape))

            def rsqrt_ops(dst, src, scale, tmp, rr, rw):
                kb.op("dve", lambda e: e.tensor_scalar(tmp, src, scale, EPS, ALU.mult, ALU.add), reads=rr, writes=rw)
                kb.op("act", lambda e: e.sqrt(tmp, tmp), reads=rw, writes=rw)
                kb.op("dve", lambda e: e.reciprocal(dst, tmp), reads=rw, writes=rw)

            def rope_ops(src, dst, H, t, T1, T2, rs, rd, rt):
                s5 = src.rearrange("p (h a c j) -> p h a c j", h=H, a=2, c=2)
                d5 = dst.rearrange("p (h a c j) -> p h a c j", h=H, a=2, c=2)
                x1, x2 = s5[:, :, :, 0, :], s5[:, :, :, 1, :]
                o1, o2 = d5[:, :, :, 0, :], d5[:, :, :, 1, :]
                cs = bc(ROPE3[:, t, 0:64].rearrange("p (a j) -> p a j", a=2).unsqueeze(1), [128, H, 2, 32])
                sn = bc(ROPE3[:, t, 64:128].rearrange("p (a j) -> p a j", a=2).unsqueeze(1), [128, H, 2, 32])
                t1 = T1.rearrange("p (h a j) -> p h a j", h=H, a=2)
                t2 = T2.rearrange("p (h a j) -> p h a j", h=H, a=2)
                kb.op("dve", lambda e: e.tensor_tensor(o1, x1, cs, ALU.mult), reads=[rs, rCONST], writes=[rd])
                kb.op("pool", lambda e: e.tensor_tensor(t1, x2, sn, ALU.mult), reads=[rs, rCONST], writes=[rt])
                kb.op("dve", lambda e: e.tensor_tensor(o1, o1, t1, ALU.subtract), reads=[rt], writes=[rd])
                kb.op("dve", lambda e: e.tensor_tensor(o2, x2, cs, ALU.mult), reads=[rs, rCONST], writes=[rd])
                kb.op("pool", lambda e: e.tensor_tensor(t2, x1, sn, ALU.mult), reads=[rs, rCONST], writes=[rt])
                kb.op("dve", lambda e: e.tensor_tensor(o2, o2, t2, ALU.add), reads=[rt], writes=[rd])

            def qk_norm_rope(src, H, gq, t, NRM, ROT, TS, SSQ, T1, T2, rsrc, rw):
                kb.op("act", lambda e: e.activation(out=TS, in_=src, func=AF.Square), reads=[rsrc], writes=[rw])
                kb.op("dve", lambda e: e.tensor_reduce(out=SSQ[:, 0:H], in_=v3(TS, H), axis=AX.X, op=ALU.add),
                      reads=[rw], writes=[rw])
                rsqrt_ops(SSQ[:, 16:16 + H], SSQ[:, 0:H], 1.0 / 128, SSQ[:, 8:8 + H], [rw], [rw])
                kb.op("dve", lambda e: e.tensor_tensor(v3(NRM, H), v3(src, H), bc(SSQ[:, 16:16 + H].unsqueeze(2), [128, H, 128]),
                                                       ALU.mult), reads=[rsrc, rw], writes=[rw])
                kb.op("dve", lambda e: e.tensor_tensor(v3(NRM, H), v3(NRM, H), bc(gq.unsqueeze(1), [128, H, 128]), ALU.mult),
                      reads=[rw, rPB], writes=[rw])
                if t < 8:
                    rope_ops(NRM, ROT, H, t, T1, T2, rw, rw, rw)
                else:
                    kb.op("dve", lambda e: e.tensor_copy(ROT, NRM), reads=[rw], writes=[rw])

            GG = A.f32(320)
            XF = A.f32(160)
            IG = A.f32(160)
            LF = A.f32(160)
            rGG = Res("gg")
            G5 = GG.rearrange("p (t d i h) -> p t d i h", t=10, d=2, i=2)
            x4 = lambda ap: ap.rearrange("p (t d h) -> p t d h", t=10, d=2)
            kb.dma("sp", v3(GG, 10), Ptok.ap()[:, 4608:4640].rearrange("(t p) c -> p t c", p=128), writes=[rGG])
            kb.op("dve", lambda e: e.tensor_tensor(x4(XF), G5[:, :, :, 1, :], bc(v3(IBFB[:, 16:32], 2).unsqueeze(1), [128, 10, 2, 8]),
                                                   ALU.add), reads=[rGG, rPB], writes=[rGG])
            kb.op("dve", lambda e: e.tensor_tensor(x4(IG), G5[:, :, :, 0, :], bc(v3(IBFB[:, 0:16], 2).unsqueeze(1), [128, 10, 2, 8]),
                                                   ALU.add), reads=[rGG, rPB], writes=[rGG])
            kb.op("act", lambda e: e.activation(out=LF, in_=XF, func=AF.Exp, scale=-1.0), reads=[rGG], writes=[rGG])
            kb.op("dve", lambda e: e.tensor_scalar_add(LF, LF, 1.0), reads=[rGG], writes=[rGG])
            kb.op("act", lambda e: e.activation(out=LF, in_=LF, func=AF.Ln), reads=[rGG], writes=[rGG])
            kb.op("dve", lambda e: e.tensor_scalar_mul(LF, LF, -1.0), reads=[rGG], writes=[rGG])
            LF3 = v3(LF, 10)
            for t in range(10):
                kb.op("pe", (lambda t=t: lambda e: e.matmul(bank(1)[:, t * 32:t * 32 + 8], lhsT=TRIF, rhs=LF3[:, t, 0:8],
                                                            start=True, stop=True))(), reads=[rGG, rCONST], writes=[BK[1]])
                kb.op("pe", (lambda t=t: lambda e: e.matmul(bank(1)[:, t * 32 + 8:t * 32 + 16], lhsT=TRIB, rhs=LF3[:, t, 8:16],
                                                            start=True, stop=True))(), reads=[rGG, rCONST], writes=[BK[1]])
                kb.op("pe", (lambda t=t: lambda e: e.matmul(bank(1)[:, t * 32 + 16:t * 32 + 32], lhsT=ONF, rhs=LF3[:, t, 0:16],
                                                            start=True, stop=True))(), reads=[rGG, rCONST], writes=[BK[1]])
            PS1 = v3(bank(1, 320), 10)
            kb.op("dve", lambda e: e.tensor_copy(GP4[:, :, 0, :], PS1[:, :, 0:16]), reads=[BK[1]], writes=[rGP])
            kb.op("dve", lambda e: e.tensor_copy(GP4[:, :, 3, :], PS1[:, :, 16:32]), reads=[BK[1]], writes=[rGP])
            kb.op("dve", lambda e: e.scalar_tensor_tensor(out=GP4[:, :, 1, :], in0=v3(IG, 10), scalar=LNSC, in1=GP4[:, :, 0, :],
                                                          op0=ALU.add, op1=ALU.subtract), reads=[rGG, rGP], writes=[rGP])
            kb.op("dve", lambda e: e.tensor_tensor(v3(XF, 10), GP4[:, :, 3, :], GP4[:, :, 1, :], ALU.add), reads=[rGP], writes=[rGG])
            kb.op("act", lambda e: e.activation(out=GP4[:, :, 2, :], in_=v3(XF, 10), func=AF.Exp), reads=[rGG], writes=[rGP])
            kb.op("act", lambda e: e.activation(out=GP4[:, :, 4, :], in_=GP4[:, :, 3, :], func=AF.Exp), reads=[rGP], writes=[rGP])
            dump("d_GP", GP, [128, 800], reads=[rGP])
            A.release(pers_mark)

            KTOK = [A.f32(1024) for _ in range(2)]
            VTOK = [A.f32(1024) for _ in range(2)]
            rKTOK = [Res("ktok0"), Res("ktok1")]
            rVTOK = [Res("vtok0"), Res("vtok1")]
            KTLs = [A.bf16(1024) for _ in range(2)]
            rKTLs = [Res("ktl0"), Res("ktl1")]
            VA = [A.bf16(8 * 256) for _ in range(2)]
            rVA = [Res("va0"), Res("va1")]
            QTb = A.bf16(1024)
            KTb = A.bf16(1024)
            rQKb = Res("qkb")
            DG = A.f32(1024)
            rDG = Res("dg")
            EB = A.f32(1024)
            rEB = Res("eb")
            WTm = A.f32(1024)
            rWTm = Res("wtm")
            PTb = A.bf16(1024)
            rPTb = Res("ptb")
            QTL = A.bf16(1024)
            rQTL = Res("qtl")
            DN = A.f32(1024)
            rDN = Res("dn")
            TMPH = A.f32(1024)
            rTMPH = Res("tmph")
            for b in range(2):
                kb.op("pool", (lambda b=b: lambda e: e.memset(v3(VA[b], 8)[:, :, 128:256], 1.0))(), writes=[rVA[b]])
            scan_ctr = [0]

            def scan(chunks, dr, emit):
                g0 = dr * 8
                MASK = TRIF if dr == 0 else TRIB
                for t in chunks:
                    b = scan_ctr[0] % 2
                    scan_ctr[0] += 1
                    va3 = v3(VA[b], 8)
                    KTL = KTLs[b]
                    rKTL = rKTLs[b]
                    stb = 2 if (emit or b == 0) else 4
                    kb.dma("sp", KTOK[b], Ptok.ap()[t * 128:(t + 1) * 128, 2560:3584], writes=[rKTOK[b]])
                    kb.dma("sp", VTOK[b], Ptok.ap()[t * 128:(t + 1) * 128, 3584:4608], writes=[rVTOK[b]])
                    kb.op("dve", (lambda b=b, t=t, KTL=KTL: lambda e: e.tensor_tensor(
                        v3(KTL, 8), v3(KTOK[b], 8), bc(GP4[:, t, 2, g0:g0 + 8].unsqueeze(2), [128, 8, 128]), ALU.mult))(),
                        reads=[rKTOK[b], rGP], writes=[rKTL])
                    kb.op("act", (lambda b=b, va3=va3: lambda e: e.copy(va3[:, :, 0:128], v3(VTOK[b], 8)))(),
                          reads=[rVTOK[b]], writes=[rVA[b]])
                    if emit:
                        kb.dma("pool", v3(QTb, 8), Pch.ap()[CR["Dq"]:CR["Dq"] + 1024, t * 128:(t + 1) * 128].rearrange(
                            "(h d) t -> d h t", d=128), writes=[rQKb])
                        kb.dma("pool", v3(KTb, 8), Pch.ap()[CR["DkT"]:CR["DkT"] + 1024, t * 128:(t + 1) * 128].rearrange(
                            "(h d) t -> d h t", d=128), writes=[rQKb])
                        for h in range(8):
                            kb.op("pe", (lambda h=h: lambda e: e.matmul(
                                PS[:, h * 128:(h + 1) * 128], lhsT=v3(KTb, 8)[:, h, :], rhs=v3(QTb, 8)[:, h, :],
                                start=True, stop=True))(), reads=[rQKb], writes=[BK[h // 4]])
                        kb.op("dve", (lambda t=t: lambda e: e.tensor_tensor(
                            v3(DG, 8), bc(IDF.unsqueeze(1), [128, 8, 128]),
                            bc(GP4[:, t, 0, g0:g0 + 8].unsqueeze(2), [128, 8, 128]), ALU.mult))(),
                            reads=[rGP, rCONST], writes=[rDG])
                        for j in range(2):
                            kb.op("pe", (lambda j=j: lambda e: e.matmul(
                                PS[:, 1024 + j * 512:1536 + j * 512], lhsT=ONF, rhs=DG[:, j * 512:(j + 1) * 512],
                                start=True, stop=True))(), reads=[rDG, rCONST], writes=[BK[2 + j]])
                            kb.op("act", (lambda j=j: lambda e: e.activation(
                                out=EB[:, j * 512:(j + 1) * 512], in_=PS[:, 1024 + j * 512:1536 + j * 512], func=AF.Exp))(),
                                reads=[BK[2 + j]], writes=[rEB])
                        for h in range(8):
                            kb.op("act", (lambda h=h, t=t: lambda e: e.activation(
                                out=WTm[:, h * 128:(h + 1) * 128], in_=PS[:, 1024 + h * 128:1152 + h * 128], func=AF.Exp,
                                bias=GP4[:, t, 1, g0 + h:g0 + h + 1]))(), reads=[BK[2 + h // 4], rGP], writes=[rWTm])
                        kb.op("pool", lambda e: e.tensor_tensor(v3(WTm, 8), v3(WTm, 8), bc(MASK.unsqueeze(1), [128, 8, 128]),
                                                                ALU.mult), reads=[rCONST], writes=[rWTm])
                        kb.op("dve", lambda e: e.tensor_tensor(PTb, PS[:, 0:1024], WTm, ALU.mult),
                              reads=[BK[0], BK[1], rWTm], writes=[rPTb])
                        kb.op("pool", lambda e: e.tensor_tensor(QTL, QTb, EB, ALU.mult), reads=[rQKb, rEB], writes=[rQTL])
                        for h in range(8):
                            kb.op("pe", (lambda h=h, va3=va3: lambda e: e.matmul(
                                PS[:, 2048 + h * 128:2176 + h * 128], lhsT=va3[:, h, 0:128], rhs=v3(PTb, 8)[:, h, :],
                                start=True, stop=False))(), reads=[rVA[b], rPTb], writes=[BK[4 + h // 4]])
                            kb.op("pe", (lambda h=h: lambda e: e.matmul(
                                PS[:, 2048 + h * 128:2176 + h * 128], lhsT=CB3[:, g0 + h, 0:128], rhs=v3(QTL, 8)[:, h, :],
                                start=False, stop=True))(), reads=[rCB, rQTL], writes=[BK[4 + h // 4]])
                        for h in range(8):
                            kb.op("pe", (lambda h=h: lambda e: e.matmul(
                                PS[:, 3072 + h * 128:3200 + h * 128], lhsT=ONB, rhs=v3(PTb, 8)[:, h, :],
                                start=True, stop=False))(), reads=[rCONST, rPTb], writes=[BK[6 + h // 4]])
                            kb.op("pe", (lambda h=h: lambda e: e.matmul(
                                PS[:, 3072 + h * 128:3200 + h * 128], lhsT=CB3[:, g0 + h, 128:256], rhs=v3(QTL, 8)[:, h, :],
                                start=False, stop=True))(), reads=[rCB, rQTL], writes=[BK[6 + h // 4]])
                        kb.op("act", lambda e: e.activation(out=DN, in_=PS[:, 3072:4096], func=AF.Abs),
                              reads=[BK[6], BK[7]], writes=[rDN])
                        kb.op("dve", lambda e: e.tensor_scalar_max(DN, DN, 1.0), reads=[rDN], writes=[rDN])
                        kb.op("dve", lambda e: e.reciprocal(DN, DN), reads=[rDN], writes=[rDN])
                        hs_sl = HS3[:, :, t * 128:(t + 1) * 128]
                        if dr == 0:
                            kb.op("dve", (lambda hs_sl=hs_sl: lambda e: e.tensor_tensor(hs_sl, v3(PS[:, 2048:3072], 8), v3(DN, 8),
                                                                                      ALU.mult))(),
                                  reads=[BK[4], BK[5], rDN], writes=[rHS])
                        else:
                            kb.op("dve", lambda e: e.tensor_tensor(TMPH, PS[:, 2048:3072], DN, ALU.mult),
                                  reads=[BK[4], BK[5], rDN], writes=[rTMPH])
                            kb.op("pool", (lambda hs_sl=hs_sl: lambda e: e.tensor_tensor(hs_sl, hs_sl, v3(TMPH, 8), ALU.add))(),
                                  reads=[rTMPH, rHS], writes=[rHS])
                    kb.op("dve", (lambda t=t: lambda e: e.tensor_tensor(
                        CS3[:, g0:g0 + 8, :], CS3[:, g0:g0 + 8, :], bc(GP4[:, t, 4, g0:g0 + 8].unsqueeze(2), [128, 8, 256]),
                        ALU.mult))(), reads=[rGP, rCS], writes=[rCS])
                    for half in range(2):
                        for hh in range(4):
                            h = half * 4 + hh
                            kb.op("pe", (lambda h=h, hh=hh, va3=va3, KTL=KTL, stb=stb: lambda e: e.matmul(
                                PS[:, stb * 512 + hh * 256:stb * 512 + 256 + hh * 256], lhsT=v3(KTL, 8)[:, h, :], rhs=va3[:, h, :],
                                start=True, stop=True))(), reads=[rKTL, rVA[b]], writes=[BK[stb + hh // 2]])
                        kb.op("dve", (lambda half=half, stb=stb: lambda e: e.tensor_tensor(
                            CS3[:, g0 + half * 4:g0 + half * 4 + 4, :], CS3[:, g0 + half * 4:g0 + half * 4 + 4, :],
                            v3(PS[:, stb * 512:stb * 512 + 1024], 4), ALU.add))(), reads=[BK[stb], BK[stb + 1], rCS], writes=[rCS])
                    if emit:
                        kb.op("act", lambda e: e.copy(CB3[:, g0:g0 + 8, :], CS3[:, g0:g0 + 8, :]), reads=[rCS], writes=[rCB])

            def zero_state():
                kb.op("dve", lambda e: e.memset(CS, 0.0), writes=[rCS])
                kb.op("pool", lambda e: e.memset(CB, 0.0), writes=[rCB])

            zero_state()
            scan([8, 9], 0, not last)
            scan([9, 8], 1, not last)
            kb.op("dve", lambda e: e.tensor_copy(STFB, CS), reads=[rCS], writes=[rSTFB])
            dump("d_STFB", STFB, [128, 4096], reads=[rSTFB])
            dump("d_HSctx", HS, [128, 8 * NT], reads=[rHS])
            zero_state()
            scan(list(range(8)), 0, False)
            scan(list(range(7, -1, -1)), 1, False)
            for dr_ in range(2):
                kb.dma("sp", exr(2 + dr_, 0, 128 * 2048).rearrange("(p x) -> p x", p=128), CS[:, dr_ * 2048:(dr_ + 1) * 2048],
                       reads=[rCS])
            DTT = A.f32(16)
            rDTT = Res("dtt")
            kb.op("dve", lambda e: e.tensor_reduce(out=DTT, in_=GP4[:, 0:8, 3, :].rearrange("p t g -> p g t"), axis=AX.X,
                                                   op=ALU.add), reads=[rGP], writes=[rDTT])
            kb.dma("sp", exr(4, 0, 128 * 16).rearrange("(p x) -> p x", p=128), DTT, reads=[rDTT])
            scan_mark = A.mark()

            if stop_after == "pre":
                return
            kb.issue_collectives("AllGather", ALU.bypass, GROUPS,
                                 [(exp_bufs[c_].ap().opt(), gat_bufs[c_].ap().opt()) for c_ in (2, 3, 4)])
            m2_mark = A.mark()
            KV = [A.f32(512) for _ in range(2)]
            rKV = [Res("kv0"), Res("kv1")]
            KTE = A.f32(2 * 1024)
            KTE3 = v3(KTE, 2)
            rKTE = Res("kte")
            KWS = []
            for i_ in range(2):
                KWS.append(dict(KN=A.f32(256), KR=A.f32(256), KTS=A.f32(256), KT1=A.f32(128), KT2=A.f32(128),
                                KRB=A.bf16(256), SSK=A.f32(24), r=Res("kw%d" % i_)))
            for t in range(10):
                b = t % 2
                W_ = KWS[b]
                bk_ = 4 * b
                kb.dma("sp", KV[b], Ptok.ap()[t * 128:(t + 1) * 128, 2048:2560], writes=[rKV[b]])
                qk_norm_rope(KV[b][:, 0:256], 2, GQK[:, 128:256], t, W_["KN"], W_["KR"], W_["KTS"], W_["SSK"], W_["KT1"], W_["KT2"],
                             rKV[b], W_["r"])
                kb.op("act", (lambda W_=W_: lambda e: e.copy(W_["KRB"], W_["KR"]))(), reads=[W_["r"]], writes=[W_["r"]])
                for g in range(2):
                    kb.op("pe", (lambda g=g, W_=W_, bk_=bk_: lambda e: e.matmul(
                        bank(bk_)[:, g * 128:(g + 1) * 128], lhsT=W_["KRB"][:, g * 128:(g + 1) * 128], rhs=IDB,
                        start=True, stop=True))(), reads=[W_["r"], rCONST], writes=[BK[bk_]])
                if t < 8:
                    kb.op("act", (lambda t=t, bk_=bk_: lambda e: e.copy(KTE3[:, :, t * 128:(t + 1) * 128], v3(bank(bk_, 256), 2)))(),
                          reads=[BK[bk_]], writes=[rKTE])
                else:
                    kb.op("act", (lambda t=t, bk_=bk_: lambda e: e.copy(KTC3[:, :, (t - 8) * 128:(t - 7) * 128],
                                                                      v3(bank(bk_, 256), 2)))(), reads=[BK[bk_]], writes=[rKTC])
            kb.dma("sp", exr(0, 0, 128 * 2048).rearrange("(d x) -> d x", d=128), KTE, reads=[rKTE])
            kb.dma("sp", exr(1, 0, 1024 * 256).rearrange("(t c) -> t c", c=256), Ptok.ap()[0:1024, 2304:2560])
            A.release(m2_mark)

            ATb = [A.f32(NT) for _ in range(2)]
            GTb = [A.f32(NT) for _ in range(2)]
            rATb = [Res("at0"), Res("at1")]
            rGTb = [Res("gt0"), Res("gt1")]
            halo_ex = exr(5, 0, 128 * 240).rearrange("(c g j) -> c g j", c=128, g=8)
            for cg in range(8):
                b = cg % 2
                kb.dma("sp", ATb[b][:, 0:NTL], Pch.ap()[CR["Ba"] + cg * 128:CR["Ba"] + (cg + 1) * 128, 0:NTL], writes=[rATb[b]])
                kb.dma("sp", GTb[b][:, 0:NTL], Pch.ap()[CR["Bg"] + cg * 128:CR["Bg"] + (cg + 1) * 128, 0:NTL], writes=[rGTb[b]])
                kb.op("act", (lambda b=b: lambda e: e.activation(out=GTb[b][:, 0:NTL], in_=GTb[b][:, 0:NTL], func=AF.Sigmoid))(),
                      reads=[rGTb[b]], writes=[rGTb[b]])
                kb.op("dve", (lambda b=b: lambda e: e.tensor_tensor(ATb[b][:, 0:NTL], ATb[b][:, 0:NTL], GTb[b][:, 0:NTL], ALU.mult))(),
                      reads=[rGTb[b], rATb[b]], writes=[rATb[b]])
                kb.dma("sp", Ych.ap()[cg * 128:(cg + 1) * 128, 0:NTL], ATb[b][:, 0:NTL], reads=[rATb[b]])
                kb.dma("sp", halo_ex[:, cg, 0:15], ATb[b][:, 0:15], reads=[rATb[b]])
                kb.dma("sp", halo_ex[:, cg, 15:30], ATb[b][:, 1009:1024], reads=[rATb[b]])
            A.release(m2_mark)

            kb.barrier()
            if stop_after == "gather":
                return
            SG = [A.f32(8 * 256) for _ in range(4)]
            rSG = Res("sg")
            DJ = A.f32(64)
            FF = A.f32(8 * 256)
            rFF = Res("ff")
            for r in range(4):
                kb.dma("sp", DJ[:, r * 16:(r + 1) * 16], gar(4, r, 0, 128 * 16).rearrange("(p x) -> p x", p=128), writes=[rSG])
            kb.op("act", lambda e: e.activation(out=DJ, in_=DJ, func=AF.Exp), reads=[rSG], writes=[rSG])
            for dr in range(2):
                for r in range(4):
                    kb.dma("sp", SG[r], gar(2 + dr, r, 0, 128 * 2048).rearrange("(p x) -> p x", p=128), writes=[rSG])
                CSd = CS[:, dr * 2048:(dr + 1) * 2048]
                kb.op("dve", (lambda dr=dr: lambda e: e.tensor_copy(FF, STFB[:, dr * 2048:(dr + 1) * 2048]))(),
                      reads=[rSTFB], writes=[rFF])
                order = [0, 1, 2] if dr == 0 else [3, 2, 1]
                fs = 0 if dr == 0 else 3
                kb.op("dve", (lambda CSd=CSd, fs=fs: lambda e: e.tensor_scalar_mul(CSd, FF, SEL[:, fs:fs + 1]))(),
                      reads=[rFF, rCONST], writes=[rCS])
                for j in order:
                    for h in range(8):
                        kb.op("dve", (lambda j=j, h=h, dr=dr: lambda e: e.scalar_tensor_tensor(
                            out=v3(FF, 8)[:, h, :], in0=v3(FF, 8)[:, h, :],
                            scalar=DJ[:, j * 16 + dr * 8 + h:j * 16 + dr * 8 + h + 1], in1=v3(SG[j], 8)[:, h, :],
                            op0=ALU.mult, op1=ALU.add))(), reads=[rSG, rFF], writes=[rFF])
                    nx = j + 1 if dr == 0 else j - 1
                    kb.op("dve", (lambda CSd=CSd, nx=nx: lambda e: e.scalar_tensor_tensor(
                        out=CSd, in0=FF, scalar=SEL[:, nx:nx + 1], in1=CSd, op0=ALU.mult, op1=ALU.add))(),
                        reads=[rFF, rCONST, rCS], writes=[rCS])
            kb.op("act", lambda e: e.copy(CB, CS), reads=[rCS], writes=[rCB])
            dump("d_INIT", CS, [128, 4096], reads=[rCS])
            kb.issue_collectives("AllGather", ALU.bypass, GROUPS,
                                 [(exp_bufs[c_].ap().opt(), gat_bufs[c_].ap().opt()) for c_ in (0, 1, 5)])
            scan(list(range(8)), 0, True)
            scan(list(range(7, -1, -1)), 1, True)
            dump("d_HS", HS, [128, 8 * NT], reads=[rHS])
            A.release(pers_mark)
            OTb = [A.f32(NT) for _ in range(2)]
            ZTb = [A.f32(NT) for _ in range(2)]
            rOTb = [Res("ot0"), Res("ot1")]
            rZTb = [Res("zt0"), Res("zt1")]
            SQs = [A.f32(NT) for _ in range(2)]
            rSQs = [Res("sq0"), Res("sq1")]
            RSs = [A.f32(NT) for _ in range(2)]
            rRSs = [Res("rs0"), Res("rs1")]
            for h in range(8):
                b = h % 2
                SQ, rSQ, RS, rRS = SQs[b], rSQs[b], RSs[b], rRSs[b]
                kb.dma("sp", OTb[b][:, 0:NTL], Pch.ap()[CR["Do"] + h * 128:CR["Do"] + (h + 1) * 128, 0:NTL], writes=[rOTb[b]])
                kb.dma("sp", ZTb[b][:, 0:NTL], Pch.ap()[CR["Dz"] + h * 128:CR["Dz"] + (h + 1) * 128, 0:NTL], writes=[rZTb[b]])
                kb.op("act", (lambda h=h, SQ=SQ: lambda e: e.activation(out=SQ[:, 0:NTL], in_=HS3[:, h, 0:NTL], func=AF.Square))(),
                      reads=[rHS], writes=[rSQ])
                for i, (t0, tn) in enumerate(tts):
                    bk = 3 * b + i
                    kb.op("pe", (lambda bk=bk, t0=t0, tn=tn, SQ=SQ: lambda e: e.matmul(bank(bk, tn), lhsT=ONF, rhs=SQ[:, t0:t0 + tn],
                                                                                   start=True, stop=True))(),
                          reads=[rSQ, rCONST], writes=[BK[bk]])
                    kb.op("dve", (lambda bk=bk, t0=t0, tn=tn, RS=RS: lambda e: e.tensor_scalar(RS[:, t0:t0 + tn], bank(bk, tn), 1.0 / 128, EPS,
                                                                                         ALU.mult, ALU.add))(),
                          reads=[BK[bk]], writes=[rRS])
                kb.op("act", (lambda RS=RS: lambda e: e.sqrt(RS[:, 0:NTL], RS[:, 0:NTL]))(), reads=[rRS], writes=[rRS])
                kb.op("dve", (lambda RS=RS: lambda e: e.reciprocal(RS[:, 0:NTL], RS[:, 0:NTL]))(), reads=[rRS], writes=[rRS])
                kb.op("dve", (lambda h=h, SQ=SQ, RS=RS: lambda e: e.tensor_tensor(SQ[:, 0:NTL], HS3[:, h, 0:NTL], RS[:, 0:NTL], ALU.mult))(),
                      reads=[rHS, rRS, rSQ], writes=[rSQ])
                kb.op("act", (lambda b=b: lambda e: e.activation(out=OTb[b][:, 0:NTL], in_=OTb[b][:, 0:NTL], func=AF.Sigmoid))(),
                      reads=[rOTb[b]], writes=[rOTb[b]])
                kb.op("act", (lambda b=b: lambda e: e.activation(out=ZTb[b][:, 0:NTL], in_=ZTb[b][:, 0:NTL], func=AF.Silu))(),
                      reads=[rZTb[b]], writes=[rZTb[b]])
                kb.op("pool", (lambda b=b, SQ=SQ: lambda e: e.tensor_tensor(SQ[:, 0:NTL], SQ[:, 0:NTL], OTb[b][:, 0:NTL], ALU.mult))(),
                      reads=[rOTb[b], rSQ], writes=[rSQ])
                stg, rstg = mix_stage()
                kb.op("dve", (lambda b=b, h=h, stg=stg, SQ=SQ: lambda e: e.scalar_tensor_tensor(
                    out=stg[:, 0:NTL], in0=SQ[:, 0:NTL], scalar=PT3[:, 24 + h:25 + h], in1=ZTb[b][:, 0:NTL],
                    op0=ALU.mult, op1=ALU.mult))(), reads=[rSQ, rZTb[b], rPT], writes=[rstg])
                mix_store(24 + h, stg, rstg)
            kb.barrier()
            A.release(pers2_mark)
            LNG = A.f32(1024)
            LNB = A.f32(1024)
            SBI = A.f32(1024)
            rAP = Res("ap")
            kb.dma("sp", LNG, sglg_in[l:l + 1, :].partition_broadcast(128), writes=[rAP])
            kb.dma("sp", LNB, sglb_in[l:l + 1, :].partition_broadcast(128), writes=[rAP])
            kb.dma("sp", SBI, sgub_in[l:l + 1, :].partition_broadcast(128), writes=[rAP])
            WSF = [A.f32(128) for _ in range(2)]
            rWSF = [Res("wsf0"), Res("wsf1")]
            WST = A.bf16(1024)
            rWST = Res("wst")
            for h in range(8):
                b = h % 2
                kb.dma("sp", WSF[b], sguw_in[l, h], writes=[rWSF[b]])
                kb.op("pe", (lambda b=b: lambda e: e.matmul(bank(b, 128), lhsT=WSF[b], rhs=IDF, start=True, stop=True))(),
                      reads=[rWSF[b], rCONST], writes=[BK[b]])
                kb.op("dve", (lambda b=b, h=h: lambda e: e.tensor_copy(WST[:, h * 128:(h + 1) * 128], bank(b, 128)))(),
                      reads=[BK[b]], writes=[rWST])
            VN = A.bf16(TL * 1024)
            VN3 = v3(VN, TL)
            rVN = Res("vn")
            VT = [A.f32(1024) for _ in range(2)]
            rVT = [Res("vt0"), Res("vt1")]
            AJ = A.f32(1024)
            rAJ = Res("aj")
            ST = A.f32(16)
            rST = Res("st")
            for t in range(TL):
                b = t % 2
                kb.dma("sp", VT[b], Ptok.ap()[t * 128:(t + 1) * 128, 0:1024], writes=[rVT[b]])
                kb.op("act", (lambda b=b: lambda e: e.activation(out=AJ, in_=VT[b], func=AF.Copy, accum_out=ST[:, 0:1]))(),
                      reads=[rVT[b]], writes=[rAJ, rST])
                kb.op("act", (lambda b=b: lambda e: e.activation(out=AJ, in_=VT[b], func=AF.Square, accum_out=ST[:, 1:2]))(),
                      reads=[rVT[b]], writes=[rAJ, rST])
                kb.op("dve", lambda e: e.tensor_scalar_mul(ST[:, 2:4], ST[:, 0:2], 1.0 / 1024), reads=[rST], writes=[rST])
                kb.op("dve", lambda e: e.tensor_tensor(ST[:, 4:5], ST[:, 2:3], ST[:, 2:3], ALU.mult), reads=[rST], writes=[rST])
                kb.op("dve", lambda e: e.tensor_tensor(ST[:, 5:6], ST[:, 3:4], ST[:, 4:5], ALU.subtract), reads=[rST], writes=[rST])
                rsqrt_ops(ST[:, 7:8], ST[:, 5:6], 1.0, ST[:, 6:7], [rST], [rST])
                kb.op("dve", (lambda b=b: lambda e: e.tensor_scalar(VT[b], VT[b], ST[:, 2:3], ST[:, 7:8], ALU.subtract, ALU.mult))(),
                      reads=[rST, rVT[b]], writes=[rVT[b]])
                kb.op("pool", (lambda b=b: lambda e: e.tensor_tensor(VT[b], VT[b], LNG, ALU.mult))(), reads=[rAP, rVT[b]], writes=[rVT[b]])
                kb.op("dve", (lambda b=b, t=t: lambda e: e.tensor_tensor(VN3[:, t, :], VT[b], LNB, ALU.add))(),
                      reads=[rAP, rVT[b]], writes=[rVN])
            UT = [A.f32(NT) for _ in range(2)]
            ZT2 = [A.f32(NT) for _ in range(2)]
            rUT = [Res("ut0"), Res("ut1")]
            rZT2 = [Res("zt20"), Res("zt21")]
            TMA = A.f32(NT)
            rTMA = Res("tma")
            for h in range(8):
                b = h % 2
                kb.dma("sp", UT[b][:, 0:NTL], Pch.ap()[CR["Au"] + h * 128:CR["Au"] + (h + 1) * 128, 0:NTL], writes=[rUT[b]])
                kb.dma("sp", ZT2[b][:, 0:NTL], Pch.ap()[CR["Az"] + h * 128:CR["Az"] + (h + 1) * 128, 0:NTL], writes=[rZT2[b]])
                kb.op("act", (lambda b=b: lambda e: e.activation(out=ZT2[b][:, 0:NTL], in_=ZT2[b][:, 0:NTL], func=AF.Silu))(),
                      reads=[rZT2[b]], writes=[rZT2[b]])
                kb.op("pool", (lambda b=b: lambda e: e.tensor_tensor(UT[b][:, 0:NTL], UT[b][:, 0:NTL], ZT2[b][:, 0:NTL], ALU.mult))(),
                      reads=[rZT2[b], rUT[b]], writes=[rUT[b]])
                for t in range(TL):
                    kb.op("pe", (lambda h=h, t=t: lambda e: e.matmul(
                        PS[:, t * 128:(t + 1) * 128], lhsT=VN3[:, t, h * 128:(h + 1) * 128], rhs=WST[:, h * 128:(h + 1) * 128],
                        start=True, stop=True))(), reads=[rVN, rWST], writes=[BK[t // 4]])
                kb.op("dve", (lambda h=h: lambda e: e.tensor_tensor(
                    v3(TMA[:, 0:NTL], TL), v3(PS[:, 0:NTL], TL), bc(SBI[:, h * 128:(h + 1) * 128].unsqueeze(1), [128, TL, 128]),
                    ALU.add))(), reads=[BK[0], BK[1], BK[2], rAP], writes=[rTMA])
                stg, rstg = mix_stage()
                kb.op("dve", (lambda b=b, stg=stg: lambda e: e.tensor_tensor(stg[:, 0:NTL], TMA[:, 0:NTL], UT[b][:, 0:NTL], ALU.mult))(),
                      reads=[rTMA, rUT[b]], writes=[rstg])
                mix_store(h, stg, rstg)
            kb.barrier()
            A.release(pers2_mark)

            HG = A.f32(4 * 240)
            rHG = Res("hg")
            for r in range(4):
                kb.dma("sp", HG[:, r * 240:(r + 1) * 240], gar(5, r, 0, 128 * 240).rearrange("(c x) -> c x", c=128), writes=[rHG])
            HG4 = HG.rearrange("p (r g j) -> p r g j", r=4, g=8)
            LH = A.f32(8 * 15)
            RH = A.f32(8 * 15)
            rLR = Res("lr")
            for r in range(4):
                for (dst, so, j0) in ((LH, 4, 15), (RH, 8, 0)):
                    src = HG4[:, r, :, j0:j0 + 15]
                    if r == 0:
                        kb.op("dve", (lambda dst=dst, src=src, so=so, r=r: lambda e: e.tensor_scalar_mul(
                            v3(dst, 8), src, SEL[:, so + r:so + r + 1]))(), reads=[rHG, rCONST], writes=[rLR])
                    else:
                        kb.op("dve", (lambda dst=dst, src=src, so=so, r=r: lambda e: e.scalar_tensor_tensor(
                            out=v3(dst, 8), in0=src, scalar=SEL[:, so + r:so + r + 1], in1=v3(dst, 8), op0=ALU.mult, op1=ALU.add))(),
                            reads=[rHG, rCONST, rLR], writes=[rLR])
            CONV = A.f32(8 * NT)
            CONV3 = v3(CONV, 8)
            rCONV = [Res("conv%d" % i) for i in range(8)]
            YP = [A.bf16(1054 + 286) for _ in range(2)]
            rYP = [Res("yp0"), Res("yp1")]
            DIAG = [A.bf16(31 * 128) for _ in range(2)]
            rDIAG = [Res("diag0"), Res("diag1")]
            CWr = A.f32(248)
            rCWr = Res("cwr")
            kb.op("dve", lambda e: e.tensor_copy(v3(CWr, 8), v3(CW, 31).rearrange("p k g -> p g k")), reads=[rPT], writes=[rCWr])
            SQB = A.f32(NT)
            rSQB = Res("sqb")
            for b in range(2):
                kb.op("pool", (lambda b=b: lambda e: e.memset(YP[b], 0.0))(), writes=[rYP[b]])
            cvb = 0
            for cg in range(8):
                b = cg % 2
                kb.dma("pool", YP[b][:, 15:1039], Ych.ap()[cg * 128:(cg + 1) * 128, 0:1024], writes=[rYP[b]])
                if not last:
                    kb.dma("pool", YP[b][:, 1054 + 15:1054 + 271], Ych.ap()[cg * 128:(cg + 1) * 128, 1024:1280], writes=[rYP[b]])
                kb.op("dve", (lambda b=b, cg=cg: lambda e: e.tensor_copy(YP[b][:, 0:15], v3(LH, 8)[:, cg, :]))(), reads=[rLR], writes=[rYP[b]])
                kb.op("dve", (lambda b=b, cg=cg: lambda e: e.tensor_copy(YP[b][:, 1039:1054], v3(RH, 8)[:, cg, :]))(), reads=[rLR], writes=[rYP[b]])
                kb.op("dve", (lambda b=b, cg=cg: lambda e: e.tensor_tensor(
                    v3(DIAG[b], 31), bc(IDF.unsqueeze(1), [128, 31, 128]), bc(v3(CWr, 8)[:, cg, :].unsqueeze(2), [128, 31, 128]),
                    ALU.mult))(), reads=[rCWr, rCONST], writes=[rDIAG[b]])
                for (ys, co, n) in [(0, 0, 512), (512, 512, 512)] + ([] if last else [(1054, 1024, 256)]):
                    bk = 6 + (cvb % 2)
                    cvb += 1
                    for k in range(31):
                        kb.op("pe", (lambda b=b, bk=bk, k=k, ys=ys, n=n: lambda e: e.matmul(
                            bank(bk, n), lhsT=v3(DIAG[b], 31)[:, k, :], rhs=YP[b][:, ys + k:ys + k + n],
                            start=(k == 0), stop=(k == 30)))(), reads=[rDIAG[b], rYP[b]], writes=[BK[bk]])
                    kb.op("act", (lambda bk=bk, cg=cg, co=co, n=n: lambda e: e.activation(
                        out=CONV3[:, cg, co:co + n], in_=bank(bk, n), func=AF.Identity, bias=PT3[:, cg:cg + 1]))(),
                        reads=[BK[bk], rPT], writes=[rCONV[cg]])
                kb.op("act", (lambda cg=cg: lambda e: e.activation(out=SQB[:, 0:NTL], in_=CONV3[:, cg, 0:NTL], func=AF.Square))(),
                      reads=[rCONV[cg]], writes=[rSQB])
                for i, (t0, tn) in enumerate(tts):
                    kb.op("pe", (lambda cg=cg, i=i, t0=t0, tn=tn: lambda e: e.matmul(
                        bank(i, tn), lhsT=ONF, rhs=CONV3[:, cg, t0:t0 + tn], start=(cg == 0), stop=(cg == 7)))(),
                        reads=[rCONV[cg], rCONST], writes=[BK[i]])
                    kb.op("pe", (lambda cg=cg, i=i, t0=t0, tn=tn: lambda e: e.matmul(
                        bank(3 + i, tn), lhsT=ONF, rhs=SQB[:, t0:t0 + tn], start=(cg == 0), stop=(cg == 7)))(),
                        reads=[rSQB, rCONST], writes=[BK[3 + i]])
            dump("d_CONV", CONV, [128, 8 * NT], reads=rCONV)
            MEAN = A.f32(NT)
            RSTD = A.f32(NT)
            MSQ = A.f32(NT)
            rMS = Res("ms")
            for i, (t0, tn) in enumerate(tts):
                kb.op("dve", (lambda i=i, t0=t0, tn=tn: lambda e: e.tensor_scalar_mul(MEAN[:, t0:t0 + tn], bank(i, tn), 1.0 / 1024))(),
                      reads=[BK[i]], writes=[rMS])
                kb.op("dve", (lambda i=i, t0=t0, tn=tn: lambda e: e.tensor_scalar_mul(RSTD[:, t0:t0 + tn], bank(3 + i, tn), 1.0 / 1024))(),
                      reads=[BK[3 + i]], writes=[rMS])
            kb.op("dve", lambda e: e.tensor_tensor(MSQ[:, 0:NTL], MEAN[:, 0:NTL], MEAN[:, 0:NTL], ALU.mult), reads=[rMS], writes=[rMS])
            kb.op("dve", lambda e: e.tensor_tensor(RSTD[:, 0:NTL], RSTD[:, 0:NTL], MSQ[:, 0:NTL], ALU.subtract), reads=[rMS], writes=[rMS])
            rsqrt_ops(RSTD[:, 0:NTL], RSTD[:, 0:NTL], 1.0, MSQ[:, 0:NTL], [rMS], [rMS])
            dump("d_MEAN", MEAN, [128, NT], reads=[rMS])
            dump("d_RSTD", RSTD, [128, NT], reads=[rMS])
            ZB = [A.f32(NT) for _ in range(2)]
            rZB = [Res("zb0"), Res("zb1")]
            for cg in range(8):
                b = cg % 2
                kb.dma("sp", ZB[b][:, 0:NTL], Pch.ap()[CR["Bz"] + cg * 128:CR["Bz"] + (cg + 1) * 128, 0:NTL], writes=[rZB[b]])
                cv = CONV3[:, cg, 0:NTL]
                kb.op("dve", (lambda cv=cv: lambda e: e.tensor_tensor(cv, cv, MEAN[:, 0:NTL], ALU.subtract))(), reads=[rMS, rCONV[cg]], writes=[rCONV[cg]])
                kb.op("pool", (lambda cv=cv: lambda e: e.tensor_tensor(cv, cv, RSTD[:, 0:NTL], ALU.mult))(), reads=[rMS, rCONV[cg]], writes=[rCONV[cg]])
                kb.op("act", (lambda cv=cv, cg=cg: lambda e: e.activation(out=cv, in_=cv, func=AF.Silu, scale=PT3[:, 8 + cg:9 + cg],
                                                                          bias=PT3[:, 16 + cg:17 + cg]))(), reads=[rPT, rCONV[cg]], writes=[rCONV[cg]])
                kb.op("act", (lambda b=b: lambda e: e.activation(out=ZB[b][:, 0:NTL], in_=ZB[b][:, 0:NTL], func=AF.Silu))(),
                      reads=[rZB[b]], writes=[rZB[b]])
                stg, rstg = mix_stage()
                kb.op("dve", (lambda cv=cv, b=b, stg=stg: lambda e: e.tensor_tensor(stg[:, 0:NTL], cv, ZB[b][:, 0:NTL], ALU.mult))(),
                      reads=[rCONV[cg], rZB[b]], writes=[rstg])
                mix_store(8 + cg, stg, rstg)
            kb.barrier()
            A.release(pers2_mark)

            W2PRE = A.bf16(32 * 512)
            rWT2 = [Res("w2t0"), Res("w2t1")]
            wsrc2 = wout_in[l].rearrange("(kc kp) n -> kp kc n", kp=128)
            for g_ in range(4):
                kb.dma("pool", v3(W2PRE, 32)[:, g_ * 8:(g_ + 1) * 8, :], wsrc2[:, g_ * 8:(g_ + 1) * 8, 0:512], writes=[rWT2[0]])
            w2_mark = A.mark()
            QT = A.bf16(8 * NT)
            QT3 = v3(QT, 8)
            rQT = Res("qt")
            QF = [A.f32(1024) for _ in range(2)]
            rQF = [Res("qf0"), Res("qf1")]
            QWS = []
            for i_ in range(2):
                QWS.append(dict(QN=A.f32(1024), QR=A.f32(1024), QTS=A.f32(1024), QT1=A.f32(512), QT2=A.f32(512),
                                QRB=A.bf16(1024), SSQ=A.f32(24), r=Res("qw%d" % i_)))
            for t in range(TL):
                b = t % 2
                W_ = QWS[b]
                pb0 = 4 * b
                kb.dma("sp", QF[b], Ptok.ap()[t * 128:(t + 1) * 128, 1024:2048], writes=[rQF[b]])
                qk_norm_rope(QF[b], 8, GQK[:, 0:128], t, W_["QN"], W_["QR"], W_["QTS"], W_["SSQ"], W_["QT1"], W_["QT2"], rQF[b], W_["r"])
                kb.op("act", (lambda W_=W_: lambda e: e.copy(W_["QRB"], W_["QR"]))(), reads=[W_["r"]], writes=[W_["r"]])
                for h in range(8):
                    kb.op("pe", (lambda h=h, W_=W_, pb0=pb0: lambda e: e.matmul(
                        PS[:, pb0 * 512 + h * 128:pb0 * 512 + (h + 1) * 128], lhsT=W_["QRB"][:, h * 128:(h + 1) * 128],
                        rhs=IDB, start=True, stop=True))(), reads=[W_["r"], rCONST], writes=[BK[pb0 + h // 4]])
                kb.op("act", (lambda t=t, pb0=pb0: lambda e: e.copy(QT3[:, :, t * 128:(t + 1) * 128],
                                                                  v3(PS[:, pb0 * 512:pb0 * 512 + 1024], 8)))(),
                      reads=[BK[pb0], BK[pb0 + 1]], writes=[rQT])
            KTA = A.bf16(2 * 4352)
            KTA3 = v3(KTA, 2)
            VAL = A.bf16(34 * 256)
            VAL3 = v3(VAL, 34)
            rKVA = Res("kva")
            for r in range(4):
                kb.dma("pool", KTA3[:, :, r * 1024:(r + 1) * 1024],
                       gar(0, r, 0, 128 * 2048).rearrange("(d g t) -> d g t", d=128, g=2), writes=[rKVA])
                kb.dma("pool", VAL3[:, r * 8:(r + 1) * 8, :],
                       gar(1, r, 0, 1024 * 256).rearrange("(kt p c) -> p kt c", p=128, c=256), writes=[rKVA])
            kb.op("dve", lambda e: e.tensor_copy(KTA3[:, :, 4096:4352], KTC3), reads=[rKTC], writes=[rKVA])
            kb.dma("pool", VAL3[:, 32:34, :], Ptok.ap()[1024:1280, 2304:2560].rearrange("(kt p) c -> p kt c", p=128), writes=[rKVA])
            dump("d_KTA", KTA, [128, 2 * 4352], BF16, reads=[rKVA])
            dump("d_VAL", VAL, [128, 34 * 256], BF16, reads=[rKVA])
            dump("d_QT", QT, [128, 8 * NT], BF16, reads=[rQT])
            SZ = [A.f32(NT) for _ in range(2)]
            rSZ = [Res("sz0"), Res("sz1")]
            PTA = [A.bf16(512) for _ in range(4)]
            rPTA = [Res("pta%d" % i) for i in range(4)]
            RL = A.f32(512)
            rRL = Res("rl")
            OA = A.f32(512)
            rOA = Res("oa")
            SC = 128.0 ** -0.5
            pcount = 0
            for h in range(8):
                g = h // 4
                b = h % 2
                kb.dma("sp", SZ[b][:, 0:NTL], Pch.ap()[CR["Cz"] + h * 128:CR["Cz"] + (h + 1) * 128, 0:NTL], writes=[rSZ[b]])
                kb.op("act", (lambda b=b: lambda e: e.activation(out=SZ[b][:, 0:NTL], in_=SZ[b][:, 0:NTL], func=AF.Silu))(),
                      reads=[rSZ[b]], writes=[rSZ[b]])
                stg, rstg = mix_stage()
                qtiles = [(0, 512, list(range(34))), (512, 512, list(range(34)))] + ([] if last else [(1024, 256, [32, 33])])
                for (q0, qn, kts) in qtiles:
                    nk = len(kts)

                    def emit_s(ki, q0=q0, qn=qn, kts=kts, g=g, h=h):
                        sb = 2 + (ki % 3)
                        kt = kts[ki]
                        kb.op("pe", (lambda: lambda e: e.matmul(
                            bank(sb, qn), lhsT=KTA3[:, g, kt * 128:(kt + 1) * 128], rhs=QT3[:, h, q0:q0 + qn],
                            start=True, stop=True))(), reads=[rKVA, rQT], writes=[BK[sb]])
                    emit_s(0)
                    if nk > 1:
                        emit_s(1)
                    for ki, kt in enumerate(kts):
                        sb = 2 + (ki % 3)
                        pb = pcount % 4
                        pcount += 1
                        if ki + 2 < nk:
                            emit_s(ki + 2)
                        kb.op("act", (lambda sb=sb, pb=pb, qn=qn: lambda e: e.activation(
                            out=PTA[pb][:, 0:qn], in_=bank(sb, qn), func=AF.Exp, scale=SC))(), reads=[BK[sb]], writes=[rPTA[pb]])
                        kb.op("pe", (lambda pb=pb, kt=kt, g=g, qn=qn, ki=ki, nk=nk: lambda e: e.matmul(
                            bank(0, qn), lhsT=VAL3[:, kt, g * 128:(g + 1) * 128], rhs=PTA[pb][:, 0:qn],
                            start=(ki == 0), stop=(ki == nk - 1)))(), reads=[rKVA, rPTA[pb]], writes=[BK[0]])
                        kb.op("pe", (lambda pb=pb, qn=qn, ki=ki, nk=nk: lambda e: e.matmul(
                            bank(1, qn), lhsT=ONB, rhs=PTA[pb][:, 0:qn], start=(ki == 0), stop=(ki == nk - 1)))(),
                            reads=[rCONST, rPTA[pb]], writes=[BK[1]])
                    kb.op("dve", (lambda qn=qn: lambda e: e.reciprocal(RL[:, 0:qn], bank(1, qn)))(), reads=[BK[1]], writes=[rRL])
                    kb.op("dve", (lambda qn=qn: lambda e: e.tensor_tensor(OA[:, 0:qn], bank(0, qn), RL[:, 0:qn], ALU.mult))(),
                          reads=[BK[0], rRL], writes=[rOA])
                    kb.op("pool", (lambda qn=qn, q0=q0, b=b, stg=stg: lambda e: e.tensor_tensor(
                        stg[:, q0:q0 + qn], OA[:, 0:qn], SZ[b][:, q0:q0 + qn], ALU.mult))(), reads=[rOA, rSZ[b]], writes=[rstg])
                mix_store(16 + h, stg, rstg)
            kb.barrier()

            A.release(w2_mark)
            BIG2 = A.f32(16 * NT)
            MX = v3(BIG2.bitcast(BF16), 32)
            rMX = Res("mx")
            for kc in range(32):
                kb.dma("sp", MX[:, kc, 0:NTL], MIXd.ap()[kc * 128:(kc + 1) * 128, 0:NTL], writes=[rMX])
            GLn = [A.f32(512) for _ in range(2)]
            GCn = [A.f32(512) for _ in range(2)]
            GBn = [A.f32(512) for _ in range(2)]
            rGn = [Res("gn0"), Res("gn1")]
            WT2 = [W2PRE, A.bf16(32 * 512)]
            XO = [A.f32(512) for _ in range(3)]
            rXO = [Res("xo%d" % i) for i in range(3)]
            OO = [A.f32(512) for _ in range(3)]
            rOO = [Res("oo%d" % i) for i in range(3)]
            def load_w2(n, b):
                w3 = v3(WT2[b], 32)
                for g in range(4):
                    kb.dma("pool", w3[:, g * 8:(g + 1) * 8, :], wsrc2[:, g * 8:(g + 1) * 8, n * 512:(n + 1) * 512], writes=[rWT2[b]])
            oc = 0
            for n in range(8):
                b = n % 2
                if n + 1 < 8:
                    load_w2(n + 1, 1 - b)
                w3 = v3(WT2[b], 32)
                cs_ = slice(2 * D + n * 512, 2 * D + (n + 1) * 512)
                kb.dma("sp", GLn[b], af[2 * l:2 * l + 1, cs_].partition_broadcast(128), writes=[rGn[b]])
                kb.dma("sp", GBn[b], bada_in[l:l + 1, cs_].partition_broadcast(128), writes=[rGn[b]])
                kb.op("dve", (lambda b=b: lambda e: e.tensor_tensor(GLn[b], GLn[b], GBn[b], ALU.add))(), reads=[rGn[b]], writes=[rGn[b]])
                if not last:
                    kb.dma("sp", GCn[b], af[2 * l + 1:2 * l + 2, cs_].partition_broadcast(128), writes=[rGn[b]])
                    kb.op("dve", (lambda b=b: lambda e: e.tensor_tensor(GCn[b], GCn[b], GBn[b], ALU.add))(), reads=[rGn[b]], writes=[rGn[b]])
                for t in range(TL):
                    bk = t % 2
                    ob = oc % 3
                    oc += 1
                    kb.dma("sp", XO[ob], tok_src(t, n * 512, (n + 1) * 512), writes=[rXO[ob]])
                    for kc in range(32):
                        kb.op("pe", (lambda bk=bk, kc=kc, t=t, w3=w3: lambda e: e.matmul(
                            bank(bk), lhsT=MX[:, kc, t * 128:(t + 1) * 128], rhs=w3[:, kc, :], start=(kc == 0), stop=(kc == 31)))(),
                            reads=[rMX, rWT2[b]], writes=[BK[bk]])
                    Gt = GLn[b] if t < 8 else GCn[b]
                    kb.op("dve", (lambda bk=bk, ob=ob, Gt=Gt: lambda e: e.tensor_tensor(
                        OO[ob], bank(bk), Gt, ALU.mult))(), reads=[BK[bk], rGn[b]], writes=[rOO[ob]])
                    kb.op("pool", (lambda ob=ob: lambda e: e.tensor_tensor(OO[ob], OO[ob], XO[ob], ALU.add))(),
                          reads=[rXO[ob], rOO[ob]], writes=[rOO[ob]])
                    if last:
                        dst = y_out[t * 128:(t + 1) * 128, n * 512:(n + 1) * 512]
                    else:
                        dst = xs.ap()[t * 128:(t + 1) * 128, n * 512:(n + 1) * 512]
                    kb.dma("sp", dst, OO[ob], reads=[rOO[ob]])
            kb.barrier()
            A.release(lay_mark)

        for l_ in range(nlayers):
            emit_layer(l_)

        kb.barrier()
        block = es.enter_context(nc.Block())
        kb.replay(block)
    return nc


def LAYER_BODY_2(env):
    pass


def _consts():
    c = np.zeros((128, 5 * 128), np.float32)
    c[:, 0:128] = np.eye(128, dtype=np.float32)
    c[:, 128:256] = 1.0
    s = np.arange(128)[:, None]
    l_ = np.arange(128)[None, :]
    c[:, 256:384] = (s <= l_).astype(np.float32)
    c[:, 384:512] = (s >= l_).astype(np.float32)
    return c


def _rope_table(seg):
    n = np.arange(seg * 1024, (seg + 1) * 1024)
    row = (n // 64).astype(np.float32)
    col = (n % 64).astype(np.float32)
    freq = (10000.0 ** (-np.arange(32, dtype=np.float32) / 32)).astype(np.float32)
    ang = np.stack([row, col], -1)[..., None] * freq
    cs = np.concatenate([np.cos(ang).reshape(1024, 64), np.sin(ang).reshape(1024, 64)], -1).astype(np.float32)
    return np.ascontiguousarray(cs.reshape(8, 128, 128).transpose(1, 0, 2))


def make_in_maps(inputs):
    f = lambda a: np.ascontiguousarray(np.asarray(a, dtype=np.float32))
    x = f(inputs["x"]); c = f(inputs["c"]); ctx = f(inputs["ctx"]); c_ctx = f(inputs["c_ctx"])
    w_ada = f(inputs["w_ada"])
    shared = {
        "b_ada": f(inputs["b_ada"]), "norm_g": f(inputs["norm_g"]), "w_in": f(inputs["w_in"]),
        "sgu_w": f(inputs["sgu_w"]), "sgu_b": f(inputs["sgu_b"]).reshape(DEPTH, 1024),
        "sgu_ln_g": f(inputs["sgu_ln_g"]), "sgu_ln_b": f(inputs["sgu_ln_b"]),
        "conv_w": f(inputs["conv_w"]).reshape(DEPTH, 248, 128),
        "conv_b": f(inputs["conv_b"]).reshape(DEPTH, 8, 128),
        "conv_ln_g": f(inputs["conv_ln_g"]).reshape(DEPTH, 8, 128),
        "conv_ln_b": f(inputs["conv_ln_b"]).reshape(DEPTH, 8, 128),
        "q_norm_g": f(inputs["q_norm_g"]), "k_norm_g": f(inputs["k_norm_g"]),
        "mlstm_i_bias": f(inputs["mlstm_i_bias"]).reshape(DEPTH, 16),
        "mlstm_f_bias": f(inputs["mlstm_f_bias"]).reshape(DEPTH, 16),
        "mh_norm_g": f(inputs["mh_norm_g"]).reshape(DEPTH, 8, 128),
        "w_out": f(inputs["w_out"]), "consts": _consts(),
    }
    maps = []
    for core in range(8):
        b, s = core // 4, core % 4
        m = dict(shared)
        m["x"] = np.ascontiguousarray(x[b, s * 1024:(s + 1) * 1024])
        m["ctx"] = np.ascontiguousarray(ctx[b])
        cc = np.stack([c[b], c_ctx], 0)
        m["ccT"] = np.ascontiguousarray(cc[:, s * 1024:(s + 1) * 1024].T)
        m["w_ada_s"] = np.ascontiguousarray(w_ada[:, s * 1024:(s + 1) * 1024, :])
        m["rope"] = _rope_table(s)
        sel = np.zeros((128, 12), np.float32)
        sel[:, s] = 1.0
        if s - 1 >= 0:
            sel[:, 4 + s - 1] = 1.0
        if s + 1 <= 3:
            sel[:, 8 + s + 1] = 1.0
        m["sel"] = sel
        maps.append(m)
    return maps


def kernel(**inputs):
    nc = build_program()
    maps = make_in_maps(inputs)
    res = run_bass_kernel_spmd(nc, maps, core_ids=list(range(8)))
    out = np.empty((2, 4096, D), np.float32)
    for core in range(8):
        b, s = core // 4, core % 4
        out[b, s * 1024:(s + 1) * 1024] = res.results[core]["y"]
    return out
```

```python
import math
import numpy as np
import ml_dtypes
from contextlib import ExitStack
import concourse.bass as bass
import concourse.mybir as mybir
from concourse.bass_utils import run_bass_kernel_spmd

F32 = mybir.dt.float32
BF16 = mybir.dt.bfloat16
AF = mybir.ActivationFunctionType
ALU = mybir.AluOpType
AX = mybir.AxisListType

D = 4096
PIN = 13856
DEPTH = 2
NT = 1280
NL = 1024
EPS = 1e-6
LNSC = math.log(128.0 ** -0.5)

TOKB = [("Av", 1024, 1024, 0), ("Cq", 6144, 1024, 1024), ("Ckv", 7168, 512, 2048),
        ("Dk", 9728, 1024, 2560), ("Dv", 10752, 1024, 3584), ("G", 13824, 32, 4608)]
NTOKC = 4640
TC = {n: d for n, _, _, d in TOKB}
CHB = [("Au", 0, 0), ("Az", 2048, 1024), ("Ba", 3072, 2048), ("Bg", 4096, 3072), ("Bz", 5120, 4096),
       ("Cz", 7680, 5120), ("Dq", 8704, 6144), ("DkT", 9728, 7168), ("Do", 11776, 8192), ("Dz", 12800, 9216)]
NCHR = 10240
CR = {n: r for n, _, r in CHB}

EXC_ROWS = [512, 512, 512, 512, 4, 60]
EX_HALO_OFF = 0
EX_DT_OFF = 128 * 240


class Res:
    __slots__ = ("name", "w", "r")

    def __init__(self, name):
        self.name = name
        self.w = None
        self.r = {}


class KB:
    CE = ["pe", "act", "dve", "pool"]

    def __init__(self, nc, es):
        self.nc = nc
        self.engs = ["pe", "act", "dve", "pool", "sp"]
        self.q = {e: [] for e in self.engs}
        self.sem = {e: es.enter_context(nc.semaphore("s_" + e)) for e in self.CE}
        self.cnt = {e: 0 for e in self.CE}
        self.known = {e: {} for e in self.engs}
        self.dsem = {"sp": [es.enter_context(nc.semaphore("d_sp%d" % i)) for i in range(16)],
                     "pool": [es.enter_context(nc.semaphore("d_pl%d" % i)) for i in range(8)]}
        self.dcnt = {"sp": 0, "pool": 0}
        self.duse = {k: [0] * len(v) for k, v in self.dsem.items()}
        self.dlast = {k: [None] * len(v) for k, v in self.dsem.items()}
        self.ccsem = es.enter_context(nc.semaphore("s_cc"))
        self.cccnt = 0
        self.cclast = None

    def _wait(self, eng, ev):
        if ev is None:
            return
        key, sem, val = ev
        if self.known[eng].get(key, 0) >= val:
            return
        self.known[eng][key] = val
        self.q[eng].append(("w", sem, val))

    def _deps(self, eng, reads, writes):
        deps = {}

        def add(ev):
            if ev is None:
                return
            k = ev[0]
            if k not in deps or deps[k][2] < ev[2]:
                deps[k] = ev
        for r in reads:
            add(r.w)
        for w in writes:
            add(w.w)
            for ev in w.r.values():
                add(ev)
        for k, ev in deps.items():
            if eng == "pe" and k == "pe":
                continue
            self._wait(eng, ev)

    def _mark(self, ev, reads, writes):
        for r in reads:
            r.r[ev[0]] = ev
        for w in writes:
            w.w = ev
            w.r = {}

    def op(self, eng, fn, reads=(), writes=()):
        self._deps(eng, reads, writes)
        self.cnt[eng] += 1
        ev = (eng, self.sem[eng], self.cnt[eng])
        self.q[eng].append(("o", fn, self.sem[eng]))
        self._mark(ev, reads, writes)
        return ev

    def dma(self, qn, out, in_, reads=(), writes=()):
        self._deps(qn, reads, writes)
        n = len(self.dsem[qn])
        slot = self.dcnt[qn] % n
        self.dcnt[qn] += 1
        self._wait(qn, self.dlast[qn][slot])
        self.duse[qn][slot] += 1
        sem = self.dsem[qn][slot]
        ev = ("%s%d" % (qn, slot), sem, 16 * self.duse[qn][slot])
        self.dlast[qn][slot] = ev
        self.q[qn].append(("d", out, in_, sem))
        self._mark(ev, reads, writes)
        return ev

    def all_events(self, include_cc=True):
        evs = [(e, self.sem[e], self.cnt[e]) for e in self.CE if self.cnt[e] > 0]
        for qn in self.dlast:
            evs += [ev for ev in self.dlast[qn] if ev is not None]
        if include_cc and self.cclast is not None:
            evs.append(self.cclast)
        return evs

    def barrier(self, include_cc=True):
        evs = self.all_events(include_cc)
        for eng in self.engs:
            for ev in evs:
                if eng == "pe" and ev[0] == "pe":
                    continue
                self._wait(eng, ev)

    def collective(self, kind, alu, groups, in_ap, out_ap):
        self.barrier()
        self.cccnt += 1
        self.q["pool"].append(("c", kind, alu, groups, in_ap, out_ap))
        self.cclast = ("cc", self.ccsem, self.cccnt)
        self.barrier()

    def issue_collectives(self, kind, alu, groups, pairs):
        self.barrier()
        for in_ap, out_ap in pairs:
            self.cccnt += 1
            self.q["pool"].append(("c", kind, alu, groups, in_ap, out_ap))
        self.cclast = ("cc", self.ccsem, self.cccnt)

    def replay(self, block):
        def mk(eng):
            def f(e):
                for it in self.q[eng]:
                    k = it[0]
                    if k == "w":
                        e.wait_ge(it[1], it[2])
                    elif k == "o":
                        it[1](e).then_inc(it[2], 1)
                    elif k == "d":
                        e.dma_start(out=it[1], in_=it[2]).then_inc(it[3], 16)
                    elif k == "c":
                        e.collective_compute(it[1], it[2], replica_groups=it[3], ins=[it[4]],
                                             outs=[it[5]]).then_inc(self.ccsem)
            return f
        block.sync(mk("sp"))
        block.gpsimd(mk("pool"))
        block.scalar(mk("act"))
        block.vector(mk("dve"))
        block.tensor(mk("pe"))


class Arena:
    def __init__(self, ap, nfloats):
        self.ap = ap
        self.n = nfloats
        self.top = 0
        self.kb = None

    def mark(self):
        return self.top

    def release(self, m):
        if self.kb is not None:
            self.kb.barrier(include_cc=False)
        self.top = m

    def f32(self, n):
        n4 = (n + 7) // 8 * 8
        assert self.top + n4 <= self.n, ("SBUF arena overflow", self.top, n4, self.n)
        v = self.ap[:, self.top:self.top + n]
        self.top += n4
        return v

    def bf16(self, n):
        nf = (n + 1) // 2
        return self.f32(nf).bitcast(BF16)[:, 0:n]


def build_program(debug=None, stop_after=None, nlayers=DEPTH):
    debug = debug or set()
    nc = bass.Bass("TRN2", target_bir_lowering=False)

    def din(name, shape, dt=F32):
        return nc.dram_tensor(name, list(shape), dt, kind="ExternalInput").ap()

    def dscr(name, shape, dt=F32):
        kind = "ExternalOutput" if name in debug else "Internal"
        return nc.dram_tensor(name, list(shape), dt, kind=kind)

    x_in = din("x", [NL, D])
    ctx_in = din("ctx", [256, D])
    ccT_in = din("ccT", [1024, 2])
    wada_in = din("w_ada_s", [DEPTH, 1024, 3 * D])
    bada_in = din("b_ada", [DEPTH, 3 * D])
    normg_in = din("norm_g", [DEPTH, D])
    win_in = din("w_in", [DEPTH, D, PIN])
    sguw_in = din("sgu_w", [DEPTH, 8, 128, 128])
    sgub_in = din("sgu_b", [DEPTH, 1024])
    sglg_in = din("sgu_ln_g", [DEPTH, 1024])
    sglb_in = din("sgu_ln_b", [DEPTH, 1024])
    convw_in = din("conv_w", [DEPTH, 248, 128])
    convb_in = din("conv_b", [DEPTH, 8, 128])
    cvlg_in = din("conv_ln_g", [DEPTH, 8, 128])
    cvlb_in = din("conv_ln_b", [DEPTH, 8, 128])
    qng_in = din("q_norm_g", [DEPTH, 128])
    kng_in = din("k_norm_g", [DEPTH, 128])
    ib_in = din("mlstm_i_bias", [DEPTH, 16])
    fb_in = din("mlstm_f_bias", [DEPTH, 16])
    mhg_in = din("mh_norm_g", [DEPTH, 8, 128])
    wout_in = din("w_out", [DEPTH, D, D])
    consts_in = din("consts", [128, 5 * 128])
    rope_in = din("rope", [128, 8, 128])
    sel_in = din("sel", [128, 12])
    y_out = nc.dram_tensor("y", [NL, D], F32, kind="ExternalOutput").ap()

    ada_part = nc.dram_tensor("ada_part", [4, 3 * D], F32)
    ada_full = nc.dram_tensor("ada_full", [4, 3 * D], F32)
    dbg_ada = dscr("dbg_ada", [4, 3 * D]) if "dbg_ada" in debug else None
    xs = dscr("xs", [NT, D])
    Ptok = dscr("Ptok", [NT, NTOKC])
    Pch = dscr("Pch", [NCHR, NT])
    Ych = dscr("Ych", [1024, NT])
    exp_bufs = [nc.dram_tensor("exp_buf%d" % c, [EXC_ROWS[c], 512], F32) for c in range(6)]
    gat_bufs = [nc.dram_tensor("gat_buf%d" % c, [4 * EXC_ROWS[c], 512], F32) for c in range(6)]
    dbg_gat = None
    dbg_mixT = dscr("dbg_mixT", [128, 32 * NT], BF16) if "dbg_mixT" in debug else None
    dbg_HT = dscr("dbg_HT", [128, 32 * NT], BF16) if "dbg_HT" in debug else None

    exp_flat = [b_.ap().rearrange("r c -> (r c)") for b_ in exp_bufs]
    gat_flat = [b_.ap().rearrange("r c -> (r c)") for b_ in gat_bufs]

    def exr(c, off, n):
        return exp_flat[c][off:off + n]

    def gar(c, r, off, n):
        o = r * EXC_ROWS[c] * 512 + off
        return gat_flat[c][o:o + n]

    GROUPS = [[0, 1, 2, 3], [4, 5, 6, 7]]

    with ExitStack() as es:
        ARN = 50 * 1024
        arena_t = es.enter_context(nc.sbuf_tensor("arena", [128, ARN], F32))
        A = Arena(arena_t[:, :], ARN)
        psum_t = es.enter_context(nc.psum_tensor("psum", [128, 4096], F32))
        PS = psum_t[:, :]
        kb = KB(nc, es)
        A.kb = kb
        BK = [Res("bank%d" % i) for i in range(8)]

        def bank(i, n=512):
            return PS[:, i * 512:i * 512 + n]

        def v3(ap, a):
            return ap.rearrange("p (a b) -> p a b", a=a)

        CONST = A.f32(4 * 128)
        IDF = CONST[:, 0:128]
        ONF = CONST[:, 128:256]
        TRIF = CONST[:, 256:384]
        TRIB = CONST[:, 384:512]
        IDB = A.bf16(128)
        ONB = A.bf16(128)
        SEL = A.f32(12)
        ROPE = A.f32(8 * 128)
        rCONST = Res("const")
        kb.dma("sp", CONST, consts_in[:, 0:512], writes=[rCONST])
        kb.dma("sp", SEL, sel_in[:, :], writes=[rCONST])
        kb.dma("sp", ROPE, rope_in.rearrange("p a b -> p (a b)"), writes=[rCONST])
        kb.op("dve", lambda e: e.tensor_copy(IDB, IDF), reads=[rCONST], writes=[rCONST])
        kb.op("dve", lambda e: e.tensor_copy(ONB, ONF), reads=[rCONST], writes=[rCONST])
        ROPE3 = v3(ROPE, 8)
        kb.barrier()

        rBIGd = Res("bigd")
        rBIGa = Res("biga")
        MIXd = dscr("MIXd", [32 * 128, NT], BF16)
        MST = [A.bf16(NT) for _ in range(2)]
        rMST = [Res("mst0"), Res("mst1")]
        mst_i = [0]

        def mix_stage():
            i = mst_i[0] % 2
            mst_i[0] += 1
            return MST[i], rMST[i]

        def mix_store(kc, stg, rstg):
            kb.dma("sp", MIXd.ap()[kc * 128:(kc + 1) * 128, 0:NTL_cur[0]], stg[:, 0:NTL_cur[0]], reads=[rstg])
        NTL_cur = [NT]
        dumped = set()

        def dump(name, ap2d, shape, dt=F32, reads=()):
            if name not in debug or name in dumped:
                return
            dumped.add(name)
            dtn = nc.dram_tensor(name, list(shape), dt, kind="ExternalOutput")
            kb.dma("sp", dtn.ap(), ap2d, reads=list(reads))

        def transpose_rows(src, R, dst, bk, rsrc, rdst):
            kb.op("pe", lambda e: e.matmul(bank(bk, R), lhsT=src[0:R, :], rhs=IDF[0:R, 0:R], start=True, stop=True),
                  reads=[rsrc, rCONST], writes=[BK[bk]])
            kb.op("dve", lambda e: e.tensor_copy(dst, bank(bk, R)), reads=[BK[bk]], writes=[rdst])

        m0 = A.mark()
        CCT = A.f32(16)
        rCCT = Res("cct")
        kb.dma("sp", v3(CCT, 8), ccT_in.rearrange("(kc kp) r -> kp kc r", kp=128), writes=[rCCT])
        kb.op("act", lambda e: e.activation(out=CCT, in_=CCT, func=AF.Silu), reads=[rCCT], writes=[rCCT])
        CCT3 = v3(CCT, 8)
        WA = [A.f32(4096) for _ in range(3)]
        rWA = [Res("wa%d" % i) for i in range(3)]
        AST = [A.f32(4096) for _ in range(2)]
        rAST = [Res("ast%d" % i) for i in range(2)]
        it = 0
        for l in range(DEPTH):
            for third in range(3):
                c0 = third * 4096
                for kc in range(8):
                    b = it % 3
                    it += 1
                    kb.dma("sp", WA[b], wada_in[l][kc * 128:(kc + 1) * 128, c0:c0 + 4096], writes=[rWA[b]])
                    for n in range(8):
                        kb.op("pe", (lambda b=b, kc=kc, n=n: lambda e: e.matmul(
                            bank(n)[0:2, :], lhsT=CCT3[:, kc, :], rhs=WA[b][:, n * 512:(n + 1) * 512],
                            start=(kc == 0), stop=(kc == 7)))(), reads=[rCCT, rWA[b]], writes=[BK[n]])
                ab = third % 2
                for n in range(8):
                    if n % 2 == 0:
                        kb.op("act", (lambda ab=ab, n=n: lambda e: e.copy(AST[ab][0:2, n * 512:(n + 1) * 512], bank(n)[0:2, :]))(),
                              reads=[BK[n]], writes=[rAST[ab]])
                    else:
                        kb.op("dve", (lambda ab=ab, n=n: lambda e: e.tensor_copy(AST[ab][0:2, n * 512:(n + 1) * 512], bank(n)[0:2, :]))(),
                              reads=[BK[n]], writes=[rAST[ab]])
                kb.dma("sp", ada_part.ap()[2 * l:2 * l + 2, c0:c0 + 4096], AST[ab][0:2, :], reads=[rAST[ab]])
        kb.collective("AllReduce", ALU.add, GROUPS, ada_part.ap().opt(), ada_full.ap().opt())
        A.release(m0)
        if dbg_ada is not None:
            kb.dma("sp", dbg_ada.ap(), ada_full.ap())
            kb.barrier()
        if stop_after == "ada":
            nlayers = 0

        def emit_layer(l):
            last = (l == DEPTH - 1)
            NTL = NL if last else NT
            TL = NTL // 128
            wl = win_in[l]
            lay_mark = A.mark()

            def tok_src(t, c0, c1, l=l):
                if l == 0:
                    if t < 8:
                        return x_in[t * 128:(t + 1) * 128, c0:c1]
                    return ctx_in[(t - 8) * 128:(t - 7) * 128, c0:c1]
                return xs.ap()[t * 128:(t + 1) * 128, c0:c1]

            R1 = A.f32(128)
            R2 = A.f32(128)
            R3 = A.f32(128)
            R4a = A.f32(128)
            R4b = A.f32(128)
            rR = Res("R")
            af = ada_full.ap()
            srcs1 = [af[2 * l:2 * l + 1, D:2 * D], af[2 * l:2 * l + 1, 0:D],
                     af[2 * l + 1:2 * l + 2, D:2 * D], af[2 * l + 1:2 * l + 2, 0:D]]
            for i, s_ in enumerate(srcs1):
                kb.dma("sp", R1[32 * i:32 * i + 32, :], s_.rearrange("o (a b) -> (o a) b", b=128), writes=[rR])
            srcs2 = [bada_in[l:l + 1, D:2 * D], bada_in[l:l + 1, 0:D], normg_in[l:l + 1, :]]
            for i, s_ in enumerate(srcs2):
                kb.dma("sp", R2[32 * i:32 * i + 32, :], s_.rearrange("o (a b) -> (o a) b", b=128), writes=[rR])
            for i, s_ in enumerate([convb_in[l], cvlg_in[l], cvlb_in[l], mhg_in[l]]):
                kb.dma("sp", R3[8 * i:8 * i + 8, :], s_, writes=[rR])
            kb.dma("sp", R4a[0:124, :], convw_in[l][0:124, :], writes=[rR])
            kb.dma("sp", R4b[0:124, :], convw_in[l][124:248, :], writes=[rR])
            PT1 = A.f32(128)
            PT2 = A.f32(96)
            PT3 = A.f32(32)
            CW = A.f32(248)
            rPT = Res("PT")
            transpose_rows(R1, 128, PT1, 0, rR, rPT)
            transpose_rows(R2, 96, PT2, 1, rR, rPT)
            transpose_rows(R3, 32, PT3, 2, rR, rPT)
            transpose_rows(R4a, 124, CW[:, 0:124], 3, rR, rPT)
            transpose_rows(R4b, 124, CW[:, 124:248], 4, rR, rPT)
            MOD = A.f32(128)
            for j in range(2):
                kb.op("dve", (lambda j=j: lambda e: e.scalar_tensor_tensor(
                    out=MOD[:, 64 * j:64 * j + 32], in0=PT1[:, 64 * j:64 * j + 32], scalar=1.0, in1=PT2[:, 0:32],
                    op0=ALU.add, op1=ALU.add))(), reads=[rPT], writes=[rPT])
                kb.op("dve", (lambda j=j: lambda e: e.tensor_tensor(
                    out=MOD[:, 64 * j:64 * j + 32], in0=MOD[:, 64 * j:64 * j + 32], in1=PT2[:, 64:96],
                    op=ALU.mult))(), reads=[rPT], writes=[rPT])
                kb.op("dve", (lambda j=j: lambda e: e.tensor_tensor(
                    out=MOD[:, 64 * j + 32:64 * j + 64], in0=PT1[:, 64 * j + 32:64 * j + 64], in1=PT2[:, 32:64],
                    op=ALU.add))(), reads=[rPT], writes=[rPT])
            kb.barrier()
            par_mark = A.mark()
            NTL_cur[0] = NTL
            BIG = A.f32(16 * NT)
            BIGB = BIG.bitcast(BF16)
            HT = v3(BIGB, 32)
            WT0 = A.bf16(32 * 512)
            rWT = [Res("wt0"), Res("wt1")]
            tiles = []
            for name, c0, ncol, d0 in TOKB:
                for j in range(0, ncol, 512):
                    tiles.append(("tok", name, c0 + j, min(512, ncol - j), d0 + j))
            for name, c0, r0 in CHB:
                for j in range(0, 1024, 512):
                    tiles.append(("ch", name, c0 + j, 512, r0 + j))
            wsrc = wl.rearrange("(kc kp) n -> kp kc n", kp=128)
            for g_ in range(4):
                kb.dma("pool", v3(WT0, 32)[:, g_ * 8:(g_ + 1) * 8, 0:tiles[0][3]],
                       wsrc[:, g_ * 8:(g_ + 1) * 8, tiles[0][2]:tiles[0][2] + tiles[0][3]], writes=[rWT[0]])
            big_mark = A.mark()

            XT = [A.f32(D) for _ in range(2)]
            rXT = [Res("xt0"), Res("xt1")]
            JUNK = A.bf16(D)
            rJ = Res("junk")
            SS = A.f32(8)
            rSS = Res("ss")
            for t in range(10):
                b = t % 2
                kb.dma("sp", XT[b], tok_src(t, 0, D), writes=[rXT[b]])
                kb.op("act", (lambda b=b: lambda e: e.activation(out=JUNK, in_=XT[b], func=AF.Square,
                                                                   accum_out=SS[:, 0:1]))(),
                      reads=[rXT[b]], writes=[rJ, rSS])
                kb.op("dve", lambda e: e.tensor_scalar(SS[:, 1:2], SS[:, 0:1], 1.0 / D, EPS, ALU.mult, ALU.add),
                      reads=[rSS], writes=[rSS])
                kb.op("act", lambda e: e.sqrt(SS[:, 3:4], SS[:, 1:2]), reads=[rSS], writes=[rSS])
                kb.op("dve", lambda e: e.reciprocal(SS[:, 2:3], SS[:, 3:4]), reads=[rSS], writes=[rSS])
                kb.op("act", (lambda b=b: lambda e: e.activation(out=XT[b], in_=XT[b], func=AF.Copy,
                                                                   scale=SS[:, 2:3]))(),
                      reads=[rXT[b], rSS], writes=[rXT[b]])
                mo = 0 if t < 8 else 64
                for g4 in range(8):
                    bk = g4 % 4
                    for j in range(4):
                        kc = g4 * 4 + j
                        kb.op("pe", (lambda b=b, kc=kc, bk=bk, j=j: lambda e: e.matmul(
                            bank(bk)[:, j * 128:(j + 1) * 128], lhsT=XT[b][:, kc * 128:(kc + 1) * 128], rhs=IDF,
                            start=True, stop=True))(), reads=[rXT[b], rCONST], writes=[BK[bk]])
                    for j in range(4):
                        kc = g4 * 4 + j
                        dst = HT[:, kc, t * 128:(t + 1) * 128]
                        src = bank(bk)[:, j * 128:(j + 1) * 128]
                        if g4 % 2 == 0:
                            kb.op("dve", (lambda dst=dst, src=src, kc=kc, mo=mo: lambda e: e.tensor_scalar(
                                dst, src, MOD[:, mo + kc:mo + kc + 1], MOD[:, mo + 32 + kc:mo + 33 + kc],
                                ALU.mult, ALU.add))(), reads=[BK[bk], rPT], writes=[rBIGd])
                        else:
                            kb.op("act", (lambda dst=dst, src=src, kc=kc, mo=mo: lambda e: e.activation(
                                out=dst, in_=src, func=AF.Identity, scale=MOD[:, mo + kc:mo + kc + 1],
                                bias=MOD[:, mo + 32 + kc:mo + 33 + kc]))(), reads=[BK[bk], rPT], writes=[rBIGa])
            kb.barrier()
            if dbg_HT is not None and l == 0:
                kb.dma("sp", dbg_HT.ap(), BIGB, reads=[rBIGd, rBIGa])
                kb.barrier()
            A.release(big_mark)
            if stop_after == "norm":
                return

            WT = [WT0, A.bf16(32 * 512)]
            STG = [A.f32(NT) for _ in range(2)]
            rSTG = [Res("stg0"), Res("stg1")]
            def load_w(src3, c0, ncol, b):
                w3 = v3(WT[b], 32)
                for g in range(4):
                    kb.dma("pool", w3[:, g * 8:(g + 1) * 8, 0:ncol], src3[:, g * 8:(g + 1) * 8, c0:c0 + ncol],
                           writes=[rWT[b]])

            stg_i = 0
            ev_i = 0
            for ti, (kind, name, c0, ncol, d0) in enumerate(tiles):
                b = ti % 2
                if ti + 1 < len(tiles):
                    load_w(wsrc, tiles[ti + 1][2], tiles[ti + 1][3], 1 - b)
                w3 = v3(WT[b], 32)
                if kind == "tok":
                    nt_ = 8 if (last and name in ("Av", "Cq")) else 10
                    for t in range(nt_):
                        bk = 6 + (t % 2)
                        for kc in range(32):
                            kb.op("pe", (lambda bk=bk, kc=kc, t=t, w3=w3, ncol=ncol: lambda e: e.matmul(
                                bank(bk, ncol), lhsT=HT[:, kc, t * 128:(t + 1) * 128], rhs=w3[:, kc, 0:ncol],
                                start=(kc == 0), stop=(kc == 31)))(), reads=[rBIGd, rBIGa, rWT[b]], writes=[BK[bk]])
                        sb = stg_i % 2
                        stg_i += 1
                        eng = "act" if ev_i % 2 == 0 else "dve"
                        ev_i += 1
                        if eng == "act":
                            kb.op("act", (lambda sb=sb, bk=bk, ncol=ncol: lambda e: e.copy(
                                STG[sb][:, 0:ncol], bank(bk, ncol)))(), reads=[BK[bk]], writes=[rSTG[sb]])
                        else:
                            kb.op("dve", (lambda sb=sb, bk=bk, ncol=ncol: lambda e: e.tensor_copy(
                                STG[sb][:, 0:ncol], bank(bk, ncol)))(), reads=[BK[bk]], writes=[rSTG[sb]])
                        kb.dma("sp", Ptok.ap()[t * 128:(t + 1) * 128, d0:d0 + ncol], STG[sb][:, 0:ncol],
                               reads=[rSTG[sb]])
                else:
                    ntok = NTL
                    tts = [(0, 512), (512, 512)] + ([(1024, 256)] if ntok > 1024 else [])
                    for cb in range(4):
                        bset = 3 * (cb % 2)
                        for kc in range(32):
                            for i, (t0, tn) in enumerate(tts):
                                kb.op("pe", (lambda bset=bset, i=i, kc=kc, cb=cb, t0=t0, tn=tn, w3=w3: lambda e: e.matmul(
                                    bank(bset + i, tn), lhsT=w3[:, kc, cb * 128:(cb + 1) * 128],
                                    rhs=HT[:, kc, t0:t0 + tn], start=(kc == 0), stop=(kc == 31)))(),
                                    reads=[rBIGd, rBIGa, rWT[b]], writes=[BK[bset + i]])
                        sb = stg_i % 2
                        stg_i += 1
                        for i, (t0, tn) in enumerate(tts):
                            eng = "act" if ev_i % 2 == 0 else "dve"
                            ev_i += 1
                            if eng == "act":
                                kb.op("act", (lambda sb=sb, bset=bset, i=i, t0=t0, tn=tn: lambda e: e.copy(
                                    STG[sb][:, t0:t0 + tn], bank(bset + i, tn)))(),
                                    reads=[BK[bset + i]], writes=[rSTG[sb]])
                            else:
                                kb.op("dve", (lambda sb=sb, bset=bset, i=i, t0=t0, tn=tn: lambda e: e.tensor_copy(
                                    STG[sb][:, t0:t0 + tn], bank(bset + i, tn)))(),
                                    reads=[BK[bset + i]], writes=[rSTG[sb]])
                        kb.dma("sp", Pch.ap()[d0 + cb * 128:d0 + (cb + 1) * 128, 0:ntok], STG[sb][:, 0:ntok],
                               reads=[rSTG[sb]])
            kb.barrier()
            A.release(par_mark)
            if stop_after == "gemm1":
                return
            tts = [(0, 512), (512, 512)] + ([(1024, 256)] if NTL > 1024 else [])
            KTC = A.bf16(2 * 256)
            KTC3 = v3(KTC, 2)
            rKTC = Res("ktc")
            IBFB = A.f32(32)
            GQK = A.f32(256)
            rPB = Res("pb")
            kb.dma("sp", IBFB[:, 0:16], ib_in[l:l + 1, :].partition_broadcast(128), writes=[rPB])
            kb.dma("sp", IBFB[:, 16:32], fb_in[l:l + 1, :].partition_broadcast(128), writes=[rPB])
            kb.dma("sp", GQK[:, 0:128], qng_in[l:l + 1, :].partition_broadcast(128), writes=[rPB])
            kb.dma("sp", GQK[:, 128:256], kng_in[l:l + 1, :].partition_broadcast(128), writes=[rPB])
            pers2_mark = A.mark()
            HS = A.f32(8 * NT)
            HS3 = v3(HS, 8)
            rHS = Res("hs")
            GP = A.f32(10 * 5 * 16)
            GP4 = GP.rearrange("p (t k g) -> p t k g", t=10, k=5)
            rGP = Res("gp")
            CS = A.f32(16 * 256)
            CS3 = v3(CS, 16)
            rCS = Res("cs")
            CB = A.bf16(16 * 256)
            CB3 = v3(CB, 16)
            rCB = Res("cb")
            STFB = A.f32(16 * 256)
            rSTFB = Res("stfb")
            pers_mark = A.mark()

            def bc(ap, shape):
                return ap.broadcast_to(list(shape))

            def rsqrt_ops(dst, src, scale, tmp, rr, rw):
                kb.op("dve", lambda e: e.tensor_scalar(tmp, src, scale, EPS, ALU.mult, ALU.add), reads=rr, writes=rw)
                kb.op("act", lambda e: e.sqrt(tmp, tmp), reads=rw, writes=rw)
                kb.op("dve", lambda e: e.reciprocal(dst, tmp), reads=rw, writes=rw)

            def rope_ops(src, dst, H, t, T1, T2, rs, rd, rt):
                s5 = src.rearrange("p (h a c j) -> p h a c j", h=H, a=2, c=2)
                d5 = dst.rearrange("p (h a c j) -> p h a c j", h=H, a=2, c=2)
                x1, x2 = s5[:, :, :, 0, :], s5[:, :, :, 1, :]
                o1, o2 = d5[:, :, :, 0, :], d5[:, :, :, 1, :]
                cs = bc(ROPE3[:, t, 0:64].rearrange("p (a j) -> p a j", a=2).unsqueeze(1), [128, H, 2, 32])
                sn = bc(ROPE3[:, t, 64:128].rearrange("p (a j) -> p a j", a=2).unsqueeze(1), [128, H, 2, 32])
                t1 = T1.rearrange("p (h a j) -> p h a j", h=H, a=2)
                t2 = T2.rearrange("p (h a j) -> p h a j", h=H, a=2)
                kb.op("dve", lambda e: e.tensor_tensor(o1, x1, cs, ALU.mult), reads=[rs, rCONST], writes=[rd])
                kb.op("pool", lambda e: e.tensor_tensor(t1, x2, sn, ALU.mult), reads=[rs, rCONST], writes=[rt])
                kb.op("dve", lambda e: e.tensor_tensor(o1, o1, t1, ALU.subtract), reads=[rt], writes=[rd])
                kb.op("dve", lambda e: e.tensor_tensor(o2, x2, cs, ALU.mult), reads=[rs, rCONST], writes=[rd])
                kb.op("pool", lambda e: e.tensor_tensor(t2, x1, sn, ALU.mult), reads=[rs, rCONST], writes=[rt])
                kb.op("dve", lambda e: e.tensor_tensor(o2, o2, t2, ALU.add), reads=[rt], writes=[rd])

            def qk_norm_rope(src, H, gq, t, NRM, ROT, TS, SSQ, T1, T2, rsrc, rw):
                kb.op("act", lambda e: e.activation(out=TS, in_=src, func=AF.Square), reads=[rsrc], writes=[rw])
                kb.op("dve", lambda e: e.tensor_reduce(out=SSQ[:, 0:H], in_=v3(TS, H), axis=AX.X, op=ALU.add),
                      reads=[rw], writes=[rw])
                rsqrt_ops(SSQ[:, 16:16 + H], SSQ[:, 0:H], 1.0 / 128, SSQ[:, 8:8 + H], [rw], [rw])
                kb.op("dve", lambda e: e.tensor_tensor(v3(NRM, H), v3(src, H), bc(SSQ[:, 16:16 + H].unsqueeze(2), [128, H, 128]),
                                                       ALU.mult), reads=[rsrc, rw], writes=[rw])
                kb.op("dve", lambda e: e.tensor_tensor(v3(NRM, H), v3(NRM, H), bc(gq.unsqueeze(1), [128, H, 128]), ALU.mult),
                      reads=[rw, rPB], writes=[rw])
                if t < 8:
                    rope_ops(NRM, ROT, H, t, T1, T2, rw, rw, rw)
                else:
                    kb.op("dve", lambda e: e.tensor_copy(ROT, NRM), reads=[rw], writes=[rw])

            GG = A.f32(320)
            XF = A.f32(160)
            IG = A.f32(160)
            LF = A.f32(160)
            rGG = Res("gg")
            G5 = GG.rearrange("p (t d i h) -> p t d i h", t=10, d=2, i=2)
            x4 = lambda ap: ap.rearrange("p (t d h) -> p t d h", t=10, d=2)
            kb.dma("sp", v3(GG, 10), Ptok.ap()[:, 4608:4640].rearrange("(t p) c -> p t c", p=128), writes=[rGG])
            kb.op("dve", lambda e: e.tensor_tensor(x4(XF), G5[:, :, :, 1, :], bc(v3(IBFB[:, 16:32], 2).unsqueeze(1), [128, 10, 2, 8]),
                                                   ALU.add), reads=[rGG, rPB], writes=[rGG])
            kb.op("dve", lambda e: e.tensor_tensor(x4(IG), G5[:, :, :, 0, :], bc(v3(IBFB[:, 0:16], 2).unsqueeze(1), [128, 10, 2, 8]),
                                                   ALU.add), reads=[rGG, rPB], writes=[rGG])
            kb.op("act", lambda e: e.activation(out=LF, in_=XF, func=AF.Exp, scale=-1.0), reads=[rGG], writes=[rGG])
            kb.op("dve", lambda e: e.tensor_scalar_add(LF, LF, 1.0), reads=[rGG], writes=[rGG])
            kb.op("act", lambda e: e.activation(out=LF, in_=LF, func=AF.Ln), reads=[rGG], writes=[rGG])
            kb.op("dve", lambda e: e.tensor_scalar_mul(LF, LF, -1.0), reads=[rGG], writes=[rGG])
            LF3 = v3(LF, 10)
            for t in range(10):
                kb.op("pe", (lambda t=t: lambda e: e.matmul(bank(1)[:, t * 32:t * 32 + 8], lhsT=TRIF, rhs=LF3[:, t, 0:8],
                                                            start=True, stop=True))(), reads=[rGG, rCONST], writes=[BK[1]])
                kb.op("pe", (lambda t=t: lambda e: e.matmul(bank(1)[:, t * 32 + 8:t * 32 + 16], lhsT=TRIB, rhs=LF3[:, t, 8:16],
                                                            start=True, stop=True))(), reads=[rGG, rCONST], writes=[BK[1]])
                kb.op("pe", (lambda t=t: lambda e: e.matmul(bank(1)[:, t * 32 + 16:t * 32 + 32], lhsT=ONF, rhs=LF3[:, t, 0:16],
                                                            start=True, stop=True))(), reads=[rGG, rCONST], writes=[BK[1]])
            PS1 = v3(bank(1, 320), 10)
            kb.op("dve", lambda e: e.tensor_copy(GP4[:, :, 0, :], PS1[:, :, 0:16]), reads=[BK[1]], writes=[rGP])
            kb.op("dve", lambda e: e.tensor_copy(GP4[:, :, 3, :], PS1[:, :, 16:32]), reads=[BK[1]], writes=[rGP])
            kb.op("dve", lambda e: e.scalar_tensor_tensor(out=GP4[:, :, 1, :], in0=v3(IG, 10), scalar=LNSC, in1=GP4[:, :, 0, :],
                                                          op0=ALU.add, op1=ALU.subtract), reads=[rGG, rGP], writes=[rGP])
            kb.op("dve", lambda e: e.tensor_tensor(v3(XF, 10), GP4[:, :, 3, :], GP4[:, :, 1, :], ALU.add), reads=[rGP], writes=[rGG])
            kb.op("act", lambda e: e.activation(out=GP4[:, :, 2, :], in_=v3(XF, 10), func=AF.Exp), reads=[rGG], writes=[rGP])
            kb.op("act", lambda e: e.activation(out=GP4[:, :, 4, :], in_=GP4[:, :, 3, :], func=AF.Exp), reads=[rGP], writes=[rGP])
            dump("d_GP", GP, [128, 800], reads=[rGP])
            A.release(pers_mark)

            KTOK = [A.f32(1024) for _ in range(2)]
            VTOK = [A.f32(1024) for _ in range(2)]
            rKTOK = [Res("ktok0"), Res("ktok1")]
            rVTOK = [Res("vtok0"), Res("vtok1")]
            KTLs = [A.bf16(1024) for _ in range(2)]
            rKTLs = [Res("ktl0"), Res("ktl1")]
            VA = [A.bf16(8 * 256) for _ in range(2)]
            rVA = [Res("va0"), Res("va1")]
            QTb = A.bf16(1024)
            KTb = A.bf16(1024)
            rQKb = Res("qkb")
            DG = A.f32(1024)
            rDG = Res("dg")
            EB = A.f32(1024)
            rEB = Res("eb")
            WTm = A.f32(1024)
            rWTm = Res("wtm")
            PTb = A.bf16(1024)
            rPTb = Res("ptb")
            QTL = A.bf16(1024)
            rQTL = Res("qtl")
            DN = A.f32(1024)
            rDN = Res("dn")
            TMPH = A.f32(1024)
            rTMPH = Res("tmph")
            for b in range(2):
                kb.op("pool", (lambda b=b: lambda e: e.memset(v3(VA[b], 8)[:, :, 128:256], 1.0))(), writes=[rVA[b]])
            scan_ctr = [0]

            def scan(chunks, dr, emit):
                g0 = dr * 8
                MASK = TRIF if dr == 0 else TRIB
                for t in chunks:
                    b = scan_ctr[0] % 2
                    scan_ctr[0] += 1
                    va3 = v3(VA[b], 8)
                    KTL = KTLs[b]
                    rKTL = rKTLs[b]
                    stb = 2 if (emit or b == 0) else 4
                    kb.dma("sp", KTOK[b], Ptok.ap()[t * 128:(t + 1) * 128, 2560:3584], writes=[rKTOK[b]])
                    kb.dma("sp", VTOK[b], Ptok.ap()[t * 128:(t + 1) * 128, 3584:4608], writes=[rVTOK[b]])
                    kb.op("dve", (lambda b=b, t=t, KTL=KTL: lambda e: e.tensor_tensor(
                        v3(KTL, 8), v3(KTOK[b], 8), bc(GP4[:, t, 2, g0:g0 + 8].unsqueeze(2), [128, 8, 128]), ALU.mult))(),
                        reads=[rKTOK[b], rGP], writes=[rKTL])
                    kb.op("act", (lambda b=b, va3=va3: lambda e: e.copy(va3[:, :, 0:128], v3(VTOK[b], 8)))(),
                          reads=[rVTOK[b]], writes=[rVA[b]])
                    if emit:
                        kb.dma("pool", v3(QTb, 8), Pch.ap()[CR["Dq"]:CR["Dq"] + 1024, t * 128:(t + 1) * 128].rearrange(
                            "(h d) t -> d h t", d=128), writes=[rQKb])
                        kb.dma("pool", v3(KTb, 8), Pch.ap()[CR["DkT"]:CR["DkT"] + 1024, t * 128:(t + 1) * 128].rearrange(
                            "(h d) t -> d h t", d=128), writes=[rQKb])
                        for h in range(8):
                            kb.op("pe", (lambda h=h: lambda e: e.matmul(
                                PS[:, h * 128:(h + 1) * 128], lhsT=v3(KTb, 8)[:, h, :], rhs=v3(QTb, 8)[:, h, :],
                                start=True, stop=True))(), reads=[rQKb], writes=[BK[h // 4]])
                        kb.op("dve", (lambda t=t: lambda e: e.tensor_tensor(
                            v3(DG, 8), bc(IDF.unsqueeze(1), [128, 8, 128]),
                            bc(GP4[:, t, 0, g0:g0 + 8].unsqueeze(2), [128, 8, 128]), ALU.mult))(),
                            reads=[rGP, rCONST], writes=[rDG])
                        for j in range(2):
                            kb.op("pe", (lambda j=j: lambda e: e.matmul(
                                PS[:, 1024 + j * 512:1536 + j * 512], lhsT=ONF, rhs=DG[:, j * 512:(j + 1) * 512],
                                start=True, stop=True))(), reads=[rDG, rCONST], writes=[BK[2 + j]])
                            kb.op("act", (lambda j=j: lambda e: e.activation(
                                out=EB[:, j * 512:(j + 1) * 512], in_=PS[:, 1024 + j * 512:1536 + j * 512], func=AF.Exp))(),
                                reads=[BK[2 + j]], writes=[rEB])
                        for h in range(8):
                            kb.op("act", (lambda h=h, t=t: lambda e: e.activation(
                                out=WTm[:, h * 128:(h + 1) * 128], in_=PS[:, 1024 + h * 128:1152 + h * 128], func=AF.Exp,
                                bias=GP4[:, t, 1, g0 + h:g0 + h + 1]))(), reads=[BK[2 + h // 4], rGP], writes=[rWTm])
                        kb.op("pool", lambda e: e.tensor_tensor(v3(WTm, 8), v3(WTm, 8), bc(MASK.unsqueeze(1), [128, 8, 128]),
                                                                ALU.mult), reads=[rCONST], writes=[rWTm])
                        kb.op("dve", lambda e: e.tensor_tensor(PTb, PS[:, 0:1024], WTm, ALU.mult),
                              reads=[BK[0], BK[1], rWTm], writes=[rPTb])
                        kb.op("pool", lambda e: e.tensor_tensor(QTL, QTb, EB, ALU.mult), reads=[rQKb, rEB], writes=[rQTL])
                        for h in range(8):
                            kb.op("pe", (lambda h=h, va3=va3: lambda e: e.matmul(
                                PS[:, 2048 + h * 128:2176 + h * 128], lhsT=va3[:, h, 0:128], rhs=v3(PTb, 8)[:, h, :],
                                start=True, stop=False))(), reads=[rVA[b], rPTb], writes=[BK[4 + h // 4]])
                            kb.op("pe", (lambda h=h: lambda e: e.matmul(
                                PS[:, 2048 + h * 128:2176 + h * 128], lhsT=CB3[:, g0 + h, 0:128], rhs=v3(QTL, 8)[:, h, :],
                                start=False, stop=True))(), reads=[rCB, rQTL], writes=[BK[4 + h // 4]])
                        for h in range(8):
                            kb.op("pe", (lambda h=h: lambda e: e.matmul(
                                PS[:, 3072 + h * 128:3200 + h * 128], lhsT=ONB, rhs=v3(PTb, 8)[:, h, :],
                                start=True, stop=False))(), reads=[rCONST, rPTb], writes=[BK[6 + h // 4]])
                            kb.op("pe", (lambda h=h: lambda e: e.matmul(
                                PS[:, 3072 + h * 128:3200 + h * 128], lhsT=CB3[:, g0 + h, 128:256], rhs=v3(QTL, 8)[:, h, :],
                                start=False, stop=True))(), reads=[rCB, rQTL], writes=[BK[6 + h // 4]])
                        kb.op("act", lambda e: e.activation(out=DN, in_=PS[:, 3072:4096], func=AF.Abs),
                              reads=[BK[6], BK[7]], writes=[rDN])
                        kb.op("dve", lambda e: e.tensor_scalar_max(DN, DN, 1.0), reads=[rDN], writes=[rDN])
                        kb.op("dve", lambda e: e.reciprocal(DN, DN), reads=[rDN], writes=[rDN])
                        hs_sl = HS3[:, :, t * 128:(t + 1) * 128]
                        if dr == 0:
                            kb.op("dve", (lambda hs_sl=hs_sl: lambda e: e.tensor_tensor(hs_sl, v3(PS[:, 2048:3072], 8), v3(DN, 8),
                                                                                      ALU.mult))(),
                                  reads=[BK[4], BK[5], rDN], writes=[rHS])
                        else:
                            kb.op("dve", lambda e: e.tensor_tensor(TMPH, PS[:, 2048:3072], DN, ALU.mult),
                                  reads=[BK[4], BK[5], rDN], writes=[rTMPH])
                            kb.op("pool", (lambda hs_sl=hs_sl: lambda e: e.tensor_tensor(hs_sl, hs_sl, v3(TMPH, 8), ALU.add))(),
                                  reads=[rTMPH, rHS], writes=[rHS])
                    kb.op("dve", (lambda t=t: lambda e: e.tensor_tensor(
                        CS3[:, g0:g0 + 8, :], CS3[:, g0:g0 + 8, :], bc(GP4[:, t, 4, g0:g0 + 8].unsqueeze(2), [128, 8, 256]),
                        ALU.mult))(), reads=[rGP, rCS], writes=[rCS])
                    for half in range(2):
                        for hh in range(4):
                            h = half * 4 + hh
                            kb.op("pe", (lambda h=h, hh=hh, va3=va3, KTL=KTL, stb=stb: lambda e: e.matmul(
                                PS[:, stb * 512 + hh * 256:stb * 512 + 256 + hh * 256], lhsT=v3(KTL, 8)[:, h, :], rhs=va3[:, h, :],
                                start=True, stop=True))(), reads=[rKTL, rVA[b]], writes=[BK[stb + hh // 2]])
                        kb.op("dve", (lambda half=half, stb=stb: lambda e: e.tensor_tensor(
                            CS3[:, g0 + half * 4:g0 + half * 4 + 4, :], CS3[:, g0 + half * 4:g0 + half * 4 + 4, :],
                            v3(PS[:, stb * 512:stb * 512 + 1024], 4), ALU.add))(), reads=[BK[stb], BK[stb + 1], rCS], writes=[rCS])
                    if emit:
                        kb.op("act", lambda e: e.copy(CB3[:, g0:g0 + 8, :], CS3[:, g0:g0 + 8, :]), reads=[rCS], writes=[rCB])

            def zero_state():
                kb.op("dve", lambda e: e.memset(CS, 0.0), writes=[rCS])
                kb.op("pool", lambda e: e.memset(CB, 0.0), writes=[rCB])

            zero_state()
            scan([8, 9], 0, not last)
            scan([9, 8], 1, not last)
            kb.op("dve", lambda e: e.tensor_copy(STFB, CS), reads=[rCS], writes=[rSTFB])
            dump("d_STFB", STFB, [128, 4096], reads=[rSTFB])
            dump("d_HSctx", HS, [128, 8 * NT], reads=[rHS])
            zero_state()
            scan(list(range(8)), 0, False)
            scan(list(range(7, -1, -1)), 1, False)
            for dr_ in range(2):
                kb.dma("sp", exr(2 + dr_, 0, 128 * 2048).rearrange("(p x) -> p x", p=128), CS[:, dr_ * 2048:(dr_ + 1) * 2048],
                       reads=[rCS])
            DTT = A.f32(16)
            rDTT = Res("dtt")
            kb.op("dve", lambda e: e.tensor_reduce(out=DTT, in_=GP4[:, 0:8, 3, :].rearrange("p t g -> p g t"), axis=AX.X,
                                                   op=ALU.add), reads=[rGP], writes=[rDTT])
            kb.dma("sp", exr(4, 0, 128 * 16).rearrange("(p x) -> p x", p=128), DTT, reads=[rDTT])
            scan_mark = A.mark()

            if stop_after == "pre":
                return
            kb.issue_collectives("AllGather", ALU.bypass, GROUPS,
                                 [(exp_bufs[c_].ap().opt(), gat_bufs[c_].ap().opt()) for c_ in (2, 3, 4)])
            m2_mark = A.mark()
            KV = [A.f32(512) for _ in range(2)]
            rKV = [Res("kv0"), Res("kv1")]
            KTE = A.f32(2 * 1024)
            KTE3 = v3(KTE, 2)
            rKTE = Res("kte")
            KWS = []
            for i_ in range(2):
                KWS.append(dict(KN=A.f32(256), KR=A.f32(256), KTS=A.f32(256), KT1=A.f32(128), KT2=A.f32(128),
                                KRB=A.bf16(256), SSK=A.f32(24), r=Res("kw%d" % i_)))
            for t in range(10):
                b = t % 2
                W_ = KWS[b]
                bk_ = 4 * b
                kb.dma("sp", KV[b], Ptok.ap()[t * 128:(t + 1) * 128, 2048:2560], writes=[rKV[b]])
                qk_norm_rope(KV[b][:, 0:256], 2, GQK[:, 128:256], t, W_["KN"], W_["KR"], W_["KTS"], W_["SSK"], W_["KT1"], W_["KT2"],
                             rKV[b], W_["r"])
                kb.op("act", (lambda W_=W_: lambda e: e.copy(W_["KRB"], W_["KR"]))(), reads=[W_["r"]], writes=[W_["r"]])
                for g in range(2):
                    kb.op("pe", (lambda g=g, W_=W_, bk_=bk_: lambda e: e.matmul(
                        bank(bk_)[:, g * 128:(g + 1) * 128], lhsT=W_["KRB"][:, g * 128:(g + 1) * 128], rhs=IDB,
                        start=True, stop=True))(), reads=[W_["r"], rCONST], writes=[BK[bk_]])
                if t < 8:
                    kb.op("act", (lambda t=t, bk_=bk_: lambda e: e.copy(KTE3[:, :, t * 128:(t + 1) * 128], v3(bank(bk_, 256), 2)))(),
                          reads=[BK[bk_]], writes=[rKTE])
                else:
                    kb.op("act", (lambda t=t, bk_=bk_: lambda e: e.copy(KTC3[:, :, (t - 8) * 128:(t - 7) * 128],
                                                                      v3(bank(bk_, 256), 2)))(), reads=[BK[bk_]], writes=[rKTC])
            kb.dma("sp", exr(0, 0, 128 * 2048).rearrange("(d x) -> d x", d=128), KTE, reads=[rKTE])
            kb.dma("sp", exr(1, 0, 1024 * 256).rearrange("(t c) -> t c", c=256), Ptok.ap()[0:1024, 2304:2560])
            A.release(m2_mark)

            ATb = [A.f32(NT) for _ in range(2)]
            GTb = [A.f32(NT) for _ in range(2)]
            rATb = [Res("at0"), Res("at1")]
            rGTb = [Res("gt0"), Res("gt1")]
            halo_ex = exr(5, 0, 128 * 240).rearrange("(c g j) -> c g j", c=128, g=8)
            for cg in range(8):
                b = cg % 2
                kb.dma("sp", ATb[b][:, 0:NTL], Pch.ap()[CR["Ba"] + cg * 128:CR["Ba"] + (cg + 1) * 128, 0:NTL], writes=[rATb[b]])
                kb.dma("sp", GTb[b][:, 0:NTL], Pch.ap()[CR["Bg"] + cg * 128:CR["Bg"] + (cg + 1) * 128, 0:NTL], writes=[rGTb[b]])
                kb.op("act", (lambda b=b: lambda e: e.activation(out=GTb[b][:, 0:NTL], in_=GTb[b][:, 0:NTL], func=AF.Sigmoid))(),
                      reads=[rGTb[b]], writes=[rGTb[b]])
                kb.op("dve", (lambda b=b: lambda e: e.tensor_tensor(ATb[b][:, 0:NTL], ATb[b][:, 0:NTL], GTb[b][:, 0:NTL], ALU.mult))(),
                      reads=[rGTb[b], rATb[b]], writes=[rATb[b]])
                kb.dma("sp", Ych.ap()[cg * 128:(cg + 1) * 128, 0:NTL], ATb[b][:, 0:NTL], reads=[rATb[b]])
                kb.dma("sp", halo_ex[:, cg, 0:15], ATb[b][:, 0:15], reads=[rATb[b]])
                kb.dma("sp", halo_ex[:, cg, 15:30], ATb[b][:, 1009:1024], reads=[rATb[b]])
            A.release(m2_mark)

            kb.barrier()
            if stop_after == "gather":
                return
            SG = [A.f32(8 * 256) for _ in range(4)]
            rSG = Res("sg")
            DJ = A.f32(64)
            FF = A.f32(8 * 256)
            rFF = Res("ff")
            for r in range(4):
                kb.dma("sp", DJ[:, r * 16:(r + 1) * 16], gar(4, r, 0, 128 * 16).rearrange("(p x) -> p x", p=128), writes=[rSG])
            kb.op("act", lambda e: e.activation(out=DJ, in_=DJ, func=AF.Exp), reads=[rSG], writes=[rSG])
            for dr in range(2):
                for r in range(4):
                    kb.dma("sp", SG[r], gar(2 + dr, r, 0, 128 * 2048).rearrange("(p x) -> p x", p=128), writes=[rSG])
                CSd = CS[:, dr * 2048:(dr + 1) * 2048]
                kb.op("dve", (lambda dr=dr: lambda e: e.tensor_copy(FF, STFB[:, dr * 2048:(dr + 1) * 2048]))(),
                      reads=[rSTFB], writes=[rFF])
                order = [0, 1, 2] if dr == 0 else [3, 2, 1]
                fs = 0 if dr == 0 else 3
                kb.op("dve", (lambda CSd=CSd, fs=fs: lambda e: e.tensor_scalar_mul(CSd, FF, SEL[:, fs:fs + 1]))(),
                      reads=[rFF, rCONST], writes=[rCS])
                for j in order:
                    for h in range(8):
                        kb.op("dve", (lambda j=j, h=h, dr=dr: lambda e: e.scalar_tensor_tensor(
                            out=v3(FF, 8)[:, h, :], in0=v3(FF, 8)[:, h, :],
                            scalar=DJ[:, j * 16 + dr * 8 + h:j * 16 + dr * 8 + h + 1], in1=v3(SG[j], 8)[:, h, :],
                            op0=ALU.mult, op1=ALU.add))(), reads=[rSG, rFF], writes=[rFF])
                    nx = j + 1 if dr == 0 else j - 1
                    kb.op("dve", (lambda CSd=CSd, nx=nx: lambda e: e.scalar_tensor_tensor(
                        out=CSd, in0=FF, scalar=SEL[:, nx:nx + 1], in1=CSd, op0=ALU.mult, op1=ALU.add))(),
                        reads=[rFF, rCONST, rCS], writes=[rCS])
            kb.op("act", lambda e: e.copy(CB, CS), reads=[rCS], writes=[rCB])
            dump("d_INIT", CS, [128, 4096], reads=[rCS])
            kb.issue_collectives("AllGather", ALU.bypass, GROUPS,
                                 [(exp_bufs[c_].ap().opt(), gat_bufs[c_].ap().opt()) for c_ in (0, 1, 5)])
            scan(list(range(8)), 0, True)
            scan(list(range(7, -1, -1)), 1, True)
            dump("d_HS", HS, [128, 8 * NT], reads=[rHS])
            A.release(pers_mark)
            OTb = [A.f32(NT) for _ in range(2)]
            ZTb = [A.f32(NT) for _ in range(2)]
            rOTb = [Res("ot0"), Res("ot1")]
            rZTb = [Res("zt0"), Res("zt1")]
            SQs = [A.f32(NT) for _ in range(2)]
            rSQs = [Res("sq0"), Res("sq1")]
            RSs = [A.f32(NT) for _ in range(2)]
            rRSs = [Res("rs0"), Res("rs1")]
            for h in range(8):
                b = h % 2
                SQ, rSQ, RS, rRS = SQs[b], rSQs[b], RSs[b], rRSs[b]
                kb.dma("sp", OTb[b][:, 0:NTL], Pch.ap()[CR["Do"] + h * 128:CR["Do"] + (h + 1) * 128, 0:NTL], writes=[rOTb[b]])
                kb.dma("sp", ZTb[b][:, 0:NTL], Pch.ap()[CR["Dz"] + h * 128:CR["Dz"] + (h + 1) * 128, 0:NTL], writes=[rZTb[b]])
                kb.op("act", (lambda h=h, SQ=SQ: lambda e: e.activation(out=SQ[:, 0:NTL], in_=HS3[:, h, 0:NTL], func=AF.Square))(),
                      reads=[rHS], writes=[rSQ])
                for i, (t0, tn) in enumerate(tts):
                    bk = 3 * b + i
                    kb.op("pe", (lambda bk=bk, t0=t0, tn=tn, SQ=SQ: lambda e: e.matmul(bank(bk, tn), lhsT=ONF, rhs=SQ[:, t0:t0 + tn],
                                                                                   start=True, stop=True))(),
                          reads=[rSQ, rCONST], writes=[BK[bk]])
                    kb.op("dve", (lambda bk=bk, t0=t0, tn=tn, RS=RS: lambda e: e.tensor_scalar(RS[:, t0:t0 + tn], bank(bk, tn), 1.0 / 128, EPS,
                                                                                         ALU.mult, ALU.add))(),
                          reads=[BK[bk]], writes=[rRS])
                kb.op("act", (lambda RS=RS: lambda e: e.sqrt(RS[:, 0:NTL], RS[:, 0:NTL]))(), reads=[rRS], writes=[rRS])
                kb.op("dve", (lambda RS=RS: lambda e: e.reciprocal(RS[:, 0:NTL], RS[:, 0:NTL]))(), reads=[rRS], writes=[rRS])
                kb.op("dve", (lambda h=h, SQ=SQ, RS=RS: lambda e: e.tensor_tensor(SQ[:, 0:NTL], HS3[:, h, 0:NTL], RS[:, 0:NTL], ALU.mult))(),
                      reads=[rHS, rRS, rSQ], writes=[rSQ])
                kb.op("act", (lambda b=b: lambda e: e.activation(out=OTb[b][:, 0:NTL], in_=OTb[b][:, 0:NTL], func=AF.Sigmoid))(),
                      reads=[rOTb[b]], writes=[rOTb[b]])
                kb.op("act", (lambda b=b: lambda e: e.activation(out=ZTb[b][:, 0:NTL], in_=ZTb[b][:, 0:NTL], func=AF.Silu))(),
                      reads=[rZTb[b]], writes=[rZTb[b]])
                kb.op("pool", (lambda b=b, SQ=SQ: lambda e: e.tensor_tensor(SQ[:, 0:NTL], SQ[:, 0:NTL], OTb[b][:, 0:NTL], ALU.mult))(),
                      reads=[rOTb[b], rSQ], writes=[rSQ])
                stg, rstg = mix_stage()
                kb.op("dve", (lambda b=b, h=h, stg=stg, SQ=SQ: lambda e: e.scalar_tensor_tensor(
                    out=stg[:, 0:NTL], in0=SQ[:, 0:NTL], scalar=PT3[:, 24 + h:25 + h], in1=ZTb[b][:, 0:NTL],
                    op0=ALU.mult, op1=ALU.mult))(), reads=[rSQ, rZTb[b], rPT], writes=[rstg])
                mix_store(24 + h, stg, rstg)
            kb.barrier()
            A.release(pers2_mark)
            LNG = A.f32(1024)
            LNB = A.f32(1024)
            SBI = A.f32(1024)
            rAP = Res("ap")
            kb.dma("sp", LNG, sglg_in[l:l + 1, :].partition_broadcast(128), writes=[rAP])
            kb.dma("sp", LNB, sglb_in[l:l + 1, :].partition_broadcast(128), writes=[rAP])
            kb.dma("sp", SBI, sgub_in[l:l + 1, :].partition_broadcast(128), writes=[rAP])
            WSF = [A.f32(128) for _ in range(2)]
            rWSF = [Res("wsf0"), Res("wsf1")]
            WST = A.bf16(1024)
            rWST = Res("wst")
            for h in range(8):
                b = h % 2
                kb.dma("sp", WSF[b], sguw_in[l, h], writes=[rWSF[b]])
                kb.op("pe", (lambda b=b: lambda e: e.matmul(bank(b, 128), lhsT=WSF[b], rhs=IDF, start=True, stop=True))(),
                      reads=[rWSF[b], rCONST], writes=[BK[b]])
                kb.op("dve", (lambda b=b, h=h: lambda e: e.tensor_copy(WST[:, h * 128:(h + 1) * 128], bank(b, 128)))(),
                      reads=[BK[b]], writes=[rWST])
            VN = A.bf16(TL * 1024)
            VN3 = v3(VN, TL)
            rVN = Res("vn")
            VT = [A.f32(1024) for _ in range(2)]
            rVT = [Res("vt0"), Res("vt1")]
            AJ = A.f32(1024)
            rAJ = Res("aj")
            ST = A.f32(16)
            rST = Res("st")
            for t in range(TL):
                b = t % 2
                kb.dma("sp", VT[b], Ptok.ap()[t * 128:(t + 1) * 128, 0:1024], writes=[rVT[b]])
                kb.op("act", (lambda b=b: lambda e: e.activation(out=AJ, in_=VT[b], func=AF.Copy, accum_out=ST[:, 0:1]))(),
                      reads=[rVT[b]], writes=[rAJ, rST])
                kb.op("act", (lambda b=b: lambda e: e.activation(out=AJ, in_=VT[b], func=AF.Square, accum_out=ST[:, 1:2]))(),
                      reads=[rVT[b]], writes=[rAJ, rST])
                kb.op("dve", lambda e: e.tensor_scalar_mul(ST[:, 2:4], ST[:, 0:2], 1.0 / 1024), reads=[rST], writes=[rST])
                kb.op("dve", lambda e: e.tensor_tensor(ST[:, 4:5], ST[:, 2:3], ST[:, 2:3], ALU.mult), reads=[rST], writes=[rST])
                kb.op("dve", lambda e: e.tensor_tensor(ST[:, 5:6], ST[:, 3:4], ST[:, 4:5], ALU.subtract), reads=[rST], writes=[rST])
                rsqrt_ops(ST[:, 7:8], ST[:, 5:6], 1.0, ST[:, 6:7], [rST], [rST])
                kb.op("dve", (lambda b=b: lambda e: e.tensor_scalar(VT[b], VT[b], ST[:, 2:3], ST[:, 7:8], ALU.subtract, ALU.mult))(),
                      reads=[rST, rVT[b]], writes=[rVT[b]])
                kb.op("pool", (lambda b=b: lambda e: e.tensor_tensor(VT[b], VT[b], LNG, ALU.mult))(), reads=[rAP, rVT[b]], writes=[rVT[b]])
                kb.op("dve", (lambda b=b, t=t: lambda e: e.tensor_tensor(VN3[:, t, :], VT[b], LNB, ALU.add))(),
                      reads=[rAP, rVT[b]], writes=[rVN])
            UT = [A.f32(NT) for _ in range(2)]
            ZT2 = [A.f32(NT) for _ in range(2)]
            rUT = [Res("ut0"), Res("ut1")]
            rZT2 = [Res("zt20"), Res("zt21")]
            TMAs = [A.f32(NT) for _ in range(2)]
            rTMAs = [Res("tma0"), Res("tma1")]
            for h in range(8):
                b = h % 2
                TMA, rTMA = TMAs[b], rTMAs[b]
                po = 1536 * b
                kb.dma("sp", UT[b][:, 0:NTL], Pch.ap()[CR["Au"] + h * 128:CR["Au"] + (h + 1) * 128, 0:NTL], writes=[rUT[b]])
                kb.dma("sp", ZT2[b][:, 0:NTL], Pch.ap()[CR["Az"] + h * 128:CR["Az"] + (h + 1) * 128, 0:NTL], writes=[rZT2[b]])
                kb.op("act", (lambda b=b: lambda e: e.activation(out=ZT2[b][:, 0:NTL], in_=ZT2[b][:, 0:NTL], func=AF.Silu))(),
                      reads=[rZT2[b]], writes=[rZT2[b]])
                kb.op("pool", (lambda b=b: lambda e: e.tensor_tensor(UT[b][:, 0:NTL], UT[b][:, 0:NTL], ZT2[b][:, 0:NTL], ALU.mult))(),
                      reads=[rZT2[b], rUT[b]], writes=[rUT[b]])
                for t in range(TL):
                    kb.op("pe", (lambda h=h, t=t, po=po: lambda e: e.matmul(
                        PS[:, po + t * 128:po + (t + 1) * 128], lhsT=VN3[:, t, h * 128:(h + 1) * 128], rhs=WST[:, h * 128:(h + 1) * 128],
                        start=True, stop=True))(), reads=[rVN, rWST], writes=[BK[3 * b + t // 4]])
                kb.op("dve", (lambda h=h, po=po, TMA=TMA: lambda e: e.tensor_tensor(
                    v3(TMA[:, 0:NTL], TL), v3(PS[:, po:po + NTL], TL), bc(SBI[:, h * 128:(h + 1) * 128].unsqueeze(1), [128, TL, 128]),
                    ALU.add))(), reads=[BK[3 * b], BK[3 * b + 1], BK[3 * b + 2], rAP], writes=[rTMA])
                stg, rstg = mix_stage()
                kb.op("dve", (lambda b=b, stg=stg, TMA=TMA: lambda e: e.tensor_tensor(stg[:, 0:NTL], TMA[:, 0:NTL], UT[b][:, 0:NTL], ALU.mult))(),
                      reads=[rTMA, rUT[b]], writes=[rstg])
                mix_store(h, stg, rstg)
            kb.barrier()
            A.release(pers2_mark)

            HG = A.f32(4 * 240)
            rHG = Res("hg")
            for r in range(4):
                kb.dma("sp", HG[:, r * 240:(r + 1) * 240], gar(5, r, 0, 128 * 240).rearrange("(c x) -> c x", c=128), writes=[rHG])
            HG4 = HG.rearrange("p (r g j) -> p r g j", r=4, g=8)
            LH = A.f32(8 * 15)
            RH = A.f32(8 * 15)
            rLR = Res("lr")
            for r in range(4):
                for (dst, so, j0) in ((LH, 4, 15), (RH, 8, 0)):
                    src = HG4[:, r, :, j0:j0 + 15]
                    if r == 0:
                        kb.op("dve", (lambda dst=dst, src=src, so=so, r=r: lambda e: e.tensor_scalar_mul(
                            v3(dst, 8), src, SEL[:, so + r:so + r + 1]))(), reads=[rHG, rCONST], writes=[rLR])
                    else:
                        kb.op("dve", (lambda dst=dst, src=src, so=so, r=r: lambda e: e.scalar_tensor_tensor(
                            out=v3(dst, 8), in0=src, scalar=SEL[:, so + r:so + r + 1], in1=v3(dst, 8), op0=ALU.mult, op1=ALU.add))(),
                            reads=[rHG, rCONST, rLR], writes=[rLR])
            CONV = A.f32(8 * NT)
            CONV3 = v3(CONV, 8)
            rCONV = [Res("conv%d" % i) for i in range(8)]
            YP = [A.bf16(1054 + 286) for _ in range(2)]
            rYP = [Res("yp0"), Res("yp1")]
            DIAG = [A.bf16(31 * 128) for _ in range(2)]
            rDIAG = [Res("diag0"), Res("diag1")]
            CWr = A.f32(248)
            rCWr = Res("cwr")
            kb.op("dve", lambda e: e.tensor_copy(v3(CWr, 8), v3(CW, 31).rearrange("p k g -> p g k")), reads=[rPT], writes=[rCWr])
            SQB = A.f32(NT)
            rSQB = Res("sqb")
            for b in range(2):
                kb.op("pool", (lambda b=b: lambda e: e.memset(YP[b], 0.0))(), writes=[rYP[b]])
            cvb = 0
            for cg in range(8):
                b = cg % 2
                kb.dma("pool", YP[b][:, 15:1039], Ych.ap()[cg * 128:(cg + 1) * 128, 0:1024], writes=[rYP[b]])
                if not last:
                    kb.dma("pool", YP[b][:, 1054 + 15:1054 + 271], Ych.ap()[cg * 128:(cg + 1) * 128, 1024:1280], writes=[rYP[b]])
                kb.op("dve", (lambda b=b, cg=cg: lambda e: e.tensor_copy(YP[b][:, 0:15], v3(LH, 8)[:, cg, :]))(), reads=[rLR], writes=[rYP[b]])
                kb.op("dve", (lambda b=b, cg=cg: lambda e: e.tensor_copy(YP[b][:, 1039:1054], v3(RH, 8)[:, cg, :]))(), reads=[rLR], writes=[rYP[b]])
                kb.op("dve", (lambda b=b, cg=cg: lambda e: e.tensor_tensor(
                    v3(DIAG[b], 31), bc(IDF.unsqueeze(1), [128, 31, 128]), bc(v3(CWr, 8)[:, cg, :].unsqueeze(2), [128, 31, 128]),
                    ALU.mult))(), reads=[rCWr, rCONST], writes=[rDIAG[b]])
                for (ys, co, n) in [(0, 0, 512), (512, 512, 512)] + ([] if last else [(1054, 1024, 256)]):
                    bk = 6 + (cvb % 2)
                    cvb += 1
                    for k in range(31):
                        kb.op("pe", (lambda b=b, bk=bk, k=k, ys=ys, n=n: lambda e: e.matmul(
                            bank(bk, n), lhsT=v3(DIAG[b], 31)[:, k, :], rhs=YP[b][:, ys + k:ys + k + n],
                            start=(k == 0), stop=(k == 30)))(), reads=[rDIAG[b], rYP[b]], writes=[BK[bk]])
                    kb.op("act", (lambda bk=bk, cg=cg, co=co, n=n: lambda e: e.activation(
                        out=CONV3[:, cg, co:co + n], in_=bank(bk, n), func=AF.Identity, bias=PT3[:, cg:cg + 1]))(),
                        reads=[BK[bk], rPT], writes=[rCONV[cg]])
                kb.op("act", (lambda cg=cg: lambda e: e.activation(out=SQB[:, 0:NTL], in_=CONV3[:, cg, 0:NTL], func=AF.Square))(),
                      reads=[rCONV[cg]], writes=[rSQB])
                for i, (t0, tn) in enumerate(tts):
                    kb.op("pe", (lambda cg=cg, i=i, t0=t0, tn=tn: lambda e: e.matmul(
                        bank(i, tn), lhsT=ONF, rhs=CONV3[:, cg, t0:t0 + tn], start=(cg == 0), stop=(cg == 7)))(),
                        reads=[rCONV[cg], rCONST], writes=[BK[i]])
                    kb.op("pe", (lambda cg=cg, i=i, t0=t0, tn=tn: lambda e: e.matmul(
                        bank(3 + i, tn), lhsT=ONF, rhs=SQB[:, t0:t0 + tn], start=(cg == 0), stop=(cg == 7)))(),
                        reads=[rSQB, rCONST], writes=[BK[3 + i]])
            dump("d_CONV", CONV, [128, 8 * NT], reads=rCONV)
            MEAN = A.f32(NT)
            RSTD = A.f32(NT)
            MSQ = A.f32(NT)
            rMS = Res("ms")
            for i, (t0, tn) in enumerate(tts):
                kb.op("dve", (lambda i=i, t0=t0, tn=tn: lambda e: e.tensor_scalar_mul(MEAN[:, t0:t0 + tn], bank(i, tn), 1.0 / 1024))(),
                      reads=[BK[i]], writes=[rMS])
                kb.op("dve", (lambda i=i, t0=t0, tn=tn: lambda e: e.tensor_scalar_mul(RSTD[:, t0:t0 + tn], bank(3 + i, tn), 1.0 / 1024))(),
                      reads=[BK[3 + i]], writes=[rMS])
            kb.op("dve", lambda e: e.tensor_tensor(MSQ[:, 0:NTL], MEAN[:, 0:NTL], MEAN[:, 0:NTL], ALU.mult), reads=[rMS], writes=[rMS])
            kb.op("dve", lambda e: e.tensor_tensor(RSTD[:, 0:NTL], RSTD[:, 0:NTL], MSQ[:, 0:NTL], ALU.subtract), reads=[rMS], writes=[rMS])
            rsqrt_ops(RSTD[:, 0:NTL], RSTD[:, 0:NTL], 1.0, MSQ[:, 0:NTL], [rMS], [rMS])
            dump("d_MEAN", MEAN, [128, NT], reads=[rMS])
            dump("d_RSTD", RSTD, [128, NT], reads=[rMS])
            ZB = [A.f32(NT) for _ in range(2)]
            rZB = [Res("zb0"), Res("zb1")]
            for cg in range(8):
                b = cg % 2
                kb.dma("sp", ZB[b][:, 0:NTL], Pch.ap()[CR["Bz"] + cg * 128:CR["Bz"] + (cg + 1) * 128, 0:NTL], writes=[rZB[b]])
                cv = CONV3[:, cg, 0:NTL]
                kb.op("dve", (lambda cv=cv: lambda e: e.tensor_tensor(cv, cv, MEAN[:, 0:NTL], ALU.subtract))(), reads=[rMS, rCONV[cg]], writes=[rCONV[cg]])
                kb.op("pool", (lambda cv=cv: lambda e: e.tensor_tensor(cv, cv, RSTD[:, 0:NTL], ALU.mult))(), reads=[rMS, rCONV[cg]], writes=[rCONV[cg]])
                kb.op("act", (lambda cv=cv, cg=cg: lambda e: e.activation(out=cv, in_=cv, func=AF.Silu, scale=PT3[:, 8 + cg:9 + cg],
                                                                          bias=PT3[:, 16 + cg:17 + cg]))(), reads=[rPT, rCONV[cg]], writes=[rCONV[cg]])
                kb.op("act", (lambda b=b: lambda e: e.activation(out=ZB[b][:, 0:NTL], in_=ZB[b][:, 0:NTL], func=AF.Silu))(),
                      reads=[rZB[b]], writes=[rZB[b]])
                stg, rstg = mix_stage()
                kb.op("dve", (lambda cv=cv, b=b, stg=stg: lambda e: e.tensor_tensor(stg[:, 0:NTL], cv, ZB[b][:, 0:NTL], ALU.mult))(),
                      reads=[rCONV[cg], rZB[b]], writes=[rstg])
                mix_store(8 + cg, stg, rstg)
            kb.barrier()
            A.release(pers2_mark)

            W2PRE = A.bf16(32 * 512)
            rWT2 = [Res("w2t0"), Res("w2t1")]
            wsrc2 = wout_in[l].rearrange("(kc kp) n -> kp kc n", kp=128)
            for g_ in range(4):
                kb.dma("pool", v3(W2PRE, 32)[:, g_ * 8:(g_ + 1) * 8, :], wsrc2[:, g_ * 8:(g_ + 1) * 8, 0:512], writes=[rWT2[0]])
            w2_mark = A.mark()
            QT = A.bf16(8 * NT)
            QT3 = v3(QT, 8)
            rQT = Res("qt")
            QF = [A.f32(1024) for _ in range(2)]
            rQF = [Res("qf0"), Res("qf1")]
            QWS = []
            for i_ in range(2):
                QWS.append(dict(QN=A.f32(1024), QR=A.f32(1024), QTS=A.f32(1024), QT1=A.f32(512), QT2=A.f32(512),
                                QRB=A.bf16(1024), SSQ=A.f32(24), r=Res("qw%d" % i_)))
            for t in range(TL):
                b = t % 2
                W_ = QWS[b]
                pb0 = 4 * b
                kb.dma("sp", QF[b], Ptok.ap()[t * 128:(t + 1) * 128, 1024:2048], writes=[rQF[b]])
                qk_norm_rope(QF[b], 8, GQK[:, 0:128], t, W_["QN"], W_["QR"], W_["QTS"], W_["SSQ"], W_["QT1"], W_["QT2"], rQF[b], W_["r"])
                kb.op("act", (lambda W_=W_: lambda e: e.copy(W_["QRB"], W_["QR"]))(), reads=[W_["r"]], writes=[W_["r"]])
                for h in range(8):
                    kb.op("pe", (lambda h=h, W_=W_, pb0=pb0: lambda e: e.matmul(
                        PS[:, pb0 * 512 + h * 128:pb0 * 512 + (h + 1) * 128], lhsT=W_["QRB"][:, h * 128:(h + 1) * 128],
                        rhs=IDB, start=True, stop=True))(), reads=[W_["r"], rCONST], writes=[BK[pb0 + h // 4]])
                kb.op("act", (lambda t=t, pb0=pb0: lambda e: e.copy(QT3[:, :, t * 128:(t + 1) * 128],
                                                                  v3(PS[:, pb0 * 512:pb0 * 512 + 1024], 8)))(),
                      reads=[BK[pb0], BK[pb0 + 1]], writes=[rQT])
            KTA = A.bf16(2 * 4352)
            KTA3 = v3(KTA, 2)
            VAL = A.bf16(34 * 256)
            VAL3 = v3(VAL, 34)
            rKVA = Res("kva")
            for r in range(4):
                kb.dma("pool", KTA3[:, :, r * 1024:(r + 1) * 1024],
                       gar(0, r, 0, 128 * 2048).rearrange("(d g t) -> d g t", d=128, g=2), writes=[rKVA])
                kb.dma("pool", VAL3[:, r * 8:(r + 1) * 8, :],
                       gar(1, r, 0, 1024 * 256).rearrange("(kt p c) -> p kt c", p=128, c=256), writes=[rKVA])
            kb.op("dve", lambda e: e.tensor_copy(KTA3[:, :, 4096:4352], KTC3), reads=[rKTC], writes=[rKVA])
            kb.dma("pool", VAL3[:, 32:34, :], Ptok.ap()[1024:1280, 2304:2560].rearrange("(kt p) c -> p kt c", p=128), writes=[rKVA])
            dump("d_KTA", KTA, [128, 2 * 4352], BF16, reads=[rKVA])
            dump("d_VAL", VAL, [128, 34 * 256], BF16, reads=[rKVA])
            dump("d_QT", QT, [128, 8 * NT], BF16, reads=[rQT])
            SZ = [A.f32(NT) for _ in range(2)]
            rSZ = [Res("sz0"), Res("sz1")]
            PTA = [A.bf16(512) for _ in range(4)]
            rPTA = [Res("pta%d" % i) for i in range(4)]
            RL = A.f32(512)
            rRL = Res("rl")
            OA = A.f32(512)
            rOA = Res("oa")
            SC = 128.0 ** -0.5
            pcount = 0
            for h in range(8):
                g = h // 4
                b = h % 2
                kb.dma("sp", SZ[b][:, 0:NTL], Pch.ap()[CR["Cz"] + h * 128:CR["Cz"] + (h + 1) * 128, 0:NTL], writes=[rSZ[b]])
                kb.op("act", (lambda b=b: lambda e: e.activation(out=SZ[b][:, 0:NTL], in_=SZ[b][:, 0:NTL], func=AF.Silu))(),
                      reads=[rSZ[b]], writes=[rSZ[b]])
                stg, rstg = mix_stage()
                qtiles = [(0, 512, list(range(34))), (512, 512, list(range(34)))] + ([] if last else [(1024, 256, [32, 33])])
                for (q0, qn, kts) in qtiles:
                    nk = len(kts)

                    def emit_s(ki, q0=q0, qn=qn, kts=kts, g=g, h=h):
                        sb = 2 + (ki % 3)
                        kt = kts[ki]
                        kb.op("pe", (lambda: lambda e: e.matmul(
                            bank(sb, qn), lhsT=KTA3[:, g, kt * 128:(kt + 1) * 128], rhs=QT3[:, h, q0:q0 + qn],
                            start=True, stop=True))(), reads=[rKVA, rQT], writes=[BK[sb]])
                    emit_s(0)
                    if nk > 1:
                        emit_s(1)
                    for ki, kt in enumerate(kts):
                        sb = 2 + (ki % 3)
                        pb = pcount % 4
                        pcount += 1
                        if ki + 2 < nk:
                            emit_s(ki + 2)
                        kb.op("act", (lambda sb=sb, pb=pb, qn=qn: lambda e: e.activation(
                            out=PTA[pb][:, 0:qn], in_=bank(sb, qn), func=AF.Exp, scale=SC))(), reads=[BK[sb]], writes=[rPTA[pb]])
                        kb.op("pe", (lambda pb=pb, kt=kt, g=g, qn=qn, ki=ki, nk=nk: lambda e: e.matmul(
                            bank(0, qn), lhsT=VAL3[:, kt, g * 128:(g + 1) * 128], rhs=PTA[pb][:, 0:qn],
                            start=(ki == 0), stop=(ki == nk - 1)))(), reads=[rKVA, rPTA[pb]], writes=[BK[0]])
                        kb.op("pe", (lambda pb=pb, qn=qn, ki=ki, nk=nk: lambda e: e.matmul(
                            bank(1, qn), lhsT=ONB, rhs=PTA[pb][:, 0:qn], start=(ki == 0), stop=(ki == nk - 1)))(),
                            reads=[rCONST, rPTA[pb]], writes=[BK[1]])
                    kb.op("dve", (lambda qn=qn: lambda e: e.reciprocal(RL[:, 0:qn], bank(1, qn)))(), reads=[BK[1]], writes=[rRL])
                    kb.op("dve", (lambda qn=qn: lambda e: e.tensor_tensor(OA[:, 0:qn], bank(0, qn), RL[:, 0:qn], ALU.mult))(),
                          reads=[BK[0], rRL], writes=[rOA])
                    kb.op("pool", (lambda qn=qn, q0=q0, b=b, stg=stg: lambda e: e.tensor_tensor(
                        stg[:, q0:q0 + qn], OA[:, 0:qn], SZ[b][:, q0:q0 + qn], ALU.mult))(), reads=[rOA, rSZ[b]], writes=[rstg])
                mix_store(16 + h, stg, rstg)
            kb.barrier()

            A.release(w2_mark)
            BIG2 = A.f32(16 * NT)
            MX = v3(BIG2.bitcast(BF16), 32)
            rMX = Res("mx")
            for kc in range(32):
                kb.dma("sp", MX[:, kc, 0:NTL], MIXd.ap()[kc * 128:(kc + 1) * 128, 0:NTL], writes=[rMX])
            GLn = [A.f32(512) for _ in range(2)]
            GCn = [A.f32(512) for _ in range(2)]
            GBn = [A.f32(512) for _ in range(2)]
            rGn = [Res("gn0"), Res("gn1")]
            WT2 = [W2PRE, A.bf16(32 * 512)]
            XO = [A.f32(512) for _ in range(3)]
            rXO = [Res("xo%d" % i) for i in range(3)]
            OO = [A.f32(512) for _ in range(3)]
            rOO = [Res("oo%d" % i) for i in range(3)]
            def load_w2(n, b):
                w3 = v3(WT2[b], 32)
                for g in range(4):
                    kb.dma("pool", w3[:, g * 8:(g + 1) * 8, :], wsrc2[:, g * 8:(g + 1) * 8, n * 512:(n + 1) * 512], writes=[rWT2[b]])
            oc = 0
            for n in range(8):
                b = n % 2
                if n + 1 < 8:
                    load_w2(n + 1, 1 - b)
                w3 = v3(WT2[b], 32)
                cs_ = slice(2 * D + n * 512, 2 * D + (n + 1) * 512)
                kb.dma("sp", GLn[b], af[2 * l:2 * l + 1, cs_].partition_broadcast(128), writes=[rGn[b]])
                kb.dma("sp", GBn[b], bada_in[l:l + 1, cs_].partition_broadcast(128), writes=[rGn[b]])
                kb.op("dve", (lambda b=b: lambda e: e.tensor_tensor(GLn[b], GLn[b], GBn[b], ALU.add))(), reads=[rGn[b]], writes=[rGn[b]])
                if not last:
                    kb.dma("sp", GCn[b], af[2 * l + 1:2 * l + 2, cs_].partition_broadcast(128), writes=[rGn[b]])
                    kb.op("dve", (lambda b=b: lambda e: e.tensor_tensor(GCn[b], GCn[b], GBn[b], ALU.add))(), reads=[rGn[b]], writes=[rGn[b]])
                for t in range(TL):
                    bk = t % 2
                    ob = oc % 3
                    oc += 1
                    kb.dma("sp", XO[ob], tok_src(t, n * 512, (n + 1) * 512), writes=[rXO[ob]])
                    for kc in range(32):
                        kb.op("pe", (lambda bk=bk, kc=kc, t=t, w3=w3: lambda e: e.matmul(
                            bank(bk), lhsT=MX[:, kc, t * 128:(t + 1) * 128], rhs=w3[:, kc, :], start=(kc == 0), stop=(kc == 31)))(),
                            reads=[rMX, rWT2[b]], writes=[BK[bk]])
                    Gt = GLn[b] if t < 8 else GCn[b]
                    kb.op("dve", (lambda bk=bk, ob=ob, Gt=Gt: lambda e: e.tensor_tensor(
                        OO[ob], bank(bk), Gt, ALU.mult))(), reads=[BK[bk], rGn[b]], writes=[rOO[ob]])
                    kb.op("pool", (lambda ob=ob: lambda e: e.tensor_tensor(OO[ob], OO[ob], XO[ob], ALU.add))(),
                          reads=[rXO[ob], rOO[ob]], writes=[rOO[ob]])
                    if last:
                        dst = y_out[t * 128:(t + 1) * 128, n * 512:(n + 1) * 512]
                    else:
                        dst = xs.ap()[t * 128:(t + 1) * 128, n * 512:(n + 1) * 512]
                    kb.dma("sp", dst, OO[ob], reads=[rOO[ob]])
            kb.barrier()
            A.release(lay_mark)

        for l_ in range(nlayers):
            emit_layer(l_)

        kb.barrier()
        block = es.enter_context(nc.Block())
        kb.replay(block)
    return nc


def LAYER_BODY_2(env):
    pass


def _consts():
    c = np.zeros((128, 5 * 128), np.float32)
    c[:, 0:128] = np.eye(128, dtype=np.float32)
    c[:, 128:256] = 1.0
    s = np.arange(128)[:, None]
    l_ = np.arange(128)[None, :]
    c[:, 256:384] = (s <= l_).astype(np.float32)
    c[:, 384:512] = (s >= l_).astype(np.float32)
    return c


def _rope_table(seg):
    n = np.arange(seg * 1024, (seg + 1) * 1024)
    row = (n // 64).astype(np.float32)
    col = (n % 64).astype(np.float32)
    freq = (10000.0 ** (-np.arange(32, dtype=np.float32) / 32)).astype(np.float32)
    ang = np.stack([row, col], -1)[..., None] * freq
    cs = np.concatenate([np.cos(ang).reshape(1024, 64), np.sin(ang).reshape(1024, 64)], -1).astype(np.float32)
    return np.ascontiguousarray(cs.reshape(8, 128, 128).transpose(1, 0, 2))


def make_in_maps(inputs):
    f = lambda a: np.ascontiguousarray(np.asarray(a, dtype=np.float32))
    x = f(inputs["x"]); c = f(inputs["c"]); ctx = f(inputs["ctx"]); c_ctx = f(inputs["c_ctx"])
    w_ada = f(inputs["w_ada"])
    shared = {
        "b_ada": f(inputs["b_ada"]), "norm_g": f(inputs["norm_g"]), "w_in": f(inputs["w_in"]),
        "sgu_w": f(inputs["sgu_w"]), "sgu_b": f(inputs["sgu_b"]).reshape(DEPTH, 1024),
        "sgu_ln_g": f(inputs["sgu_ln_g"]), "sgu_ln_b": f(inputs["sgu_ln_b"]),
        "conv_w": f(inputs["conv_w"]).reshape(DEPTH, 248, 128),
        "conv_b": f(inputs["conv_b"]).reshape(DEPTH, 8, 128),
        "conv_ln_g": f(inputs["conv_ln_g"]).reshape(DEPTH, 8, 128),
        "conv_ln_b": f(inputs["conv_ln_b"]).reshape(DEPTH, 8, 128),
        "q_norm_g": f(inputs["q_norm_g"]), "k_norm_g": f(inputs["k_norm_g"]),
        "mlstm_i_bias": f(inputs["mlstm_i_bias"]).reshape(DEPTH, 16),
        "mlstm_f_bias": f(inputs["mlstm_f_bias"]).reshape(DEPTH, 16),
        "mh_norm_g": f(inputs["mh_norm_g"]).reshape(DEPTH, 8, 128),
        "w_out": f(inputs["w_out"]), "consts": _consts(),
    }
    maps = []
    for core in range(8):
        b, s = core // 4, core % 4
        m = dict(shared)
        m["x"] = np.ascontiguousarray(x[b, s * 1024:(s + 1) * 1024])
        m["ctx"] = np.ascontiguousarray(ctx[b])
        cc = np.stack([c[b], c_ctx], 0)
        m["ccT"] = np.ascontiguousarray(cc[:, s * 1024:(s + 1) * 1024].T)
        m["w_ada_s"] = np.ascontiguousarray(w_ada[:, s * 1024:(s + 1) * 1024, :])
        m["rope"] = _rope_table(s)
        sel = np.zeros((128, 12), np.float32)
        sel[:, s] = 1.0
        if s - 1 >= 0:
            sel[:, 4 + s - 1] = 1.0
        if s + 1 <= 3:
            sel[:, 8 + s + 1] = 1.0
        m["sel"] = sel
        maps.append(m)
    return maps


def kernel(**inputs):
    nc = build_program()
    maps = make_in_maps(inputs)
    res = run_bass_kernel_spmd(nc, maps, core_ids=list(range(8)))
    out = np.empty((2, 4096, D), np.float32)
    for core in range(8):
        b, s = core // 4, core % 4
        out[b, s * 1024:(s + 1) * 1024] = res.results[core]["y"]
    return out
```

```python
import math
import numpy as np
import ml_dtypes
from contextlib import ExitStack
import concourse.bass as bass
import concourse.mybir as mybir
from concourse.bass_utils import run_bass_kernel_spmd

F32 = mybir.dt.float32
BF16 = mybir.dt.bfloat16
AF = mybir.ActivationFunctionType
ALU = mybir.AluOpType
AX = mybir.AxisListType

D = 4096
PIN = 13856
DEPTH = 2
NT = 1280
NL = 1024
EPS = 1e-6
LNSC = math.log(128.0 ** -0.5)

TOKB = [("Av", 1024, 1024, 0), ("Cq", 6144, 1024, 1024), ("Ckv", 7168, 512, 2048),
        ("Dk", 9728, 1024, 2560), ("Dv", 10752, 1024, 3584), ("G", 13824, 32, 4608)]
NTOKC = 4640
TC = {n: d for n, _, _, d in TOKB}
CHB = [("Au", 0, 0), ("Az", 2048, 1024), ("Ba", 3072, 2048), ("Bg", 4096, 3072), ("Bz", 5120, 4096),
       ("Cz", 7680, 5120), ("Dq", 8704, 6144), ("DkT", 9728, 7168), ("Do", 11776, 8192), ("Dz", 12800, 9216)]
NCHR = 10240
CR = {n: r for n, _, r in CHB}

EXC_ROWS = [512, 512, 512, 512, 4, 60]
EX_HALO_OFF = 0
EX_DT_OFF = 128 * 240


class Res:
    __slots__ = ("name", "w", "r")

    def __init__(self, name):
        self.name = name
        self.w = None
        self.r = {}


class KB:
    CE = ["pe", "act", "dve", "pool"]

    def __init__(self, nc, es):
        self.nc = nc
        self.engs = ["pe", "act", "dve", "pool", "sp"]
        self.q = {e: [] for e in self.engs}
        self.sem = {e: es.enter_context(nc.semaphore("s_" + e)) for e in self.CE}
        self.cnt = {e: 0 for e in self.CE}
        self.known = {e: {} for e in self.engs}
        self.dsem = {"sp": [es.enter_context(nc.semaphore("d_sp%d" % i)) for i in range(16)],
                     "pool": [es.enter_context(nc.semaphore("d_pl%d" % i)) for i in range(8)]}
        self.dcnt = {"sp": 0, "pool": 0}
        self.duse = {k: [0] * len(v) for k, v in self.dsem.items()}
        self.dlast = {k: [None] * len(v) for k, v in self.dsem.items()}
        self.ccsem = es.enter_context(nc.semaphore("s_cc"))
        self.cccnt = 0
        self.cclast = None

    def _wait(self, eng, ev):
        if ev is None:
            return
        key, sem, val = ev
        if self.known[eng].get(key, 0) >= val:
            return
        self.known[eng][key] = val
        self.q[eng].append(("w", sem, val))

    def _deps(self, eng, reads, writes):
        deps = {}

        def add(ev):
            if ev is None:
                return
            k = ev[0]
            if k not in deps or deps[k][2] < ev[2]:
                deps[k] = ev
        for r in reads:
            add(r.w)
        for w in writes:
            add(w.w)
            for ev in w.r.values():
                add(ev)
        for k, ev in deps.items():
            if eng == "pe" and k == "pe":
                continue
            self._wait(eng, ev)

    def _mark(self, ev, reads, writes):
        for r in reads:
            r.r[ev[0]] = ev
        for w in writes:
            w.w = ev
            w.r = {}

    def op(self, eng, fn, reads=(), writes=()):
        self._deps(eng, reads, writes)
        self.cnt[eng] += 1
        ev = (eng, self.sem[eng], self.cnt[eng])
        self.q[eng].append(("o", fn, self.sem[eng]))
        self._mark(ev, reads, writes)
        return ev

    def dma(self, qn, out, in_, reads=(), writes=()):
        self._deps(qn, reads, writes)
        n = len(self.dsem[qn])
        slot = self.dcnt[qn] % n
        self.dcnt[qn] += 1
        self._wait(qn, self.dlast[qn][slot])
        self.duse[qn][slot] += 1
        sem = self.dsem[qn][slot]
        ev = ("%s%d" % (qn, slot), sem, 16 * self.duse[qn][slot])
        self.dlast[qn][slot] = ev
        self.q[qn].append(("d", out, in_, sem))
        self._mark(ev, reads, writes)
        return ev

    def all_events(self, include_cc=True):
        evs = [(e, self.sem[e], self.cnt[e]) for e in self.CE if self.cnt[e] > 0]
        for qn in self.dlast:
            evs += [ev for ev in self.dlast[qn] if ev is not None]
        if include_cc and self.cclast is not None:
            evs.append(self.cclast)
        return evs

    def barrier(self, include_cc=True):
        evs = self.all_events(include_cc)
        for eng in self.engs:
            for ev in evs:
                if eng == "pe" and ev[0] == "pe":
                    continue
                self._wait(eng, ev)

    def collective(self, kind, alu, groups, in_ap, out_ap):
        self.barrier()
        self.cccnt += 1
        self.q["pool"].append(("c", kind, alu, groups, in_ap, out_ap))
        self.cclast = ("cc", self.ccsem, self.cccnt)
        self.barrier()

    def issue_collectives(self, kind, alu, groups, pairs):
        self.barrier()
        for in_ap, out_ap in pairs:
            self.cccnt += 1
            self.q["pool"].append(("c", kind, alu, groups, in_ap, out_ap))
        self.cclast = ("cc", self.ccsem, self.cccnt)

    def replay(self, block):
        def mk(eng):
            def f(e):
                for it in self.q[eng]:
                    k = it[0]
                    if k == "w":
                        e.wait_ge(it[1], it[2])
                    elif k == "o":
                        it[1](e).then_inc(it[2], 1)
                    elif k == "d":
                        e.dma_start(out=it[1], in_=it[2]).then_inc(it[3], 16)
                    elif k == "c":
                        e.collective_compute(it[1], it[2], replica_groups=it[3], ins=[it[4]],
                                             outs=[it[5]]).then_inc(self.ccsem)
            return f
        block.sync(mk("sp"))
        block.gpsimd(mk("pool"))
        block.scalar(mk("act"))
        block.vector(mk("dve"))
        block.tensor(mk("pe"))


class Arena:
    def __init__(self, ap, nfloats):
        self.ap = ap
        self.n = nfloats
        self.top = 0
        self.kb = None

    def mark(self):
        return self.top

    def release(self, m):
        if self.kb is not None:
            self.kb.barrier(include_cc=False)
        self.top = m

    def f32(self, n):
        n4 = (n + 7) // 8 * 8
        assert self.top + n4 <= self.n, ("SBUF arena overflow", self.top, n4, self.n)
        v = self.ap[:, self.top:self.top + n]
        self.top += n4
        return v

    def bf16(self, n):
        nf = (n + 1) // 2
        return self.f32(nf).bitcast(BF16)[:, 0:n]


def build_program(debug=None, stop_after=None, nlayers=DEPTH):
    debug = debug or set()
    nc = bass.Bass("TRN2", target_bir_lowering=False)

    def din(name, shape, dt=F32):
        return nc.dram_tensor(name, list(shape), dt, kind="ExternalInput").ap()

    def dscr(name, shape, dt=F32):
        kind = "ExternalOutput" if name in debug else "Internal"
        return nc.dram_tensor(name, list(shape), dt, kind=kind)

    x_in = din("x", [NL, D])
    ctx_in = din("ctx", [256, D])
    ccT_in = din("ccT", [1024, 2])
    wada_in = din("w_ada_s", [DEPTH, 1024, 3 * D])
    bada_in = din("b_ada", [DEPTH, 3 * D])
    normg_in = din("norm_g", [DEPTH, D])
    win_in = din("w_in", [DEPTH, D, PIN])
    sguw_in = din("sgu_w", [DEPTH, 8, 128, 128])
    sgub_in = din("sgu_b", [DEPTH, 1024])
    sglg_in = din("sgu_ln_g", [DEPTH, 1024])
    sglb_in = din("sgu_ln_b", [DEPTH, 1024])
    convw_in = din("conv_w", [DEPTH, 248, 128])
    convb_in = din("conv_b", [DEPTH, 8, 128])
    cvlg_in = din("conv_ln_g", [DEPTH, 8, 128])
    cvlb_in = din("conv_ln_b", [DEPTH, 8, 128])
    qng_in = din("q_norm_g", [DEPTH, 128])
    kng_in = din("k_norm_g", [DEPTH, 128])
    ib_in = din("mlstm_i_bias", [DEPTH, 16])
    fb_in = din("mlstm_f_bias", [DEPTH, 16])
    mhg_in = din("mh_norm_g", [DEPTH, 8, 128])
    wout_in = din("w_out", [DEPTH, D, D])
    consts_in = din("consts", [128, 5 * 128])
    rope_in = din("rope", [128, 8, 128])
    sel_in = din("sel", [128, 12])
    y_out = nc.dram_tensor("y", [NL, D], F32, kind="ExternalOutput").ap()

    ada_part = nc.dram_tensor("ada_part", [4, 3 * D], F32)
    ada_full = nc.dram_tensor("ada_full", [4, 3 * D], F32)
    dbg_ada = dscr("dbg_ada", [4, 3 * D]) if "dbg_ada" in debug else None
    xs = dscr("xs", [NT, D])
    Ptok = dscr("Ptok", [NT, NTOKC])
    Pch = dscr("Pch", [NCHR, NT])
    Ych = dscr("Ych", [1024, NT])
    exp_bufs = [nc.dram_tensor("exp_buf%d" % c, [EXC_ROWS[c], 512], F32) for c in range(6)]
    gat_bufs = [nc.dram_tensor("gat_buf%d" % c, [4 * EXC_ROWS[c], 512], F32) for c in range(6)]
    dbg_gat = None
    dbg_mixT = dscr("dbg_mixT", [128, 32 * NT], BF16) if "dbg_mixT" in debug else None
    dbg_HT = dscr("dbg_HT", [128, 32 * NT], BF16) if "dbg_HT" in debug else None

    exp_flat = [b_.ap().rearrange("r c -> (r c)") for b_ in exp_bufs]
    gat_flat = [b_.ap().rearrange("r c -> (r c)") for b_ in gat_bufs]

    def exr(c, off, n):
        return exp_flat[c][off:off + n]

    def gar(c, r, off, n):
        o = r * EXC_ROWS[c] * 512 + off
        return gat_flat[c][o:o + n]

    GROUPS = [[0, 1, 2, 3], [4, 5, 6, 7]]

    with ExitStack() as es:
        ARN = 50 * 1024
        arena_t = es.enter_context(nc.sbuf_tensor("arena", [128, ARN], F32))
        A = Arena(arena_t[:, :], ARN)
        psum_t = es.enter_context(nc.psum_tensor("psum", [128, 4096], F32))
        PS = psum_t[:, :]
        kb = KB(nc, es)
        A.kb = kb
        BK = [Res("bank%d" % i) for i in range(8)]

        def bank(i, n=512):
            return PS[:, i * 512:i * 512 + n]

        def v3(ap, a):
            return ap.rearrange("p (a b) -> p a b", a=a)

        CONST = A.f32(4 * 128)
        IDF = CONST[:, 0:128]
        ONF = CONST[:, 128:256]
        TRIF = CONST[:, 256:384]
        TRIB = CONST[:, 384:512]
        IDB = A.bf16(128)
        ONB = A.bf16(128)
        SEL = A.f32(12)
        ROPE = A.f32(8 * 128)
        rCONST = Res("const")
        kb.dma("sp", CONST, consts_in[:, 0:512], writes=[rCONST])
        kb.dma("sp", SEL, sel_in[:, :], writes=[rCONST])
        kb.dma("sp", ROPE, rope_in.rearrange("p a b -> p (a b)"), writes=[rCONST])
        kb.op("dve", lambda e: e.tensor_copy(IDB, IDF), reads=[rCONST], writes=[rCONST])
        kb.op("dve", lambda e: e.tensor_copy(ONB, ONF), reads=[rCONST], writes=[rCONST])
        ROPE3 = v3(ROPE, 8)
        kb.barrier()

        rBIGd = Res("bigd")
        rBIGa = Res("biga")
        MIXd = dscr("MIXd", [32 * 128, NT], BF16)
        MST = [A.bf16(NT) for _ in range(2)]
        rMST = [Res("mst0"), Res("mst1")]
        mst_i = [0]

        def mix_stage():
            i = mst_i[0] % 2
            mst_i[0] += 1
            return MST[i], rMST[i]

        def mix_store(kc, stg, rstg):
            kb.dma("sp", MIXd.ap()[kc * 128:(kc + 1) * 128, 0:NTL_cur[0]], stg[:, 0:NTL_cur[0]], reads=[rstg])
        NTL_cur = [NT]
        dumped = set()

        def dump(name, ap2d, shape, dt=F32, reads=()):
            if name not in debug or name in dumped:
                return
            dumped.add(name)
            dtn = nc.dram_tensor(name, list(shape), dt, kind="ExternalOutput")
            kb.dma("sp", dtn.ap(), ap2d, reads=list(reads))

        def transpose_rows(src, R, dst, bk, rsrc, rdst):
            kb.op("pe", lambda e: e.matmul(bank(bk, R), lhsT=src[0:R, :], rhs=IDF[0:R, 0:R], start=True, stop=True),
                  reads=[rsrc, rCONST], writes=[BK[bk]])
            kb.op("dve", lambda e: e.tensor_copy(dst, bank(bk, R)), reads=[BK[bk]], writes=[rdst])

        m0 = A.mark()
        CCT = A.f32(16)
        rCCT = Res("cct")
        kb.dma("sp", v3(CCT, 8), ccT_in.rearrange("(kc kp) r -> kp kc r", kp=128), writes=[rCCT])
        kb.op("act", lambda e: e.activation(out=CCT, in_=CCT, func=AF.Silu), reads=[rCCT], writes=[rCCT])
        CCT3 = v3(CCT, 8)
        WA = [A.f32(4096) for _ in range(3)]
        rWA = [Res("wa%d" % i) for i in range(3)]
        AST = [A.f32(4096) for _ in range(2)]
        rAST = [Res("ast%d" % i) for i in range(2)]
        it = 0
        for l in range(DEPTH):
            for third in range(3):
                c0 = third * 4096
                for kc in range(8):
                    b = it % 3
                    it += 1
                    kb.dma("sp", WA[b], wada_in[l][kc * 128:(kc + 1) * 128, c0:c0 + 4096], writes=[rWA[b]])
                    for n in range(8):
                        kb.op("pe", (lambda b=b, kc=kc, n=n: lambda e: e.matmul(
                            bank(n)[0:2, :], lhsT=CCT3[:, kc, :], rhs=WA[b][:, n * 512:(n + 1) * 512],
                            start=(kc == 0), stop=(kc == 7)))(), reads=[rCCT, rWA[b]], writes=[BK[n]])
                ab = third % 2
                for n in range(8):
                    if n % 2 == 0:
                        kb.op("act", (lambda ab=ab, n=n: lambda e: e.copy(AST[ab][0:2, n * 512:(n + 1) * 512], bank(n)[0:2, :]))(),
                              reads=[BK[n]], writes=[rAST[ab]])
                    else:
                        kb.op("dve", (lambda ab=ab, n=n: lambda e: e.tensor_copy(AST[ab][0:2, n * 512:(n + 1) * 512], bank(n)[0:2, :]))(),
                              reads=[BK[n]], writes=[rAST[ab]])
                kb.dma("sp", ada_part.ap()[2 * l:2 * l + 2, c0:c0 + 4096], AST[ab][0:2, :], reads=[rAST[ab]])
        kb.collective("AllReduce", ALU.add, GROUPS, ada_part.ap().opt(), ada_full.ap().opt())
        A.release(m0)
        if dbg_ada is not None:
            kb.dma("sp", dbg_ada.ap(), ada_full.ap())
            kb.barrier()
        if stop_after == "ada":
            nlayers = 0

        def emit_layer(l):
            last = (l == DEPTH - 1)
            NTL = NL if last else NT
            TL = NTL // 128
            wl = win_in[l]
            lay_mark = A.mark()

            def tok_src(t, c0, c1, l=l):
                if l == 0:
                    if t < 8:
                        return x_in[t * 128:(t + 1) * 128, c0:c1]
                    return ctx_in[(t - 8) * 128:(t - 7) * 128, c0:c1]
                return xs.ap()[t * 128:(t + 1) * 128, c0:c1]

            R1 = A.f32(128)
            R2 = A.f32(128)
            R3 = A.f32(128)
            R4a = A.f32(128)
            R4b = A.f32(128)
            rR = Res("R")
            af = ada_full.ap()
            srcs1 = [af[2 * l:2 * l + 1, D:2 * D], af[2 * l:2 * l + 1, 0:D],
                     af[2 * l + 1:2 * l + 2, D:2 * D], af[2 * l + 1:2 * l + 2, 0:D]]
            for i, s_ in enumerate(srcs1):
                kb.dma("sp", R1[32 * i:32 * i + 32, :], s_.rearrange("o (a b) -> (o a) b", b=128), writes=[rR])
            srcs2 = [bada_in[l:l + 1, D:2 * D], bada_in[l:l + 1, 0:D], normg_in[l:l + 1, :]]
            for i, s_ in enumerate(srcs2):
                kb.dma("sp", R2[32 * i:32 * i + 32, :], s_.rearrange("o (a b) -> (o a) b", b=128), writes=[rR])
            for i, s_ in enumerate([convb_in[l], cvlg_in[l], cvlb_in[l], mhg_in[l]]):
                kb.dma("sp", R3[8 * i:8 * i + 8, :], s_, writes=[rR])
            kb.dma("sp", R4a[0:124, :], convw_in[l][0:124, :], writes=[rR])
            kb.dma("sp", R4b[0:124, :], convw_in[l][124:248, :], writes=[rR])
            PT1 = A.f32(128)
            PT2 = A.f32(96)
            PT3 = A.f32(32)
            CW = A.f32(248)
            rPT = Res("PT")
            transpose_rows(R1, 128, PT1, 0, rR, rPT)
            transpose_rows(R2, 96, PT2, 1, rR, rPT)
            transpose_rows(R3, 32, PT3, 2, rR, rPT)
            transpose_rows(R4a, 124, CW[:, 0:124], 3, rR, rPT)
            transpose_rows(R4b, 124, CW[:, 124:248], 4, rR, rPT)
            MOD = A.f32(128)
            for j in range(2):
                kb.op("dve", (lambda j=j: lambda e: e.scalar_tensor_tensor(
                    out=MOD[:, 64 * j:64 * j + 32], in0=PT1[:, 64 * j:64 * j + 32], scalar=1.0, in1=PT2[:, 0:32],
                    op0=ALU.add, op1=ALU.add))(), reads=[rPT], writes=[rPT])
                kb.op("dve", (lambda j=j: lambda e: e.tensor_tensor(
                    out=MOD[:, 64 * j:64 * j + 32], in0=MOD[:, 64 * j:64 * j + 32], in1=PT2[:, 64:96],
                    op=ALU.mult))(), reads=[rPT], writes=[rPT])
                kb.op("dve", (lambda j=j: lambda e: e.tensor_tensor(
                    out=MOD[:, 64 * j + 32:64 * j + 64], in0=PT1[:, 64 * j + 32:64 * j + 64], in1=PT2[:, 32:64],
                    op=ALU.add))(), reads=[rPT], writes=[rPT])
            kb.barrier()
            par_mark = A.mark()
            NTL_cur[0] = NTL
            BIG = A.f32(16 * NT)
            BIGB = BIG.bitcast(BF16)
            HT = v3(BIGB, 32)
            WT0 = A.bf16(32 * 512)
            rWT = [Res("wt0"), Res("wt1")]
            tiles = []
            for name, c0, ncol, d0 in TOKB:
                for j in range(0, ncol, 512):
                    tiles.append(("tok", name, c0 + j, min(512, ncol - j), d0 + j))
            for name, c0, r0 in CHB:
                for j in range(0, 1024, 512):
                    tiles.append(("ch", name, c0 + j, 512, r0 + j))
            wsrc = wl.rearrange("(kc kp) n -> kp kc n", kp=128)
            for g_ in range(4):
                kb.dma("pool", v3(WT0, 32)[:, g_ * 8:(g_ + 1) * 8, 0:tiles[0][3]],
                       wsrc[:, g_ * 8:(g_ + 1) * 8, tiles[0][2]:tiles[0][2] + tiles[0][3]], writes=[rWT[0]])
            big_mark = A.mark()

            XT = [A.f32(D) for _ in range(3)]
            rXT = [Res("xt0"), Res("xt1"), Res("xt2")]
            JUNK = A.bf16(D)
            rJ = Res("junk")
            SS = A.f32(8)
            rSS = Res("ss")
            for t in range(10):
                b = t % 3
                kb.dma("sp", XT[b], tok_src(t, 0, D), writes=[rXT[b]])
                kb.op("act", (lambda b=b: lambda e: e.activation(out=JUNK, in_=XT[b], func=AF.Square,
                                                                   accum_out=SS[:, 0:1]))(),
                      reads=[rXT[b]], writes=[rJ, rSS])
                kb.op("dve", lambda e: e.tensor_scalar(SS[:, 1:2], SS[:, 0:1], 1.0 / D, EPS, ALU.mult, ALU.add),
                      reads=[rSS], writes=[rSS])
                kb.op("act", lambda e: e.sqrt(SS[:, 3:4], SS[:, 1:2]), reads=[rSS], writes=[rSS])
                kb.op("dve", lambda e: e.reciprocal(SS[:, 2:3], SS[:, 3:4]), reads=[rSS], writes=[rSS])
                kb.op("dve", (lambda b=b: lambda e: e.tensor_scalar_mul(XT[b], XT[b], SS[:, 2:3]))(),
                      reads=[rXT[b], rSS], writes=[rXT[b]])
                mo = 0 if t < 8 else 64
                for g4 in range(8):
                    bk = g4 % 4
                    for j in range(4):
                        kc = g4 * 4 + j
                        kb.op("pe", (lambda b=b, kc=kc, bk=bk, j=j: lambda e: e.matmul(
                            bank(bk)[:, j * 128:(j + 1) * 128], lhsT=XT[b][:, kc * 128:(kc + 1) * 128], rhs=IDF,
                            start=True, stop=True))(), reads=[rXT[b], rCONST], writes=[BK[bk]])
                    for j in range(4):
                        kc = g4 * 4 + j
                        dst = HT[:, kc, t * 128:(t + 1) * 128]
                        src = bank(bk)[:, j * 128:(j + 1) * 128]
                        if g4 % 2 == 0:
                            kb.op("dve", (lambda dst=dst, src=src, kc=kc, mo=mo: lambda e: e.tensor_scalar(
                                dst, src, MOD[:, mo + kc:mo + kc + 1], MOD[:, mo + 32 + kc:mo + 33 + kc],
                                ALU.mult, ALU.add))(), reads=[BK[bk], rPT], writes=[rBIGd])
                        else:
                            kb.op("act", (lambda dst=dst, src=src, kc=kc, mo=mo: lambda e: e.activation(
                                out=dst, in_=src, func=AF.Identity, scale=MOD[:, mo + kc:mo + kc + 1],
                                bias=MOD[:, mo + 32 + kc:mo + 33 + kc]))(), reads=[BK[bk], rPT], writes=[rBIGa])
            kb.barrier()
            if dbg_HT is not None and l == 0:
                kb.dma("sp", dbg_HT.ap(), BIGB, reads=[rBIGd, rBIGa])
                kb.barrier()
            A.release(big_mark)
            if stop_after == "norm":
                return

            WT = [WT0, A.bf16(32 * 512)]
            STG = [A.f32(NT) for _ in range(2)]
            rSTG = [Res("stg0"), Res("stg1")]
            def load_w(src3, c0, ncol, b):
                w3 = v3(WT[b], 32)
                for g in range(4):
                    kb.dma("pool", w3[:, g * 8:(g + 1) * 8, 0:ncol], src3[:, g * 8:(g + 1) * 8, c0:c0 + ncol],
                           writes=[rWT[b]])

            stg_i = 0
            ev_i = 0
            for ti, (kind, name, c0, ncol, d0) in enumerate(tiles):
                b = ti % 2
                if ti + 1 < len(tiles):
                    load_w(wsrc, tiles[ti + 1][2], tiles[ti + 1][3], 1 - b)
                w3 = v3(WT[b], 32)
                if kind == "tok":
                    nt_ = 8 if (last and name in ("Av", "Cq")) else 10
                    for t in range(nt_):
                        bk = 6 + (t % 2)
                        for kc in range(32):
                            kb.op("pe", (lambda bk=bk, kc=kc, t=t, w3=w3, ncol=ncol: lambda e: e.matmul(
                                bank(bk, ncol), lhsT=HT[:, kc, t * 128:(t + 1) * 128], rhs=w3[:, kc, 0:ncol],
                                start=(kc == 0), stop=(kc == 31)))(), reads=[rBIGd, rBIGa, rWT[b]], writes=[BK[bk]])
                        sb = stg_i % 2
                        stg_i += 1
                        eng = "act" if ev_i % 2 == 0 else "dve"
                        ev_i += 1
                        if eng == "act":
                            kb.op("act", (lambda sb=sb, bk=bk, ncol=ncol: lambda e: e.copy(
                                STG[sb][:, 0:ncol], bank(bk, ncol)))(), reads=[BK[bk]], writes=[rSTG[sb]])
                        else:
                            kb.op("dve", (lambda sb=sb, bk=bk, ncol=ncol: lambda e: e.tensor_copy(
                                STG[sb][:, 0:ncol], bank(bk, ncol)))(), reads=[BK[bk]], writes=[rSTG[sb]])
                        kb.dma("sp", Ptok.ap()[t * 128:(t + 1) * 128, d0:d0 + ncol], STG[sb][:, 0:ncol],
                               reads=[rSTG[sb]])
                else:
                    ntok = NTL
                    tts = [(0, 512), (512, 512)] + ([(1024, 256)] if ntok > 1024 else [])
                    for cb in range(4):
                        bset = 3 * (cb % 2)
                        for kc in range(32):
                            for i, (t0, tn) in enumerate(tts):
                                kb.op("pe", (lambda bset=bset, i=i, kc=kc, cb=cb, t0=t0, tn=tn, w3=w3: lambda e: e.matmul(
                                    bank(bset + i, tn), lhsT=w3[:, kc, cb * 128:(cb + 1) * 128],
                                    rhs=HT[:, kc, t0:t0 + tn], start=(kc == 0), stop=(kc == 31)))(),
                                    reads=[rBIGd, rBIGa, rWT[b]], writes=[BK[bset + i]])
                        sb = stg_i % 2
                        stg_i += 1
                        for i, (t0, tn) in enumerate(tts):
                            eng = "act" if ev_i % 2 == 0 else "dve"
                            ev_i += 1
                            if eng == "act":
                                kb.op("act", (lambda sb=sb, bset=bset, i=i, t0=t0, tn=tn: lambda e: e.copy(
                                    STG[sb][:, t0:t0 + tn], bank(bset + i, tn)))(),
                                    reads=[BK[bset + i]], writes=[rSTG[sb]])
                            else:
                                kb.op("dve", (lambda sb=sb, bset=bset, i=i, t0=t0, tn=tn: lambda e: e.tensor_copy(
                                    STG[sb][:, t0:t0 + tn], bank(bset + i, tn)))(),
                                    reads=[BK[bset + i]], writes=[rSTG[sb]])
                        kb.dma("sp", Pch.ap()[d0 + cb * 128:d0 + (cb + 1) * 128, 0:ntok], STG[sb][:, 0:ntok],
                               reads=[rSTG[sb]])
            kb.barrier()
            A.release(par_mark)
            if stop_after == "gemm1":
                return
            tts = [(0, 512), (512, 512)] + ([(1024, 256)] if NTL > 1024 else [])
            KTC = A.bf16(2 * 256)
            KTC3 = v3(KTC, 2)
            rKTC = Res("ktc")
            IBFB = A.f32(32)
            GQK = A.f32(256)
            rPB = Res("pb")
            kb.dma("sp", IBFB[:, 0:16], ib_in[l:l + 1, :].partition_broadcast(128), writes=[rPB])
            kb.dma("sp", IBFB[:, 16:32], fb_in[l:l + 1, :].partition_broadcast(128), writes=[rPB])
            kb.dma("sp", GQK[:, 0:128], qng_in[l:l + 1, :].partition_broadcast(128), writes=[rPB])
            kb.dma("sp", GQK[:, 128:256], kng_in[l:l + 1, :].partition_broadcast(128), writes=[rPB])
            pers2_mark = A.mark()
            HS = A.f32(8 * NT)
            HS3 = v3(HS, 8)
            rHS = Res("hs")
            GP = A.f32(10 * 5 * 16)
            GP4 = GP.rearrange("p (t k g) -> p t k g", t=10, k=5)
            rGP = Res("gp")
            CS = A.f32(16 * 256)
            CS3 = v3(CS, 16)
            rCS = Res("cs")
            CB = A.bf16(16 * 256)
            CB3 = v3(CB, 16)
            rCB = Res("cb")
            STFB = A.f32(16 * 256)
            rSTFB = Res("stfb")
            pers_mark = A.mark()

            def bc(ap, shape):
                return ap.broadcast_to(list(shape))

            def rsqrt_ops(dst, src, scale, tmp, rr, rw):
                kb.op("dve", lambda e: e.tensor_scalar(tmp, src, scale, EPS, ALU.mult, ALU.add), reads=rr, writes=rw)
                kb.op("act", lambda e: e.sqrt(tmp, tmp), reads=rw, writes=rw)
                kb.op("dve", lambda e: e.reciprocal(dst, tmp), reads=rw, writes=rw)

            def rope_ops(src, dst, H, t, T1, T2, rs, rd, rt):
                s5 = src.rearrange("p (h a c j) -> p h a c j", h=H, a=2, c=2)
                d5 = dst.rearrange("p (h a c j) -> p h a c j", h=H, a=2, c=2)
                x1, x2 = s5[:, :, :, 0, :], s5[:, :, :, 1, :]
                o1, o2 = d5[:, :, :, 0, :], d5[:, :, :, 1, :]
                cs = bc(ROPE3[:, t, 0:64].rearrange("p (a j) -> p a j", a=2).unsqueeze(1), [128, H, 2, 32])
                sn = bc(ROPE3[:, t, 64:128].rearrange("p (a j) -> p a j", a=2).unsqueeze(1), [128, H, 2, 32])
                t1 = T1.rearrange("p (h a j) -> p h a j", h=H, a=2)
                t2 = T2.rearrange("p (h a j) -> p h a j", h=H, a=2)
                kb.op("dve", lambda e: e.tensor_tensor(o1, x1, cs, ALU.mult), reads=[rs, rCONST], writes=[rd])
                kb.op("pool", lambda e: e.tensor_tensor(t1, x2, sn, ALU.mult), reads=[rs, rCONST], writes=[rt])
                kb.op("dve", lambda e: e.tensor_tensor(o1, o1, t1, ALU.subtract), reads=[rt], writes=[rd])
                kb.op("dve", lambda e: e.tensor_tensor(o2, x2, cs, ALU.mult), reads=[rs, rCONST], writes=[rd])
                kb.op("pool", lambda e: e.tensor_tensor(t2, x1, sn, ALU.mult), reads=[rs, rCONST], writes=[rt])
                kb.op("dve", lambda e: e.tensor_tensor(o2, o2, t2, ALU.add), reads=[rt], writes=[rd])

            def qk_norm_rope(src, H, gq, t, NRM, ROT, TS, SSQ, T1, T2, rsrc, rw):
                kb.op("act", lambda e: e.activation(out=TS, in_=src, func=AF.Square), reads=[rsrc], writes=[rw])
                kb.op("dve", lambda e: e.tensor_reduce(out=SSQ[:, 0:H], in_=v3(TS, H), axis=AX.X, op=ALU.add),
                      reads=[rw], writes=[rw])
                rsqrt_ops(SSQ[:, 16:16 + H], SSQ[:, 0:H], 1.0 / 128, SSQ[:, 8:8 + H], [rw], [rw])
                kb.op("dve", lambda e: e.tensor_tensor(v3(NRM, H), v3(src, H), bc(SSQ[:, 16:16 + H].unsqueeze(2), [128, H, 128]),
                                                       ALU.mult), reads=[rsrc, rw], writes=[rw])
                kb.op("dve", lambda e: e.tensor_tensor(v3(NRM, H), v3(NRM, H), bc(gq.unsqueeze(1), [128, H, 128]), ALU.mult),
                      reads=[rw, rPB], writes=[rw])
                if t < 8:
                    rope_ops(NRM, ROT, H, t, T1, T2, rw, rw, rw)
                else:
                    kb.op("dve", lambda e: e.tensor_copy(ROT, NRM), reads=[rw], writes=[rw])

            GG = A.f32(320)
            XF = A.f32(160)
            IG = A.f32(160)
            LF = A.f32(160)
            rGG = Res("gg")
            G5 = GG.rearrange("p (t d i h) -> p t d i h", t=10, d=2, i=2)
            x4 = lambda ap: ap.rearrange("p (t d h) -> p t d h", t=10, d=2)
            kb.dma("sp", v3(GG, 10), Ptok.ap()[:, 4608:4640].rearrange("(t p) c -> p t c", p=128), writes=[rGG])
            kb.op("dve", lambda e: e.tensor_tensor(x4(XF), G5[:, :, :, 1, :], bc(v3(IBFB[:, 16:32], 2).unsqueeze(1), [128, 10, 2, 8]),
                                                   ALU.add), reads=[rGG, rPB], writes=[rGG])
            kb.op("dve", lambda e: e.tensor_tensor(x4(IG), G5[:, :, :, 0, :], bc(v3(IBFB[:, 0:16], 2).unsqueeze(1), [128, 10, 2, 8]),
                                                   ALU.add), reads=[rGG, rPB], writes=[rGG])
            kb.op("act", lambda e: e.activation(out=LF, in_=XF, func=AF.Exp, scale=-1.0), reads=[rGG], writes=[rGG])
            kb.op("dve", lambda e: e.tensor_scalar_add(LF, LF, 1.0), reads=[rGG], writes=[rGG])
            kb.op("act", lambda e: e.activation(out=LF, in_=LF, func=AF.Ln), reads=[rGG], writes=[rGG])
            kb.op("dve", lambda e: e.tensor_scalar_mul(LF, LF, -1.0), reads=[rGG], writes=[rGG])
            LF3 = v3(LF, 10)
            for t in range(10):
                kb.op("pe", (lambda t=t: lambda e: e.matmul(bank(1)[:, t * 32:t * 32 + 8], lhsT=TRIF, rhs=LF3[:, t, 0:8],
                                                            start=True, stop=True))(), reads=[rGG, rCONST], writes=[BK[1]])
                kb.op("pe", (lambda t=t: lambda e: e.matmul(bank(1)[:, t * 32 + 8:t * 32 + 16], lhsT=TRIB, rhs=LF3[:, t, 8:16],
                                                            start=True, stop=True))(), reads=[rGG, rCONST], writes=[BK[1]])
                kb.op("pe", (lambda t=t: lambda e: e.matmul(bank(1)[:, t * 32 + 16:t * 32 + 32], lhsT=ONF, rhs=LF3[:, t, 0:16],
                                                            start=True, stop=True))(), reads=[rGG, rCONST], writes=[BK[1]])
            PS1 = v3(bank(1, 320), 10)
            kb.op("dve", lambda e: e.tensor_copy(GP4[:, :, 0, :], PS1[:, :, 0:16]), reads=[BK[1]], writes=[rGP])
            kb.op("dve", lambda e: e.tensor_copy(GP4[:, :, 3, :], PS1[:, :, 16:32]), reads=[BK[1]], writes=[rGP])
            kb.op("dve", lambda e: e.scalar_tensor_tensor(out=GP4[:, :, 1, :], in0=v3(IG, 10), scalar=LNSC, in1=GP4[:, :, 0, :],
                                                          op0=ALU.add, op1=ALU.subtract), reads=[rGG, rGP], writes=[rGP])
            kb.op("dve", lambda e: e.tensor_tensor(v3(XF, 10), GP4[:, :, 3, :], GP4[:, :, 1, :], ALU.add), reads=[rGP], writes=[rGG])
            kb.op("act", lambda e: e.activation(out=GP4[:, :, 2, :], in_=v3(XF, 10), func=AF.Exp), reads=[rGG], writes=[rGP])
            kb.op("act", lambda e: e.activation(out=GP4[:, :, 4, :], in_=GP4[:, :, 3, :], func=AF.Exp), reads=[rGP], writes=[rGP])
            dump("d_GP", GP, [128, 800], reads=[rGP])
            A.release(pers_mark)

            KTOK = [A.f32(1024) for _ in range(2)]
            VTOK = [A.f32(1024) for _ in range(2)]
            rKTOK = [Res("ktok0"), Res("ktok1")]
            rVTOK = [Res("vtok0"), Res("vtok1")]
            KTLs = [A.bf16(1024) for _ in range(2)]
            rKTLs = [Res("ktl0"), Res("ktl1")]
            VA = [A.bf16(8 * 256) for _ in range(2)]
            rVA = [Res("va0"), Res("va1")]
            QTb = A.bf16(1024)
            KTb = A.bf16(1024)
            rQKb = Res("qkb")
            DG = A.f32(1024)
            rDG = Res("dg")
            EB = A.f32(1024)
            rEB = Res("eb")
            WTm = A.f32(1024)
            rWTm = Res("wtm")
            PTb = A.bf16(1024)
            rPTb = Res("ptb")
            QTL = A.bf16(1024)
            rQTL = Res("qtl")
            DN = A.f32(1024)
            rDN = Res("dn")
            TMPH = A.f32(1024)
            rTMPH = Res("tmph")
            for b in range(2):
                kb.op("pool", (lambda b=b: lambda e: e.memset(v3(VA[b], 8)[:, :, 128:256], 1.0))(), writes=[rVA[b]])
            scan_ctr = [0]

            def scan(chunks, dr, emit):
                g0 = dr * 8
                MASK = TRIF if dr == 0 else TRIB
                for t in chunks:
                    b = scan_ctr[0] % 2
                    scan_ctr[0] += 1
                    va3 = v3(VA[b], 8)
                    KTL = KTLs[b]
                    rKTL = rKTLs[b]
                    stb = 2 if (emit or b == 0) else 4
                    kb.dma("sp", KTOK[b], Ptok.ap()[t * 128:(t + 1) * 128, 2560:3584], writes=[rKTOK[b]])
                    kb.dma("sp", VTOK[b], Ptok.ap()[t * 128:(t + 1) * 128, 3584:4608], writes=[rVTOK[b]])
                    kb.op("dve", (lambda b=b, t=t, KTL=KTL: lambda e: e.tensor_tensor(
                        v3(KTL, 8), v3(KTOK[b], 8), bc(GP4[:, t, 2, g0:g0 + 8].unsqueeze(2), [128, 8, 128]), ALU.mult))(),
                        reads=[rKTOK[b], rGP], writes=[rKTL])
                    kb.op("act", (lambda b=b, va3=va3: lambda e: e.copy(va3[:, :, 0:128], v3(VTOK[b], 8)))(),
                          reads=[rVTOK[b]], writes=[rVA[b]])
                    if emit:
                        kb.dma("pool", v3(QTb, 8), Pch.ap()[CR["Dq"]:CR["Dq"] + 1024, t * 128:(t + 1) * 128].rearrange(
                            "(h d) t -> d h t", d=128), writes=[rQKb])
                        kb.dma("pool", v3(KTb, 8), Pch.ap()[CR["DkT"]:CR["DkT"] + 1024, t * 128:(t + 1) * 128].rearrange(
                            "(h d) t -> d h t", d=128), writes=[rQKb])
                        for h in range(8):
                            kb.op("pe", (lambda h=h: lambda e: e.matmul(
                                PS[:, h * 128:(h + 1) * 128], lhsT=v3(KTb, 8)[:, h, :], rhs=v3(QTb, 8)[:, h, :],
                                start=True, stop=True))(), reads=[rQKb], writes=[BK[h // 4]])
                        kb.op("dve", (lambda t=t: lambda e: e.tensor_tensor(
                            v3(DG, 8), bc(IDF.unsqueeze(1), [128, 8, 128]),
                            bc(GP4[:, t, 0, g0:g0 + 8].unsqueeze(2), [128, 8, 128]), ALU.mult))(),
                            reads=[rGP, rCONST], writes=[rDG])
                        for j in range(2):
                            kb.op("pe", (lambda j=j: lambda e: e.matmul(
                                PS[:, 1024 + j * 512:1536 + j * 512], lhsT=ONF, rhs=DG[:, j * 512:(j + 1) * 512],
                                start=True, stop=True))(), reads=[rDG, rCONST], writes=[BK[2 + j]])
                            kb.op("act", (lambda j=j: lambda e: e.activation(
                                out=EB[:, j * 512:(j + 1) * 512], in_=PS[:, 1024 + j * 512:1536 + j * 512], func=AF.Exp))(),
                                reads=[BK[2 + j]], writes=[rEB])
                        for h in range(8):
                            kb.op("act", (lambda h=h, t=t: lambda e: e.activation(
                                out=WTm[:, h * 128:(h + 1) * 128], in_=PS[:, 1024 + h * 128:1152 + h * 128], func=AF.Exp,
                                bias=GP4[:, t, 1, g0 + h:g0 + h + 1]))(), reads=[BK[2 + h // 4], rGP], writes=[rWTm])
                        kb.op("pool", lambda e: e.tensor_tensor(v3(WTm, 8), v3(WTm, 8), bc(MASK.unsqueeze(1), [128, 8, 128]),
                                                                ALU.mult), reads=[rCONST], writes=[rWTm])
                        kb.op("dve", lambda e: e.tensor_tensor(PTb, PS[:, 0:1024], WTm, ALU.mult),
                              reads=[BK[0], BK[1], rWTm], writes=[rPTb])
                        kb.op("pool", lambda e: e.tensor_tensor(QTL, QTb, EB, ALU.mult), reads=[rQKb, rEB], writes=[rQTL])
                        for h in range(8):
                            kb.op("pe", (lambda h=h, va3=va3: lambda e: e.matmul(
                                PS[:, 2048 + h * 128:2176 + h * 128], lhsT=va3[:, h, 0:128], rhs=v3(PTb, 8)[:, h, :],
                                start=True, stop=False))(), reads=[rVA[b], rPTb], writes=[BK[4 + h // 4]])
                            kb.op("pe", (lambda h=h: lambda e: e.matmul(
                                PS[:, 2048 + h * 128:2176 + h * 128], lhsT=CB3[:, g0 + h, 0:128], rhs=v3(QTL, 8)[:, h, :],
                                start=False, stop=True))(), reads=[rCB, rQTL], writes=[BK[4 + h // 4]])
                        for h in range(8):
                            kb.op("pe", (lambda h=h: lambda e: e.matmul(
                                PS[:, 3072 + h * 128:3200 + h * 128], lhsT=ONB, rhs=v3(PTb, 8)[:, h, :],
                                start=True, stop=False))(), reads=[rCONST, rPTb], writes=[BK[6 + h // 4]])
                            kb.op("pe", (lambda h=h: lambda e: e.matmul(
                                PS[:, 3072 + h * 128:3200 + h * 128], lhsT=CB3[:, g0 + h, 128:256], rhs=v3(QTL, 8)[:, h, :],
                                start=False, stop=True))(), reads=[rCB, rQTL], writes=[BK[6 + h // 4]])
                        kb.op("act", lambda e: e.activation(out=DN, in_=PS[:, 3072:4096], func=AF.Abs),
                              reads=[BK[6], BK[7]], writes=[rDN])
                        kb.op("dve", lambda e: e.tensor_scalar_max(DN, DN, 1.0), reads=[rDN], writes=[rDN])
                        kb.op("dve", lambda e: e.reciprocal(DN, DN), reads=[rDN], writes=[rDN])
                        hs_sl = HS3[:, :, t * 128:(t + 1) * 128]
                        if dr == 0:
                            kb.op("dve", (lambda hs_sl=hs_sl: lambda e: e.tensor_tensor(hs_sl, v3(PS[:, 2048:3072], 8), v3(DN, 8),
                                                                                      ALU.mult))(),
                                  reads=[BK[4], BK[5], rDN], writes=[rHS])
                        else:
                            kb.op("dve", lambda e: e.tensor_tensor(TMPH, PS[:, 2048:3072], DN, ALU.mult),
                                  reads=[BK[4], BK[5], rDN], writes=[rTMPH])
                            kb.op("pool", (lambda hs_sl=hs_sl: lambda e: e.tensor_tensor(hs_sl, hs_sl, v3(TMPH, 8), ALU.add))(),
                                  reads=[rTMPH, rHS], writes=[rHS])
                    kb.op("dve", (lambda t=t: lambda e: e.tensor_tensor(
                        CS3[:, g0:g0 + 8, :], CS3[:, g0:g0 + 8, :], bc(GP4[:, t, 4, g0:g0 + 8].unsqueeze(2), [128, 8, 256]),
                        ALU.mult))(), reads=[rGP, rCS], writes=[rCS])
                    for half in range(2):
                        for hh in range(4):
                            h = half * 4 + hh
                            kb.op("pe", (lambda h=h, hh=hh, va3=va3, KTL=KTL, stb=stb: lambda e: e.matmul(
                                PS[:, stb * 512 + hh * 256:stb * 512 + 256 + hh * 256], lhsT=v3(KTL, 8)[:, h, :], rhs=va3[:, h, :],
                                start=True, stop=True))(), reads=[rKTL, rVA[b]], writes=[BK[stb + hh // 2]])
                        kb.op("dve", (lambda half=half, stb=stb: lambda e: e.tensor_tensor(
                            CS3[:, g0 + half * 4:g0 + half * 4 + 4, :], CS3[:, g0 + half * 4:g0 + half * 4 + 4, :],
                            v3(PS[:, stb * 512:stb * 512 + 1024], 4), ALU.add))(), reads=[BK[stb], BK[stb + 1], rCS], writes=[rCS])
                    if emit:
                        kb.op("act", lambda e: e.copy(CB3[:, g0:g0 + 8, :], CS3[:, g0:g0 + 8, :]), reads=[rCS], writes=[rCB])

            def zero_state():
                kb.op("dve", lambda e: e.memset(CS, 0.0), writes=[rCS])
                kb.op("pool", lambda e: e.memset(CB, 0.0), writes=[rCB])

            zero_state()
            scan([8, 9], 0, not last)
            scan([9, 8], 1, not last)
            kb.op("dve", lambda e: e.tensor_copy(STFB, CS), reads=[rCS], writes=[rSTFB])
            dump("d_STFB", STFB, [128, 4096], reads=[rSTFB])
            dump("d_HSctx", HS, [128, 8 * NT], reads=[rHS])
            zero_state()
            scan(list(range(8)), 0, False)
            scan(list(range(7, -1, -1)), 1, False)
            for dr_ in range(2):
                kb.dma("sp", exr(2 + dr_, 0, 128 * 2048).rearrange("(p x) -> p x", p=128), CS[:, dr_ * 2048:(dr_ + 1) * 2048],
                       reads=[rCS])
            DTT = A.f32(16)
            rDTT = Res("dtt")
            kb.op("dve", lambda e: e.tensor_reduce(out=DTT, in_=GP4[:, 0:8, 3, :].rearrange("p t g -> p g t"), axis=AX.X,
                                                   op=ALU.add), reads=[rGP], writes=[rDTT])
            kb.dma("sp", exr(4, 0, 128 * 16).rearrange("(p x) -> p x", p=128), DTT, reads=[rDTT])
            scan_mark = A.mark()

            if stop_after == "pre":
                return
            kb.issue_collectives("AllGather", ALU.bypass, GROUPS,
                                 [(exp_bufs[c_].ap().opt(), gat_bufs[c_].ap().opt()) for c_ in (2, 3, 4)])
            m2_mark = A.mark()
            KV = [A.f32(512) for _ in range(2)]
            rKV = [Res("kv0"), Res("kv1")]
            KTE = A.f32(2 * 1024)
            KTE3 = v3(KTE, 2)
            rKTE = Res("kte")
            KWS = []
            for i_ in range(2):
                KWS.append(dict(KN=A.f32(256), KR=A.f32(256), KTS=A.f32(256), KT1=A.f32(128), KT2=A.f32(128),
                                KRB=A.bf16(256), SSK=A.f32(24), r=Res("kw%d" % i_)))
            for t in range(10):
                b = t % 2
                W_ = KWS[b]
                bk_ = 4 * b
                kb.dma("sp", KV[b], Ptok.ap()[t * 128:(t + 1) * 128, 2048:2560], writes=[rKV[b]])
                qk_norm_rope(KV[b][:, 0:256], 2, GQK[:, 128:256], t, W_["KN"], W_["KR"], W_["KTS"], W_["SSK"], W_["KT1"], W_["KT2"],
                             rKV[b], W_["r"])
                kb.op("act", (lambda W_=W_: lambda e: e.copy(W_["KRB"], W_["KR"]))(), reads=[W_["r"]], writes=[W_["r"]])
                for g in range(2):
                    kb.op("pe", (lambda g=g, W_=W_, bk_=bk_: lambda e: e.matmul(
                        bank(bk_)[:, g * 128:(g + 1) * 128], lhsT=W_["KRB"][:, g * 128:(g + 1) * 128], rhs=IDB,
                        start=True, stop=True))(), reads=[W_["r"], rCONST], writes=[BK[bk_]])
                if t < 8:
                    kb.op("act", (lambda t=t, bk_=bk_: lambda e: e.copy(KTE3[:, :, t * 128:(t + 1) * 128], v3(bank(bk_, 256), 2)))(),
                          reads=[BK[bk_]], writes=[rKTE])
                else:
                    kb.op("act", (lambda t=t, bk_=bk_: lambda e: e.copy(KTC3[:, :, (t - 8) * 128:(t - 7) * 128],
                                                                      v3(bank(bk_, 256), 2)))(), reads=[BK[bk_]], writes=[rKTC])
            kb.dma("sp", exr(0, 0, 128 * 2048).rearrange("(d x) -> d x", d=128), KTE, reads=[rKTE])
            kb.dma("sp", exr(1, 0, 1024 * 256).rearrange("(t c) -> t c", c=256), Ptok.ap()[0:1024, 2304:2560])
            A.release(m2_mark)

            ATb = [A.f32(NT) for _ in range(2)]
            GTb = [A.f32(NT) for _ in range(2)]
            rATb = [Res("at0"), Res("at1")]
            rGTb = [Res("gt0"), Res("gt1")]
            halo_ex = exr(5, 0, 128 * 240).rearrange("(c g j) -> c g j", c=128, g=8)
            for cg in range(8):
                b = cg % 2
                kb.dma("sp", ATb[b][:, 0:NTL], Pch.ap()[CR["Ba"] + cg * 128:CR["Ba"] + (cg + 1) * 128, 0:NTL], writes=[rATb[b]])
                kb.dma("sp", GTb[b][:, 0:NTL], Pch.ap()[CR["Bg"] + cg * 128:CR["Bg"] + (cg + 1) * 128, 0:NTL], writes=[rGTb[b]])
                kb.op("act", (lambda b=b: lambda e: e.activation(out=GTb[b][:, 0:NTL], in_=GTb[b][:, 0:NTL], func=AF.Sigmoid))(),
                      reads=[rGTb[b]], writes=[rGTb[b]])
                kb.op("dve", (lambda b=b: lambda e: e.tensor_tensor(ATb[b][:, 0:NTL], ATb[b][:, 0:NTL], GTb[b][:, 0:NTL], ALU.mult))(),
                      reads=[rGTb[b], rATb[b]], writes=[rATb[b]])
                kb.dma("sp", Ych.ap()[cg * 128:(cg + 1) * 128, 0:NTL], ATb[b][:, 0:NTL], reads=[rATb[b]])
                kb.dma("sp", halo_ex[:, cg, 0:15], ATb[b][:, 0:15], reads=[rATb[b]])
                kb.dma("sp", halo_ex[:, cg, 15:30], ATb[b][:, 1009:1024], reads=[rATb[b]])
            A.release(m2_mark)

            kb.barrier()
            if stop_after == "gather":
                return
            SG = [A.f32(8 * 256) for _ in range(4)]
            rSG = Res("sg")
            DJ = A.f32(64)
            FF = A.f32(8 * 256)
            rFF = Res("ff")
            for r in range(4):
                kb.dma("sp", DJ[:, r * 16:(r + 1) * 16], gar(4, r, 0, 128 * 16).rearrange("(p x) -> p x", p=128), writes=[rSG])
            kb.op("act", lambda e: e.activation(out=DJ, in_=DJ, func=AF.Exp), reads=[rSG], writes=[rSG])
            for dr in range(2):
                for r in range(4):
                    kb.dma("sp", SG[r], gar(2 + dr, r, 0, 128 * 2048).rearrange("(p x) -> p x", p=128), writes=[rSG])
                CSd = CS[:, dr * 2048:(dr + 1) * 2048]
                kb.op("dve", (lambda dr=dr: lambda e: e.tensor_copy(FF, STFB[:, dr * 2048:(dr + 1) * 2048]))(),
                      reads=[rSTFB], writes=[rFF])
                order = [0, 1, 2] if dr == 0 else [3, 2, 1]
                fs = 0 if dr == 0 else 3
                kb.op("dve", (lambda CSd=CSd, fs=fs: lambda e: e.tensor_scalar_mul(CSd, FF, SEL[:, fs:fs + 1]))(),
                      reads=[rFF, rCONST], writes=[rCS])
                for j in order:
                    for h in range(8):
                        kb.op("dve", (lambda j=j, h=h, dr=dr: lambda e: e.scalar_tensor_tensor(
                            out=v3(FF, 8)[:, h, :], in0=v3(FF, 8)[:, h, :],
                            scalar=DJ[:, j * 16 + dr * 8 + h:j * 16 + dr * 8 + h + 1], in1=v3(SG[j], 8)[:, h, :],
                            op0=ALU.mult, op1=ALU.add))(), reads=[rSG, rFF], writes=[rFF])
                    nx = j + 1 if dr == 0 else j - 1
                    kb.op("dve", (lambda CSd=CSd, nx=nx: lambda e: e.scalar_tensor_tensor(
                        out=CSd, in0=FF, scalar=SEL[:, nx:nx + 1], in1=CSd, op0=ALU.mult, op1=ALU.add))(),
                        reads=[rFF, rCONST, rCS], writes=[rCS])
            kb.op("act", lambda e: e.copy(CB, CS), reads=[rCS], writes=[rCB])
            dump("d_INIT", CS, [128, 4096], reads=[rCS])
            kb.issue_collectives("AllGather", ALU.bypass, GROUPS,
                                 [(exp_bufs[c_].ap().opt(), gat_bufs[c_].ap().opt()) for c_ in (0, 1, 5)])
            scan(list(range(8)), 0, True)
            scan(list(range(7, -1, -1)), 1, True)
            dump("d_HS", HS, [128, 8 * NT], reads=[rHS])
            A.release(pers_mark)
            OTb = [A.f32(NT) for _ in range(2)]
            ZTb = [A.f32(NT) for _ in range(2)]
            rOTb = [Res("ot0"), Res("ot1")]
            rZTb = [Res("zt0"), Res("zt1")]
            SQs = [A.f32(NT) for _ in range(2)]
            rSQs = [Res("sq0"), Res("sq1")]
            RSs = [A.f32(NT) for _ in range(2)]
            rRSs = [Res("rs0"), Res("rs1")]
            for h in range(8):
                b = h % 2
                SQ, rSQ, RS, rRS = SQs[b], rSQs[b], RSs[b], rRSs[b]
                kb.dma("sp", OTb[b][:, 0:NTL], Pch.ap()[CR["Do"] + h * 128:CR["Do"] + (h + 1) * 128, 0:NTL], writes=[rOTb[b]])
                kb.dma("sp", ZTb[b][:, 0:NTL], Pch.ap()[CR["Dz"] + h * 128:CR["Dz"] + (h + 1) * 128, 0:NTL], writes=[rZTb[b]])
                kb.op("act", (lambda h=h, SQ=SQ: lambda e: e.activation(out=SQ[:, 0:NTL], in_=HS3[:, h, 0:NTL], func=AF.Square))(),
                      reads=[rHS], writes=[rSQ])
                for i, (t0, tn) in enumerate(tts):
                    bk = 3 * b + i
                    kb.op("pe", (lambda bk=bk, t0=t0, tn=tn, SQ=SQ: lambda e: e.matmul(bank(bk, tn), lhsT=ONF, rhs=SQ[:, t0:t0 + tn],
                                                                                   start=True, stop=True))(),
                          reads=[rSQ, rCONST], writes=[BK[bk]])
                    kb.op("dve", (lambda bk=bk, t0=t0, tn=tn, RS=RS: lambda e: e.tensor_scalar(RS[:, t0:t0 + tn], bank(bk, tn), 1.0 / 128, EPS,
                                                                                         ALU.mult, ALU.add))(),
                          reads=[BK[bk]], writes=[rRS])
                kb.op("act", (lambda RS=RS: lambda e: e.sqrt(RS[:, 0:NTL], RS[:, 0:NTL]))(), reads=[rRS], writes=[rRS])
                kb.op("dve", (lambda RS=RS: lambda e: e.reciprocal(RS[:, 0:NTL], RS[:, 0:NTL]))(), reads=[rRS], writes=[rRS])
                kb.op("dve", (lambda h=h, SQ=SQ, RS=RS: lambda e: e.tensor_tensor(SQ[:, 0:NTL], HS3[:, h, 0:NTL], RS[:, 0:NTL], ALU.mult))(),
                      reads=[rHS, rRS, rSQ], writes=[rSQ])
                kb.op("act", (lambda b=b: lambda e: e.activation(out=OTb[b][:, 0:NTL], in_=OTb[b][:, 0:NTL], func=AF.Sigmoid))(),
                      reads=[rOTb[b]], writes=[rOTb[b]])
                kb.op("act", (lambda b=b: lambda e: e.activation(out=ZTb[b][:, 0:NTL], in_=ZTb[b][:, 0:NTL], func=AF.Silu))(),
                      reads=[rZTb[b]], writes=[rZTb[b]])
                kb.op("pool", (lambda b=b, SQ=SQ: lambda e: e.tensor_tensor(SQ[:, 0:NTL], SQ[:, 0:NTL], OTb[b][:, 0:NTL], ALU.mult))(),
                      reads=[rOTb[b], rSQ], writes=[rSQ])
                stg, rstg = mix_stage()
                kb.op("dve", (lambda b=b, h=h, stg=stg, SQ=SQ: lambda e: e.scalar_tensor_tensor(
                    out=stg[:, 0:NTL], in0=SQ[:, 0:NTL], scalar=PT3[:, 24 + h:25 + h], in1=ZTb[b][:, 0:NTL],
                    op0=ALU.mult, op1=ALU.mult))(), reads=[rSQ, rZTb[b], rPT], writes=[rstg])
                mix_store(24 + h, stg, rstg)
            kb.barrier()
            A.release(pers2_mark)
            LNG = A.f32(1024)
            LNB = A.f32(1024)
            SBI = A.f32(1024)
            rAP = Res("ap")
            kb.dma("sp", LNG, sglg_in[l:l + 1, :].partition_broadcast(128), writes=[rAP])
            kb.dma("sp", LNB, sglb_in[l:l + 1, :].partition_broadcast(128), writes=[rAP])
            kb.dma("sp", SBI, sgub_in[l:l + 1, :].partition_broadcast(128), writes=[rAP])
            WSF = [A.f32(128) for _ in range(2)]
            rWSF = [Res("wsf0"), Res("wsf1")]
            WST = A.bf16(1024)
            rWST = Res("wst")
            for h in range(8):
                b = h % 2
                kb.dma("sp", WSF[b], sguw_in[l, h], writes=[rWSF[b]])
                kb.op("pe", (lambda b=b: lambda e: e.matmul(bank(b, 128), lhsT=WSF[b], rhs=IDF, start=True, stop=True))(),
                      reads=[rWSF[b], rCONST], writes=[BK[b]])
                kb.op("dve", (lambda b=b, h=h: lambda e: e.tensor_copy(WST[:, h * 128:(h + 1) * 128], bank(b, 128)))(),
                      reads=[BK[b]], writes=[rWST])
            VN = A.bf16(TL * 1024)
            VN3 = v3(VN, TL)
            rVN = Res("vn")
            VT = [A.f32(1024) for _ in range(2)]
            rVT = [Res("vt0"), Res("vt1")]
            AJ = A.f32(1024)
            rAJ = Res("aj")
            ST = A.f32(16)
            rST = Res("st")
            for t in range(TL):
                b = t % 2
                kb.dma("sp", VT[b], Ptok.ap()[t * 128:(t + 1) * 128, 0:1024], writes=[rVT[b]])
                kb.op("act", (lambda b=b: lambda e: e.activation(out=AJ, in_=VT[b], func=AF.Copy, accum_out=ST[:, 0:1]))(),
                      reads=[rVT[b]], writes=[rAJ, rST])
                kb.op("act", (lambda b=b: lambda e: e.activation(out=AJ, in_=VT[b], func=AF.Square, accum_out=ST[:, 1:2]))(),
                      reads=[rVT[b]], writes=[rAJ, rST])
                kb.op("dve", lambda e: e.tensor_scalar_mul(ST[:, 2:4], ST[:, 0:2], 1.0 / 1024), reads=[rST], writes=[rST])
                kb.op("dve", lambda e: e.tensor_tensor(ST[:, 4:5], ST[:, 2:3], ST[:, 2:3], ALU.mult), reads=[rST], writes=[rST])
                kb.op("dve", lambda e: e.tensor_tensor(ST[:, 5:6], ST[:, 3:4], ST[:, 4:5], ALU.subtract), reads=[rST], writes=[rST])
                rsqrt_ops(ST[:, 7:8], ST[:, 5:6], 1.0, ST[:, 6:7], [rST], [rST])
                kb.op("dve", (lambda b=b: lambda e: e.tensor_scalar(VT[b], VT[b], ST[:, 2:3], ST[:, 7:8], ALU.subtract, ALU.mult))(),
                      reads=[rST, rVT[b]], writes=[rVT[b]])
                kb.op("pool", (lambda b=b: lambda e: e.tensor_tensor(VT[b], VT[b], LNG, ALU.mult))(), reads=[rAP, rVT[b]], writes=[rVT[b]])
                kb.op("dve", (lambda b=b, t=t: lambda e: e.tensor_tensor(VN3[:, t, :], VT[b], LNB, ALU.add))(),
                      reads=[rAP, rVT[b]], writes=[rVN])
            UT = [A.f32(NT) for _ in range(2)]
            ZT2 = [A.f32(NT) for _ in range(2)]
            rUT = [Res("ut0"), Res("ut1")]
            rZT2 = [Res("zt20"), Res("zt21")]
            TMAs = [A.f32(NT) for _ in range(2)]
            rTMAs = [Res("tma0"), Res("tma1")]
            for h in range(8):
                b = h % 2
                TMA, rTMA = TMAs[b], rTMAs[b]
                po = 1536 * b
                kb.dma("sp", UT[b][:, 0:NTL], Pch.ap()[CR["Au"] + h * 128:CR["Au"] + (h + 1) * 128, 0:NTL], writes=[rUT[b]])
                kb.dma("sp", ZT2[b][:, 0:NTL], Pch.ap()[CR["Az"] + h * 128:CR["Az"] + (h + 1) * 128, 0:NTL], writes=[rZT2[b]])
                kb.op("act", (lambda b=b: lambda e: e.activation(out=ZT2[b][:, 0:NTL], in_=ZT2[b][:, 0:NTL], func=AF.Silu))(),
                      reads=[rZT2[b]], writes=[rZT2[b]])
                kb.op("pool", (lambda b=b: lambda e: e.tensor_tensor(UT[b][:, 0:NTL], UT[b][:, 0:NTL], ZT2[b][:, 0:NTL], ALU.mult))(),
                      reads=[rZT2[b], rUT[b]], writes=[rUT[b]])
                for t in range(TL):
                    kb.op("pe", (lambda h=h, t=t, po=po: lambda e: e.matmul(
                        PS[:, po + t * 128:po + (t + 1) * 128], lhsT=VN3[:, t, h * 128:(h + 1) * 128], rhs=WST[:, h * 128:(h + 1) * 128],
                        start=True, stop=True))(), reads=[rVN, rWST], writes=[BK[3 * b + t // 4]])
                kb.op("dve", (lambda h=h, po=po, TMA=TMA: lambda e: e.tensor_tensor(
                    v3(TMA[:, 0:NTL], TL), v3(PS[:, po:po + NTL], TL), bc(SBI[:, h * 128:(h + 1) * 128].unsqueeze(1), [128, TL, 128]),
                    ALU.add))(), reads=[BK[3 * b], BK[3 * b + 1], BK[3 * b + 2], rAP], writes=[rTMA])
                stg, rstg = mix_stage()
                kb.op("dve", (lambda b=b, stg=stg, TMA=TMA: lambda e: e.tensor_tensor(stg[:, 0:NTL], TMA[:, 0:NTL], UT[b][:, 0:NTL], ALU.mult))(),
                      reads=[rTMA, rUT[b]], writes=[rstg])
                mix_store(h, stg, rstg)
            kb.barrier()
            A.release(pers2_mark)

            HG = A.f32(4 * 240)
            rHG = Res("hg")
            for r in range(4):
                kb.dma("sp", HG[:, r * 240:(r + 1) * 240], gar(5, r, 0, 128 * 240).rearrange("(c x) -> c x", c=128), writes=[rHG])
            HG4 = HG.rearrange("p (r g j) -> p r g j", r=4, g=8)
            LH = A.f32(8 * 15)
            RH = A.f32(8 * 15)
            rLR = Res("lr")
            for r in range(4):
                for (dst, so, j0) in ((LH, 4, 15), (RH, 8, 0)):
                    src = HG4[:, r, :, j0:j0 + 15]
                    if r == 0:
                        kb.op("dve", (lambda dst=dst, src=src, so=so, r=r: lambda e: e.tensor_scalar_mul(
                            v3(dst, 8), src, SEL[:, so + r:so + r + 1]))(), reads=[rHG, rCONST], writes=[rLR])
                    else:
                        kb.op("dve", (lambda dst=dst, src=src, so=so, r=r: lambda e: e.scalar_tensor_tensor(
                            out=v3(dst, 8), in0=src, scalar=SEL[:, so + r:so + r + 1], in1=v3(dst, 8), op0=ALU.mult, op1=ALU.add))(),
                            reads=[rHG, rCONST, rLR], writes=[rLR])
            CONV = A.f32(8 * NT)
            CONV3 = v3(CONV, 8)
            rCONV = [Res("conv%d" % i) for i in range(8)]
            YP = [A.bf16(1054 + 286) for _ in range(2)]
            rYP = [Res("yp0"), Res("yp1")]
            DIAG = [A.bf16(31 * 128) for _ in range(2)]
            rDIAG = [Res("diag0"), Res("diag1")]
            CWr = A.f32(248)
            rCWr = Res("cwr")
            kb.op("dve", lambda e: e.tensor_copy(v3(CWr, 8), v3(CW, 31).rearrange("p k g -> p g k")), reads=[rPT], writes=[rCWr])
            SQB = A.f32(NT)
            rSQB = Res("sqb")
            for b in range(2):
                kb.op("pool", (lambda b=b: lambda e: e.memset(YP[b], 0.0))(), writes=[rYP[b]])
            cvb = 0
            for cg in range(8):
                b = cg % 2
                kb.dma("pool", YP[b][:, 15:1039], Ych.ap()[cg * 128:(cg + 1) * 128, 0:1024], writes=[rYP[b]])
                if not last:
                    kb.dma("pool", YP[b][:, 1054 + 15:1054 + 271], Ych.ap()[cg * 128:(cg + 1) * 128, 1024:1280], writes=[rYP[b]])
                kb.op("dve", (lambda b=b, cg=cg: lambda e: e.tensor_copy(YP[b][:, 0:15], v3(LH, 8)[:, cg, :]))(), reads=[rLR], writes=[rYP[b]])
                kb.op("dve", (lambda b=b, cg=cg: lambda e: e.tensor_copy(YP[b][:, 1039:1054], v3(RH, 8)[:, cg, :]))(), reads=[rLR], writes=[rYP[b]])
                kb.op("dve", (lambda b=b, cg=cg: lambda e: e.tensor_tensor(
                    v3(DIAG[b], 31), bc(IDF.unsqueeze(1), [128, 31, 128]), bc(v3(CWr, 8)[:, cg, :].unsqueeze(2), [128, 31, 128]),
                    ALU.mult))(), reads=[rCWr, rCONST], writes=[rDIAG[b]])
                for (ys, co, n) in [(0, 0, 512), (512, 512, 512)] + ([] if last else [(1054, 1024, 256)]):
                    bk = 6 + (cvb % 2)
                    cvb += 1
                    for k in range(31):
                        kb.op("pe", (lambda b=b, bk=bk, k=k, ys=ys, n=n: lambda e: e.matmul(
                            bank(bk, n), lhsT=v3(DIAG[b], 31)[:, k, :], rhs=YP[b][:, ys + k:ys + k + n],
                            start=(k == 0), stop=(k == 30)))(), reads=[rDIAG[b], rYP[b]], writes=[BK[bk]])
                    kb.op("act", (lambda bk=bk, cg=cg, co=co, n=n: lambda e: e.activation(
                        out=CONV3[:, cg, co:co + n], in_=bank(bk, n), func=AF.Identity, bias=PT3[:, cg:cg + 1]))(),
                        reads=[BK[bk], rPT], writes=[rCONV[cg]])
                kb.op("act", (lambda cg=cg: lambda e: e.activation(out=SQB[:, 0:NTL], in_=CONV3[:, cg, 0:NTL], func=AF.Square))(),
                      reads=[rCONV[cg]], writes=[rSQB])
                for i, (t0, tn) in enumerate(tts):
                    kb.op("pe", (lambda cg=cg, i=i, t0=t0, tn=tn: lambda e: e.matmul(
                        bank(i, tn), lhsT=ONF, rhs=CONV3[:, cg, t0:t0 + tn], start=(cg == 0), stop=(cg == 7)))(),
                        reads=[rCONV[cg], rCONST], writes=[BK[i]])
                    kb.op("pe", (lambda cg=cg, i=i, t0=t0, tn=tn: lambda e: e.matmul(
                        bank(3 + i, tn), lhsT=ONF, rhs=SQB[:, t0:t0 + tn], start=(cg == 0), stop=(cg == 7)))(),
                        reads=[rSQB, rCONST], writes=[BK[3 + i]])
            dump("d_CONV", CONV, [128, 8 * NT], reads=rCONV)
            MEAN = A.f32(NT)
            RSTD = A.f32(NT)
            MSQ = A.f32(NT)
            rMS = Res("ms")
            for i, (t0, tn) in enumerate(tts):
                kb.op("dve", (lambda i=i, t0=t0, tn=tn: lambda e: e.tensor_scalar_mul(MEAN[:, t0:t0 + tn], bank(i, tn), 1.0 / 1024))(),
                      reads=[BK[i]], writes=[rMS])
                kb.op("dve", (lambda i=i, t0=t0, tn=tn: lambda e: e.tensor_scalar_mul(RSTD[:, t0:t0 + tn], bank(3 + i, tn), 1.0 / 1024))(),
                      reads=[BK[3 + i]], writes=[rMS])
            kb.op("dve", lambda e: e.tensor_tensor(MSQ[:, 0:NTL], MEAN[:, 0:NTL], MEAN[:, 0:NTL], ALU.mult), reads=[rMS], writes=[rMS])
            kb.op("dve", lambda e: e.tensor_tensor(RSTD[:, 0:NTL], RSTD[:, 0:NTL], MSQ[:, 0:NTL], ALU.subtract), reads=[rMS], writes=[rMS])
            rsqrt_ops(RSTD[:, 0:NTL], RSTD[:, 0:NTL], 1.0, MSQ[:, 0:NTL], [rMS], [rMS])
            dump("d_MEAN", MEAN, [128, NT], reads=[rMS])
            dump("d_RSTD", RSTD, [128, NT], reads=[rMS])
            ZB = [A.f32(NT) for _ in range(2)]
            rZB = [Res("zb0"), Res("zb1")]
            for cg in range(8):
                b = cg % 2
                kb.dma("sp", ZB[b][:, 0:NTL], Pch.ap()[CR["Bz"] + cg * 128:CR["Bz"] + (cg + 1) * 128, 0:NTL], writes=[rZB[b]])
                cv = CONV3[:, cg, 0:NTL]
                kb.op("dve", (lambda cv=cv: lambda e: e.tensor_tensor(cv, cv, MEAN[:, 0:NTL], ALU.subtract))(), reads=[rMS, rCONV[cg]], writes=[rCONV[cg]])
                kb.op("pool", (lambda cv=cv: lambda e: e.tensor_tensor(cv, cv, RSTD[:, 0:NTL], ALU.mult))(), reads=[rMS, rCONV[cg]], writes=[rCONV[cg]])
                kb.op("act", (lambda cv=cv, cg=cg: lambda e: e.activation(out=cv, in_=cv, func=AF.Silu, scale=PT3[:, 8 + cg:9 + cg],
                                                                          bias=PT3[:, 16 + cg:17 + cg]))(), reads=[rPT, rCONV[cg]], writes=[rCONV[cg]])
                kb.op("act", (lambda b=b: lambda e: e.activation(out=ZB[b][:, 0:NTL], in_=ZB[b][:, 0:NTL], func=AF.Silu))(),
                      reads=[rZB[b]], writes=[rZB[b]])
                stg, rstg = mix_stage()
                kb.op("dve", (lambda cv=cv, b=b, stg=stg: lambda e: e.tensor_tensor(stg[:, 0:NTL], cv, ZB[b][:, 0:NTL], ALU.mult))(),
                      reads=[rCONV[cg], rZB[b]], writes=[rstg])
                mix_store(8 + cg, stg, rstg)
            kb.barrier()
            A.release(pers2_mark)

            W2PRE = A.bf16(32 * 512)
            rWT2 = [Res("w2t0"), Res("w2t1")]
            wsrc2 = wout_in[l].rearrange("(kc kp) n -> kp kc n", kp=128)
            for g_ in range(4):
                kb.dma("pool", v3(W2PRE, 32)[:, g_ * 8:(g_ + 1) * 8, :], wsrc2[:, g_ * 8:(g_ + 1) * 8, 0:512], writes=[rWT2[0]])
            w2_mark = A.mark()
            QT = A.bf16(8 * NT)
            QT3 = v3(QT, 8)
            rQT = Res("qt")
            QF = [A.f32(1024) for _ in range(2)]
            rQF = [Res("qf0"), Res("qf1")]
            QWS = []
            for i_ in range(2):
                QWS.append(dict(QN=A.f32(1024), QR=A.f32(1024), QTS=A.f32(1024), QT1=A.f32(512), QT2=A.f32(512),
                                QRB=A.bf16(1024), SSQ=A.f32(24), r=Res("qw%d" % i_)))
            for t in range(TL):
                b = t % 2
                W_ = QWS[b]
                pb0 = 4 * b
                kb.dma("sp", QF[b], Ptok.ap()[t * 128:(t + 1) * 128, 1024:2048], writes=[rQF[b]])
                qk_norm_rope(QF[b], 8, GQK[:, 0:128], t, W_["QN"], W_["QR"], W_["QTS"], W_["SSQ"], W_["QT1"], W_["QT2"], rQF[b], W_["r"])
                kb.op("act", (lambda W_=W_: lambda e: e.copy(W_["QRB"], W_["QR"]))(), reads=[W_["r"]], writes=[W_["r"]])
                for h in range(8):
                    kb.op("pe", (lambda h=h, W_=W_, pb0=pb0: lambda e: e.matmul(
                        PS[:, pb0 * 512 + h * 128:pb0 * 512 + (h + 1) * 128], lhsT=W_["QRB"][:, h * 128:(h + 1) * 128],
                        rhs=IDB, start=True, stop=True))(), reads=[W_["r"], rCONST], writes=[BK[pb0 + h // 4]])
                kb.op("act", (lambda t=t, pb0=pb0: lambda e: e.copy(QT3[:, :, t * 128:(t + 1) * 128],
                                                                  v3(PS[:, pb0 * 512:pb0 * 512 + 1024], 8)))(),
                      reads=[BK[pb0], BK[pb0 + 1]], writes=[rQT])
            KTA = A.bf16(2 * 4352)
            KTA3 = v3(KTA, 2)
            VAL = A.bf16(34 * 256)
            VAL3 = v3(VAL, 34)
            rKVA = Res("kva")
            for r in range(4):
                kb.dma("pool", KTA3[:, :, r * 1024:(r + 1) * 1024],
                       gar(0, r, 0, 128 * 2048).rearrange("(d g t) -> d g t", d=128, g=2), writes=[rKVA])
                kb.dma("pool", VAL3[:, r * 8:(r + 1) * 8, :],
                       gar(1, r, 0, 1024 * 256).rearrange("(kt p c) -> p kt c", p=128, c=256), writes=[rKVA])
            kb.op("dve", lambda e: e.tensor_copy(KTA3[:, :, 4096:4352], KTC3), reads=[rKTC], writes=[rKVA])
            kb.dma("pool", VAL3[:, 32:34, :], Ptok.ap()[1024:1280, 2304:2560].rearrange("(kt p) c -> p kt c", p=128), writes=[rKVA])
            dump("d_KTA", KTA, [128, 2 * 4352], BF16, reads=[rKVA])
            dump("d_VAL", VAL, [128, 34 * 256], BF16, reads=[rKVA])
            dump("d_QT", QT, [128, 8 * NT], BF16, reads=[rQT])
            SZ = [A.f32(NT) for _ in range(2)]
            rSZ = [Res("sz0"), Res("sz1")]
            PTA = [A.bf16(512) for _ in range(4)]
            rPTA = [Res("pta%d" % i) for i in range(4)]
            RL = A.f32(512)
            rRL = Res("rl")
            OA = A.f32(512)
            rOA = Res("oa")
            SC = 128.0 ** -0.5
            pcount = 0
            for h in range(8):
                g = h // 4
                b = h % 2
                kb.dma("sp", SZ[b][:, 0:NTL], Pch.ap()[CR["Cz"] + h * 128:CR["Cz"] + (h + 1) * 128, 0:NTL], writes=[rSZ[b]])
                kb.op("act", (lambda b=b: lambda e: e.activation(out=SZ[b][:, 0:NTL], in_=SZ[b][:, 0:NTL], func=AF.Silu))(),
                      reads=[rSZ[b]], writes=[rSZ[b]])
                stg, rstg = mix_stage()
                qtiles = [(0, 512, list(range(34))), (512, 512, list(range(34)))] + ([] if last else [(1024, 256, [32, 33])])
                for (q0, qn, kts) in qtiles:
                    nk = len(kts)

                    def emit_s(ki, q0=q0, qn=qn, kts=kts, g=g, h=h):
                        sb = 2 + (ki % 3)
                        kt = kts[ki]
                        kb.op("pe", (lambda: lambda e: e.matmul(
                            bank(sb, qn), lhsT=KTA3[:, g, kt * 128:(kt + 1) * 128], rhs=QT3[:, h, q0:q0 + qn],
                            start=True, stop=True))(), reads=[rKVA, rQT], writes=[BK[sb]])
                    emit_s(0)
                    if nk > 1:
                        emit_s(1)
                    for ki, kt in enumerate(kts):
                        sb = 2 + (ki % 3)
                        pb = pcount % 4
                        pcount += 1
                        if ki + 2 < nk:
                            emit_s(ki + 2)
                        kb.op("act", (lambda sb=sb, pb=pb, qn=qn: lambda e: e.activation(
                            out=PTA[pb][:, 0:qn], in_=bank(sb, qn), func=AF.Exp, scale=SC))(), reads=[BK[sb]], writes=[rPTA[pb]])
                        kb.op("pe", (lambda pb=pb, kt=kt, g=g, qn=qn, ki=ki, nk=nk: lambda e: e.matmul(
                            bank(0, qn), lhsT=VAL3[:, kt, g * 128:(g + 1) * 128], rhs=PTA[pb][:, 0:qn],
                            start=(ki == 0), stop=(ki == nk - 1)))(), reads=[rKVA, rPTA[pb]], writes=[BK[0]])
                        kb.op("pe", (lambda pb=pb, qn=qn, ki=ki, nk=nk: lambda e: e.matmul(
                            bank(1, qn), lhsT=ONB, rhs=PTA[pb][:, 0:qn], start=(ki == 0), stop=(ki == nk - 1)))(),
                            reads=[rCONST, rPTA[pb]], writes=[BK[1]])
                    kb.op("dve", (lambda qn=qn: lambda e: e.reciprocal(RL[:, 0:qn], bank(1, qn)))(), reads=[BK[1]], writes=[rRL])
                    kb.op("dve", (lambda qn=qn: lambda e: e.tensor_tensor(OA[:, 0:qn], bank(0, qn), RL[:, 0:qn], ALU.mult))(),
                          reads=[BK[0], rRL], writes=[rOA])
                    kb.op("pool", (lambda qn=qn, q0=q0, b=b, stg=stg: lambda e: e.tensor_tensor(
                        stg[:, q0:q0 + qn], OA[:, 0:qn], SZ[b][:, q0:q0 + qn], ALU.mult))(), reads=[rOA, rSZ[b]], writes=[rstg])
                mix_store(16 + h, stg, rstg)
            kb.barrier()

            A.release(w2_mark)
            BIG2 = A.f32(16 * NT)
            MX = v3(BIG2.bitcast(BF16), 32)
            rMX = Res("mx")
            for kc in range(32):
                kb.dma("sp", MX[:, kc, 0:NTL], MIXd.ap()[kc * 128:(kc + 1) * 128, 0:NTL], writes=[rMX])
            GLn = [A.f32(512) for _ in range(2)]
            GCn = [A.f32(512) for _ in range(2)]
            GBn = [A.f32(512) for _ in range(2)]
            rGn = [Res("gn0"), Res("gn1")]
            WT2 = [W2PRE, A.bf16(32 * 512)]
            XO = [A.f32(512) for _ in range(3)]
            rXO = [Res("xo%d" % i) for i in range(3)]
            OO = [A.f32(512) for _ in range(3)]
            rOO = [Res("oo%d" % i) for i in range(3)]
            def load_w2(n, b):
                w3 = v3(WT2[b], 32)
                for g in range(4):
                    kb.dma("pool", w3[:, g * 8:(g + 1) * 8, :], wsrc2[:, g * 8:(g + 1) * 8, n * 512:(n + 1) * 512], writes=[rWT2[b]])
            oc = 0
            for n in range(8):
                b = n % 2
                if n + 1 < 8:
                    load_w2(n + 1, 1 - b)
                w3 = v3(WT2[b], 32)
                cs_ = slice(2 * D + n * 512, 2 * D + (n + 1) * 512)
                kb.dma("sp", GLn[b], af[2 * l:2 * l + 1, cs_].partition_broadcast(128), writes=[rGn[b]])
                kb.dma("sp", GBn[b], bada_in[l:l + 1, cs_].partition_broadcast(128), writes=[rGn[b]])
                kb.op("dve", (lambda b=b: lambda e: e.tensor_tensor(GLn[b], GLn[b], GBn[b], ALU.add))(), reads=[rGn[b]], writes=[rGn[b]])
                if not last:
                    kb.dma("sp", GCn[b], af[2 * l + 1:2 * l + 2, cs_].partition_broadcast(128), writes=[rGn[b]])
                    kb.op("dve", (lambda b=b: lambda e: e.tensor_tensor(GCn[b], GCn[b], GBn[b], ALU.add))(), reads=[rGn[b]], writes=[rGn[b]])
                for t in range(TL):
                    bk = t % 2
                    ob = oc % 3
                    oc += 1
                    kb.dma("sp", XO[ob], tok_src(t, n * 512, (n + 1) * 512), writes=[rXO[ob]])
                    for kc in range(32):
                        kb.op("pe", (lambda bk=bk, kc=kc, t=t, w3=w3: lambda e: e.matmul(
                            bank(bk), lhsT=MX[:, kc, t * 128:(t + 1) * 128], rhs=w3[:, kc, :], start=(kc == 0), stop=(kc == 31)))(),
                            reads=[rMX, rWT2[b]], writes=[BK[bk]])
                    Gt = GLn[b] if t < 8 else GCn[b]
                    kb.op("dve", (lambda bk=bk, ob=ob, Gt=Gt: lambda e: e.tensor_tensor(
                        OO[ob], bank(bk), Gt, ALU.mult))(), reads=[BK[bk], rGn[b]], writes=[rOO[ob]])
                    kb.op("pool", (lambda ob=ob: lambda e: e.tensor_tensor(OO[ob], OO[ob], XO[ob], ALU.add))(),
                          reads=[rXO[ob], rOO[ob]], writes=[rOO[ob]])
                    if last:
                        dst = y_out[t * 128:(t + 1) * 128, n * 512:(n + 1) * 512]
                    else:
                        dst = xs.ap()[t * 128:(t + 1) * 128, n * 512:(n + 1) * 512]
                    kb.dma("sp", dst, OO[ob], reads=[rOO[ob]])
            kb.barrier()
            A.release(lay_mark)

        for l_ in range(nlayers):
            emit_layer(l_)

        kb.barrier()
        block = es.enter_context(nc.Block())
        kb.replay(block)
    return nc


def LAYER_BODY_2(env):
    pass


def _consts():
    c = np.zeros((128, 5 * 128), np.float32)
    c[:, 0:128] = np.eye(128, dtype=np.float32)
    c[:, 128:256] = 1.0
    s = np.arange(128)[:, None]
    l_ = np.arange(128)[None, :]
    c[:, 256:384] = (s <= l_).astype(np.float32)
    c[:, 384:512] = (s >= l_).astype(np.float32)
    return c


def _rope_table(seg):
    n = np.arange(seg * 1024, (seg + 1) * 1024)
    row = (n // 64).astype(np.float32)
    col = (n % 64).astype(np.float32)
    freq = (10000.0 ** (-np.arange(32, dtype=np.float32) / 32)).astype(np.float32)
    ang = np.stack([row, col], -1)[..., None] * freq
    cs = np.concatenate([np.cos(ang).reshape(1024, 64), np.sin(ang).reshape(1024, 64)], -1).astype(np.float32)
    return np.ascontiguousarray(cs.reshape(8, 128, 128).transpose(1, 0, 2))


def make_in_maps(inputs):
    f = lambda a: np.ascontiguousarray(np.asarray(a, dtype=np.float32))
    x = f(inputs["x"]); c = f(inputs["c"]); ctx = f(inputs["ctx"]); c_ctx = f(inputs["c_ctx"])
    w_ada = f(inputs["w_ada"])
    shared = {
        "b_ada": f(inputs["b_ada"]), "norm_g": f(inputs["norm_g"]), "w_in": f(inputs["w_in"]),
        "sgu_w": f(inputs["sgu_w"]), "sgu_b": f(inputs["sgu_b"]).reshape(DEPTH, 1024),
        "sgu_ln_g": f(inputs["sgu_ln_g"]), "sgu_ln_b": f(inputs["sgu_ln_b"]),
        "conv_w": f(inputs["conv_w"]).reshape(DEPTH, 248, 128),
        "conv_b": f(inputs["conv_b"]).reshape(DEPTH, 8, 128),
        "conv_ln_g": f(inputs["conv_ln_g"]).reshape(DEPTH, 8, 128),
        "conv_ln_b": f(inputs["conv_ln_b"]).reshape(DEPTH, 8, 128),
        "q_norm_g": f(inputs["q_norm_g"]), "k_norm_g": f(inputs["k_norm_g"]),
        "mlstm_i_bias": f(inputs["mlstm_i_bias"]).reshape(DEPTH, 16),
        "mlstm_f_bias": f(inputs["mlstm_f_bias"]).reshape(DEPTH, 16),
        "mh_norm_g": f(inputs["mh_norm_g"]).reshape(DEPTH, 8, 128),
        "w_out": f(inputs["w_out"]), "consts": _consts(),
    }
    maps = []
    for core in range(8):
        b, s = core // 4, core % 4
        m = dict(shared)
        m["x"] = np.ascontiguousarray(x[b, s * 1024:(s + 1) * 1024])
        m["ctx"] = np.ascontiguousarray(ctx[b])
        cc = np.stack([c[b], c_ctx], 0)
        m["ccT"] = np.ascontiguousarray(cc[:, s * 1024:(s + 1) * 1024].T)
        m["w_ada_s"] = np.ascontiguousarray(w_ada[:, s * 1024:(s + 1) * 1024, :])
        m["rope"] = _rope_table(s)
        sel = np.zeros((128, 12), np.float32)
        sel[:, s] = 1.0
        if s - 1 >= 0:
            sel[:, 4 + s - 1] = 1.0
        if s + 1 <= 3:
            sel[:, 8 + s + 1] = 1.0
        m["sel"] = sel
        maps.append(m)
    return maps


def kernel(**inputs):
    nc = build_program()
    maps = make_in_maps(inputs)
    res = run_bass_kernel_spmd(nc, maps, core_ids=list(range(8)))
    out = np.empty((2, 4096, D), np.float32)
    for core in range(8):
        b, s = core // 4, core % 4
        out[b, s * 1024:(s + 1) * 1024] = res.results[core]["y"]
    return out
```
